# Optimizing a Trainium2 kernel written in Bass

```python
import math
import jax, jax.numpy as jnp
from jax import lax
import numpy as np

D_MODEL = 1024
BATCH = 8
SEQ = 2048
DEPTH = 1
DEC_BATCH = 32
DEC_SEQ = 1
PAST_LEN = 8192
PAGE_SIZE = 128

D_MIX = D_MODEL
D_ATT = D_MIX // 2
D_SSM = D_MIX - D_ATT
ATT_HEAD_DIM = 64
N_ATT_HEADS = D_ATT // (2 * ATT_HEAD_DIM)
KD = 2 * ATT_HEAD_DIM
V_HEAD_DIM = 2 * ATT_HEAD_DIM
ROT_DIM = ATT_HEAD_DIM // 4
ROPE_THETA = 500000.0
SSM_GROUP = 16
N_SSM_GROUPS = D_SSM // SSM_GROUP
SSM_STATE = 64
D_FF = 2816
CONV_W = 3
Q_BLOCK = 128
NORM_EPS = 1e-6
SUBLN_EPS = 1e-5
NEG_INF = -1e30
D_PROJ = 3 * D_ATT + D_SSM

kernel_name = "hybrid_diffattn_s5_convffn_step"


def rmsnorm(x, g, eps=NORM_EPS):
    xf = x.astype(jnp.float32)
    r = lax.rsqrt(jnp.mean(xf * xf, axis=-1, keepdims=True) + eps)
    return (xf * r * g.astype(jnp.float32)).astype(x.dtype)


def lambda_init_for(layer_idx):
    return 0.8 - 0.6 * math.exp(-0.3 * layer_idx)


def rope_partial(x, pos):
    half = ROT_DIM // 2
    inv = ROPE_THETA ** (-jnp.arange(0, ROT_DIM, 2, dtype=jnp.float32) / ROT_DIM)
    ang = pos.astype(jnp.float32)[:, None] * inv[None, :]
    cos = jnp.cos(ang)[None, :, None, None, :]
    sin = jnp.sin(ang)[None, :, None, None, :]
    xr = x[..., :ROT_DIM].astype(jnp.float32)
    x1, x2 = xr[..., :half], xr[..., half:]
    rot = jnp.concatenate([x1 * cos - x2 * sin, x2 * cos + x1 * sin], axis=-1)
    return jnp.concatenate([rot.astype(x.dtype), x[..., ROT_DIM:]], axis=-1)


def diff_attend(q, k, v, q_pos, k_pos, lam):
    s = jnp.einsum('bqhcd,bkhcd->bchqk', q.astype(jnp.float32), k.astype(jnp.float32)) * (ATT_HEAD_DIM ** -0.5)
    mask = k_pos[None, :] <= q_pos[:, None]
    s = jnp.where(mask, s, NEG_INF)
    p = jax.nn.softmax(s, axis=-1)
    pd = p[:, 0] - lam * p[:, 1]
    return jnp.einsum('bhqk,bkhe->bqhe', pd, v.astype(jnp.float32))


def prompt_attend(q, k, v, lam):
    B, S = q.shape[0], q.shape[1]
    nb = S // Q_BLOCK
    k_pos = jnp.arange(S, dtype=jnp.int32)
    qb = jnp.moveaxis(q.reshape(B, nb, Q_BLOCK, N_ATT_HEADS, 2, ATT_HEAD_DIM), 1, 0)
    pb = k_pos.reshape(nb, Q_BLOCK)
    o = lax.map(lambda a: diff_attend(a[0], k, v, a[1], k_pos, lam), (qb, pb))
    return jnp.moveaxis(o, 0, 1).reshape(B, S, N_ATT_HEADS, V_HEAD_DIM)


def make_sample_attend(k_past, v_past, past_len):
    def attend(q, k, v, lam):
        T = q.shape[1]
        kk = jnp.concatenate([k_past.astype(k.dtype), k], axis=1)
        vv = jnp.concatenate([v_past.astype(v.dtype), v], axis=1)
        k_pos = jnp.arange(past_len + T, dtype=jnp.int32)
        q_pos = past_len + jnp.arange(T, dtype=jnp.int32)
        return diff_attend(q, kk, vv, q_pos, k_pos, lam)
    return attend


def _complex_scan_combine(e1, e2):
    a1r, a1i, b1r, b1i = e1
    a2r, a2i, b2r, b2i = e2
    ar = a2r * a1r - a2i * a1i
    ai = a2r * a1i + a2i * a1r
    br = a2r * b1r - a2i * b1i + b2r
    bi = a2r * b1i + a2i * b1r + b2i
    return (ar, ai, br, bi)


def ssm_mixer(u, h0_re, h0_im, a_re, a_im, log_dt, b_re, b_im, c_re, c_im, d_skip, w_glu):
    B, S, _ = u.shape
    f32 = jnp.float32
    ug = u.astype(f32).reshape(B, S, N_SSM_GROUPS, SSM_GROUP)
    ar_, ai_ = a_re.astype(f32), a_im.astype(f32)
    dt = jnp.exp(log_dt.astype(f32))[:, None]
    mag = jnp.exp(ar_ * dt)
    lb_re = mag * jnp.cos(ai_ * dt)
    lb_im = mag * jnp.sin(ai_ * dt)
    den = ar_ * ar_ + ai_ * ai_
    nr, ni = lb_re - 1.0, lb_im
    f_re = (nr * ar_ + ni * ai_) / den
    f_im = (ni * ar_ - nr * ai_) / den
    br_, bi_ = b_re.astype(f32), b_im.astype(f32)
    bb_re = f_re[..., None] * br_ - f_im[..., None] * bi_
    bb_im = f_re[..., None] * bi_ + f_im[..., None] * br_
    bu_re = jnp.einsum('bsgc,gnc->bsgn', ug, bb_re)
    bu_im = jnp.einsum('bsgc,gnc->bsgn', ug, bb_im)
    a_full_re = jnp.broadcast_to(lb_re, bu_re.shape)
    a_full_im = jnp.broadcast_to(lb_im, bu_im.shape)
    acr, aci, bcr, bci = lax.associative_scan(_complex_scan_combine, (a_full_re, a_full_im, bu_re, bu_im), axis=1)
    h_re = acr * h0_re[:, None] - aci * h0_im[:, None] + bcr
    h_im = acr * h0_im[:, None] + aci * h0_re[:, None] + bci
    y = (jnp.einsum('bsgn,gcn->bsgc', h_re, c_re.astype(f32))
         - jnp.einsum('bsgn,gcn->bsgc', h_im, c_im.astype(f32))
         + d_skip.astype(f32).reshape(N_SSM_GROUPS, SSM_GROUP) * ug)
    y = jax.nn.gelu(y.reshape(B, S, D_SSM))
    z = y * jax.nn.sigmoid(y @ w_glu.astype(f32))
    return z.astype(u.dtype), h_re[:, -1], h_im[:, -1]


def conv_ffn(x, buf, w_up, conv_w, conv_b, w_down):
    S = x.shape[1]
    h = x @ w_up
    a, g = h[..., :D_FF], h[..., D_FF:]
    ap = jnp.concatenate([buf.astype(a.dtype), a], axis=1)
    c = conv_b + sum(conv_w[j] * ap[:, j:j + S] for j in range(CONV_W))
    out = (jax.nn.silu(c) * g) @ w_down
    return out, ap[:, S:]


def decoder_layer(x, pos, attend, h0_re, h0_im, conv_buf, lambda_init, lw):
    B, S, _ = x.shape
    hn = rmsnorm(x, lw['norm_mix'])
    proj = hn @ lw['w_in']
    q = proj[..., :D_ATT].reshape(B, S, N_ATT_HEADS, 2, ATT_HEAD_DIM)
    k = proj[..., D_ATT:2 * D_ATT].reshape(B, S, N_ATT_HEADS, 2, ATT_HEAD_DIM)
    v = proj[..., 2 * D_ATT:3 * D_ATT].reshape(B, S, N_ATT_HEADS, V_HEAD_DIM)
    u = proj[..., 3 * D_ATT:]
    q = rope_partial(q, pos)
    k = rope_partial(k, pos)
    f32 = jnp.float32
    lam = (jnp.exp(jnp.sum(lw['lambda_q1'].astype(f32) * lw['lambda_k1'].astype(f32)))
           - jnp.exp(jnp.sum(lw['lambda_q2'].astype(f32) * lw['lambda_k2'].astype(f32)))
           + lambda_init)
    o = attend(q, k, v, lam)
    o = rmsnorm(o, lw['subln_g'], SUBLN_EPS) * (1.0 - lambda_init)
    o = o.reshape(B, S, D_ATT).astype(x.dtype)
    z, hT_re, hT_im = ssm_mixer(u, h0_re, h0_im, lw['ssm_a_re'], lw['ssm_a_im'], lw['ssm_log_dt'],
                                lw['ssm_b_re'], lw['ssm_b_im'], lw['ssm_c_re'], lw['ssm_c_im'],
                                lw['ssm_d'], lw['w_glu'])
    x = x + jnp.concatenate([o, z], axis=-1) @ lw['w_o']
    f, new_buf = conv_ffn(rmsnorm(x, lw['norm_ffn']), conv_buf, lw['w_up'], lw['conv_w'], lw['conv_b'], lw['w_down'])
    x = x + f
    return x, k.reshape(B, S, N_ATT_HEADS, KD), v, hT_re, hT_im, new_buf


def setup_inputs(seed: int = 0) -> dict:
    key = jax.random.key(seed)
    ks = jax.random.split(key, 32)
    f32 = jnp.float32
    n_pages = PAST_LEN // PAGE_SIZE
    n_used = DEC_BATCH * n_pages
    n_pool = (n_used * 5 + 3) // 4
    nrm = lambda k, shape, s: jax.random.normal(k, shape, f32) * s
    page_table = jax.random.permutation(ks[0], n_pool)[:n_used].reshape(DEC_BATCH, n_pages).astype(jnp.int32)
    n_idx = jnp.arange(SSM_STATE, dtype=f32)
    return {
        'x_prompt': nrm(ks[1], (BATCH, SEQ, D_MODEL), 1.0),
        'x_sample': nrm(ks[2], (DEC_BATCH, DEC_SEQ, D_MODEL), 1.0),
        'cache_k': nrm(ks[3], (DEPTH, n_pool, PAGE_SIZE, N_ATT_HEADS, KD), 1.0),
        'cache_v': nrm(ks[4], (DEPTH, n_pool, PAGE_SIZE, N_ATT_HEADS, V_HEAD_DIM), 1.0),
        'state_ssm_re': nrm(ks[5], (DEPTH, DEC_BATCH, N_SSM_GROUPS, SSM_STATE), 0.5),
        'state_ssm_im': nrm(ks[6], (DEPTH, DEC_BATCH, N_SSM_GROUPS, SSM_STATE), 0.5),
        'state_conv': nrm(ks[7], (DEPTH, DEC_BATCH, CONV_W - 1, D_FF), 1.0),
        'page_table': page_table,
        'norm_mix': 1.0 + nrm(ks[8], (DEPTH, D_MODEL), 0.02),
        'w_in': nrm(ks[9], (DEPTH, D_MODEL, D_PROJ), D_MODEL ** -0.5),
        'lambda_q1': nrm(ks[10], (DEPTH, ATT_HEAD_DIM), 0.1),
        'lambda_k1': nrm(ks[11], (DEPTH, ATT_HEAD_DIM), 0.1),
        'lambda_q2': nrm(ks[12], (DEPTH, ATT_HEAD_DIM), 0.1),
        'lambda_k2': nrm(ks[13], (DEPTH, ATT_HEAD_DIM), 0.1),
        'subln_g': 1.0 + nrm(ks[14], (DEPTH, V_HEAD_DIM), 0.02),
        'ssm_a_re': -0.5 + nrm(ks[15], (DEPTH, N_SSM_GROUPS, SSM_STATE), 0.01),
        'ssm_a_im': math.pi * n_idx + nrm(ks[16], (DEPTH, N_SSM_GROUPS, SSM_STATE), 0.01),
        'ssm_log_dt': jax.random.uniform(ks[17], (DEPTH, N_SSM_GROUPS), f32, math.log(1e-3), math.log(1e-1)),
        'ssm_b_re': nrm(ks[18], (DEPTH, N_SSM_GROUPS, SSM_STATE, SSM_GROUP), (2.0 * SSM_GROUP) ** -0.5),
        'ssm_b_im': nrm(ks[19], (DEPTH, N_SSM_GROUPS, SSM_STATE, SSM_GROUP), (2.0 * SSM_GROUP) ** -0.5),
        'ssm_c_re': nrm(ks[20], (DEPTH, N_SSM_GROUPS, SSM_GROUP, SSM_STATE), (2.0 * SSM_STATE) ** -0.5),
        'ssm_c_im': nrm(ks[21], (DEPTH, N_SSM_GROUPS, SSM_GROUP, SSM_STATE), (2.0 * SSM_STATE) ** -0.5),
        'ssm_d': nrm(ks[22], (DEPTH, D_SSM), 1.0),
        'w_glu': nrm(ks[23], (DEPTH, D_SSM, D_SSM), D_SSM ** -0.5),
        'w_o': nrm(ks[24], (DEPTH, D_MIX, D_MODEL), D_MIX ** -0.5),
        'norm_ffn': 1.0 + nrm(ks[25], (DEPTH, D_MODEL), 0.02),
        'w_up': nrm(ks[26], (DEPTH, D_MODEL, 2 * D_FF), D_MODEL ** -0.5),
        'conv_w': nrm(ks[27], (DEPTH, CONV_W, D_FF), CONV_W ** -0.5),
        'conv_b': nrm(ks[28], (DEPTH, D_FF), 0.02),
        'w_down': nrm(ks[29], (DEPTH, D_FF, D_MODEL), D_FF ** -0.5),
        'final_norm': 1.0 + nrm(ks[30], (D_MODEL,), 0.02),
    }


def reference(x_prompt, x_sample, cache_k, cache_v, state_ssm_re, state_ssm_im, state_conv, page_table,
              norm_mix, w_in, lambda_q1, lambda_k1, lambda_q2, lambda_k2, subln_g,
              ssm_a_re, ssm_a_im, ssm_log_dt, ssm_b_re, ssm_b_im, ssm_c_re, ssm_c_im, ssm_d, w_glu,
              w_o, norm_ffn, w_up, conv_w, conv_b, w_down, final_norm):
    Bp, Sp = x_prompt.shape[0], x_prompt.shape[1]
    Bs, Ts = x_sample.shape[0], x_sample.shape[1]
    n_pages = page_table.shape[1]
    past_len = n_pages * cache_k.shape[2]
    pos_p = jnp.arange(Sp, dtype=jnp.int32)
    pos_s = past_len + jnp.arange(Ts, dtype=jnp.int32)
    xp, xs = x_prompt, x_sample
    kp_l, vp_l, hrp_l, hip_l, cp_l = [], [], [], [], []
    ks_l, vs_l, hrs_l, his_l, cs_l = [], [], [], [], []
    for i in range(DEPTH):
        lw = {
            'norm_mix': norm_mix[i], 'w_in': w_in[i],
            'lambda_q1': lambda_q1[i], 'lambda_k1': lambda_k1[i],
            'lambda_q2': lambda_q2[i], 'lambda_k2': lambda_k2[i], 'subln_g': subln_g[i],
            'ssm_a_re': ssm_a_re[i], 'ssm_a_im': ssm_a_im[i], 'ssm_log_dt': ssm_log_dt[i],
            'ssm_b_re': ssm_b_re[i], 'ssm_b_im': ssm_b_im[i], 'ssm_c_re': ssm_c_re[i], 'ssm_c_im': ssm_c_im[i],
            'ssm_d': ssm_d[i], 'w_glu': w_glu[i], 'w_o': w_o[i], 'norm_ffn': norm_ffn[i],
            'w_up': w_up[i], 'conv_w': conv_w[i], 'conv_b': conv_b[i], 'w_down': w_down[i],
        }
        lam_init = lambda_init_for(i)
        h0 = jnp.zeros((Bp, N_SSM_GROUPS, SSM_STATE), jnp.float32)
        buf0 = jnp.zeros((Bp, CONV_W - 1, D_FF), xp.dtype)
        xp, kp, vp, hrp, hip, cbp = decoder_layer(xp, pos_p, prompt_attend, h0, h0, buf0, lam_init, lw)
        k_past = cache_k[i][page_table].reshape(Bs, past_len, N_ATT_HEADS, 2, ATT_HEAD_DIM)
        v_past = cache_v[i][page_table].reshape(Bs, past_len, N_ATT_HEADS, V_HEAD_DIM)
        attend_s = make_sample_attend(k_past, v_past, past_len)
        xs, kss, vss, hrs, his, cbs = decoder_layer(xs, pos_s, attend_s,
                                                    state_ssm_re[i].astype(jnp.float32),
                                                    state_ssm_im[i].astype(jnp.float32),
                                                    state_conv[i], lam_init, lw)
        kp_l.append(kp); vp_l.append(vp); hrp_l.append(hrp); hip_l.append(hip); cp_l.append(cbp)
        ks_l.append(kss); vs_l.append(vss); hrs_l.append(hrs); his_l.append(his); cs_l.append(cbs)
    y_prompt = rmsnorm(xp, final_norm)
    y_sample = rmsnorm(xs, final_norm)
    return (y_prompt, y_sample,
            jnp.stack(kp_l), jnp.stack(vp_l), jnp.stack(hrp_l), jnp.stack(hip_l), jnp.stack(cp_l),
            jnp.stack(ks_l), jnp.stack(vs_l), jnp.stack(hrs_l), jnp.stack(his_l), jnp.stack(cs_l))
```

```python
import numpy as np
from contextlib import ExitStack
import concourse.bass as bass
import concourse.mybir as mybir
from concourse.bass_utils import run_bass_kernel_spmd

F32 = mybir.dt.float32
BF16 = mybir.dt.bfloat16
I32 = mybir.dt.int32
ALU = mybir.AluOpType
AF = mybir.ActivationFunctionType
AX = mybir.AxisListType

D = 1024
SEQ = 2048
NSC = 4
NCOL = SEQ + NSC
DPROJ = 2048
DFF = 2816
NFC = DFF // 128
PAGE = 128
NPAGES = 64
PAST = PAGE * NPAGES
EPS = 1e-6
SUBLN_EPS = 1e-5
LAM_INIT = 0.8 - 0.6 * 1.0
TILES = [(128 * i, 128) for i in range(16)] + [(SEQ, NSC)]
BLOCKS = [(512 * i, 512) for i in range(4)] + [(SEQ, NSC)]


class Res:
    __slots__ = ("name", "w", "r")

    def __init__(self, name):
        self.name = name
        self.w = None
        self.r = []


class _Eng:
    def __init__(self, name, eng, sem):
        self.name = name
        self.eng = eng
        self.sem = sem
        self.count = 0
        self.seen = {}


class Sched:
    def __init__(self, nc, es):
        self.nc = nc
        self.es = es
        self.sems = {}
        self.engs = {}
        for name, eng in (("pe", nc.tensor), ("dve", nc.vector), ("act", nc.scalar),
                          ("pool", nc.gpsimd), ("sp", nc.sync)):
            sem = es.enter_context(nc.semaphore("sem_" + name))
            self.sems[name] = sem
            self.engs[name] = _Eng(name, eng, sem)
        self.dma_sems = {}
        self.n_inst = 0
        self.n_wait = 0

    def _dma_sem(self, key):
        if key not in self.dma_sems:
            sem = self.es.enter_context(self.nc.semaphore("dsem_" + key))
            self.sems["d:" + key] = sem
            self.dma_sems[key] = [sem, 0]
        return self.dma_sems[key]

    @staticmethod
    def _deps(reads, writes):
        deps = {}

        def add(d):
            if d is not None and deps.get(d[0], 0) < d[1]:
                deps[d[0]] = d[1]
        for r in reads:
            add(r.w)
        for w in writes:
            add(w.w)
            for rd in w.r:
                add(rd)
        return deps

    def _wait(self, E, deps):
        for k, v in deps.items():
            if E.seen.get(k, 0) >= v:
                continue
            E.eng.wait_ge(self.sems[k], v)
            E.seen[k] = v
            self.n_wait += 1

    @staticmethod
    def _mark(reads, writes, stamp):
        for r in reads:
            r.r.append(stamp)
        for w in writes:
            w.w = stamp
            w.r = []

    def op(self, ename, fn, reads=(), writes=()):
        E = self.engs[ename]
        deps = self._deps(reads, writes)
        if ename == "pe":
            deps.pop("pe", None)
        self._wait(E, deps)
        ins = fn(E.eng)
        E.count += 1
        ins.then_inc(E.sem, 1)
        self._mark(reads, writes, (ename, E.count))
        self.n_inst += 1
        return ins

    def dma(self, ename, fn, skey, reads=(), writes=()):
        E = self.engs[ename]
        self._wait(E, self._deps(reads, writes))
        ds = self._dma_sem(skey)
        ins = fn(E.eng)
        ds[1] += 16
        ins.then_inc(ds[0], 16)
        self._mark(reads, writes, ("d:" + skey, ds[1]))
        self.n_inst += 1
        return ins

    def barrier(self):
        deps = {n: e.count for n, e in self.engs.items() if e.count}
        for k, (sem, c) in self.dma_sems.items():
            if c:
                deps["d:" + k] = c
        for E in self.engs.values():
            d = dict(deps)
            self._wait(E, d)

    def finish(self, ename="sp"):
        E = self.engs[ename]
        deps = {n: e.count for n, e in self.engs.items() if e.count}
        for k, (sem, c) in self.dma_sems.items():
            if c:
                deps["d:" + k] = c
        self._wait(E, deps)


def AP(t, off, dims):
    return bass.AP(t, off, dims)


import math
PI = math.pi


def range_reduce(S, eng, out, a, ki, kf, m, n_part, R_out, R_tmp, reads):
    S.op(eng, lambda e: e.tensor_scalar(out=ki, in0=a, scalar1=1.0 / (2 * PI), scalar2=None, op0=ALU.mult),
         reads=reads, writes=[R_tmp])
    S.op(eng, lambda e: e.tensor_copy(out=kf, in_=ki), reads=[R_tmp], writes=[R_tmp])
    S.op("dve", lambda e: e.scalar_tensor_tensor(out=out, in0=kf, scalar=-2 * PI, in1=a, op0=ALU.mult, op1=ALU.add),
         reads=[R_tmp] + list(reads), writes=[R_out])
    for thr, corr in ((PI, -2 * PI),):
        S.op(eng, lambda e: e.tensor_scalar(out=m, in0=out, scalar1=thr, scalar2=corr, op0=ALU.is_gt, op1=ALU.mult),
             reads=[R_out], writes=[R_tmp])
        S.op(eng, lambda e: e.tensor_tensor(out=out, in0=out, in1=m, op=ALU.add), reads=[R_out, R_tmp], writes=[R_out])
    S.op(eng, lambda e: e.tensor_scalar(out=m, in0=out, scalar1=-PI, scalar2=2 * PI, op0=ALU.is_lt, op1=ALU.mult),
         reads=[R_out], writes=[R_tmp])
    S.op(eng, lambda e: e.tensor_tensor(out=out, in0=out, in1=m, op=ALU.add), reads=[R_out, R_tmp], writes=[R_out])


def sincos(S, eng, s_out, c_out, ang, scr, n_part, R_s, R_c, R_scr, reads, nbias):
    range_reduce(S, eng, scr["y"], ang, scr["ki"], scr["kf"], scr["m"], n_part, R_scr, R_scr, reads)
    S.op("act", lambda e: e.activation(out=s_out, in_=scr["y"], func=AF.Sin), reads=[R_scr], writes=[R_s])
    S.op(eng, lambda e: e.tensor_scalar(out=scr["a"], in0=ang, scalar1=PI / 2, scalar2=None, op0=ALU.add),
         reads=list(reads) + [R_scr], writes=[R_scr])
    range_reduce(S, eng, scr["y"], scr["a"], scr["ki"], scr["kf"], scr["m"], n_part, R_scr, R_scr, [R_scr])
    S.op("act", lambda e: e.activation(out=c_out, in_=scr["y"], func=AF.Sin), reads=[R_scr], writes=[R_c])


def ssm_phase(nc, S, sb, ps, L):
    uT, R_uT, mixT, R_mix, idf, R_id = L["uT"], L["R_uT"], L["mixT"], L["R_mix"], L["idf"], L["R_id"]
    stage = L["stage"]
    with ExitStack() as pz:
        bbr = sb("bbr", [128, 16, 128], BF16, pz)
        bbi = sb("bbi", [128, 16, 128], BF16, pz)
        ccr = sb("ccr", [128, 16, 128], BF16, pz)
        cci = sb("cci", [128, 16, 128], BF16, pz)
        R_bb, R_cc = Res("bb"), Res("cc")
        col = sb("col", [128, 16, 16], F32, pz)
        R_col = Res("col")
        dco = sb("dco", [128, 4], F32, pz)
        h0r = sb("h0r", [128, 16, NSC], F32, pz)
        h0i = sb("h0i", [128, 16, NSC], F32, pz)
        R_h0 = Res("h0")
        tr1 = sb("tr1", [128, 512], F32, pz)
        R_tr1 = Res("tr1")
        wgb = sb("wgb", [128, 4, 512], BF16, pz)
        R_wgb = Res("wgb")
        ygT = sb("ygT", [128, 4, NCOL], BF16, pz)
        R_yg = [Res("yg%d" % b) for b in range(5)]
        hfin = sb("hfin", [128, 2, 16], F32, pz)
        R_hfin = Res("hfin")
        hsmp = sb("hsmp", [128, 2, NSC, 16], F32, pz)
        R_hsmp = Res("hsmp")
        S.dma("sp", lambda e: e.dma_start(out=dco[:], in_=L["dcol"]), "z_c", writes=[R_col])
        S.dma("sp", lambda e: e.dma_start(out=col[:, 0, :], in_=L["acol_re"]), "z_c", writes=[R_col])
        S.dma("sp", lambda e: e.dma_start(out=col[:, 1, :], in_=L["acol_im"]), "z_c", writes=[R_col])
        S.dma("sp", lambda e: e.dma_start(out=col[:, 2, :], in_=L["ldt_col"]), "z_c", writes=[R_col])
        S.dma("sp", lambda e: e.dma_start(out=h0r[:], in_=L["h0c_re"]), "z_h", writes=[R_h0])
        S.dma("sp", lambda e: e.dma_start(out=h0i[:], in_=L["h0c_im"]), "z_h", writes=[R_h0])
        S.dma("sp", lambda e: e.dma_start(out=tr1[:], in_=L["trow1"]), "z_t", writes=[R_tr1])

        with ExitStack() as pc:
            cki = sb("c_ki", [128, 16], I32, pc)
            cs = {k: sb("c_" + k, [128, 16], F32, pc) for k in ("a", "kf", "m", "y")}
            S.op("act", lambda e: e.activation(out=col[:, 2, :], in_=col[:, 2, :], func=AF.Exp), reads=[R_col], writes=[R_col])
            S.op("dve", lambda e: e.tensor_tensor(out=col[:, 3, :], in0=col[:, 1, :], in1=col[:, 2, :], op=ALU.mult), reads=[R_col], writes=[R_col])
            S.op("dve", lambda e: e.tensor_tensor(out=col[:, 9, :], in0=col[:, 0, :], in1=col[:, 2, :], op=ALU.mult), reads=[R_col], writes=[R_col])
            S.op("act", lambda e: e.activation(out=col[:, 4, :], in_=col[:, 9, :], func=AF.Exp), reads=[R_col], writes=[R_col])
            scr = dict(a=cs["a"][:], ki=cki[:], kf=cs["kf"][:], m=cs["m"][:], y=cs["y"][:])
            sincos(S, "dve", col[:, 5, :], col[:, 6, :], col[:, 3, :], scr, 128, R_col, R_col, R_col, [R_col], None)
            S.op("dve", lambda e: e.tensor_tensor(out=col[:, 7, :], in0=col[:, 6, :], in1=col[:, 4, :], op=ALU.mult), reads=[R_col], writes=[R_col])
            S.op("dve", lambda e: e.tensor_tensor(out=col[:, 8, :], in0=col[:, 5, :], in1=col[:, 4, :], op=ALU.mult), reads=[R_col], writes=[R_col])

        S.barrier()
        with ExitStack() as pr:
            bst = [sb("r_bst%d" % j, [128, 16, 128], F32, pr) for j in range(2)]
            tmp = [sb("r_tmp%d" % j, [128, 16, 128], F32, pr) for j in range(2)]
            bcb = [sb("r_bcb%d" % j, [128, 16, 128], BF16, pr) for j in range(2)]
            ptb = ps("r_ptb", [128, 8, 128], BF16, pr)
            wst_ = sb("r_wst", [128, 512], F32, pr)
            Rb, Rt, Rc, Rp = Res("bst"), Res("btmp"), Res("bcb"), Res("ptb")
            S.dma("sp", lambda e: e.dma_start(out=bst[0][:], in_=L["bcol_re"]), "z_b", writes=[Rb])
            S.dma("sp", lambda e: e.dma_start(out=bst[1][:], in_=L["bcol_im"]), "z_b", writes=[Rb])
            cr = lambda k: col[:, k, :]
            tt_ = lambda o, x, y, op: S.op("dve", lambda e: e.tensor_tensor(out=cr(o), in0=cr(x), in1=cr(y), op=op), reads=[R_col], writes=[R_col])
            S.op("dve", lambda e: e.tensor_scalar(out=cr(9), in0=cr(7), scalar1=-1.0, scalar2=None, op0=ALU.add), reads=[R_col], writes=[R_col])
            tt_(11, 0, 0, ALU.mult); tt_(12, 1, 1, ALU.mult); tt_(10, 11, 12, ALU.add)
            S.op("dve", lambda e: e.reciprocal(out=cr(10), in_=cr(10)), reads=[R_col], writes=[R_col])
            tt_(11, 9, 0, ALU.mult); tt_(12, 8, 1, ALU.mult); tt_(11, 11, 12, ALU.add); tt_(13, 11, 10, ALU.mult)
            tt_(11, 8, 0, ALU.mult); tt_(12, 9, 1, ALU.mult); tt_(11, 11, 12, ALU.subtract); tt_(14, 11, 10, ALU.mult)
            fb = lambda k: col[:, k, :].unsqueeze(2).to_broadcast([128, 16, 128])
            S.op("dve", lambda e: e.tensor_tensor(out=tmp[0][:], in0=bst[0][:], in1=fb(13), op=ALU.mult), reads=[Rb, R_col], writes=[Rt])
            S.op("dve", lambda e: e.tensor_tensor(out=tmp[1][:], in0=bst[1][:], in1=fb(14), op=ALU.mult), reads=[Rb, R_col], writes=[Rt])
            S.op("dve", lambda e: e.tensor_tensor(out=bcb[0][:], in0=tmp[0][:], in1=tmp[1][:], op=ALU.subtract), reads=[Rt], writes=[Rc])
            S.op("dve", lambda e: e.tensor_tensor(out=tmp[0][:], in0=bst[1][:], in1=fb(13), op=ALU.mult), reads=[Rb, R_col, Rc], writes=[Rt])
            S.op("dve", lambda e: e.tensor_tensor(out=tmp[1][:], in0=bst[0][:], in1=fb(14), op=ALU.mult), reads=[Rb, R_col, Rc], writes=[Rt])
            S.op("dve", lambda e: e.tensor_tensor(out=bcb[1][:], in0=tmp[0][:], in1=tmp[1][:], op=ALU.add), reads=[Rt], writes=[Rc])
            for comp, dstb in enumerate((bbr, bbi)):
                for half in range(2):
                    for k in range(8):
                        S.op("pe", lambda e: e.transpose(out=ptb[:, k, :], in_=bcb[comp][:, half * 8 + k, :], identity=L["idb"][:, :]),
                             reads=[Rc, R_id], writes=[Rp])
                    S.op("act", lambda e: e.activation(out=dstb[:, half * 8:(half + 1) * 8, :], in_=ptb[:], func=AF.Copy), reads=[Rp], writes=[R_bb])
            S.dma("sp", lambda e: e.dma_start(out=bst[0][:], in_=L["cpad_re"]), "z_b", reads=[Rb], writes=[Rb])
            S.dma("sp", lambda e: e.dma_start(out=bst[1][:], in_=L["cpad_im"]), "z_b", reads=[Rb], writes=[Rb])
            S.op("act", lambda e: e.activation(out=ccr[:], in_=bst[0][:], func=AF.Copy), reads=[Rb], writes=[R_cc])
            S.op("act", lambda e: e.activation(out=cci[:], in_=bst[1][:], func=AF.Copy, scale=-1.0), reads=[Rb], writes=[R_cc])
            for kc in range(4):
                S.dma("pool", lambda e: e.dma_start(out=wgb[:, kc, :], in_=L["w_glu"][kc * 128:(kc + 1) * 128, :]), "z_w", writes=[R_wgb])
            S.barrier()

        with ExitStack() as pm:
            etc = sb("etc", [128, 4, 512], F32, pm)
            ets = sb("ets", [128, 4, 512], F32, pm)
            R_et = [Res("et%d" % k) for k in range(4)]
            eki = sb("e_ki", [128, 512], I32, pm)
            pbr = [ps("pbr%d" % j, [128, 512], F32, pm) for j in range(2)]
            pbi = [ps("pbi%d" % j, [128, 512], F32, pm) for j in range(2)]
            R_pb = [Res("pb%d" % j) for j in range(2)]
            py = [ps("py%d" % j, [128, 512], F32, pm) for j in range(2)]
            R_py = [Res("py%d" % j) for j in range(2)]
            tt = [sb("tt%d" % k, [128, 512], F32, pm) for k in range(4)]
            R_tt = Res("tt")
            w_r = [sb("w_r0", [128, 512], F32, pm)] * 2
            w_i = [sb("w_i0", [128, 512], F32, pm)] * 2
            R_w = [Res("w0")] * 2
            g_r = [sb("g_r%d" % j, [128, 512], F32, pm) for j in range(2)]
            g_i = [sb("g_i%d" % j, [128, 512], F32, pm) for j in range(2)]
            R_g = [Res("g%d" % j) for j in range(2)]
            uu = [sb("uu%d" % k, [128, 512], F32, pm) for k in range(2)]
            R_uu = Res("uu")
            hr = [sb("hr%d" % j, [128, 512], F32, pm) for j in range(2)]
            hi = [sb("hi%d" % j, [128, 512], F32, pm) for j in range(2)]
            R_h = [Res("h%d" % j) for j in range(2)]
            hrb = [sb("hrb%d" % j, [128, 512], BF16, pm) for j in range(2)]
            hib = [sb("hib%d" % j, [128, 512], BF16, pm) for j in range(2)]
            R_hb = [Res("hb%d" % j) for j in range(2)]
            yf = sb("yf", [128, 512], F32, pm)
            R_yf = Res("yf")
            carry = sb("carry", [128, 2, 16], F32, pm)
            R_carry = [Res("carry%d" % p) for p in range(16)]
            S.op("dve", lambda e: e.memset(carry[:], 0.0), writes=R_carry)
            def tables(q):
                for pl in range(4):
                    pr_ = 4 * q + pl
                    S.op("dve", lambda e: e.tensor_scalar(out=w_r[0][:], in0=tr1[:], scalar1=col[:, 3, pr_:pr_ + 1],
                                                          scalar2=None, op0=ALU.mult), reads=[R_tr1, R_col], writes=[R_w[0]])
                    scr = dict(a=tt[0][:], ki=eki[:], kf=tt[1][:], m=tt[2][:], y=tt[3][:])
                    sincos(S, "dve", ets[:, pl, :], etc[:, pl, :], w_r[0][:], scr, 128, R_et[pl], R_et[pl], R_tt, [R_w[0]], None)

            def stA(n, q, b, pl):
                c0, nb = BLOCKS[b]
                pr_, j = 4 * q + pl, n % 2
                S.op("pe", lambda e: e.matmul(pbr[j][:, :nb], lhsT=bbr[:, pr_, :], rhs=uT[:, q, c0:c0 + nb], start=True, stop=True),
                     reads=[R_bb, R_uT[b]], writes=[R_pb[j]])
                S.op("pe", lambda e: e.matmul(pbi[j][:, :nb], lhsT=bbi[:, pr_, :], rhs=uT[:, q, c0:c0 + nb], start=True, stop=True),
                     reads=[R_bb, R_uT[b]], writes=[R_pb[j]])

            def stB(n, q, b, pl):
                c0, nb = BLOCKS[b]
                pr_, j = 4 * q + pl, n % 2
                if b < 4:
                    c_, s_ = etc[:, pl, :], ets[:, pl, :]
                    S.op("dve", lambda e: e.tensor_tensor(out=tt[0][:], in0=pbr[j][:], in1=c_, op=ALU.mult), reads=[R_pb[j], R_et[pl]], writes=[R_tt])
                    S.op("dve", lambda e: e.tensor_tensor(out=tt[1][:], in0=pbi[j][:], in1=s_, op=ALU.mult), reads=[R_pb[j], R_et[pl]], writes=[R_tt])
                    S.op("dve", lambda e: e.tensor_tensor(out=tt[2][:], in0=pbi[j][:], in1=c_, op=ALU.mult), reads=[R_pb[j], R_et[pl]], writes=[R_tt])
                    S.op("dve", lambda e: e.tensor_tensor(out=tt[3][:], in0=pbr[j][:], in1=s_, op=ALU.mult), reads=[R_pb[j], R_et[pl]], writes=[R_tt])
                    S.op("dve", lambda e: e.tensor_tensor(out=w_r[j][:], in0=tt[0][:], in1=tt[1][:], op=ALU.add), reads=[R_tt], writes=[R_w[j]])
                    S.op("dve", lambda e: e.tensor_tensor(out=w_i[j][:], in0=tt[2][:], in1=tt[3][:], op=ALU.subtract), reads=[R_tt], writes=[R_w[j]])
                    S.op("dve", lambda e: e.tensor_tensor_scan(out=g_r[j][:], data0=col[:, 4, pr_:pr_ + 1].to_broadcast([128, 512]), data1=w_r[j][:],
                                                               initial=carry[:, 0, pr_:pr_ + 1], op0=ALU.mult, op1=ALU.add),
                         reads=[R_w[j], R_col, R_carry[pr_]], writes=[R_g[j]])
                    S.op("dve", lambda e: e.tensor_tensor_scan(out=g_i[j][:], data0=col[:, 4, pr_:pr_ + 1].to_broadcast([128, 512]), data1=w_i[j][:],
                                                               initial=carry[:, 1, pr_:pr_ + 1], op0=ALU.mult, op1=ALU.add),
                         reads=[R_w[j], R_col, R_carry[pr_]], writes=[R_g[j]])
                else:
                    lr, li = col[:, 7, pr_:pr_ + 1], col[:, 8, pr_:pr_ + 1]
                    S.op("dve", lambda e: e.tensor_scalar(out=tt[0][:, :nb], in0=h0r[:, pr_, :], scalar1=lr, scalar2=None, op0=ALU.mult), reads=[R_h0, R_col], writes=[R_tt])
                    S.op("dve", lambda e: e.tensor_scalar(out=tt[1][:, :nb], in0=h0i[:, pr_, :], scalar1=li, scalar2=None, op0=ALU.mult), reads=[R_h0, R_col], writes=[R_tt])
                    S.op("dve", lambda e: e.tensor_scalar(out=tt[2][:, :nb], in0=h0i[:, pr_, :], scalar1=lr, scalar2=None, op0=ALU.mult), reads=[R_h0, R_col], writes=[R_tt])
                    S.op("dve", lambda e: e.tensor_scalar(out=tt[3][:, :nb], in0=h0r[:, pr_, :], scalar1=li, scalar2=None, op0=ALU.mult), reads=[R_h0, R_col], writes=[R_tt])
                    S.op("dve", lambda e: e.tensor_tensor(out=tt[0][:, :nb], in0=tt[0][:, :nb], in1=tt[1][:, :nb], op=ALU.subtract), reads=[R_tt], writes=[R_tt])
                    S.op("dve", lambda e: e.tensor_tensor(out=tt[2][:, :nb], in0=tt[2][:, :nb], in1=tt[3][:, :nb], op=ALU.add), reads=[R_tt], writes=[R_tt])
                    S.op("dve", lambda e: e.tensor_tensor(out=hr[j][:, :nb], in0=tt[0][:, :nb], in1=pbr[j][:, :nb], op=ALU.add), reads=[R_tt, R_pb[j]], writes=[R_h[j]])
                    S.op("dve", lambda e: e.tensor_tensor(out=hi[j][:, :nb], in0=tt[2][:, :nb], in1=pbi[j][:, :nb], op=ALU.add), reads=[R_tt, R_pb[j]], writes=[R_h[j]])

            def stC(n, q, b, pl):
                c0, nb = BLOCKS[b]
                pr_, j = 4 * q + pl, n % 2
                if b < 4:
                    c_, s_ = etc[:, pl, :], ets[:, pl, :]
                    S.op("pool", lambda e: e.tensor_tensor(out=uu[0][:], in0=g_r[j][:], in1=c_, op=ALU.mult), reads=[R_g[j], R_et[pl]], writes=[R_uu])
                    S.op("pool", lambda e: e.tensor_tensor(out=uu[1][:], in0=g_i[j][:], in1=s_, op=ALU.mult), reads=[R_g[j], R_et[pl]], writes=[R_uu])
                    S.op("pool", lambda e: e.tensor_tensor(out=hr[j][:], in0=uu[0][:], in1=uu[1][:], op=ALU.subtract), reads=[R_uu], writes=[R_h[j]])
                    S.op("pool", lambda e: e.tensor_tensor(out=uu[0][:], in0=g_i[j][:], in1=c_, op=ALU.mult), reads=[R_g[j], R_et[pl]], writes=[R_uu])
                    S.op("pool", lambda e: e.tensor_tensor(out=uu[1][:], in0=g_r[j][:], in1=s_, op=ALU.mult), reads=[R_g[j], R_et[pl]], writes=[R_uu])
                    S.op("pool", lambda e: e.tensor_tensor(out=hi[j][:], in0=uu[0][:], in1=uu[1][:], op=ALU.add), reads=[R_uu], writes=[R_h[j]])
                    S.op("act", lambda e: e.activation(out=carry[:, 0, pr_:pr_ + 1], in_=hr[j][:, 511:512], func=AF.Copy), reads=[R_h[j]], writes=[R_carry[pr_]])
                    S.op("act", lambda e: e.activation(out=carry[:, 1, pr_:pr_ + 1], in_=hi[j][:, 511:512], func=AF.Copy), reads=[R_h[j]], writes=[R_carry[pr_]])
                    if b == 3:
                        S.op("act", lambda e: e.activation(out=hfin[:, 0, pr_:pr_ + 1], in_=hr[j][:, 511:512], func=AF.Copy), reads=[R_h[j]], writes=[R_hfin])
                        S.op("act", lambda e: e.activation(out=hfin[:, 1, pr_:pr_ + 1], in_=hi[j][:, 511:512], func=AF.Copy), reads=[R_h[j]], writes=[R_hfin])
                else:
                    S.op("act", lambda e: e.activation(out=hsmp[:, 0, :, pr_], in_=hr[j][:, :nb], func=AF.Copy), reads=[R_h[j]], writes=[R_hsmp])
                    S.op("act", lambda e: e.activation(out=hsmp[:, 1, :, pr_], in_=hi[j][:, :nb], func=AF.Copy), reads=[R_h[j]], writes=[R_hsmp])
                S.op("act", lambda e: e.activation(out=hrb[j][:, :nb], in_=hr[j][:, :nb], func=AF.Copy), reads=[R_h[j]], writes=[R_hb[j]])
                S.op("act", lambda e: e.activation(out=hib[j][:, :nb], in_=hi[j][:, :nb], func=AF.Copy), reads=[R_h[j]], writes=[R_hb[j]])

            def stD(n, q, b, pl):
                c0, nb = BLOCKS[b]
                pr_, j = 4 * q + pl, n % 2
                jy = (q * 5 + b) % 2
                S.op("pe", lambda e: e.matmul(py[jy][:, :nb], lhsT=ccr[:, pr_, :], rhs=hrb[j][:, :nb], start=(pl == 0), stop=False),
                     reads=[R_cc, R_hb[j]], writes=[R_py[jy]])
                S.op("pe", lambda e: e.matmul(py[jy][:, :nb], lhsT=cci[:, pr_, :], rhs=hib[j][:, :nb], start=False, stop=(pl == 3)),
                     reads=[R_cc, R_hb[j]], writes=[R_py[jy]])
                if pl == 3:
                    S.op("dve", lambda e: e.scalar_tensor_tensor(out=yf[:, :nb], in0=uT[:, q, c0:c0 + nb], scalar=dco[:, q:q + 1],
                                                                 in1=py[jy][:, :nb], op0=ALU.mult, op1=ALU.add),
                         reads=[R_uT[b], R_col, R_py[jy]], writes=[R_yf])
                    S.op("act", lambda e: e.activation(out=ygT[:, q, c0:c0 + nb], in_=yf[:, :nb], func=AF.Gelu_apprx_tanh),
                         reads=[R_yf], writes=[R_yg[b]])

            its = [(q, b, pl) for q in range(4) for b in range(5) for pl in range(4)]
            tables(0)
            stA(0, *its[0])
            for n, itx in enumerate(its):
                nxt = its[n + 1] if n + 1 < len(its) else None
                if nxt is not None and nxt[0] == itx[0]:
                    stA(n + 1, *nxt)
                stB(n, *itx)
                stC(n, *itx)
                stD(n, *itx)
                if nxt is not None and nxt[0] != itx[0]:
                    tables(nxt[0])
                    stA(n + 1, *nxt)
            for co in range(4):
                for b, (c0, nb) in enumerate(BLOCKS):
                    jy = (co * 5 + b) % 2
                    for kc in range(4):
                        S.op("pe", lambda e: e.matmul(py[jy][:, :nb], lhsT=wgb[:, kc, co * 128:(co + 1) * 128], rhs=ygT[:, kc, c0:c0 + nb],
                                                      start=(kc == 0), stop=(kc == 3)),
                             reads=[R_wgb, R_yg[b]], writes=[R_py[jy]])
                    S.op("act", lambda e: e.activation(out=yf[:, :nb], in_=py[jy][:, :nb], func=AF.Sigmoid), reads=[R_py[jy]], writes=[R_yf])
                    S.op("dve", lambda e: e.tensor_tensor(out=mixT[:, 4 + co, c0:c0 + nb], in0=yf[:, :nb], in1=ygT[:, co, c0:c0 + nb], op=ALU.mult),
                         reads=[R_yf, R_yg[b]], writes=[R_mix[4 + co]])
            pst = py[0]
            ost = sb("ost", [16, 2 + 2 * NSC, 128], F32, pm)
            R_ost = Res("ost")
            srcs = [hfin[:, 0, :], hfin[:, 1, :]] + [hsmp[:, ri, s_, :] for ri in range(2) for s_ in range(NSC)]
            for k, src in enumerate(srcs):
                S.op("pe", lambda e: e.transpose(out=pst[0:16, 0:128], in_=src, identity=idf[:, :]),
                     reads=[R_hfin, R_hsmp, R_id], writes=[R_py[0]])
                S.op("dve", lambda e: e.tensor_copy(out=ost[:, k, :], in_=pst[0:16, 0:128]), reads=[R_py[0]], writes=[R_ost])
            S.dma("sp", lambda e: e.dma_start(out=L["o_hre"], in_=ost[:, 0, :]), "z_o", reads=[R_ost])
            S.dma("sp", lambda e: e.dma_start(out=L["o_him"], in_=ost[:, 1, :]), "z_o", reads=[R_ost])
            S.dma("sp", lambda e: e.dma_start(out=L["o_hres"].rearrange("s q m -> q s m"), in_=ost[:, 2:2 + NSC, :]), "z_o", reads=[R_ost])
            S.dma("sp", lambda e: e.dma_start(out=L["o_hims"].rearrange("s q m -> q s m"), in_=ost[:, 2 + NSC:2 + 2 * NSC, :]), "z_o", reads=[R_ost])
            S.barrier()

def attn_phase(nc, S, sb, ps, L):
    qT, kT, vA, mixT = L["qT"], L["kT"], L["vA"], L["mixT"]
    R_qT, R_kT, R_vA, R_mix = L["R_qT"], L["R_kT"], L["R_vA"], L["R_mix"]
    idb, idf, R_id = L["idb"], L["idf"], L["R_id"]
    with ExitStack() as pa:
        lv = sb("lv", [128, 4, 64], F32, pa)
        lam = sb("lam", [128, 8], F32, pa)
        gs = sb("gs", [128, 128], F32, pa)
        trf = sb("trf", [128, 128], F32, pa)
        tri = sb("tri", [128, 128], BF16, pa)
        R_par = Res("apar")
        S.dma("sp", lambda e: e.dma_start(out=lv[:].rearrange("p a b -> p (a b)"), in_=AP(L["lamv"].tensor, 0, [[0, 128], [1, 256]])), "a_p", writes=[R_par])
        S.dma("sp", lambda e: e.dma_start(out=gs[:], in_=AP(L["gsub"].tensor, 0, [[0, 128], [1, 128]])), "a_p", writes=[R_par])
        S.dma("sp", lambda e: e.dma_start(out=trf[:], in_=L["trimask"]), "a_p", writes=[R_par])
        S.op("dve", lambda e: e.tensor_copy(out=tri[:], in_=trf[:]), reads=[R_par], writes=[R_par])
        S.op("dve", lambda e: e.tensor_scalar(out=gs[:], in0=gs[:], scalar1=1.0 - LAM_INIT, scalar2=None, op0=ALU.mult), reads=[R_par], writes=[R_par])
        S.op("dve", lambda e: e.tensor_tensor(out=lv[:, 0, :], in0=lv[:, 0, :], in1=lv[:, 1, :], op=ALU.mult), reads=[R_par], writes=[R_par])
        S.op("dve", lambda e: e.tensor_tensor(out=lv[:, 2, :], in0=lv[:, 2, :], in1=lv[:, 3, :], op=ALU.mult), reads=[R_par], writes=[R_par])
        S.op("dve", lambda e: e.reduce_sum(out=lam[:, 1:2], in_=lv[:, 0, :], axis=AX.X), reads=[R_par], writes=[R_par])
        S.op("dve", lambda e: e.reduce_sum(out=lam[:, 2:3], in_=lv[:, 2, :], axis=AX.X), reads=[R_par], writes=[R_par])
        S.op("act", lambda e: e.activation(out=lam[:, 1:3], in_=lam[:, 1:3], func=AF.Exp), reads=[R_par], writes=[R_par])
        S.op("dve", lambda e: e.tensor_tensor(out=lam[:, 0:1], in0=lam[:, 1:2], in1=lam[:, 2:3], op=ALU.subtract), reads=[R_par], writes=[R_par])
        S.op("dve", lambda e: e.tensor_scalar(out=lam[:, 0:1], in0=lam[:, 0:1], scalar1=LAM_INIT, scalar2=None, op0=ALU.add), reads=[R_par], writes=[R_par])

        ptp = ps("ptp", [128, 128], BF16, pa)
        R_ptp = Res("ptp")

        def subln_store(o_ap, n, dst_fn, scr):
            junk, st, ob, R_s = scr
            S.op("act", lambda e: e.activation(out=junk[:n, :], in_=o_ap, func=AF.Square, accum_out=st[:n, 0:1]), reads=[R_s], writes=[R_s])
            S.op("dve", lambda e: e.tensor_scalar(out=st[:n, 1:2], in0=st[:n, 0:1], scalar1=1.0 / 128, scalar2=SUBLN_EPS, op0=ALU.mult, op1=ALU.add), reads=[R_s], writes=[R_s])
            S.op("act", lambda e: e.sqrt(out=st[:n, 1:2], in_=st[:n, 1:2]), reads=[R_s], writes=[R_s])
            S.op("dve", lambda e: e.reciprocal(out=st[:n, 1:2], in_=st[:n, 1:2]), reads=[R_s], writes=[R_s])
            S.op("dve", lambda e: e.scalar_tensor_tensor(out=ob[:n, :], in0=o_ap, scalar=st[:n, 1:2], in1=gs[:n, :], op0=ALU.mult, op1=ALU.mult),
                 reads=[R_s, R_par], writes=[R_s])
            S.op("pe", lambda e: e.transpose(out=ptp[:, :n], in_=ob[:n, :], identity=idb[:n, :n]), reads=[R_s, R_id], writes=[R_ptp])
            dst_fn()

        psS = [ps("psS%d" % j, [128, 512], F32, pa) for j in range(2)]
        R_S = [Res("psS%d" % j) for j in range(2)]
        psO = ps("psO", [128, 2, 512], F32, pa)
        _rb = [Res("psO_b0"), Res("psO_b1")]
        R_O = [_rb[0], _rb[0], _rb[0], _rb[1]]
        accO = lambda k: psO[:, k // 3, (k % 3) * 129:(k % 3) * 129 + 129]
        eE = [sb("eE%d" % j, [128, 512], BF16, pa) for j in range(2)]
        R_E = [Res("eE%d" % j) for j in range(2)]
        osb = [sb("osb%d" % g, [128, 2, 4, 129], F32, pa) for g in range(2)]
        R_osb = [[Res("osb%d_%d" % (g, qt)) for qt in range(4)] for g in range(2)]
        of = [sb("of%d" % qt, [128, 128], F32, pa) for qt in range(4)]
        junk = [sb("ajunk%d" % qt, [128, 128], BF16, pa) for qt in range(4)]
        st = [sb("ast%d" % qt, [128, 4], F32, pa) for qt in range(4)]
        ob = [sb("aob%d" % qt, [128, 128], BF16, pa) for qt in range(4)]
        R_scr = [Res("ascr%d" % qt) for qt in range(4)]

        def prompt_group(gi):
            h, QB = gi // 4, gi % 4
            g2 = gi % 2
            nkb = 4 * QB + 4
            for c in range(2):
                def qk(kb):
                    o = kb - 4 * QB
                    qlo = max(o, 0) * 128
                    S.op("pe", lambda e: e.matmul(psS[kb % 2][:, qlo:512], lhsT=kT[64 * c:64 * c + 64, h, kb * 128:(kb + 1) * 128],
                                                  rhs=qT[64 * c:64 * c + 64, h, QB * 512 + qlo:QB * 512 + 512], start=True, stop=True),
                         reads=[R_kT[kb]] + R_qT[4 * QB:4 * QB + 4], writes=[R_S[kb % 2]])
                qk(0)
                for kb in range(nkb):
                    o = kb - 4 * QB
                    qlo = max(o, 0) * 128
                    j = kb % 2
                    if kb + 1 < nkb:
                        qk(kb + 1)
                    if kb % 2 == c:
                        sample_step()
                    S.op("act", lambda e: e.activation(out=eE[j][:, qlo:512], in_=psS[j][:, qlo:512], func=AF.Exp),
                         reads=[R_S[j]], writes=[R_E[j]])
                    if o >= 0:
                        S.op("pool", lambda e: e.tensor_tensor(out=eE[j][:, qlo:qlo + 128], in0=eE[j][:, qlo:qlo + 128], in1=tri[:], op=ALU.mult),
                             reads=[R_E[j], R_par], writes=[R_E[j]])
                    for qt in range(max(o, 0), 4):
                        S.op("pe", lambda e: e.matmul(accO(qt), lhsT=eE[j][:, qt * 128:(qt + 1) * 128], rhs=vA[:, kb, h, :],
                                                      start=(kb == 0 and qt in (0, 3)), stop=(kb == 4 * QB + qt), skip_group_check=True),
                             reads=[R_E[j], R_vA[kb]], writes=[R_O[qt]])
                for qt in range(4):
                    S.op("act", lambda e: e.activation(out=osb[g2][:, c, qt, :], in_=accO(qt), func=AF.Copy), reads=[R_O[qt]], writes=[R_osb[g2][qt]])
            for qt in range(4):
                o1, o2 = osb[g2][:, 0, qt, :], osb[g2][:, 1, qt, :]
                Rq = R_scr[qt]
                S.op("dve", lambda e: e.reciprocal(out=st[qt][:, 2:3], in_=o1[:, 128:129]), reads=[R_osb[g2][qt]], writes=[Rq])
                S.op("dve", lambda e: e.reciprocal(out=st[qt][:, 3:4], in_=o2[:, 128:129]), reads=[R_osb[g2][qt]], writes=[Rq])
                S.op("dve", lambda e: e.tensor_tensor(out=st[qt][:, 3:4], in0=st[qt][:, 3:4], in1=lam[:, 0:1], op=ALU.mult), reads=[Rq, R_par], writes=[Rq])
                S.op("dve", lambda e: e.tensor_scalar(out=o2[:, 0:128], in0=o2[:, 0:128], scalar1=st[qt][:, 3:4], scalar2=None, op0=ALU.mult),
                     reads=[Rq, R_osb[g2][qt]], writes=[R_osb[g2][qt]])
                S.op("dve", lambda e: e.scalar_tensor_tensor(out=of[qt][:, :], in0=o1[:, 0:128], scalar=st[qt][:, 2:3], in1=o2[:, 0:128], op0=ALU.mult, op1=ALU.subtract),
                     reads=[Rq, R_osb[g2][qt]], writes=[Rq])
                c0 = QB * 512 + qt * 128
                def dst(c0=c0, h=h):
                    S.op("act", lambda e: e.activation(out=mixT[:, h, c0:c0 + 128], in_=ptp[:, :128], func=AF.Copy), reads=[R_ptp], writes=[R_mix[h]])
                subln_store(of[qt][:, :], 128, dst, (junk[qt], st[qt], ob[qt], Rq))

        do_samples = L["stage"] >= 8
        sample_gen = None
        state = {"gen": None}

        def sample_step():
            if state["gen"] is not None:
                try:
                    next(state["gen"])
                except StopIteration:
                    state["gen"] = None
        if do_samples:
            ptr_ = sb("s_ptr", [128, NSC, 16], I32, pa)
            pof = sb("s_pof", [128, NSC * 16], F32, pa)
            pff = sb("s_pff", [128, NSC * 16], F32, pa)
            idx = sb("s_idx", [128, NSC, 16], I32, pa)
            R_idx = Res("idx")
            selt = sb("s_sel", [NSC, NSC, 128], F32, pa)
            m0 = sb("s_m0", [128, 1], F32, pa)
            S.dma("sp", lambda e: e.dma_start(out=ptr_[:], in_=L["ptrep"].rearrange("s p j -> p s j")), "s_p", writes=[R_idx])
            S.dma("sp", lambda e: e.dma_start(out=pof[:], in_=L["poff"]), "s_p", writes=[R_idx])
            S.dma("sp", lambda e: e.dma_start(out=selt[:], in_=L["selq"]), "s_p", writes=[R_idx])
            S.dma("sp", lambda e: e.dma_start(out=m0[:], in_=L["mask0"]), "s_p", writes=[R_idx])
            f2 = lambda t: t[:].rearrange("p a b -> p (a b)")
            S.op("dve", lambda e: e.tensor_copy(out=pff[:, :], in_=f2(ptr_)), reads=[R_idx], writes=[R_idx])
            S.op("dve", lambda e: e.scalar_tensor_tensor(out=pff[:, :], in0=pff[:, :], scalar=32.0, in1=pof[:, :], op0=ALU.mult, op1=ALU.add), reads=[R_idx], writes=[R_idx])
            S.op("dve", lambda e: e.tensor_copy(out=f2(idx), in_=pff[:, :]), reads=[R_idx], writes=[R_idx])
            NKB = 3
            kt_ = [sb("s_kt%d" % j, [128, 4, 512], F32, pa) for j in range(NKB)]
            R_kt = [Res("s_kt%d" % j) for j in range(NKB)]
            NVB = 3
            vb_ = [sb("s_vb%d" % j, [128, 4, 512], BF16, pa) for j in range(NVB)]
            R_vb = [Res("s_vb%d" % j) for j in range(NVB)]
            prod = [sb("s_prod%d" % j, [128, 4, 512], F32, pa) for j in range(2)]
            R_prod = [Res("s_prod%d" % j) for j in range(2)]
            qbc = [sb("s_qbc%d" % j, [128, 4, 512], F32, pa) for j in range(2)]
            R_qbc = [Res("qbc%d" % j) for j in range(2)]
            R_qscr = Res("qscr")
            sc = sb("s_sc", [128, 64, 8], F32, pa)
            R_sc = Res("s_sc")
            pdz = sb("s_pdz", [128, 64, 4, NSC], BF16, pa)
            R_pdz = Res("pdz")
            zz = sb("s_zz", [128, 6, 8], F32, pa)
            R_zz = Res("zz")
            sself = sb("s_self", [NSC, 4, 8], F32, pa)
            prs = sb("s_prs", [NSC, 512], F32, pa)
            R_self = Res("sself")
            coef = sb("s_coef", [NSC, 2, 8], F32, pa)
            R_coef = Res("coef")
            osm = sb("s_osm", [NSC, 4, 128], F32, pa)
            sjunk = sb("s_junk", [NSC, 128], BF16, pa)
            sst = sb("s_st", [NSC, 4], F32, pa)
            sob = sb("s_ob", [NSC, 128], BF16, pa)
            R_sscr = Res("s_scr")
            pz = ps("s_pz", [128, 16], F32, pa)
            R_pz = Res("pz")
            pso = ps("s_pso", [NSC, 512], F32, pa)
            R_pso = Res("pso")
            S.op("pool", lambda e: e.memset(pdz[:], 0.0), writes=[R_pdz])
            S.op("dve", lambda e: e.memset(coef[:], 0.0), writes=[R_coef])
            S.dma("sp", lambda e: e.dma_start(out=L["qscr"], in_=L["qsf"][:, :]), "s_q", reads=[L["R_qsf"]], writes=[R_qscr])
            S.op("dve", lambda e: e.tensor_tensor(out=prs[:, :], in0=L["qsf"][:, :], in1=L["ksf"][:, :], op=ALU.mult), reads=[L["R_qsf"], L["R_ksf"]], writes=[R_self])
            S.op("dve", lambda e: e.reduce_sum(out=sself[:, 0, :], in_=prs[:, :].rearrange("p (a d) -> p a d", d=64), axis=AX.X), reads=[R_self], writes=[R_self])
            S.op("act", lambda e: e.activation(out=sself[:, 1, :], in_=sself[:, 0, :], func=AF.Exp, scale=0.125), reads=[R_self], writes=[R_self])

            def sample_k(s_):
                def gather(jj):
                    j = (s_ * 16 + jj) % NKB
                    S.dma("pool", lambda e: e.indirect_dma_start(out=kt_[j][:].rearrange("p a b -> p (a b)"), out_offset=None, in_=L["cache_k"],
                                                                 in_offset=bass.IndirectOffsetOnAxis(ap=idx[:, s_, jj:jj + 1], axis=0)),
                          "s_kt%d" % j, reads=[R_idx], writes=[R_kt[j]])
                qb_ = qbc[s_ % 2]
                S.dma("sp", lambda e: e.dma_start(out=qb_[:], in_=AP(L["qscr"].tensor, s_ * 512, [[0, 128], [0, 4], [1, 512]])),
                      "s_qb%d" % (s_ % 2), reads=[R_qscr], writes=[R_qbc[s_ % 2]])
                gather(0)
                gather(1)
                for jj in range(16):
                    j = (s_ * 16 + jj) % NKB
                    if jj + 2 < 16:
                        gather(jj + 2)
                    S.op("pool", lambda e: e.tensor_tensor(out=prod[jj % 2][:], in0=kt_[j][:], in1=qb_[:], op=ALU.mult),
                         reads=[R_kt[j], R_qbc[s_ % 2]], writes=[R_prod[jj % 2]])
                    S.op("dve", lambda e: e.reduce_sum(out=sc[:, jj * 4:(jj + 1) * 4, :], in_=prod[jj % 2][:].rearrange("p t (a d) -> p t a d", d=64), axis=AX.X),
                         reads=[R_prod[jj % 2]], writes=[R_sc])
                    yield
                S.op("act", lambda e: e.activation(out=sc[:], in_=sc[:], func=AF.Exp, scale=0.125), reads=[R_sc], writes=[R_sc])
                S.op("dve", lambda e: e.reduce_sum(out=zz[:, 0, :], in_=sc[:].rearrange("p t a -> p a t"), axis=AX.X), reads=[R_sc], writes=[R_zz])
                S.op("pe", lambda e: e.matmul(pz[:, 0:8], lhsT=selt[:, s_, :], rhs=sself[:, 1, :], start=True, stop=True), reads=[R_idx, R_self], writes=[R_pz])
                S.op("dve", lambda e: e.scalar_tensor_tensor(out=zz[:, 0, :], in0=pz[:, 0:8], scalar=m0[:, 0:1], in1=zz[:, 0, :], op0=ALU.mult, op1=ALU.add),
                     reads=[R_pz, R_idx, R_zz], writes=[R_zz])
                S.op("pe", lambda e: e.matmul(pz[:, 8:16], lhsT=L["ones_f"][:, :], rhs=zz[:, 0, :], start=True, stop=True), reads=[R_zz, R_id], writes=[R_pz])
                S.op("dve", lambda e: e.reciprocal(out=zz[:, 1, :], in_=pz[:, 8:16]), reads=[R_pz], writes=[R_zz])
                z4 = lambda r: zz[:, r, :].rearrange("p (h c) -> p h c", c=2)
                S.op("dve", lambda e: e.tensor_scalar(out=zz[:, 2, :], in0=zz[:, 1, :], scalar1=lam[:, 0:1], scalar2=None, op0=ALU.mult), reads=[R_zz, R_par], writes=[R_zz])
                for r_ in range(2):
                    S.op("dve", lambda e: e.scalar_tensor_tensor(out=coef[:, r_, :], in0=zz[0:NSC, 1 + r_, :], scalar=idf[0:NSC, s_:s_ + 1], in1=coef[:, r_, :],
                                                                 op0=ALU.mult, op1=ALU.add), reads=[R_zz, R_id, R_coef], writes=[R_coef])
                sc4 = sc[:].rearrange("p t (h c) -> p t h c", c=2)
                S.op("dve", lambda e: e.tensor_tensor(out=sc4[:, :, :, 0], in0=sc4[:, :, :, 0], in1=z4(1)[:, :, 0].unsqueeze(1).to_broadcast([128, 64, 4]), op=ALU.mult), reads=[R_sc, R_zz], writes=[R_sc])
                S.op("dve", lambda e: e.tensor_tensor(out=sc4[:, :, :, 1], in0=sc4[:, :, :, 1], in1=z4(2)[:, :, 1].unsqueeze(1).to_broadcast([128, 64, 4]), op=ALU.mult), reads=[R_sc, R_zz], writes=[R_sc])
                if s_ > 0:
                    S.op("pool", lambda e: e.memset(pdz[:, :, :, s_ - 1], 0.0), writes=[R_pdz])
                S.op("dve", lambda e: e.tensor_tensor(out=pdz[:, :, :, s_], in0=sc4[:, :, :, 0], in1=sc4[:, :, :, 1], op=ALU.subtract), reads=[R_sc], writes=[R_pdz])

            def sample_v(s_):
                def gather(jj):
                    jv = jj % NVB
                    S.dma("pool", lambda e: e.indirect_dma_start(out=vb_[jv][:].rearrange("p a b -> p (a b)"), out_offset=None, in_=L["cache_v"],
                                                                 in_offset=bass.IndirectOffsetOnAxis(ap=idx[:, s_, jj:jj + 1], axis=0)),
                          "s_vb%d" % jv, reads=[R_idx], writes=[R_vb[jv]])
                gather(0)
                gather(1)
                for jj in range(16):
                    jv = jj % NVB
                    if jj + 2 < 16:
                        gather(jj + 2)
                    for t4 in range(4):
                        for h in range(4):
                            first = (s_ == 0 and jj == 0 and t4 == 0 and h == 0)
                            last = (s_ == NSC - 1 and jj == 15 and t4 == 3)
                            S.op("pe", lambda e: e.matmul(pso[:, h * 128:(h + 1) * 128], lhsT=pdz[:, jj * 4 + t4, h, :], rhs=vb_[jv][:, t4, h * 128:(h + 1) * 128],
                                                          start=first, stop=last, skip_group_check=True),
                                 reads=[R_pdz, R_vb[jv]], writes=[R_pso])
                    yield

            def sample_all():
                for s_ in range(NSC):
                    yield from sample_k(s_)
                    yield from sample_v(s_)
            sample_gen = sample_all()


        if do_samples:
            state["gen"] = sample_gen
        for gi in range(16):
            prompt_group(gi)
        while state["gen"] is not None:
            sample_step()

        if do_samples:
            e4 = sself[:, 1, :].rearrange("p (h c) -> p h c", c=2)
            cA = coef[:, 0, :].rearrange("p (h c) -> p h c", c=2)
            cB = coef[:, 1, :].rearrange("p (h c) -> p h c", c=2)
            pd4 = sself[:, 2, :].rearrange("p (h c) -> p h c", c=2)
            S.op("dve", lambda e: e.tensor_tensor(out=pd4[:, :, 0], in0=e4[:, :, 0], in1=cA[:, :, 0], op=ALU.mult), reads=[R_self, R_coef], writes=[R_self])
            S.op("dve", lambda e: e.tensor_tensor(out=pd4[:, :, 1], in0=e4[:, :, 1], in1=cB[:, :, 1], op=ALU.mult), reads=[R_self, R_coef], writes=[R_self])
            S.op("dve", lambda e: e.tensor_tensor(out=pd4[:, :, 0], in0=pd4[:, :, 0], in1=pd4[:, :, 1], op=ALU.subtract), reads=[R_self], writes=[R_self])
            for h in range(4):
                S.op("dve", lambda e: e.scalar_tensor_tensor(out=osm[:, h, :], in0=L["vS"][:, h * 128:(h + 1) * 128], scalar=pd4[:, h, 0:1], in1=pso[:, h * 128:(h + 1) * 128],
                                                             op0=ALU.mult, op1=ALU.add), reads=[L["R_vS"], R_self, R_pso], writes=[R_sscr])
                def dst(h=h):
                    S.op("act", lambda e: e.activation(out=mixT[:, h, SEQ:NCOL], in_=ptp[:, :NSC], func=AF.Copy), reads=[R_ptp], writes=[R_mix[h]])
                subln_store(osm[:, h, :], NSC, dst, (sjunk, sst, sob, R_sscr))
        S.barrier()


def ffn_phase(nc, S, sb, ps, L):
    mixT, R_mix, idb, idf, R_id = L["mixT"], L["R_mix"], L["idb"], L["idf"], L["R_id"]
    xp, xs, xscr = L["xp"], L["xs"], L["xscr"]
    with ExitStack() as pf:
        mT = sb("mT", [128, NFC, NCOL], BF16, pf)
        R_mT = [Res("mT%d" % b) for b in range(5)]
        rs2 = sb("rs2", [128, 17, 4], F32, pf)
        R_rs2 = [Res("rs2_%d" % i) for i in range(17)]
        small = sb("fsmall", [128, 8 + 3 * NFC + NFC], F32, pf)
        R_small = Res("fsmall")
        S.dma("sp", lambda e: e.dma_start(out=small[:, 8:8 + 3 * NFC], in_=L["convw"].rearrange("p a b -> p (a b)")), "f_s", writes=[R_small])
        S.dma("sp", lambda e: e.dma_start(out=small[:, 8 + 3 * NFC:], in_=L["convb"]), "f_s", writes=[R_small])
        cw = lambda j, fc: small[:, 8 + j * NFC + fc:8 + j * NFC + fc + 1]
        cb = lambda fc: small[:, 8 + 3 * NFC + fc:8 + 3 * NFC + fc + 1]
        asp = sb("asp", [8, DFF], F32, pf)
        R_asp = Res("asp")
        with ExitStack() as pu:
            hn2T = sb("hn2T", [128, 8, NCOL], BF16, pu)
            R_hn2 = [Res("hn2_%d" % i) for i in range(17)]
            with ExitStack() as po:
                wob = sb("wob", [128, 8, D], BF16, po)
                R_wob = Res("wob")
                for kc in range(8):
                    S.dma("pool", lambda e: e.dma_start(out=wob[:, kc, :], in_=L["w_o"][kc * 128:(kc + 1) * 128, :]), "wob", writes=[R_wob])
                gfx = sb("gfx", [128, D], F32, po)
                R_gfx = Res("gfx")
                S.dma("sp", lambda e: e.dma_start(out=gfx[:], in_=AP(L["nffn"].tensor, 0, [[0, 128], [1, D]])), "f_g2", writes=[R_gfx])
                xt = [sb("oxt%d" % j, [128, D], F32, po) for j in range(2)]
                R_xt = [Res("oxt%d" % j) for j in range(2)]
                xb = [sb("oxb0", [128, D], BF16, po)] * 2
                R_xb = [Res("oxb0")] * 2
                junk = xb[0]
                R_junk = R_xb[0]
                px = [ps("opx%d" % j, [128, 2, 512], F32, po) for j in range(2)]
                R_px = [Res("opx%d" % j) for j in range(2)]
                ptr = [ps("optr%d" % j, [128, 8, 128], BF16, po) for j in range(2)]
                R_ptr = [Res("optr%d" % j) for j in range(2)]
                for i, (c0, n) in enumerate(TILES):
                    j = i % 2
                    src = xp[c0:c0 + n, :] if i < 16 else xs
                    S.dma("sp", lambda e: e.dma_start(out=xt[j][:n, :], in_=src), "oxt%d" % j, writes=[R_xt[j]])
                    for hf in range(2):
                        for kc in range(8):
                            S.op("pe", lambda e: e.matmul(px[j][:n, hf, :], lhsT=mixT[:, kc, c0:c0 + n], rhs=wob[:, kc, hf * 512:(hf + 1) * 512],
                                                          start=(kc == 0), stop=(kc == 7)), reads=R_mix + [R_wob], writes=[R_px[j]])
                    S.op("dve", lambda e: e.tensor_tensor(out=xt[j][:n, :], in0=xt[j][:n, :], in1=px[j][:n, :, :].rearrange("p a b -> p (a b)"), op=ALU.add),
                         reads=[R_xt[j], R_px[j]], writes=[R_xt[j]])
                    S.dma("sp", lambda e: e.dma_start(out=xscr[c0:c0 + n, :], in_=xt[j][:n, :]), "oxt%d" % j, reads=[R_xt[j]])
                    S.op("act", lambda e: e.activation(out=junk[:n, :], in_=xt[j][:n, :], func=AF.Square, accum_out=rs2[:n, i, 0:1]),
                         reads=[R_xt[j]], writes=[R_junk, R_rs2[i]])
                    S.op("dve", lambda e: e.tensor_scalar(out=rs2[:n, i, 1:2], in0=rs2[:n, i, 0:1], scalar1=1.0 / D, scalar2=EPS, op0=ALU.mult, op1=ALU.add),
                         reads=[R_rs2[i]], writes=[R_rs2[i]])
                    S.op("act", lambda e: e.sqrt(out=rs2[:n, i, 2:3], in_=rs2[:n, i, 1:2]), reads=[R_rs2[i]], writes=[R_rs2[i]])
                    S.op("dve", lambda e: e.reciprocal(out=rs2[:n, i, 3:4], in_=rs2[:n, i, 2:3]), reads=[R_rs2[i]], writes=[R_rs2[i]])
                    S.op("dve", lambda e: e.scalar_tensor_tensor(out=xb[j][:n, :], in0=xt[j][:n, :], scalar=rs2[:n, i, 3:4], in1=gfx[:n, :],
                                                                 op0=ALU.mult, op1=ALU.mult),
                         reads=[R_xt[j], R_rs2[i], R_gfx], writes=[R_xb[j]])
                    for kc in range(8):
                        S.op("pe", lambda e: e.transpose(out=ptr[j][:, kc, :n], in_=xb[j][:n, kc * 128:(kc + 1) * 128], identity=idb[:n, :n]),
                             reads=[R_xb[j], R_id], writes=[R_ptr[j]])
                    S.op("dve", lambda e: e.tensor_copy(out=hn2T[:, :, c0:c0 + n], in_=ptr[j][:, :, :n]), reads=[R_ptr[j]], writes=[R_hn2[i]])
                S.barrier()
            with ExitStack() as pv:
                wab = [sb("wab%d" % j, [128, 8, 128], BF16, pv) for j in range(2)]
                wgb_ = [sb("wgb_%d" % j, [128, 8, 128], BF16, pv) for j in range(2)]
                R_wab = [Res("wab%d" % j) for j in range(2)]
                R_wgb = [Res("wgb_%d" % j) for j in range(2)]
                pa_ = [ps("fpa%d" % j, [128, 512], F32, pv) for j in range(2)]
                pg_ = [ps("fpg%d" % j, [128, 512], F32, pv) for j in range(2)]
                R_pa = [Res("fpa%d" % j) for j in range(2)]
                R_pg = [Res("fpg%d" % j) for j in range(2)]
                psp = ps("fpsp", [8, 128], F32, pv)
                R_psp = Res("fpsp")
                pbt = ps("fpbt", [128, NFC, 8], F32, pv)
                R_pbt = Res("fpbt")
                abuf = sb("abuf", [128, 2 + 512], F32, pv)
                cbuf = sb("cbuf", [128, 512], F32, pv)
                R_ab, R_cbuf = Res("abuf"), Res("cbuf")
                scv = sb("scv", [8, DFF], F32, pv)
                bufT = sb("bufT", [128, NFC, 8], F32, pv)
                R_bufT = Res("bufT")
                S.dma("sp", lambda e: e.dma_start(out=scv[:, :], in_=L["sconv"]), "f_c", writes=[R_bufT])
                for fc in range(NFC):
                    S.op("pe", lambda e: e.transpose(out=pbt[:, fc, :], in_=scv[:, fc * 128:(fc + 1) * 128], identity=idf[:8, :8]),
                         reads=[R_bufT, R_id], writes=[R_pbt])
                S.op("dve", lambda e: e.tensor_copy(out=bufT[:], in_=pbt[:]), reads=[R_pbt], writes=[R_bufT])
                for fc in range(NFC):
                    jw = fc % 2
                    for half, (wb, R_wb) in enumerate(((wab, R_wab), (wgb_, R_wgb))):
                        col0 = half * DFF + fc * 128
                        S.dma("pool", lambda e: e.dma_start(out=wb[jw][:], in_=L["w_up"][:, col0:col0 + 128].rearrange("(k p) c -> p k c", p=128)),
                              "wup%d%d" % (half, jw), writes=[R_wb[jw]])
                    for kc in range(8):
                        S.op("pe", lambda e: e.matmul(psp[0:6, :], lhsT=hn2T[:, kc, SEQ - 2:NCOL], rhs=wab[jw][:, kc, :], start=(kc == 0), stop=(kc == 7)),
                             reads=R_hn2[15:17] + [R_wab[jw]], writes=[R_psp])
                    S.op("act", lambda e: e.activation(out=asp[0:6, fc * 128:(fc + 1) * 128], in_=psp[0:6, :], func=AF.Copy), reads=[R_psp], writes=[R_asp])
                    for b, (c0, nb) in enumerate(BLOCKS):
                        j = (fc * 5 + b) % 2
                        for kc in range(8):
                            S.op("pe", lambda e: e.matmul(pa_[j][:, :nb], lhsT=wab[jw][:, kc, :], rhs=hn2T[:, kc, c0:c0 + nb], start=(kc == 0), stop=(kc == 7)),
                                 reads=R_hn2 + [R_wab[jw]], writes=[R_pa[j]])
                        for kc in range(8):
                            S.op("pe", lambda e: e.matmul(pg_[j][:, :nb], lhsT=wgb_[jw][:, kc, :], rhs=hn2T[:, kc, c0:c0 + nb], start=(kc == 0), stop=(kc == 7)),
                                 reads=R_hn2 + [R_wgb[jw]], writes=[R_pg[j]])
                        if b < 4:
                            if b == 0:
                                S.op("dve", lambda e: e.memset(abuf[:, 0:2], 0.0), reads=[R_ab], writes=[R_ab])
                            else:
                                S.op("dve", lambda e: e.tensor_copy(out=abuf[:, 0:2], in_=abuf[:, 512:514]), reads=[R_ab], writes=[R_ab])
                            S.op("act", lambda e: e.activation(out=abuf[:, 2:514], in_=pa_[j][:, :], func=AF.Copy), reads=[R_pa[j], R_ab], writes=[R_ab])
                            S.op("act", lambda e: e.activation(out=cbuf[:, :], in_=pa_[j][:, :], func=AF.Identity, scale=cw(2, fc), bias=cb(fc)),
                                 reads=[R_pa[j], R_small], writes=[R_cbuf])
                            S.op("dve", lambda e: e.scalar_tensor_tensor(out=cbuf[:, :], in0=abuf[:, 1:513], scalar=cw(1, fc), in1=cbuf[:, :], op0=ALU.mult, op1=ALU.add),
                                 reads=[R_ab, R_small, R_cbuf], writes=[R_cbuf])
                            S.op("dve", lambda e: e.scalar_tensor_tensor(out=cbuf[:, :], in0=abuf[:, 0:512], scalar=cw(0, fc), in1=cbuf[:, :], op0=ALU.mult, op1=ALU.add),
                                 reads=[R_ab, R_small, R_cbuf], writes=[R_cbuf])
                        else:
                            b3 = bufT[:, fc, :].rearrange("p (s j) -> p s j", j=2)
                            S.op("act", lambda e: e.activation(out=cbuf[:, :nb], in_=pa_[j][:, :nb], func=AF.Identity, scale=cw(2, fc), bias=cb(fc)),
                                 reads=[R_pa[j], R_small], writes=[R_cbuf])
                            S.op("dve", lambda e: e.scalar_tensor_tensor(out=cbuf[:, :nb], in0=b3[:, :, 1], scalar=cw(1, fc), in1=cbuf[:, :nb], op0=ALU.mult, op1=ALU.add),
                                 reads=[R_bufT, R_small, R_cbuf], writes=[R_cbuf])
                            S.op("dve", lambda e: e.scalar_tensor_tensor(out=cbuf[:, :nb], in0=b3[:, :, 0], scalar=cw(0, fc), in1=cbuf[:, :nb], op0=ALU.mult, op1=ALU.add),
                                 reads=[R_bufT, R_small, R_cbuf], writes=[R_cbuf])
                        S.op("act", lambda e: e.activation(out=cbuf[:, :nb], in_=cbuf[:, :nb], func=AF.Silu), reads=[R_cbuf], writes=[R_cbuf])
                        S.op("dve", lambda e: e.tensor_tensor(out=mT[:, fc, c0:c0 + nb], in0=cbuf[:, :nb], in1=pg_[j][:, :nb], op=ALU.mult),
                             reads=[R_cbuf, R_pg[j]], writes=[R_mT[b]])
                S.dma("sp", lambda e: e.dma_start(out=L["o_cp"], in_=asp[0:2, :]), "f_o", reads=[R_asp])
                S.dma("sp", lambda e: e.dma_start(out=L["o_cs"][:, 1, :], in_=asp[2:6, :]), "f_o", reads=[R_asp])
                S.dma("sp", lambda e: e.dma_start(out=L["o_cs"][:, 0, :], in_=L["sconv1"]), "f_o2")
                S.barrier()
        with ExitStack() as pd_:
            wdb = sb("wdb", [128, NFC, D], BF16, pd_)
            R_wdb = Res("wdb")
            for fc in range(NFC):
                S.dma("pool", lambda e: e.dma_start(out=wdb[:, fc, :], in_=L["w_down"][fc * 128:(fc + 1) * 128, :]), "wdb", writes=[R_wdb])
            gfn = sb("gfn", [128, D], F32, pd_)
            R_gfn = Res("gfn")
            S.dma("sp", lambda e: e.dma_start(out=gfn[:], in_=AP(L["fnorm"].tensor, 0, [[0, 128], [1, D]])), "f_g", writes=[R_gfn])
            xt = [sb("dxt%d" % j, [128, D], F32, pd_) for j in range(2)]
            R_xt = [Res("dxt%d" % j) for j in range(2)]
            junk = sb("djunk", [128, D], BF16, pd_)
            R_junk = Res("djunk")
            px = [ps("dpx%d" % j, [128, 2, 512], F32, pd_) for j in range(2)]
            R_px = [Res("dpx%d" % j) for j in range(2)]
            for i, (c0, n) in enumerate(TILES):
                j = i % 2
                S.dma("sp", lambda e: e.dma_start(out=xt[j][:n, :], in_=xscr[c0:c0 + n, :]), "dxt%d" % j, writes=[R_xt[j]])
                for hf in range(2):
                    for fc in range(NFC):
                        S.op("pe", lambda e: e.matmul(px[j][:n, hf, :], lhsT=mT[:, fc, c0:c0 + n], rhs=wdb[:, fc, hf * 512:(hf + 1) * 512],
                                                      start=(fc == 0), stop=(fc == NFC - 1)), reads=R_mT + [R_wdb], writes=[R_px[j]])
                S.op("dve", lambda e: e.tensor_tensor(out=xt[j][:n, :], in0=xt[j][:n, :], in1=px[j][:n, :, :].rearrange("p a b -> p (a b)"), op=ALU.add),
                     reads=[R_xt[j], R_px[j]], writes=[R_xt[j]])
                S.op("act", lambda e: e.activation(out=junk[:n, :], in_=xt[j][:n, :], func=AF.Square, accum_out=rs2[:n, i, 0:1]),
                     reads=[R_xt[j]], writes=[R_junk, R_rs2[i]])
                S.op("dve", lambda e: e.tensor_scalar(out=rs2[:n, i, 1:2], in0=rs2[:n, i, 0:1], scalar1=1.0 / D, scalar2=EPS, op0=ALU.mult, op1=ALU.add),
                     reads=[R_rs2[i]], writes=[R_rs2[i]])
                S.op("act", lambda e: e.sqrt(out=rs2[:n, i, 2:3], in_=rs2[:n, i, 1:2]), reads=[R_rs2[i]], writes=[R_rs2[i]])
                S.op("dve", lambda e: e.reciprocal(out=rs2[:n, i, 3:4], in_=rs2[:n, i, 2:3]), reads=[R_rs2[i]], writes=[R_rs2[i]])
                S.op("dve", lambda e: e.scalar_tensor_tensor(out=xt[j][:n, :], in0=xt[j][:n, :], scalar=rs2[:n, i, 3:4], in1=gfn[:n, :], op0=ALU.mult, op1=ALU.mult),
                     reads=[R_xt[j], R_rs2[i], R_gfn], writes=[R_xt[j]])
                dstd = L["o_yp"][c0:c0 + n, :] if i < 16 else L["o_ys"]
                S.dma("sp", lambda e: e.dma_start(out=dstd, in_=xt[j][:n, :]), "dxt%d" % j, reads=[R_xt[j]])
            S.barrier()


def build_program(n_pool, stage=99):
    nc = bass.Bass("TRN2", target_bir_lowering=False)
    din = lambda name, shape, dt=F32: nc.dram_tensor(name, shape, dt, kind="ExternalInput").ap()
    dout = lambda name, shape, dt=F32: nc.dram_tensor(name, shape, dt, kind="ExternalOutput").ap()

    xp = din("xp", [SEQ, D])
    xs = din("xs", [NSC, D])
    w_in = din("w_in", [D, DPROJ])
    nmix = din("nmix", [1, D])
    ropec = din("ropec", [NCOL, 8])
    ropes = din("ropes", [NCOL, 8])
    ident = din("ident", [128, 128])

    bcol_re = din("bcol_re", [128, 16, 128])
    bcol_im = din("bcol_im", [128, 16, 128])
    cpad_re = din("cpad_re", [128, 16, 128])
    cpad_im = din("cpad_im", [128, 16, 128])
    acol_re = din("acol_re", [128, 16])
    acol_im = din("acol_im", [128, 16])
    ldt_col = din("ldt_col", [128, 16])
    dcol = din("dcol", [128, 4])
    h0c_re = din("h0c_re", [128, 16, NSC])
    h0c_im = din("h0c_im", [128, 16, NSC])
    w_glu = din("w_glu", [512, 512])
    trow1 = din("trow1", [128, 512])
    o_hre = dout("o_hre", [16, 128])
    o_him = dout("o_him", [16, 128])
    o_hres = dout("o_hres", [NSC, 16, 128])
    o_hims = dout("o_hims", [NSC, 16, 128])

    lamv = din("lamv", [1, 4 * 64])
    gsub = din("gsub", [1, 128])
    trimask = din("trimask", [128, 128])
    cache_k = din("cache_k", [n_pool * 32, 2048])
    cache_v = din("cache_v", [n_pool * 32, 2048])
    ptrep = din("ptrep", [NSC, 128, 16], I32)
    poff = din("poff", [128, NSC * 16])
    selq = din("selq", [NSC, NSC, 128])
    mask0 = din("mask0", [128, 1])
    w_o = din("w_o", [D, D])
    nffn = din("nffn", [1, D])
    w_up = din("w_up", [D, 2 * DFF])
    convw = din("convw", [128, 3, NFC])
    convb = din("convb", [128, NFC])
    w_down = din("w_down", [DFF, D])
    fnorm = din("fnorm", [1, D])
    sconv = din("sconv", [NSC * 2, DFF])
    sconv1 = din("sconv1", [NSC, DFF])
    xscr = nc.dram_tensor("xscr", [NCOL, D], F32, kind="Internal").ap()
    qscr = nc.dram_tensor("qscr", [1, NSC * 512], F32, kind="Internal").ap()
    o_yp = dout("o_yp", [SEQ, D])
    o_ys = dout("o_ys", [NSC, D])
    o_cp = dout("o_cp", [2, DFF])
    o_cs = dout("o_cs", [NSC, 2, DFF])
    o_kp = dout("o_kp", [SEQ, 512])
    o_vp = dout("o_vp", [SEQ, 512])
    o_ks = dout("o_ks", [NSC, 512])
    o_vs = dout("o_vs", [NSC, 512])

    with ExitStack() as es:
        S = Sched(nc, es)

        def sb(name, shape, dt, stack=es):
            return stack.enter_context(nc.sbuf_tensor(name, shape, dt))

        def ps(name, shape, dt, stack=es):
            return stack.enter_context(nc.psum_tensor(name, shape, dt))

        idf = sb("idf", [128, 128], F32)
        idb = sb("idb", [128, 128], BF16)
        R_id = Res("id")
        S.dma("sp", lambda e: e.dma_start(out=idf[:], in_=ident), "c_id", writes=[R_id])
        S.op("dve", lambda e: e.tensor_copy(out=idb[:], in_=idf[:]), reads=[R_id], writes=[R_id])

        ones_f = sb("ones_f", [128, 128], F32)
        S.op("pool", lambda e: e.memset(ones_f[:], 1.0), writes=[R_id])
        ropc = sb("ropc", [128, 17, 8], F32)
        rops = sb("rops", [128, 17, 8], F32)
        R_rope = Res("rope")
        for tbl, src, key in ((ropc, ropec, "c_rc"), (rops, ropes, "c_rs")):
            S.dma("sp", lambda e: e.dma_start(out=tbl[:, 0:16, :],
                                              in_=src[0:SEQ, :].rearrange("(i p) e -> p i e", p=128)),
                  key, writes=[R_rope])
            S.dma("sp", lambda e: e.dma_start(out=tbl[0:NSC, 16, :], in_=src[SEQ:NCOL, :]), key, writes=[R_rope])


        mixT = sb("mixT", [128, 8, NCOL], BF16)
        R_mix = [Res("mix%d" % c) for c in range(8)]
        vS = sb("vS", [NSC, 512], BF16)
        R_vS = Res("vS")
        ksf = sb("ksf", [NSC, 512], F32)
        qsf = sb("qsf", [NSC, 512], F32)
        R_ksf, R_qsf = Res("ksf"), Res("qsf")
        pA = es.enter_context(ExitStack())
        qT = sb("qT", [128, 4, NCOL], BF16, pA)
        kT = sb("kT", [128, 4, NCOL], BF16, pA)
        R_qT = [Res("qT%d" % i) for i in range(17)]
        R_kT = [Res("kT%d" % i) for i in range(17)]
        vA = sb("vA", [128, 16, 4, 129], BF16, pA)
        R_vA = [Res("vA%d" % i) for i in range(16)]
        uT = sb("uT", [128, 4, NCOL], BF16, pA)
        R_uT = [Res("uT%d" % i) for i in range(5)]
        rr = sb("rr", [128, 17, 4], F32, pA)
        R_rr = [Res("rr%d" % i) for i in range(17)]

        pw = pA.enter_context(ExitStack())
        hnT = sb("hnT", [128, 8, NCOL], BF16, pw)
        R_hnT = [Res("hnT%d" % i) for i in range(17)]
        winb = sb("winb", [128, 8, DPROJ], BF16, pw)
        R_winb = [Res("winb")] * 8
        with ExitStack() as p12:
            xt = [sb("xt%d" % j, [128, D], F32, p12) for j in range(2)]
            R_xt = [Res("xt%d" % j) for j in range(2)]
            xb = [sb("xb%d" % j, [128, D], BF16, p12) for j in range(2)]
            R_xb = [Res("xb%d" % j) for j in range(2)]
            junk = sb("junk", [128, D], BF16, p12)
            R_junk = Res("junk")
            gmx = sb("gmx", [128, D], F32, p12)
            R_gmx = Res("gmx")
            S.dma("sp", lambda e: e.dma_start(out=gmx[:], in_=AP(nmix.tensor, 0, [[0, 128], [1, D]])), "c_nm", writes=[R_gmx])
            ptr = [ps("ptr%d" % j, [128, 8, 128], BF16, p12) for j in range(2)]
            R_ptr = [Res("ptr%d" % j) for j in range(2)]

            for kc in range(8):
                S.dma("pool", lambda e: e.dma_start(out=winb[:, kc, :], in_=w_in[kc * 128:(kc + 1) * 128, :]),
                      "winb", writes=[R_winb[kc]])

            for i, (c0, n) in enumerate(TILES):
                j = i % 2
                src = xp[c0:c0 + n, :] if i < 16 else xs
                S.dma("sp", lambda e: e.dma_start(out=xt[j][:n, :], in_=src), "xt%d" % j, writes=[R_xt[j]])
                S.op("act", lambda e: e.activation(out=junk[:n, :], in_=xt[j][:n, :], func=AF.Square,
                                                   accum_out=rr[:n, i, 0:1]),
                     reads=[R_xt[j]], writes=[R_junk, R_rr[i]])
                S.op("dve", lambda e: e.tensor_scalar(out=rr[:n, i, 1:2], in0=rr[:n, i, 0:1], scalar1=1.0 / D,
                                                      scalar2=EPS, op0=ALU.mult, op1=ALU.add),
                     reads=[R_rr[i]], writes=[R_rr[i]])
                S.op("act", lambda e: e.sqrt(out=rr[:n, i, 2:3], in_=rr[:n, i, 1:2]), reads=[R_rr[i]], writes=[R_rr[i]])
                S.op("dve", lambda e: e.reciprocal(out=rr[:n, i, 3:4], in_=rr[:n, i, 2:3]),
                     reads=[R_rr[i]], writes=[R_rr[i]])
                S.op("dve", lambda e: e.scalar_tensor_tensor(out=xb[j][:n, :], in0=xt[j][:n, :], scalar=rr[:n, i, 3:4], in1=gmx[:n, :],
                                                             op0=ALU.mult, op1=ALU.mult),
                     reads=[R_xt[j], R_rr[i], R_gmx], writes=[R_xb[j]])
                for kc in range(8):
                    S.op("pe", lambda e: e.transpose(out=ptr[j][:, kc, :n], in_=xb[j][:n, kc * 128:(kc + 1) * 128],
                                                     identity=idb[:n, :n]),
                         reads=[R_xb[j], R_id], writes=[R_ptr[j]])
                S.op("dve", lambda e: e.tensor_copy(out=hnT[:, :, c0:c0 + n], in_=ptr[j][:, :, :n]),
                     reads=[R_ptr[j]], writes=[R_hnT[i]])

        S.barrier()
        if stage <= 0:
            S.finish("sp")
            return nc
        with ExitStack() as p2:
            pq = [ps("pq%d" % j, [128, 512], F32, p2) for j in range(2)]
            pk = [ps("pk%d" % j, [128, 512], F32, p2) for j in range(2)]
            pv = [ps("pv%d" % j, [128, 512], F32, p2) for j in range(2)]
            R_pq = [Res("pq%d" % j) for j in range(2)]
            R_pk = [Res("pk%d" % j) for j in range(2)]
            R_pv = [Res("pv%d" % j) for j in range(2)]
            ptq = ps("ptq", [128, 2, 4, 128], BF16, p2)
            R_ptq = [Res("ptq"), Res("ptk")]
            qf = [sb("qf%d" % j, [128, 512], F32, p2) for j in range(2)]
            kf = [sb("kf%d" % j, [128, 512], F32, p2) for j in range(2)]
            vf = [sb("vf%d" % j, [128, 512], F32, p2) for j in range(2)]
            R_qf = [Res("qf%d" % j) for j in range(2)]
            R_kf = [Res("kf%d" % j) for j in range(2)]
            R_vf = [Res("vf%d" % j) for j in range(2)]
            qb = sb("qb", [128, 512], BF16, p2)
            kb = sb("kb", [128, 512], BF16, p2)
            R_qb, R_kb = Res("qb"), Res("kb")
            rtmp = sb("rtmp", [128, 4, 8, 8], F32, p2)
            R_rtmp = Res("rtmp")

            S.op("pool", lambda e: e.memset(vA[:, :, :, 128:129], 1.0), writes=R_vA)

            def rope(buf, n, i, R_buf):
                x1 = AP(buf, 0, [[512, n], [64, 8], [1, 8]])
                x2 = AP(buf, 8, [[512, n], [64, 8], [1, 8]])
                cs = AP(ropc, i * 8, [[17 * 8, n], [0, 8], [1, 8]])
                sn = AP(rops, i * 8, [[17 * 8, n], [0, 8], [1, 8]])
                for t, (a, b) in enumerate(((x1, cs), (x2, sn), (x2, cs), (x1, sn))):
                    S.op("dve", lambda e: e.tensor_tensor(out=rtmp[:n, t], in0=a, in1=b, op=ALU.mult),
                         reads=[R_buf, R_rope], writes=[R_rtmp])
                S.op("dve", lambda e: e.tensor_tensor(out=x1, in0=rtmp[:n, 0], in1=rtmp[:n, 1], op=ALU.subtract),
                     reads=[R_rtmp], writes=[R_buf])
                S.op("dve", lambda e: e.tensor_tensor(out=x2, in0=rtmp[:n, 2], in1=rtmp[:n, 3], op=ALU.add),
                     reads=[R_rtmp], writes=[R_buf])

            for i, (c0, n) in enumerate(TILES):
                j = i % 2
                for cb, (pX, R_pX) in enumerate(((pq, R_pq), (pk, R_pk), (pv, R_pv))):
                    for kc in range(8):
                        S.op("pe", lambda e: e.matmul(pX[j][:n, :], lhsT=hnT[:, kc, c0:c0 + n],
                                                      rhs=winb[:, kc, cb * 512:(cb + 1) * 512],
                                                      start=(kc == 0), stop=(kc == 7)),
                             reads=[R_hnT[i], R_winb[kc]], writes=[R_pX[j]])
                S.op("act", lambda e: e.activation(out=vf[j][:n, :], in_=pv[j][:n, :], func=AF.Copy),
                     reads=[R_pv[j]], writes=[R_vf[j]])
                if i < 16:
                    S.dma("sp", lambda e: e.dma_start(out=o_vp[c0:c0 + n, :], in_=vf[j][:n, :]), "vf%d" % j,
                          reads=[R_vf[j]])
                    S.op("pool", lambda e: e.tensor_copy(out=vA[:, i, :, 0:128],
                                                         in_=vf[j][:, :].rearrange("p (h e) -> p h e", h=4)),
                         reads=[R_vf[j]], writes=[R_vA[i]])
                else:
                    S.dma("sp", lambda e: e.dma_start(out=o_vs, in_=vf[j][:n, :]), "vf%d" % j, reads=[R_vf[j]])
                    S.op("pool", lambda e: e.tensor_copy(out=vS[:, :], in_=vf[j][:n, :]),
                         reads=[R_vf[j]], writes=[R_vS])
                if stage < 2:
                    continue
                S.op("dve", lambda e: e.tensor_copy(out=qf[j][:n, :], in_=pq[j][:n, :]),
                     reads=[R_pq[j]], writes=[R_qf[j]])
                rope(qf[j], n, i, R_qf[j])
                S.op("act", lambda e: e.activation(out=qb[:n, :], in_=qf[j][:n, :], func=AF.Copy, scale=0.125),
                     reads=[R_qf[j]], writes=[R_qb])
                if i == 16:
                    S.op("pool", lambda e: e.tensor_copy(out=qsf[:, :], in_=qf[j][:n, :]),
                         reads=[R_qf[j]], writes=[R_qsf])
                for h in range(4):
                    S.op("pe", lambda e: e.transpose(out=ptq[:, 0, h, :n], in_=qb[:n, h * 128:(h + 1) * 128],
                                                     identity=idb[:n, :n]),
                         reads=[R_qb, R_id], writes=[R_ptq[0]])
                S.op("act", lambda e: e.activation(out=qT[:, :, c0:c0 + n], in_=ptq[:, 0, :, :n], func=AF.Copy),
                     reads=[R_ptq[0]], writes=[R_qT[i]])
                if stage < 3:
                    continue
                S.op("dve", lambda e: e.tensor_copy(out=kf[j][:n, :], in_=pk[j][:n, :]),
                     reads=[R_pk[j]], writes=[R_kf[j]])
                rope(kf[j], n, i, R_kf[j])
                if i < 16:
                    S.dma("sp", lambda e: e.dma_start(out=o_kp[c0:c0 + n, :], in_=kf[j][:n, :]), "kf%d" % j,
                          reads=[R_kf[j]])
                else:
                    S.dma("sp", lambda e: e.dma_start(out=o_ks, in_=kf[j][:n, :]), "kf%d" % j, reads=[R_kf[j]])
                    S.op("pool", lambda e: e.tensor_copy(out=ksf[:, :], in_=kf[j][:n, :]),
                         reads=[R_kf[j]], writes=[R_ksf])
                S.op("act", lambda e: e.activation(out=kb[:n, :], in_=kf[j][:n, :], func=AF.Copy),
                     reads=[R_kf[j]], writes=[R_kb])
                for h in range(4):
                    S.op("pe", lambda e: e.transpose(out=ptq[:, 1, h, :n], in_=kb[:n, h * 128:(h + 1) * 128],
                                                     identity=idb[:n, :n]),
                         reads=[R_kb, R_id], writes=[R_ptq[1]])
                S.op("act", lambda e: e.activation(out=kT[:, :, c0:c0 + n], in_=ptq[:, 1, :, :n], func=AF.Copy),
                     reads=[R_ptq[1]], writes=[R_kT[i]])

            for cu in range(4 if stage >= 4 else 0):
                for bi, (c0, nb) in enumerate(BLOCKS):
                    j = (cu * 5 + bi) % 2
                    for kc in range(8):
                        S.op("pe", lambda e: e.matmul(pq[j][:, :nb], lhsT=winb[:, kc, 1536 + cu * 128:1536 + (cu + 1) * 128],
                                                      rhs=hnT[:, kc, c0:c0 + nb], start=(kc == 0), stop=(kc == 7)),
                             reads=R_hnT + [R_winb[kc]], writes=[R_pq[j]])
                    S.op("act", lambda e: e.activation(out=uT[:, cu, c0:c0 + nb], in_=pq[j][:, :nb], func=AF.Copy),
                         reads=[R_pq[j]], writes=[R_uT[bi]])

        pw.close()
        S.barrier()
        if stage >= 5:
            ssm_phase(nc, S, sb, ps, locals())
        if stage >= 6:
            attn_phase(nc, S, sb, ps, locals())
        pA.close()
        S.barrier()
        if stage >= 7:
            ffn_phase(nc, S, sb, ps, locals())

        S.finish("sp")
        print("insts", S.n_inst, "waits", S.n_wait)
    return nc


def _rope_tables():
    half = 8
    inv = (500000.0 ** (-np.arange(0, 16, 2, dtype=np.float32) / np.float32(16))).astype(np.float32)
    pos = np.concatenate([np.arange(SEQ), np.full(NSC, PAST)]).astype(np.float32)
    ang = (pos[:, None] * inv[None, :]).astype(np.float32)
    return np.cos(ang).astype(np.float32), np.sin(ang).astype(np.float32)


def _ssm_maps(inp, c):
    f = np.float32
    a_re, a_im = inp["ssm_a_re"][0], inp["ssm_a_im"][0]
    ldt = inp["ssm_log_dt"][0]
    b_re, b_im = inp["ssm_b_re"][0], inp["ssm_b_im"][0]
    c_re, c_im = inp["ssm_c_re"][0], inp["ssm_c_im"][0]
    bcol = [np.zeros((128, 16, 128), f) for _ in range(2)]
    cpad = [np.zeros((128, 16, 128), f) for _ in range(2)]
    for g in range(32):
        pr, g2 = g // 2, g % 2
        pl = pr % 4
        r0 = pl * 32 + g2 * 16
        for k, (bb, cc) in enumerate(((b_re, c_re), (b_im, c_im))):
            bcol[k][g2 * 64:(g2 + 1) * 64, pr, r0:r0 + 16] = bb[g]
            cpad[k][g2 * 64:(g2 + 1) * 64, pr, r0:r0 + 16] = cc[g].T
    col = lambda a: np.ascontiguousarray(a.reshape(16, 128).T)
    h0 = lambda a: np.ascontiguousarray(a[0, NSC * c:NSC * (c + 1)].reshape(NSC, 16, 128).transpose(2, 1, 0))
    return dict(
        bcol_re=bcol[0], bcol_im=bcol[1], cpad_re=cpad[0], cpad_im=cpad[1],
        acol_re=col(a_re), acol_im=col(a_im), ldt_col=col(np.repeat(ldt[:, None], 64, axis=1)),
        dcol=np.ascontiguousarray(inp["ssm_d"][0].reshape(4, 128).T),
        h0c_re=h0(inp["state_ssm_re"]), h0c_im=h0(inp["state_ssm_im"]),
        w_glu=np.ascontiguousarray(inp["w_glu"][0]),
        trow1=np.ascontiguousarray(np.broadcast_to(np.arange(1, 513, dtype=f)[None, :], (128, 512))),
    )


def _attn_maps(inp, c, dev_pool=None):
    f = np.float32
    lamv = np.concatenate([inp["lambda_q1"][0], inp["lambda_k1"][0], inp["lambda_q2"][0], inp["lambda_k2"][0]]).reshape(1, 256)
    kk, qq = np.meshgrid(np.arange(128), np.arange(128), indexing="ij")
    tri = (qq >= kk).astype(f)
    pt = inp["page_table"][NSC * c:NSC * (c + 1)]
    p = np.arange(128)
    ptrep = np.stack([pt[:, 4 * jj + p // 32] for jj in range(16)], axis=2).astype(np.int32)
    if dev_pool == "compact":
        used = pt.reshape(-1)
        ck = np.ascontiguousarray(inp["cache_k"][0][used]).reshape(-1, 2048)
        cv = np.ascontiguousarray(inp["cache_v"][0][used]).reshape(-1, 2048)
        pt = np.arange(used.size).reshape(pt.shape)
        ptrep = np.stack([pt[:, 4 * jj + p // 32] for jj in range(16)], axis=2).astype(np.int32)
    elif dev_pool is not None:
        ck = np.zeros((dev_pool * 32, 2048), f)
        cv = np.zeros((dev_pool * 32, 2048), f)
        ptrep = ptrep % dev_pool
    else:
        ck = inp["cache_k"][0].reshape(-1, 2048)
        cv = inp["cache_v"][0].reshape(-1, 2048)
    selq = np.zeros((NSC, NSC, 128), f)
    for s_ in range(NSC):
        selq[s_, s_, :] = 1.0
    m0 = np.zeros((128, 1), f)
    m0[0, 0] = 1.0
    return dict(lamv=lamv.astype(f), gsub=inp["subln_g"][0].reshape(1, 128), trimask=tri, cache_k=ck, cache_v=cv,
                ptrep=np.ascontiguousarray(ptrep), poff=np.ascontiguousarray(np.broadcast_to((p % 32).astype(np.float32)[:, None], (128, NSC * 16))), selq=selq, mask0=m0)


def _ffn_maps(inp, c):
    sc = inp["state_conv"][0, NSC * c:NSC * (c + 1)]
    return dict(
        w_o=np.ascontiguousarray(inp["w_o"][0]),
        nffn=np.ascontiguousarray(inp["norm_ffn"][0].reshape(1, D)),
        w_up=np.ascontiguousarray(inp["w_up"][0]),
        convw=np.ascontiguousarray(inp["conv_w"][0].reshape(3, NFC, 128).transpose(2, 0, 1)),
        convb=np.ascontiguousarray(inp["conv_b"][0].reshape(NFC, 128).T),
        w_down=np.ascontiguousarray(inp["w_down"][0]),
        fnorm=inp["final_norm"].reshape(1, D),
        sconv=np.ascontiguousarray(sc.reshape(NSC * 2, DFF)),
        sconv1=np.ascontiguousarray(sc[:, 1, :]),
    )


def make_in_maps(inp, cores, dev_pool=None):
    cosT, sinT = _rope_tables()
    ident = np.eye(128, dtype=np.float32)
    maps = []
    for c in cores:
        m = dict(
            xp=np.ascontiguousarray(inp["x_prompt"][c]),
            xs=np.ascontiguousarray(inp["x_sample"][NSC * c:NSC * (c + 1), 0, :]),
            w_in=np.ascontiguousarray(inp["w_in"][0]),
            nmix=np.ascontiguousarray(inp["norm_mix"][0].reshape(1, D)),
            ropec=cosT, ropes=sinT, ident=ident,
        )
        m.update(_ssm_maps(inp, c))
        m.update(_attn_maps(inp, c, dev_pool))
        m.update(_ffn_maps(inp, c))
        maps.append(m)
    return maps


def kernel(**inp):
    inp = {k: np.asarray(v) for k, v in inp.items()}
    n_pool = inp["cache_k"].shape[1]
    nc = build_program(n_pool)
    cores = list(range(8))
    res = run_bass_kernel_spmd(nc, make_in_maps(inp, cores), core_ids=cores)
    r = res.results
    f = np.float32
    cat = lambda key: np.stack([np.asarray(r[c][key], dtype=f) for c in cores])
    y_prompt = cat("o_yp").reshape(8, SEQ, D)
    y_sample = cat("o_ys").reshape(32, 1, D)
    k_prompt = cat("o_kp").reshape(1, 8, SEQ, 4, 128)
    v_prompt = cat("o_vp").reshape(1, 8, SEQ, 4, 128)
    hre_p = cat("o_hre").reshape(1, 8, 32, 64)
    him_p = cat("o_him").reshape(1, 8, 32, 64)
    conv_p = cat("o_cp").reshape(1, 8, 2, DFF)
    k_sample = cat("o_ks").reshape(1, 32, 1, 4, 128)
    v_sample = cat("o_vs").reshape(1, 32, 1, 4, 128)
    hre_s = cat("o_hres").reshape(1, 32, 32, 64)
    him_s = cat("o_hims").reshape(1, 32, 32, 64)
    conv_s = cat("o_cs").reshape(1, 32, 2, DFF)
    return (y_prompt, y_sample, k_prompt, v_prompt, hre_p, him_p, conv_p,
            k_sample, v_sample, hre_s, him_s, conv_s)
```

```python
import numpy as np
from contextlib import ExitStack
import concourse.bass as bass
import concourse.mybir as mybir
from concourse.bass_utils import run_bass_kernel_spmd

F32 = mybir.dt.float32
BF16 = mybir.dt.bfloat16
I32 = mybir.dt.int32
ALU = mybir.AluOpType
AF = mybir.ActivationFunctionType
AX = mybir.AxisListType

D = 1024
SEQ = 2048
NSC = 4
NCOL = SEQ + NSC
DPROJ = 2048
DFF = 2816
NFC = DFF // 128
PAGE = 128
NPAGES = 64
PAST = PAGE * NPAGES
EPS = 1e-6
SUBLN_EPS = 1e-5
LAM_INIT = 0.8 - 0.6 * 1.0
SELF_ORDERED = ("pe",)
TILES = [(128 * i, 128) for i in range(16)] + [(SEQ, NSC)]
BLOCKS = [(512 * i, 512) for i in range(4)] + [(SEQ, NSC)]


class Res:
    __slots__ = ("name", "w", "r")

    def __init__(self, name):
        self.name = name
        self.w = None
        self.r = []


class _Eng:
    def __init__(self, name, eng, sem):
        self.name = name
        self.eng = eng
        self.sem = sem
        self.count = 0
        self.seen = {}


class Sched:
    def __init__(self, nc, es):
        self.nc = nc
        self.es = es
        self.sems = {}
        self.engs = {}
        for name, eng in (("pe", nc.tensor), ("dve", nc.vector), ("act", nc.scalar),
                          ("pool", nc.gpsimd), ("sp", nc.sync)):
            sem = es.enter_context(nc.semaphore("sem_" + name))
            self.sems[name] = sem
            self.engs[name] = _Eng(name, eng, sem)
        self.dma_sems = {}
        self.n_inst = 0
        self.n_wait = 0

    def _dma_sem(self, key):
        if key not in self.dma_sems:
            sem = self.es.enter_context(self.nc.semaphore("dsem_" + key))
            self.sems["d:" + key] = sem
            self.dma_sems[key] = [sem, 0]
        return self.dma_sems[key]

    @staticmethod
    def _deps(reads, writes):
        deps = {}

        def add(d):
            if d is not None and deps.get(d[0], 0) < d[1]:
                deps[d[0]] = d[1]
        for r in reads:
            add(r.w)
        for w in writes:
            add(w.w)
            for rd in w.r:
                add(rd)
        return deps

    def _wait(self, E, deps):
        for k, v in deps.items():
            if E.seen.get(k, 0) >= v:
                continue
            E.eng.wait_ge(self.sems[k], v)
            E.seen[k] = v
            self.n_wait += 1

    @staticmethod
    def _mark(reads, writes, stamp):
        for r in reads:
            r.r.append(stamp)
        for w in writes:
            w.w = stamp
            w.r = []

    def op(self, ename, fn, reads=(), writes=()):
        E = self.engs[ename]
        deps = self._deps(reads, writes)
        if ename in SELF_ORDERED:
            deps.pop(ename, None)
        self._wait(E, deps)
        ins = fn(E.eng)
        E.count += 1
        ins.then_inc(E.sem, 1)
        self._mark(reads, writes, (ename, E.count))
        self.n_inst += 1
        return ins

    def dma(self, ename, fn, skey, reads=(), writes=()):
        E = self.engs[ename]
        self._wait(E, self._deps(reads, writes))
        ds = self._dma_sem(skey)
        ins = fn(E.eng)
        ds[1] += 16
        ins.then_inc(ds[0], 16)
        self._mark(reads, writes, ("d:" + skey, ds[1]))
        self.n_inst += 1
        return ins

    def barrier(self):
        deps = {n: e.count for n, e in self.engs.items() if e.count}
        for k, (sem, c) in self.dma_sems.items():
            if c:
                deps["d:" + k] = c
        for E in self.engs.values():
            d = dict(deps)
            self._wait(E, d)

    def finish(self, ename="sp"):
        E = self.engs[ename]
        deps = {n: e.count for n, e in self.engs.items() if e.count}
        for k, (sem, c) in self.dma_sems.items():
            if c:
                deps["d:" + k] = c
        self._wait(E, deps)


def AP(t, off, dims):
    return bass.AP(t, off, dims)


import math
PI = math.pi


def range_reduce(S, eng, out, a, ki, kf, m, n_part, R_out, R_tmp, reads):
    S.op(eng, lambda e: e.tensor_scalar(out=ki, in0=a, scalar1=1.0 / (2 * PI), scalar2=None, op0=ALU.mult),
         reads=reads, writes=[R_tmp])
    S.op(eng, lambda e: e.tensor_copy(out=kf, in_=ki), reads=[R_tmp], writes=[R_tmp])
    S.op("dve", lambda e: e.scalar_tensor_tensor(out=out, in0=kf, scalar=-2 * PI, in1=a, op0=ALU.mult, op1=ALU.add),
         reads=[R_tmp] + list(reads), writes=[R_out])
    for thr, corr in ((PI, -2 * PI),):
        S.op(eng, lambda e: e.tensor_scalar(out=m, in0=out, scalar1=thr, scalar2=corr, op0=ALU.is_gt, op1=ALU.mult),
             reads=[R_out], writes=[R_tmp])
        S.op(eng, lambda e: e.tensor_tensor(out=out, in0=out, in1=m, op=ALU.add), reads=[R_out, R_tmp], writes=[R_out])
    S.op(eng, lambda e: e.tensor_scalar(out=m, in0=out, scalar1=-PI, scalar2=2 * PI, op0=ALU.is_lt, op1=ALU.mult),
         reads=[R_out], writes=[R_tmp])
    S.op(eng, lambda e: e.tensor_tensor(out=out, in0=out, in1=m, op=ALU.add), reads=[R_out, R_tmp], writes=[R_out])


def sincos(S, eng, s_out, c_out, ang, scr, n_part, R_s, R_c, R_scr, reads, nbias):
    range_reduce(S, eng, scr["y"], ang, scr["ki"], scr["kf"], scr["m"], n_part, R_scr, R_scr, reads)
    S.op("act", lambda e: e.activation(out=s_out, in_=scr["y"], func=AF.Sin), reads=[R_scr], writes=[R_s])
    S.op(eng, lambda e: e.tensor_scalar(out=scr["a"], in0=ang, scalar1=PI / 2, scalar2=None, op0=ALU.add),
         reads=list(reads) + [R_scr], writes=[R_scr])
    range_reduce(S, eng, scr["y"], scr["a"], scr["ki"], scr["kf"], scr["m"], n_part, R_scr, R_scr, [R_scr])
    S.op("act", lambda e: e.activation(out=c_out, in_=scr["y"], func=AF.Sin), reads=[R_scr], writes=[R_c])


def ssm_phase(nc, S, sb, ps, L):
    uT, R_uT, mixT, R_mix, idf, R_id = L["uT"], L["R_uT"], L["mixT"], L["R_mix"], L["idf"], L["R_id"]
    stage = L["stage"]
    with ExitStack() as pz:
        col = sb("col", [128, 26, 16], F32, pz)
        R_col = Res("col")
        dco = sb("dco", [128, 4], F32, pz)
        h0r = sb("h0r", [128, 16, NSC], F32, pz)
        h0i = sb("h0i", [128, 16, NSC], F32, pz)
        R_h0 = Res("h0")
        tr1 = sb("tr1", [128, 128], F32, pz)
        R_tr1 = Res("tr1")
        wgb = sb("wgb", [128, 4, 512], BF16, pz)
        R_wgb = Res("wgb")
        ygT = sb("ygT", [128, 4, NCOL], BF16, pz)
        R_yg = [Res("yg%d" % b) for b in range(5)]
        hfin = sb("hfin", [128, 2, 16], F32, pz)
        R_hfin = Res("hfin")
        hsmp = sb("hsmp", [128, 2, NSC, 16], F32, pz)
        R_hsmp = Res("hsmp")
        S.dma("sp", lambda e: e.dma_start(out=dco[:], in_=L["dcol"]), "z_c", writes=[R_col])
        S.dma("sp", lambda e: e.dma_start(out=col[:, 0, :], in_=L["acol_re"]), "z_c", writes=[R_col])
        S.dma("sp", lambda e: e.dma_start(out=col[:, 1, :], in_=L["acol_im"]), "z_c", writes=[R_col])
        S.dma("sp", lambda e: e.dma_start(out=col[:, 2, :], in_=L["ldt_col"]), "z_c", writes=[R_col])
        S.dma("sp", lambda e: e.dma_start(out=h0r[:], in_=L["h0c_re"]), "z_h", writes=[R_h0])
        S.dma("sp", lambda e: e.dma_start(out=h0i[:], in_=L["h0c_im"]), "z_h", writes=[R_h0])
        S.dma("sp", lambda e: e.dma_start(out=tr1[:], in_=L["trow1"][:, 0:128]), "z_t", writes=[R_tr1])

        with ExitStack() as pc:
            cki = sb("c_ki", [128, 16], I32, pc)
            cs = {k: sb("c_" + k, [128, 16], F32, pc) for k in ("a", "kf", "m", "y")}
            S.op("act", lambda e: e.activation(out=col[:, 2, :], in_=col[:, 2, :], func=AF.Exp), reads=[R_col], writes=[R_col])
            S.op("dve", lambda e: e.tensor_tensor(out=col[:, 3, :], in0=col[:, 1, :], in1=col[:, 2, :], op=ALU.mult), reads=[R_col], writes=[R_col])
            S.op("dve", lambda e: e.tensor_tensor(out=col[:, 9, :], in0=col[:, 0, :], in1=col[:, 2, :], op=ALU.mult), reads=[R_col], writes=[R_col])
            S.op("act", lambda e: e.activation(out=col[:, 4, :], in_=col[:, 9, :], func=AF.Exp), reads=[R_col], writes=[R_col])
            scr = dict(a=cs["a"][:], ki=cki[:], kf=cs["kf"][:], m=cs["m"][:], y=cs["y"][:])
            sincos(S, "dve", col[:, 5, :], col[:, 6, :], col[:, 3, :], scr, 128, R_col, R_col, R_col, [R_col], None)
            S.op("dve", lambda e: e.tensor_tensor(out=col[:, 7, :], in0=col[:, 6, :], in1=col[:, 4, :], op=ALU.mult), reads=[R_col], writes=[R_col])
            S.op("dve", lambda e: e.tensor_tensor(out=col[:, 8, :], in0=col[:, 5, :], in1=col[:, 4, :], op=ALU.mult), reads=[R_col], writes=[R_col])

        cr = lambda k: col[:, k, :]
        tt_ = lambda o, x, y, op: S.op("dve", lambda e: e.tensor_tensor(out=cr(o), in0=cr(x), in1=cr(y), op=op), reads=[R_col], writes=[R_col])
        S.op("dve", lambda e: e.tensor_scalar(out=cr(9), in0=cr(7), scalar1=-1.0, scalar2=None, op0=ALU.add), reads=[R_col], writes=[R_col])
        tt_(11, 0, 0, ALU.mult); tt_(12, 1, 1, ALU.mult); tt_(10, 11, 12, ALU.add)
        S.op("dve", lambda e: e.reciprocal(out=cr(10), in_=cr(10)), reads=[R_col], writes=[R_col])
        tt_(11, 9, 0, ALU.mult); tt_(12, 8, 1, ALU.mult); tt_(11, 11, 12, ALU.add); tt_(13, 11, 10, ALU.mult)
        tt_(11, 8, 0, ALU.mult); tt_(12, 9, 1, ALU.mult); tt_(11, 11, 12, ALU.subtract); tt_(14, 11, 10, ALU.mult)

        def cmul(o_re, o_im, a_re, a_im, b_re, b_im):
            tt_(11, a_re, b_re, ALU.mult); tt_(12, a_im, b_im, ALU.mult); tt_(o_re, 11, 12, ALU.subtract)
            tt_(11, a_re, b_im, ALU.mult); tt_(12, a_im, b_re, ALU.mult); tt_(o_im, 11, 12, ALU.add)
        cmul(16, 17, 7, 8, 7, 8)
        cmul(18, 19, 16, 17, 7, 8)
        cmul(20, 21, 16, 17, 16, 17)
        S.op("dve", lambda e: e.tensor_scalar(out=cr(22), in0=cr(3), scalar1=4.0, scalar2=None, op0=ALU.mult), reads=[R_col], writes=[R_col])
        tt_(24, 4, 4, ALU.mult); tt_(23, 24, 24, ALU.mult)
        LBK = {1: (7, 8), 2: (16, 17), 3: (18, 19), 4: (20, 21)}
        for kc in range(4):
            S.dma("pool", lambda e: e.dma_start(out=wgb[:, kc, :], in_=L["w_glu"][kc * 128:(kc + 1) * 128, :]), "z_w", writes=[R_wgb])
        S.barrier()

        with ExitStack() as pm:
            wB_ = [[sb("wB%d%d" % (qq, c), [128, 4, 4, 128], BF16, pm) for c in range(2)] for qq in range(2)]
            wC_ = [[sb("wC%d%d" % (qq, c), [128, 4, 5, 128], BF16, pm) for c in range(2)] for qq in range(2)]
            KK_ = [sb("KK%d" % qq, [128, 4, 128], BF16, pm) for qq in range(2)]
            R_wB_ = [Res("wB%d" % qq) for qq in range(2)]
            R_wC_ = [Res("wC%d" % qq) for qq in range(2)]
            R_KK_ = [Res("KK%d" % qq) for qq in range(2)]
            braw = [sb("braw%d" % c, [128, 4, 128], F32, pm) for c in range(2)]
            craw = braw
            bbf = [sb("bbf%d" % c, [128, 4, 128], F32, pm) for c in range(2)]
            tq = [sb("tq%d" % c, [128, 4, 128], F32, pm) for c in range(2)]
            pcb = [[sb("pcb%d%d" % (k, c), [128, 4, 128], BF16, pm) for c in range(2)] for k in range(4)]
            R_raw, R_bbf, R_tq = Res("raw"), Res("bbf"), Res("tq")
            R_pcb = [Res("pcb%d" % k) for k in range(4)]
            ptb = ps("r_ptb", [128, 4, 128], BF16, pm)
            pkk = ps("r_pkk", [128, 128], F32, pm)
            Rp, Rpk = Res("ptb"), Res("pkk")
            etc_ = [sb("etc%d" % qq, [128, 4, 128], F32, pm) for qq in range(2)]
            ets_ = [sb("ets%d" % qq, [128, 4, 128], F32, pm) for qq in range(2)]
            R_et_ = [[Res("et%d%d" % (qq, k)) for k in range(4)] for qq in range(2)]
            eki = sb("e_ki", [128, 128], I32, pm)
            esc = [sb("esc%d" % k, [128, 128], F32, pm) for k in range(5)]
            R_esc = Res("esc")
            pbr = [ps("pbr%d" % j, [128, 128], F32, pm) for j in range(2)]
            pbi = [ps("pbi%d" % j, [128, 128], F32, pm) for j in range(2)]
            R_pb = [Res("pb%d" % j) for j in range(2)]
            py = [ps("py%d" % j, [128, 512], F32, pm) for j in range(2)]
            R_py = [Res("py%d" % j) for j in range(2)]
            tt = [sb("tt%d" % k, [128, 128], F32, pm) for k in range(4)]
            R_tt = Res("tt")
            w_r = [sb("w_r0", [128, 128], F32, pm)] * 2
            w_i = [sb("w_i0", [128, 128], F32, pm)] * 2
            R_w = [Res("w0")] * 2
            g_r = [sb("g_r%d" % j, [128, 128], F32, pm) for j in range(2)]
            g_i = [sb("g_i%d" % j, [128, 128], F32, pm) for j in range(2)]
            R_g = [Res("g%d" % j) for j in range(2)]
            uu = [sb("uu%d" % k, [128, 128], F32, pm) for k in range(2)]
            R_uu = Res("uu")
            hr = [sb("hr%d" % j, [128, 128], F32, pm) for j in range(2)]
            hi = [sb("hi%d" % j, [128, 128], F32, pm) for j in range(2)]
            R_h = [Res("h%d" % j) for j in range(2)]
            hrb = [sb("hrb%d" % j, [128, 128], BF16, pm) for j in range(2)]
            hib = [sb("hib%d" % j, [128, 128], BF16, pm) for j in range(2)]
            R_hb = [Res("hb%d" % j) for j in range(2)]
            yf = sb("yf", [128, 512], F32, pm)
            R_yf = Res("yf")
            carry = sb("carry", [128, 2, 16], F32, pm)
            R_carry = [Res("carry%d" % p) for p in range(16)]
            S.op("dve", lambda e: e.memset(carry[:], 0.0), writes=R_carry)

            def prep(q):
                qs = slice(4 * q, 4 * q + 4)
                wB, wC, KK, R_wB, R_wC, R_KK = wB_[q % 2], wC_[q % 2], KK_[q % 2], R_wB_[q % 2], R_wC_[q % 2], R_KK_[q % 2]
                etc, ets, R_et = etc_[q % 2], ets_[q % 2], R_et_[q % 2]
                cb = lambda row: col[:, row, qs].unsqueeze(2).to_broadcast([128, 4, 128])
                S.dma("sp", lambda e: e.dma_start(out=braw[0][:], in_=L["bcol_re"][:, qs, :]), "z_b", writes=[R_raw])
                S.dma("sp", lambda e: e.dma_start(out=braw[1][:], in_=L["bcol_im"][:, qs, :]), "z_b", writes=[R_raw])

                def cplx(o_re, o_im, R_o, x_re, x_im, R_x, rr_, ri_, neg_im=False):
                    S.op("pool", lambda e: e.tensor_tensor(out=tq[0][:], in0=x_re, in1=cb(rr_), op=ALU.mult), reads=[R_x, R_col], writes=[R_tq])
                    S.op("pool", lambda e: e.tensor_tensor(out=tq[1][:], in0=x_im, in1=cb(ri_), op=ALU.mult), reads=[R_x, R_col], writes=[R_tq])
                    S.op("pool", lambda e: e.tensor_tensor(out=o_re, in0=tq[0][:], in1=tq[1][:], op=ALU.subtract), reads=[R_tq], writes=[R_o])
                    yield
                    S.op("pool", lambda e: e.tensor_tensor(out=tq[0][:], in0=x_im, in1=cb(rr_), op=ALU.mult), reads=[R_x, R_col, R_o], writes=[R_tq])
                    S.op("pool", lambda e: e.tensor_tensor(out=tq[1][:], in0=x_re, in1=cb(ri_), op=ALU.mult), reads=[R_x, R_col, R_o], writes=[R_tq])
                    if neg_im:
                        S.op("pool", lambda e: e.tensor_tensor(out=tq[0][:], in0=tq[0][:], in1=tq[1][:], op=ALU.add), reads=[R_tq], writes=[R_tq])
                        S.op("pool", lambda e: e.tensor_scalar(out=o_im, in0=tq[0][:], scalar1=-1.0, scalar2=None, op0=ALU.mult), reads=[R_tq], writes=[R_o])
                    else:
                        S.op("pool", lambda e: e.tensor_tensor(out=o_im, in0=tq[0][:], in1=tq[1][:], op=ALU.add), reads=[R_tq], writes=[R_o])
                    yield
                yield from cplx(bbf[0][:], bbf[1][:], R_bbf, braw[0][:], braw[1][:], R_raw, 13, 14)
                S.dma("sp", lambda e: e.dma_start(out=craw[0][:], in_=L["cpad_re"][:, qs, :]), "z_b", writes=[R_raw])
                S.dma("sp", lambda e: e.dma_start(out=craw[1][:], in_=L["cpad_im"][:, qs, :]), "z_b", writes=[R_raw])
                S.op("act", lambda e: e.activation(out=pcb[0][0][:], in_=bbf[0][:], func=AF.Copy), reads=[R_bbf], writes=[R_pcb[0]])
                S.op("act", lambda e: e.activation(out=pcb[0][1][:], in_=bbf[1][:], func=AF.Copy), reads=[R_bbf], writes=[R_pcb[0]])
                for k in (1, 2, 3):
                    yield from cplx(pcb[k][0][:], pcb[k][1][:], R_pcb[k], bbf[0][:], bbf[1][:], R_bbf, *LBK[k])
                for c in range(2):
                    for k in range(4):
                        for pl in range(4):
                            S.op("pe", lambda e: e.transpose(out=ptb[:, pl, :], in_=pcb[k][c][:, pl, :], identity=L["idb"][:, :]),
                                 reads=[R_pcb[k], R_id], writes=[Rp])
                        S.op("act", lambda e: e.activation(out=wB[c][:, :, k, :], in_=ptb[:], func=AF.Copy), reads=[Rp], writes=[R_wB])
                        yield
                S.op("act", lambda e: e.activation(out=wC[0][:, :, 0, :], in_=craw[0][:], func=AF.Copy), reads=[R_raw], writes=[R_wC])
                S.op("act", lambda e: e.activation(out=wC[1][:, :, 0, :], in_=craw[1][:], func=AF.Copy, scale=-1.0), reads=[R_raw], writes=[R_wC])
                for m in (1, 2, 3, 4):
                    yield from cplx(wC[0][:, :, m, :], wC[1][:, :, m, :], R_wC, craw[0][:], craw[1][:], R_raw, *LBK[m], neg_im=True)
                for tau in range(4):
                    n_ = 0
                    for pl in range(4):
                        for c in range(2):
                            S.op("pe", lambda e: e.matmul(pkk[:, :], lhsT=pcb[tau][c][:, pl, :], rhs=wC[c][:, pl, 0, :], start=(n_ == 0), stop=(n_ == 7)),
                                 reads=[R_pcb[tau], R_wC], writes=[Rpk])
                            n_ += 1
                    S.op("act", lambda e: e.activation(out=KK[:, tau, :], in_=pkk[:, :], func=AF.Copy), reads=[Rpk], writes=[R_KK])
                    yield
                for pl in range(4):
                    pr_ = 4 * q + pl
                    S.op("dve", lambda e: e.tensor_scalar(out=esc[4][:], in0=tr1[:, 0:128], scalar1=col[:, 22, pr_:pr_ + 1],
                                                          scalar2=None, op0=ALU.mult), reads=[R_tr1, R_col], writes=[R_esc])
                    scr = dict(a=esc[0][:], ki=eki[:], kf=esc[1][:], m=esc[2][:], y=esc[3][:])
                    sincos(S, "dve", ets[:, pl, :], etc[:, pl, :], esc[4][:], scr, 128, R_et[pl], R_et[pl], R_esc, [R_esc], None)
                    yield

            def u_str(q, c0, off, n):
                return AP(uT, q * NCOL + c0 + off, [[4 * NCOL, 128], [4, n]])

            def stA(n, q, b, pl):
                c0, nb = BLOCKS[b]
                wB, wC, KK, R_wB, R_wC, R_KK = wB_[q % 2], wC_[q % 2], KK_[q % 2], R_wB_[q % 2], R_wC_[q % 2], R_KK_[q % 2]
                etc, ets, R_et = etc_[q % 2], ets_[q % 2], R_et_[q % 2]
                j = n % 2
                if b < 4:
                    for c, pX in enumerate((pbr, pbi)):
                        for r_ in range(4):
                            S.op("pe", lambda e: e.matmul(pX[j][:, :], lhsT=wB[c][:, pl, 3 - r_, :], rhs=u_str(q, c0, r_, 128), start=(r_ == 0), stop=(r_ == 3)),
                                 reads=[R_wB, R_uT[b]], writes=[R_pb[j]])
                else:
                    for c, pX in enumerate((pbr, pbi)):
                        S.op("pe", lambda e: e.matmul(pX[j][:, :nb], lhsT=wB[c][:, pl, 0, :], rhs=uT[:, q, c0:c0 + nb], start=True, stop=True),
                             reads=[R_wB, R_uT[b]], writes=[R_pb[j]])

            def stB(n, q, b, pl):
                c0, nb = BLOCKS[b]
                wB, wC, KK, R_wB, R_wC, R_KK = wB_[q % 2], wC_[q % 2], KK_[q % 2], R_wB_[q % 2], R_wC_[q % 2], R_KK_[q % 2]
                etc, ets, R_et = etc_[q % 2], ets_[q % 2], R_et_[q % 2]
                pr_, j = 4 * q + pl, n % 2
                if b < 4:
                    c_, s_ = etc[:, pl, :], ets[:, pl, :]
                    S.op("dve", lambda e: e.tensor_tensor(out=tt[0][:], in0=pbr[j][:], in1=c_, op=ALU.mult), reads=[R_pb[j], R_et[pl]], writes=[R_tt])
                    S.op("dve", lambda e: e.tensor_tensor(out=tt[1][:], in0=pbi[j][:], in1=s_, op=ALU.mult), reads=[R_pb[j], R_et[pl]], writes=[R_tt])
                    S.op("dve", lambda e: e.tensor_tensor(out=tt[2][:], in0=pbi[j][:], in1=c_, op=ALU.mult), reads=[R_pb[j], R_et[pl]], writes=[R_tt])
                    S.op("dve", lambda e: e.tensor_tensor(out=tt[3][:], in0=pbr[j][:], in1=s_, op=ALU.mult), reads=[R_pb[j], R_et[pl]], writes=[R_tt])
                    S.op("dve", lambda e: e.tensor_tensor(out=w_r[j][:], in0=tt[0][:], in1=tt[1][:], op=ALU.add), reads=[R_tt], writes=[R_w[j]])
                    S.op("dve", lambda e: e.tensor_tensor(out=w_i[j][:], in0=tt[2][:], in1=tt[3][:], op=ALU.subtract), reads=[R_tt], writes=[R_w[j]])
                    S.op("dve", lambda e: e.tensor_tensor_scan(out=g_r[j][:], data0=col[:, 23, pr_:pr_ + 1].to_broadcast([128, 128]), data1=w_r[j][:],
                                                               initial=carry[:, 0, pr_:pr_ + 1], op0=ALU.mult, op1=ALU.add),
                         reads=[R_w[j], R_col, R_carry[pr_]], writes=[R_g[j]])
                    S.op("dve", lambda e: e.tensor_tensor_scan(out=g_i[j][:], data0=col[:, 23, pr_:pr_ + 1].to_broadcast([128, 128]), data1=w_i[j][:],
                                                               initial=carry[:, 1, pr_:pr_ + 1], op0=ALU.mult, op1=ALU.add),
                         reads=[R_w[j], R_col, R_carry[pr_]], writes=[R_g[j]])
                else:
                    lr, li = col[:, 7, pr_:pr_ + 1], col[:, 8, pr_:pr_ + 1]
                    S.op("dve", lambda e: e.tensor_scalar(out=tt[0][:, :nb], in0=h0r[:, pr_, :], scalar1=lr, scalar2=None, op0=ALU.mult), reads=[R_h0, R_col], writes=[R_tt])
                    S.op("dve", lambda e: e.tensor_scalar(out=tt[1][:, :nb], in0=h0i[:, pr_, :], scalar1=li, scalar2=None, op0=ALU.mult), reads=[R_h0, R_col], writes=[R_tt])
                    S.op("dve", lambda e: e.tensor_scalar(out=tt[2][:, :nb], in0=h0i[:, pr_, :], scalar1=lr, scalar2=None, op0=ALU.mult), reads=[R_h0, R_col], writes=[R_tt])
                    S.op("dve", lambda e: e.tensor_scalar(out=tt[3][:, :nb], in0=h0r[:, pr_, :], scalar1=li, scalar2=None, op0=ALU.mult), reads=[R_h0, R_col], writes=[R_tt])
                    S.op("dve", lambda e: e.tensor_tensor(out=tt[0][:, :nb], in0=tt[0][:, :nb], in1=tt[1][:, :nb], op=ALU.subtract), reads=[R_tt], writes=[R_tt])
                    S.op("dve", lambda e: e.tensor_tensor(out=tt[2][:, :nb], in0=tt[2][:, :nb], in1=tt[3][:, :nb], op=ALU.add), reads=[R_tt], writes=[R_tt])
                    S.op("dve", lambda e: e.tensor_tensor(out=hr[j][:, :nb], in0=tt[0][:, :nb], in1=pbr[j][:, :nb], op=ALU.add), reads=[R_tt, R_pb[j]], writes=[R_h[j]])
                    S.op("dve", lambda e: e.tensor_tensor(out=hi[j][:, :nb], in0=tt[2][:, :nb], in1=pbi[j][:, :nb], op=ALU.add), reads=[R_tt, R_pb[j]], writes=[R_h[j]])

            def stC(n, q, b, pl):
                c0, nb = BLOCKS[b]
                wB, wC, KK, R_wB, R_wC, R_KK = wB_[q % 2], wC_[q % 2], KK_[q % 2], R_wB_[q % 2], R_wC_[q % 2], R_KK_[q % 2]
                etc, ets, R_et = etc_[q % 2], ets_[q % 2], R_et_[q % 2]
                pr_, j = 4 * q + pl, n % 2
                if b < 4:
                    c_, s_ = etc[:, pl, :], ets[:, pl, :]
                    S.op("pool", lambda e: e.tensor_tensor(out=uu[0][:], in0=g_r[j][:], in1=c_, op=ALU.mult), reads=[R_g[j], R_et[pl]], writes=[R_uu])
                    S.op("pool", lambda e: e.tensor_tensor(out=uu[1][:], in0=g_i[j][:], in1=s_, op=ALU.mult), reads=[R_g[j], R_et[pl]], writes=[R_uu])
                    S.op("pool", lambda e: e.tensor_tensor(out=hr[j][:], in0=uu[0][:], in1=uu[1][:], op=ALU.subtract), reads=[R_uu], writes=[R_h[j]])
                    S.op("pool", lambda e: e.tensor_tensor(out=uu[0][:], in0=g_i[j][:], in1=c_, op=ALU.mult), reads=[R_g[j], R_et[pl]], writes=[R_uu])
                    S.op("pool", lambda e: e.tensor_tensor(out=uu[1][:], in0=g_r[j][:], in1=s_, op=ALU.mult), reads=[R_g[j], R_et[pl]], writes=[R_uu])
                    S.op("pool", lambda e: e.tensor_tensor(out=hi[j][:], in0=uu[0][:], in1=uu[1][:], op=ALU.add), reads=[R_uu], writes=[R_h[j]])
                    S.op("act", lambda e: e.activation(out=hrb[j][:, 0:1], in_=carry[:, 0, pr_:pr_ + 1], func=AF.Copy), reads=[R_carry[pr_]], writes=[R_hb[j]])
                    S.op("act", lambda e: e.activation(out=hib[j][:, 0:1], in_=carry[:, 1, pr_:pr_ + 1], func=AF.Copy), reads=[R_carry[pr_]], writes=[R_hb[j]])
                    S.op("act", lambda e: e.activation(out=hrb[j][:, 1:128], in_=hr[j][:, 0:127], func=AF.Copy), reads=[R_h[j]], writes=[R_hb[j]])
                    S.op("act", lambda e: e.activation(out=hib[j][:, 1:128], in_=hi[j][:, 0:127], func=AF.Copy), reads=[R_h[j]], writes=[R_hb[j]])
                    S.op("act", lambda e: e.activation(out=carry[:, 0, pr_:pr_ + 1], in_=hr[j][:, 127:128], func=AF.Copy), reads=[R_h[j], R_hb[j]], writes=[R_carry[pr_]])
                    S.op("act", lambda e: e.activation(out=carry[:, 1, pr_:pr_ + 1], in_=hi[j][:, 127:128], func=AF.Copy), reads=[R_h[j], R_hb[j]], writes=[R_carry[pr_]])
                    if b == 3:
                        S.op("act", lambda e: e.activation(out=hfin[:, 0, pr_:pr_ + 1], in_=hr[j][:, 127:128], func=AF.Copy), reads=[R_h[j]], writes=[R_hfin])
                        S.op("act", lambda e: e.activation(out=hfin[:, 1, pr_:pr_ + 1], in_=hi[j][:, 127:128], func=AF.Copy), reads=[R_h[j]], writes=[R_hfin])
                else:
                    S.op("act", lambda e: e.activation(out=hsmp[:, 0, :, pr_], in_=hr[j][:, :nb], func=AF.Copy), reads=[R_h[j]], writes=[R_hsmp])
                    S.op("act", lambda e: e.activation(out=hsmp[:, 1, :, pr_], in_=hi[j][:, :nb], func=AF.Copy), reads=[R_h[j]], writes=[R_hsmp])
                    S.op("act", lambda e: e.activation(out=hrb[j][:, :nb], in_=hr[j][:, :nb], func=AF.Copy), reads=[R_h[j]], writes=[R_hb[j]])
                    S.op("act", lambda e: e.activation(out=hib[j][:, :nb], in_=hi[j][:, :nb], func=AF.Copy), reads=[R_h[j]], writes=[R_hb[j]])

            def stD(n, q, b, pl):
                c0, nb = BLOCKS[b]
                wB, wC, KK, R_wB, R_wC, R_KK = wB_[q % 2], wC_[q % 2], KK_[q % 2], R_wB_[q % 2], R_wC_[q % 2], R_KK_[q % 2]
                etc, ets, R_et = etc_[q % 2], ets_[q % 2], R_et_[q % 2]
                pr_, j = 4 * q + pl, n % 2
                jy = (q * 5 + b) % 2
                if b < 4:
                    for r_ in range(4):
                        o_ap = AP(py[jy], r_, [[512, 128], [4, 128]])
                        for c, hX in enumerate((hrb, hib)):
                            S.op("pe", lambda e: e.matmul(o_ap, lhsT=wC[c][:, pl, r_ + 1, :], rhs=hX[j][:, :], start=(pl == 0 and r_ == 0 and c == 0), stop=False,
                                                          skip_group_check=True),
                                 reads=[R_wC, R_hb[j]], writes=[R_py[jy]])
                    if pl == 3:
                        for r_ in range(4):
                            o_ap = AP(py[jy], r_, [[512, 128], [4, 128]])
                            for tau in range(r_ + 1):
                                S.op("pe", lambda e: e.matmul(o_ap, lhsT=KK[:, tau, :], rhs=u_str(q, c0, r_ - tau, 128), start=False, stop=(r_ == 3 and tau == 3),
                                                              skip_group_check=True),
                                     reads=[R_KK, R_uT[b]], writes=[R_py[jy]])
                else:
                    S.op("pe", lambda e: e.matmul(py[jy][:, :nb], lhsT=wC[0][:, pl, 0, :], rhs=hrb[j][:, :nb], start=(pl == 0), stop=False),
                         reads=[R_wC, R_hb[j]], writes=[R_py[jy]])
                    S.op("pe", lambda e: e.matmul(py[jy][:, :nb], lhsT=wC[1][:, pl, 0, :], rhs=hib[j][:, :nb], start=False, stop=(pl == 3)),
                         reads=[R_wC, R_hb[j]], writes=[R_py[jy]])
                if pl == 3:
                    S.op("dve", lambda e: e.scalar_tensor_tensor(out=yf[:, :nb], in0=uT[:, q, c0:c0 + nb], scalar=dco[:, q:q + 1],
                                                                 in1=py[jy][:, :nb], op0=ALU.mult, op1=ALU.add),
                         reads=[R_uT[b], R_col, R_py[jy]], writes=[R_yf])
                    S.op("act", lambda e: e.activation(out=ygT[:, q, c0:c0 + nb], in_=yf[:, :nb], func=AF.Gelu_apprx_tanh),
                         reads=[R_yf], writes=[R_yg[b]])

            its = [(q, b, pl) for q in range(4) for b in range(5) for pl in range(4)]
            for _ in prep(0):
                pass
            pg = prep(1)
            stA(0, *its[0])
            for n, itx in enumerate(its):
                nxt = its[n + 1] if n + 1 < len(its) else None
                if nxt is not None and nxt[0] == itx[0]:
                    stA(n + 1, *nxt)
                stB(n, *itx)
                stC(n, *itx)
                stD(n, *itx)
                if pg is not None:
                    for _ in range(3):
                        if next(pg, "end") == "end":
                            pg = None
                            break
                if nxt is not None and nxt[0] != itx[0]:
                    if pg is not None:
                        for _ in pg:
                            pass
                    pg = prep(nxt[0] + 1) if nxt[0] + 1 < 4 else None
                    stA(n + 1, *nxt)
            for co in range(4):
                for b, (c0, nb) in enumerate(BLOCKS):
                    jy = (co * 5 + b) % 2
                    for kc in range(4):
                        S.op("pe", lambda e: e.matmul(py[jy][:, :nb], lhsT=wgb[:, kc, co * 128:(co + 1) * 128], rhs=ygT[:, kc, c0:c0 + nb],
                                                      start=(kc == 0), stop=(kc == 3)),
                             reads=[R_wgb, R_yg[b]], writes=[R_py[jy]])
                    S.op("act", lambda e: e.activation(out=yf[:, :nb], in_=py[jy][:, :nb], func=AF.Sigmoid), reads=[R_py[jy]], writes=[R_yf])
                    S.op("dve", lambda e: e.tensor_tensor(out=mixT[:, 4 + co, c0:c0 + nb], in0=yf[:, :nb], in1=ygT[:, co, c0:c0 + nb], op=ALU.mult),
                         reads=[R_yf, R_yg[b]], writes=[R_mix[4 + co]])
            S.barrier()
            pst = py[0]
            _ot = (braw[0], braw[1], bbf[0])
            oslot = lambda k: _ot[k // 4][0:16, k % 4, :]
            R_ost = Res("ost")
            srcs = [hfin[:, 0, :], hfin[:, 1, :]] + [hsmp[:, ri, s_, :] for ri in range(2) for s_ in range(NSC)]
            for k, src in enumerate(srcs):
                S.op("pe", lambda e: e.transpose(out=pst[0:16, 0:128], in_=src, identity=idf[:, :]),
                     reads=[R_hfin, R_hsmp, R_id], writes=[R_py[0]])
                S.op("dve", lambda e: e.tensor_copy(out=oslot(k), in_=pst[0:16, 0:128]), reads=[R_py[0]], writes=[R_ost])
            S.dma("sp", lambda e: e.dma_start(out=L["o_hre"], in_=oslot(0)), "z_o", reads=[R_ost])
            S.dma("sp", lambda e: e.dma_start(out=L["o_him"], in_=oslot(1)), "z_o", reads=[R_ost])
            for s_ in range(NSC):
                S.dma("sp", lambda e: e.dma_start(out=L["o_hres"][s_], in_=oslot(2 + s_)), "z_o", reads=[R_ost])
                S.dma("sp", lambda e: e.dma_start(out=L["o_hims"][s_], in_=oslot(2 + NSC + s_)), "z_o", reads=[R_ost])
            S.barrier()

def attn_phase(nc, S, sb, ps, L):
    qT, kT, vA, mixT = L["qT"], L["kT"], L["vA"], L["mixT"]
    R_qT, R_kT, R_vA, R_mix = L["R_qT"], L["R_kT"], L["R_vA"], L["R_mix"]
    idb, idf, R_id = L["idb"], L["idf"], L["R_id"]
    with ExitStack() as pa:
        lv = sb("lv", [128, 4, 64], F32, pa)
        lam = sb("lam", [128, 8], F32, pa)
        gs = sb("gs", [128, 128], F32, pa)
        trf = sb("trf", [128, 128], F32, pa)
        tri = sb("tri", [128, 128], BF16, pa)
        R_par = Res("apar")
        S.dma("sp", lambda e: e.dma_start(out=lv[:].rearrange("p a b -> p (a b)"), in_=AP(L["lamv"].tensor, 0, [[0, 128], [1, 256]])), "a_p", writes=[R_par])
        S.dma("sp", lambda e: e.dma_start(out=gs[:], in_=AP(L["gsub"].tensor, 0, [[0, 128], [1, 128]])), "a_p", writes=[R_par])
        S.dma("sp", lambda e: e.dma_start(out=trf[:], in_=L["trimask"]), "a_p", writes=[R_par])
        S.op("dve", lambda e: e.tensor_copy(out=tri[:], in_=trf[:]), reads=[R_par], writes=[R_par])
        S.op("dve", lambda e: e.tensor_scalar(out=gs[:], in0=gs[:], scalar1=1.0 - LAM_INIT, scalar2=None, op0=ALU.mult), reads=[R_par], writes=[R_par])
        S.op("dve", lambda e: e.tensor_tensor(out=lv[:, 0, :], in0=lv[:, 0, :], in1=lv[:, 1, :], op=ALU.mult), reads=[R_par], writes=[R_par])
        S.op("dve", lambda e: e.tensor_tensor(out=lv[:, 2, :], in0=lv[:, 2, :], in1=lv[:, 3, :], op=ALU.mult), reads=[R_par], writes=[R_par])
        S.op("dve", lambda e: e.reduce_sum(out=lam[:, 1:2], in_=lv[:, 0, :], axis=AX.X), reads=[R_par], writes=[R_par])
        S.op("dve", lambda e: e.reduce_sum(out=lam[:, 2:3], in_=lv[:, 2, :], axis=AX.X), reads=[R_par], writes=[R_par])
        S.op("act", lambda e: e.activation(out=lam[:, 1:3], in_=lam[:, 1:3], func=AF.Exp), reads=[R_par], writes=[R_par])
        S.op("dve", lambda e: e.tensor_tensor(out=lam[:, 0:1], in0=lam[:, 1:2], in1=lam[:, 2:3], op=ALU.subtract), reads=[R_par], writes=[R_par])
        S.op("dve", lambda e: e.tensor_scalar(out=lam[:, 0:1], in0=lam[:, 0:1], scalar1=LAM_INIT, scalar2=None, op0=ALU.add), reads=[R_par], writes=[R_par])

        ptp = ps("ptp", [128, 128], BF16, pa)
        R_ptp = Res("ptp")

        def subln_store(o_ap, n, dst_fn, scr):
            junk, st, ob, R_s = scr
            S.op("act", lambda e: e.activation(out=junk[:n, :], in_=o_ap, func=AF.Square, accum_out=st[:n, 0:1]), reads=[R_s], writes=[R_s])
            S.op("dve", lambda e: e.tensor_scalar(out=st[:n, 1:2], in0=st[:n, 0:1], scalar1=1.0 / 128, scalar2=SUBLN_EPS, op0=ALU.mult, op1=ALU.add), reads=[R_s], writes=[R_s])
            S.op("act", lambda e: e.sqrt(out=st[:n, 1:2], in_=st[:n, 1:2]), reads=[R_s], writes=[R_s])
            S.op("dve", lambda e: e.reciprocal(out=st[:n, 1:2], in_=st[:n, 1:2]), reads=[R_s], writes=[R_s])
            S.op("dve", lambda e: e.scalar_tensor_tensor(out=ob[:n, :], in0=o_ap, scalar=st[:n, 1:2], in1=gs[:n, :], op0=ALU.mult, op1=ALU.mult),
                 reads=[R_s, R_par], writes=[R_s])
            S.op("pe", lambda e: e.transpose(out=ptp[:, :n], in_=ob[:n, :], identity=idb[:n, :n]), reads=[R_s, R_id], writes=[R_ptp])
            dst_fn()

        psS = [ps("psS%d" % j, [128, 512], F32, pa) for j in range(2)]
        R_S = [Res("psS%d" % j) for j in range(2)]
        psO = ps("psO", [128, 2, 512], F32, pa)
        _rb = [Res("psO_b0"), Res("psO_b1")]
        R_O = [_rb[0], _rb[0], _rb[0], _rb[1]]
        accO = lambda k: psO[:, k // 3, (k % 3) * 129:(k % 3) * 129 + 129]
        eE = [sb("eE%d" % j, [128, 512], BF16, pa) for j in range(2)]
        R_E = [Res("eE%d" % j) for j in range(2)]
        osb = [sb("osb%d" % g, [128, 2, 4, 129], F32, pa) for g in range(2)]
        R_osb = [[Res("osb%d_%d" % (g, qt)) for qt in range(4)] for g in range(2)]
        of = [sb("of%d" % qt, [128, 128], F32, pa) for qt in range(4)]
        junk = [sb("ajunk%d" % qt, [128, 128], BF16, pa) for qt in range(4)]
        st = [sb("ast%d" % qt, [128, 4], F32, pa) for qt in range(4)]
        ob = [sb("aob%d" % qt, [128, 128], BF16, pa) for qt in range(4)]
        R_scr = [Res("ascr%d" % qt) for qt in range(4)]

        def prompt_group(gi):
            h, QB = gi // 4, gi % 4
            g2 = gi % 2
            nkb = 4 * QB + 4
            for c in range(2):
                def qk(kb):
                    o = kb - 4 * QB
                    qlo = max(o, 0) * 128
                    S.op("pe", lambda e: e.matmul(psS[kb % 2][:, qlo:512], lhsT=kT[64 * c:64 * c + 64, h, kb * 128:(kb + 1) * 128],
                                                  rhs=qT[64 * c:64 * c + 64, h, QB * 512 + qlo:QB * 512 + 512], start=True, stop=True),
                         reads=[R_kT[kb]] + R_qT[4 * QB:4 * QB + 4], writes=[R_S[kb % 2]])
                qk(0)
                for kb in range(nkb):
                    o = kb - 4 * QB
                    qlo = max(o, 0) * 128
                    j = kb % 2
                    if kb + 1 < nkb:
                        qk(kb + 1)
                    if kb % 2 == c:
                        sample_step()
                    S.op("act", lambda e: e.activation(out=eE[j][:, qlo:512], in_=psS[j][:, qlo:512], func=AF.Exp),
                         reads=[R_S[j]], writes=[R_E[j]])
                    if o >= 0:
                        S.op("pool", lambda e: e.tensor_tensor(out=eE[j][:, qlo:qlo + 128], in0=eE[j][:, qlo:qlo + 128], in1=tri[:], op=ALU.mult),
                             reads=[R_E[j], R_par], writes=[R_E[j]])
                    for qt in range(max(o, 0), 4):
                        S.op("pe", lambda e: e.matmul(accO(qt), lhsT=eE[j][:, qt * 128:(qt + 1) * 128], rhs=vA[:, kb, h, :],
                                                      start=(kb == 0 and qt in (0, 3)), stop=(kb == 4 * QB + qt), skip_group_check=True),
                             reads=[R_E[j], R_vA[kb]], writes=[R_O[qt]])
                for qt in range(4):
                    S.op("act", lambda e: e.activation(out=osb[g2][:, c, qt, :], in_=accO(qt), func=AF.Copy), reads=[R_O[qt]], writes=[R_osb[g2][qt]])
            for qt in range(4):
                o1, o2 = osb[g2][:, 0, qt, :], osb[g2][:, 1, qt, :]
                Rq = R_scr[qt]
                S.op("dve", lambda e: e.reciprocal(out=st[qt][:, 2:3], in_=o1[:, 128:129]), reads=[R_osb[g2][qt]], writes=[Rq])
                S.op("dve", lambda e: e.reciprocal(out=st[qt][:, 3:4], in_=o2[:, 128:129]), reads=[R_osb[g2][qt]], writes=[Rq])
                S.op("dve", lambda e: e.tensor_tensor(out=st[qt][:, 3:4], in0=st[qt][:, 3:4], in1=lam[:, 0:1], op=ALU.mult), reads=[Rq, R_par], writes=[Rq])
                S.op("dve", lambda e: e.tensor_scalar(out=o2[:, 0:128], in0=o2[:, 0:128], scalar1=st[qt][:, 3:4], scalar2=None, op0=ALU.mult),
                     reads=[Rq, R_osb[g2][qt]], writes=[R_osb[g2][qt]])
                S.op("dve", lambda e: e.scalar_tensor_tensor(out=of[qt][:, :], in0=o1[:, 0:128], scalar=st[qt][:, 2:3], in1=o2[:, 0:128], op0=ALU.mult, op1=ALU.subtract),
                     reads=[Rq, R_osb[g2][qt]], writes=[Rq])
                c0 = QB * 512 + qt * 128
                def dst(c0=c0, h=h):
                    S.op("act", lambda e: e.activation(out=mixT[:, h, c0:c0 + 128], in_=ptp[:, :128], func=AF.Copy), reads=[R_ptp], writes=[R_mix[h]])
                subln_store(of[qt][:, :], 128, dst, (junk[qt], st[qt], ob[qt], Rq))

        do_samples = L["stage"] >= 8
        sample_gen = None
        state = {"gen": None}

        def sample_step():
            if state["gen"] is not None:
                try:
                    next(state["gen"])
                except StopIteration:
                    state["gen"] = None
        if do_samples:
            ptr_ = sb("s_ptr", [128, NSC, 16], I32, pa)
            pof = sb("s_pof", [128, NSC * 16], F32, pa)
            pff = sb("s_pff", [128, NSC * 16], F32, pa)
            idx = sb("s_idx", [128, NSC, 16], I32, pa)
            R_idx = Res("idx")
            selt = sb("s_sel", [NSC, NSC, 128], F32, pa)
            m0 = sb("s_m0", [128, 1], F32, pa)
            S.dma("sp", lambda e: e.dma_start(out=ptr_[:], in_=L["ptrep"].rearrange("s p j -> p s j")), "s_p", writes=[R_idx])
            S.dma("sp", lambda e: e.dma_start(out=pof[:], in_=L["poff"]), "s_p", writes=[R_idx])
            S.dma("sp", lambda e: e.dma_start(out=selt[:], in_=L["selq"]), "s_p", writes=[R_idx])
            S.dma("sp", lambda e: e.dma_start(out=m0[:], in_=L["mask0"]), "s_p", writes=[R_idx])
            f2 = lambda t: t[:].rearrange("p a b -> p (a b)")
            S.op("dve", lambda e: e.tensor_copy(out=pff[:, :], in_=f2(ptr_)), reads=[R_idx], writes=[R_idx])
            S.op("dve", lambda e: e.scalar_tensor_tensor(out=pff[:, :], in0=pff[:, :], scalar=32.0, in1=pof[:, :], op0=ALU.mult, op1=ALU.add), reads=[R_idx], writes=[R_idx])
            S.op("dve", lambda e: e.tensor_copy(out=f2(idx), in_=pff[:, :]), reads=[R_idx], writes=[R_idx])
            NKB = 3
            kt_ = [sb("s_kt%d" % j, [128, 4, 512], F32, pa) for j in range(NKB)]
            R_kt = [Res("s_kt%d" % j) for j in range(NKB)]
            NVB = 3
            vb_ = [sb("s_vb%d" % j, [128, 4, 512], BF16, pa) for j in range(NVB)]
            R_vb = [Res("s_vb%d" % j) for j in range(NVB)]
            prod = [sb("s_prod%d" % j, [128, 4, 512], F32, pa) for j in range(2)]
            R_prod = [Res("s_prod%d" % j) for j in range(2)]
            qbc = [sb("s_qbc%d" % j, [128, 4, 512], F32, pa) for j in range(2)]
            R_qbc = [Res("qbc%d" % j) for j in range(2)]
            R_qscr = Res("qscr")
            sc = sb("s_sc", [128, 64, 8], F32, pa)
            R_sc = Res("s_sc")
            pdz = sb("s_pdz", [128, 64, 4, NSC], BF16, pa)
            R_pdz = Res("pdz")
            zz = sb("s_zz", [128, 6, 8], F32, pa)
            R_zz = Res("zz")
            sself = sb("s_self", [NSC, 4, 8], F32, pa)
            prs = sb("s_prs", [NSC, 512], F32, pa)
            R_self = Res("sself")
            coef = sb("s_coef", [NSC, 2, 8], F32, pa)
            R_coef = Res("coef")
            osm = sb("s_osm", [NSC, 4, 128], F32, pa)
            sjunk = sb("s_junk", [NSC, 128], BF16, pa)
            sst = sb("s_st", [NSC, 4], F32, pa)
            sob = sb("s_ob", [NSC, 128], BF16, pa)
            R_sscr = Res("s_scr")
            pz = ps("s_pz", [128, 16], F32, pa)
            R_pz = Res("pz")
            pso = ps("s_pso", [NSC, 512], F32, pa)
            R_pso = Res("pso")
            S.op("pool", lambda e: e.memset(pdz[:], 0.0), writes=[R_pdz])
            S.op("dve", lambda e: e.memset(coef[:], 0.0), writes=[R_coef])
            S.dma("sp", lambda e: e.dma_start(out=L["qscr"], in_=L["qsf"][:, :]), "s_q", reads=[L["R_qsf"]], writes=[R_qscr])
            S.op("dve", lambda e: e.tensor_tensor(out=prs[:, :], in0=L["qsf"][:, :], in1=L["ksf"][:, :], op=ALU.mult), reads=[L["R_qsf"], L["R_ksf"]], writes=[R_self])
            S.op("dve", lambda e: e.reduce_sum(out=sself[:, 0, :], in_=prs[:, :].rearrange("p (a d) -> p a d", d=64), axis=AX.X), reads=[R_self], writes=[R_self])
            S.op("act", lambda e: e.activation(out=sself[:, 1, :], in_=sself[:, 0, :], func=AF.Exp, scale=0.125), reads=[R_self], writes=[R_self])

            def sample_k(s_):
                def gather(jj):
                    j = (s_ * 16 + jj) % NKB
                    S.dma("pool", lambda e: e.indirect_dma_start(out=kt_[j][:].rearrange("p a b -> p (a b)"), out_offset=None, in_=L["cache_k"],
                                                                 in_offset=bass.IndirectOffsetOnAxis(ap=idx[:, s_, jj:jj + 1], axis=0)),
                          "s_kt%d" % j, reads=[R_idx], writes=[R_kt[j]])
                qb_ = qbc[s_ % 2]
                S.dma("sp", lambda e: e.dma_start(out=qb_[:], in_=AP(L["qscr"].tensor, s_ * 512, [[0, 128], [0, 4], [1, 512]])),
                      "s_qb%d" % (s_ % 2), reads=[R_qscr], writes=[R_qbc[s_ % 2]])
                gather(0)
                gather(1)
                for jj in range(16):
                    j = (s_ * 16 + jj) % NKB
                    if jj + 2 < 16:
                        gather(jj + 2)
                    S.op("dve", lambda e: e.tensor_tensor(out=prod[jj % 2][:], in0=kt_[j][:], in1=qb_[:], op=ALU.mult),
                         reads=[R_kt[j], R_qbc[s_ % 2]], writes=[R_prod[jj % 2]])
                    S.op("dve", lambda e: e.reduce_sum(out=sc[:, jj * 4:(jj + 1) * 4, :], in_=prod[jj % 2][:].rearrange("p t (a d) -> p t a d", d=64), axis=AX.X),
                         reads=[R_prod[jj % 2]], writes=[R_sc])
                    yield
                S.op("act", lambda e: e.activation(out=sc[:], in_=sc[:], func=AF.Exp, scale=0.125), reads=[R_sc], writes=[R_sc])
                S.op("dve", lambda e: e.reduce_sum(out=zz[:, 0, :], in_=sc[:].rearrange("p t a -> p a t"), axis=AX.X), reads=[R_sc], writes=[R_zz])
                S.op("pe", lambda e: e.matmul(pz[:, 0:8], lhsT=selt[:, s_, :], rhs=sself[:, 1, :], start=True, stop=True), reads=[R_idx, R_self], writes=[R_pz])
                S.op("dve", lambda e: e.scalar_tensor_tensor(out=zz[:, 0, :], in0=pz[:, 0:8], scalar=m0[:, 0:1], in1=zz[:, 0, :], op0=ALU.mult, op1=ALU.add),
                     reads=[R_pz, R_idx, R_zz], writes=[R_zz])
                S.op("pe", lambda e: e.matmul(pz[:, 8:16], lhsT=L["ones_f"][:, :], rhs=zz[:, 0, :], start=True, stop=True), reads=[R_zz, R_id], writes=[R_pz])
                S.op("dve", lambda e: e.reciprocal(out=zz[:, 1, :], in_=pz[:, 8:16]), reads=[R_pz], writes=[R_zz])
                z4 = lambda r: zz[:, r, :].rearrange("p (h c) -> p h c", c=2)
                S.op("dve", lambda e: e.tensor_scalar(out=zz[:, 2, :], in0=zz[:, 1, :], scalar1=lam[:, 0:1], scalar2=None, op0=ALU.mult), reads=[R_zz, R_par], writes=[R_zz])
                for r_ in range(2):
                    S.op("dve", lambda e: e.scalar_tensor_tensor(out=coef[:, r_, :], in0=zz[0:NSC, 1 + r_, :], scalar=idf[0:NSC, s_:s_ + 1], in1=coef[:, r_, :],
                                                                 op0=ALU.mult, op1=ALU.add), reads=[R_zz, R_id, R_coef], writes=[R_coef])
                sc4 = sc[:].rearrange("p t (h c) -> p t h c", c=2)
                S.op("dve", lambda e: e.tensor_tensor(out=sc4[:, :, :, 0], in0=sc4[:, :, :, 0], in1=z4(1)[:, :, 0].unsqueeze(1).to_broadcast([128, 64, 4]), op=ALU.mult), reads=[R_sc, R_zz], writes=[R_sc])
                S.op("dve", lambda e: e.tensor_tensor(out=sc4[:, :, :, 1], in0=sc4[:, :, :, 1], in1=z4(2)[:, :, 1].unsqueeze(1).to_broadcast([128, 64, 4]), op=ALU.mult), reads=[R_sc, R_zz], writes=[R_sc])
                if s_ > 0:
                    S.op("pool", lambda e: e.memset(pdz[:, :, :, s_ - 1], 0.0), writes=[R_pdz])
                S.op("dve", lambda e: e.tensor_tensor(out=pdz[:, :, :, s_], in0=sc4[:, :, :, 0], in1=sc4[:, :, :, 1], op=ALU.subtract), reads=[R_sc], writes=[R_pdz])

            def sample_v(s_):
                def gather(jj):
                    jv = jj % NVB
                    S.dma("pool", lambda e: e.indirect_dma_start(out=vb_[jv][:].rearrange("p a b -> p (a b)"), out_offset=None, in_=L["cache_v"],
                                                                 in_offset=bass.IndirectOffsetOnAxis(ap=idx[:, s_, jj:jj + 1], axis=0)),
                          "s_vb%d" % jv, reads=[R_idx], writes=[R_vb[jv]])
                gather(0)
                gather(1)
                for jj in range(16):
                    jv = jj % NVB
                    if jj + 2 < 16:
                        gather(jj + 2)
                    for t4 in range(4):
                        for h in range(4):
                            first = (s_ == 0 and jj == 0 and t4 == 0 and h == 0)
                            last = (s_ == NSC - 1 and jj == 15 and t4 == 3)
                            S.op("pe", lambda e: e.matmul(pso[:, h * 128:(h + 1) * 128], lhsT=pdz[:, jj * 4 + t4, h, :], rhs=vb_[jv][:, t4, h * 128:(h + 1) * 128],
                                                          start=first, stop=last, skip_group_check=True),
                                 reads=[R_pdz, R_vb[jv]], writes=[R_pso])
                    yield

            def sample_all():
                for s_ in range(NSC):
                    yield from sample_k(s_)
                    yield from sample_v(s_)
            sample_gen = sample_all()


        if do_samples:
            state["gen"] = sample_gen
        for gi in range(16):
            prompt_group(gi)
        while state["gen"] is not None:
            sample_step()

        if do_samples:
            e4 = sself[:, 1, :].rearrange("p (h c) -> p h c", c=2)
            cA = coef[:, 0, :].rearrange("p (h c) -> p h c", c=2)
            cB = coef[:, 1, :].rearrange("p (h c) -> p h c", c=2)
            pd4 = sself[:, 2, :].rearrange("p (h c) -> p h c", c=2)
            S.op("dve", lambda e: e.tensor_tensor(out=pd4[:, :, 0], in0=e4[:, :, 0], in1=cA[:, :, 0], op=ALU.mult), reads=[R_self, R_coef], writes=[R_self])
            S.op("dve", lambda e: e.tensor_tensor(out=pd4[:, :, 1], in0=e4[:, :, 1], in1=cB[:, :, 1], op=ALU.mult), reads=[R_self, R_coef], writes=[R_self])
            S.op("dve", lambda e: e.tensor_tensor(out=pd4[:, :, 0], in0=pd4[:, :, 0], in1=pd4[:, :, 1], op=ALU.subtract), reads=[R_self], writes=[R_self])
            for h in range(4):
                S.op("dve", lambda e: e.scalar_tensor_tensor(out=osm[:, h, :], in0=L["vS"][:, h * 128:(h + 1) * 128], scalar=pd4[:, h, 0:1], in1=pso[:, h * 128:(h + 1) * 128],
                                                             op0=ALU.mult, op1=ALU.add), reads=[L["R_vS"], R_self, R_pso], writes=[R_sscr])
                def dst(h=h):
                    S.op("act", lambda e: e.activation(out=mixT[:, h, SEQ:NCOL], in_=ptp[:, :NSC], func=AF.Copy), reads=[R_ptp], writes=[R_mix[h]])
                subln_store(osm[:, h, :], NSC, dst, (sjunk, sst, sob, R_sscr))
        S.barrier()


def ffn_phase(nc, S, sb, ps, L):
    mixT, R_mix, idb, idf, R_id = L["mixT"], L["R_mix"], L["idb"], L["idf"], L["R_id"]
    xp, xs, xscr = L["xp"], L["xs"], L["xscr"]
    with ExitStack() as pf:
        mT = sb("mT", [128, NFC, NCOL], BF16, pf)
        R_mT = [Res("mT%d" % b) for b in range(5)]
        rs2 = sb("rs2", [128, 17, 4], F32, pf)
        R_rs2 = [Res("rs2_%d" % i) for i in range(17)]
        small = sb("fsmall", [128, 8 + 3 * NFC + NFC], F32, pf)
        R_small = Res("fsmall")
        S.dma("sp", lambda e: e.dma_start(out=small[:, 8:8 + 3 * NFC], in_=L["convw"].rearrange("p a b -> p (a b)")), "f_s", writes=[R_small])
        S.dma("sp", lambda e: e.dma_start(out=small[:, 8 + 3 * NFC:], in_=L["convb"]), "f_s", writes=[R_small])
        cw = lambda j, fc: small[:, 8 + j * NFC + fc:8 + j * NFC + fc + 1]
        cb = lambda fc: small[:, 8 + 3 * NFC + fc:8 + 3 * NFC + fc + 1]
        asp = sb("asp", [8, DFF], F32, pf)
        R_asp = Res("asp")
        with ExitStack() as pu:
            hn2T = sb("hn2T", [128, 8, NCOL], BF16, pu)
            R_hn2 = [Res("hn2_%d" % i) for i in range(17)]
            with ExitStack() as po:
                wob = sb("wob", [128, 8, D], BF16, po)
                R_wob = Res("wob")
                for kc in range(8):
                    S.dma("pool", lambda e: e.dma_start(out=wob[:, kc, :], in_=L["w_o"][kc * 128:(kc + 1) * 128, :]), "wob", writes=[R_wob])
                gfx = sb("gfx", [128, D], F32, po)
                R_gfx = Res("gfx")
                S.dma("sp", lambda e: e.dma_start(out=gfx[:], in_=AP(L["nffn"].tensor, 0, [[0, 128], [1, D]])), "f_g2", writes=[R_gfx])
                xt = [sb("oxt%d" % j, [128, D], F32, po) for j in range(2)]
                R_xt = [Res("oxt%d" % j) for j in range(2)]
                xb = [sb("oxb0", [128, D], BF16, po)] * 2
                R_xb = [Res("oxb0")] * 2
                junk = xb[0]
                R_junk = R_xb[0]
                px = [ps("opx%d" % j, [128, 2, 512], F32, po) for j in range(2)]
                R_px = [Res("opx%d" % j) for j in range(2)]
                ptr = [ps("optr%d" % j, [128, 8, 128], BF16, po) for j in range(2)]
                R_ptr = [Res("optr%d" % j) for j in range(2)]
                for i, (c0, n) in enumerate(TILES):
                    j = i % 2
                    src = xp[c0:c0 + n, :] if i < 16 else xs
                    S.dma("sp", lambda e: e.dma_start(out=xt[j][:n, :], in_=src), "oxt%d" % j, writes=[R_xt[j]])
                    for hf in range(2):
                        for kc in range(8):
                            S.op("pe", lambda e: e.matmul(px[j][:n, hf, :], lhsT=mixT[:, kc, c0:c0 + n], rhs=wob[:, kc, hf * 512:(hf + 1) * 512],
                                                          start=(kc == 0), stop=(kc == 7)), reads=R_mix + [R_wob], writes=[R_px[j]])
                    S.op("dve", lambda e: e.tensor_tensor(out=xt[j][:n, :], in0=xt[j][:n, :], in1=px[j][:n, :, :].rearrange("p a b -> p (a b)"), op=ALU.add),
                         reads=[R_xt[j], R_px[j]], writes=[R_xt[j]])
                    S.dma("sp", lambda e: e.dma_start(out=xscr[c0:c0 + n, :], in_=xt[j][:n, :]), "oxt%d" % j, reads=[R_xt[j]])
                    S.op("act", lambda e: e.activation(out=junk[:n, :], in_=xt[j][:n, :], func=AF.Square, accum_out=rs2[:n, i, 0:1]),
                         reads=[R_xt[j]], writes=[R_junk, R_rs2[i]])
                    S.op("dve", lambda e: e.tensor_scalar(out=rs2[:n, i, 1:2], in0=rs2[:n, i, 0:1], scalar1=1.0 / D, scalar2=EPS, op0=ALU.mult, op1=ALU.add),
                         reads=[R_rs2[i]], writes=[R_rs2[i]])
                    S.op("act", lambda e: e.sqrt(out=rs2[:n, i, 2:3], in_=rs2[:n, i, 1:2]), reads=[R_rs2[i]], writes=[R_rs2[i]])
                    S.op("dve", lambda e: e.reciprocal(out=rs2[:n, i, 3:4], in_=rs2[:n, i, 2:3]), reads=[R_rs2[i]], writes=[R_rs2[i]])
                    S.op("dve", lambda e: e.scalar_tensor_tensor(out=xb[j][:n, :], in0=xt[j][:n, :], scalar=rs2[:n, i, 3:4], in1=gfx[:n, :],
                                                                 op0=ALU.mult, op1=ALU.mult),
                         reads=[R_xt[j], R_rs2[i], R_gfx], writes=[R_xb[j]])
                    for kc in range(8):
                        S.op("pe", lambda e: e.transpose(out=ptr[j][:, kc, :n], in_=xb[j][:n, kc * 128:(kc + 1) * 128], identity=idb[:n, :n]),
                             reads=[R_xb[j], R_id], writes=[R_ptr[j]])
                    S.op("dve", lambda e: e.tensor_copy(out=hn2T[:, :, c0:c0 + n], in_=ptr[j][:, :, :n]), reads=[R_ptr[j]], writes=[R_hn2[i]])
                S.barrier()
            with ExitStack() as pv:
                wab = [sb("wab%d" % j, [128, 8, 128], BF16, pv) for j in range(2)]
                wgb_ = [sb("wgb_%d" % j, [128, 8, 128], BF16, pv) for j in range(2)]
                R_wab = [Res("wab%d" % j) for j in range(2)]
                R_wgb = [Res("wgb_%d" % j) for j in range(2)]
                pa_ = [ps("fpa%d" % j, [128, 512], F32, pv) for j in range(2)]
                pg_ = [ps("fpg%d" % j, [128, 512], F32, pv) for j in range(2)]
                R_pa = [Res("fpa%d" % j) for j in range(2)]
                R_pg = [Res("fpg%d" % j) for j in range(2)]
                psp = ps("fpsp", [8, 128], F32, pv)
                R_psp = Res("fpsp")
                pbt = ps("fpbt", [128, NFC, 8], F32, pv)
                R_pbt = Res("fpbt")
                abuf = sb("abuf", [128, 2 + 512], F32, pv)
                cbuf = sb("cbuf", [128, 512], F32, pv)
                R_ab, R_cbuf = Res("abuf"), Res("cbuf")
                scv = sb("scv", [8, DFF], F32, pv)
                bufT = sb("bufT", [128, NFC, 8], F32, pv)
                R_bufT = Res("bufT")
                S.dma("sp", lambda e: e.dma_start(out=scv[:, :], in_=L["sconv"]), "f_c", writes=[R_bufT])
                for fc in range(NFC):
                    S.op("pe", lambda e: e.transpose(out=pbt[:, fc, :], in_=scv[:, fc * 128:(fc + 1) * 128], identity=idf[:8, :8]),
                         reads=[R_bufT, R_id], writes=[R_pbt])
                S.op("dve", lambda e: e.tensor_copy(out=bufT[:], in_=pbt[:]), reads=[R_pbt], writes=[R_bufT])
                for fc in range(NFC):
                    jw = fc % 2
                    for half, (wb, R_wb) in enumerate(((wab, R_wab), (wgb_, R_wgb))):
                        col0 = half * DFF + fc * 128
                        S.dma("pool", lambda e: e.dma_start(out=wb[jw][:], in_=L["w_up"][:, col0:col0 + 128].rearrange("(k p) c -> p k c", p=128)),
                              "wup%d%d" % (half, jw), writes=[R_wb[jw]])
                    for kc in range(8):
                        S.op("pe", lambda e: e.matmul(psp[0:6, :], lhsT=hn2T[:, kc, SEQ - 2:NCOL], rhs=wab[jw][:, kc, :], start=(kc == 0), stop=(kc == 7)),
                             reads=R_hn2[15:17] + [R_wab[jw]], writes=[R_psp])
                    S.op("act", lambda e: e.activation(out=asp[0:6, fc * 128:(fc + 1) * 128], in_=psp[0:6, :], func=AF.Copy), reads=[R_psp], writes=[R_asp])
                    for b, (c0, nb) in enumerate(BLOCKS):
                        j = (fc * 5 + b) % 2
                        for kc in range(8):
                            S.op("pe", lambda e: e.matmul(pa_[j][:, :nb], lhsT=wab[jw][:, kc, :], rhs=hn2T[:, kc, c0:c0 + nb], start=(kc == 0), stop=(kc == 7)),
                                 reads=R_hn2 + [R_wab[jw]], writes=[R_pa[j]])
                        for kc in range(8):
                            S.op("pe", lambda e: e.matmul(pg_[j][:, :nb], lhsT=wgb_[jw][:, kc, :], rhs=hn2T[:, kc, c0:c0 + nb], start=(kc == 0), stop=(kc == 7)),
                                 reads=R_hn2 + [R_wgb[jw]], writes=[R_pg[j]])
                        if b < 4:
                            if b == 0:
                                S.op("dve", lambda e: e.memset(abuf[:, 0:2], 0.0), reads=[R_ab], writes=[R_ab])
                            else:
                                S.op("dve", lambda e: e.tensor_copy(out=abuf[:, 0:2], in_=abuf[:, 512:514]), reads=[R_ab], writes=[R_ab])
                            S.op("act", lambda e: e.activation(out=abuf[:, 2:514], in_=pa_[j][:, :], func=AF.Copy), reads=[R_pa[j], R_ab], writes=[R_ab])
                            S.op("act", lambda e: e.activation(out=cbuf[:, :], in_=pa_[j][:, :], func=AF.Identity, scale=cw(2, fc), bias=cb(fc)),
                                 reads=[R_pa[j], R_small], writes=[R_cbuf])
                            S.op("dve", lambda e: e.scalar_tensor_tensor(out=cbuf[:, :], in0=abuf[:, 1:513], scalar=cw(1, fc), in1=cbuf[:, :], op0=ALU.mult, op1=ALU.add),
                                 reads=[R_ab, R_small, R_cbuf], writes=[R_cbuf])
                            S.op("dve", lambda e: e.scalar_tensor_tensor(out=cbuf[:, :], in0=abuf[:, 0:512], scalar=cw(0, fc), in1=cbuf[:, :], op0=ALU.mult, op1=ALU.add),
                                 reads=[R_ab, R_small, R_cbuf], writes=[R_cbuf])
                        else:
                            b3 = bufT[:, fc, :].rearrange("p (s j) -> p s j", j=2)
                            S.op("act", lambda e: e.activation(out=cbuf[:, :nb], in_=pa_[j][:, :nb], func=AF.Identity, scale=cw(2, fc), bias=cb(fc)),
                                 reads=[R_pa[j], R_small], writes=[R_cbuf])
                            S.op("dve", lambda e: e.scalar_tensor_tensor(out=cbuf[:, :nb], in0=b3[:, :, 1], scalar=cw(1, fc), in1=cbuf[:, :nb], op0=ALU.mult, op1=ALU.add),
                                 reads=[R_bufT, R_small, R_cbuf], writes=[R_cbuf])
                            S.op("dve", lambda e: e.scalar_tensor_tensor(out=cbuf[:, :nb], in0=b3[:, :, 0], scalar=cw(0, fc), in1=cbuf[:, :nb], op0=ALU.mult, op1=ALU.add),
                                 reads=[R_bufT, R_small, R_cbuf], writes=[R_cbuf])
                        S.op("act", lambda e: e.activation(out=cbuf[:, :nb], in_=cbuf[:, :nb], func=AF.Silu), reads=[R_cbuf], writes=[R_cbuf])
                        S.op("dve", lambda e: e.tensor_tensor(out=mT[:, fc, c0:c0 + nb], in0=cbuf[:, :nb], in1=pg_[j][:, :nb], op=ALU.mult),
                             reads=[R_cbuf, R_pg[j]], writes=[R_mT[b]])
                S.dma("sp", lambda e: e.dma_start(out=L["o_cp"], in_=asp[0:2, :]), "f_o", reads=[R_asp])
                S.dma("sp", lambda e: e.dma_start(out=L["o_cs"][:, 1, :], in_=asp[2:6, :]), "f_o", reads=[R_asp])
                S.dma("sp", lambda e: e.dma_start(out=L["o_cs"][:, 0, :], in_=L["sconv1"]), "f_o2")
                S.barrier()
        with ExitStack() as pd_:
            wdb = sb("wdb", [128, NFC, D], BF16, pd_)
            R_wdb = Res("wdb")
            for fc in range(NFC):
                S.dma("pool", lambda e: e.dma_start(out=wdb[:, fc, :], in_=L["w_down"][fc * 128:(fc + 1) * 128, :]), "wdb", writes=[R_wdb])
            gfn = sb("gfn", [128, D], F32, pd_)
            R_gfn = Res("gfn")
            S.dma("sp", lambda e: e.dma_start(out=gfn[:], in_=AP(L["fnorm"].tensor, 0, [[0, 128], [1, D]])), "f_g", writes=[R_gfn])
            xt = [sb("dxt%d" % j, [128, D], F32, pd_) for j in range(2)]
            R_xt = [Res("dxt%d" % j) for j in range(2)]
            junk = sb("djunk", [128, D], BF16, pd_)
            R_junk = Res("djunk")
            px = [ps("dpx%d" % j, [128, 2, 512], F32, pd_) for j in range(2)]
            R_px = [Res("dpx%d" % j) for j in range(2)]
            for i, (c0, n) in enumerate(TILES):
                j = i % 2
                S.dma("sp", lambda e: e.dma_start(out=xt[j][:n, :], in_=xscr[c0:c0 + n, :]), "dxt%d" % j, writes=[R_xt[j]])
                for hf in range(2):
                    for fc in range(NFC):
                        S.op("pe", lambda e: e.matmul(px[j][:n, hf, :], lhsT=mT[:, fc, c0:c0 + n], rhs=wdb[:, fc, hf * 512:(hf + 1) * 512],
                                                      start=(fc == 0), stop=(fc == NFC - 1)), reads=R_mT + [R_wdb], writes=[R_px[j]])
                S.op("dve", lambda e: e.tensor_tensor(out=xt[j][:n, :], in0=xt[j][:n, :], in1=px[j][:n, :, :].rearrange("p a b -> p (a b)"), op=ALU.add),
                     reads=[R_xt[j], R_px[j]], writes=[R_xt[j]])
                S.op("act", lambda e: e.activation(out=junk[:n, :], in_=xt[j][:n, :], func=AF.Square, accum_out=rs2[:n, i, 0:1]),
                     reads=[R_xt[j]], writes=[R_junk, R_rs2[i]])
                S.op("dve", lambda e: e.tensor_scalar(out=rs2[:n, i, 1:2], in0=rs2[:n, i, 0:1], scalar1=1.0 / D, scalar2=EPS, op0=ALU.mult, op1=ALU.add),
                     reads=[R_rs2[i]], writes=[R_rs2[i]])
                S.op("act", lambda e: e.sqrt(out=rs2[:n, i, 2:3], in_=rs2[:n, i, 1:2]), reads=[R_rs2[i]], writes=[R_rs2[i]])
                S.op("dve", lambda e: e.reciprocal(out=rs2[:n, i, 3:4], in_=rs2[:n, i, 2:3]), reads=[R_rs2[i]], writes=[R_rs2[i]])
                S.op("dve", lambda e: e.scalar_tensor_tensor(out=xt[j][:n, :], in0=xt[j][:n, :], scalar=rs2[:n, i, 3:4], in1=gfn[:n, :], op0=ALU.mult, op1=ALU.mult),
                     reads=[R_xt[j], R_rs2[i], R_gfn], writes=[R_xt[j]])
                dstd = L["o_yp"][c0:c0 + n, :] if i < 16 else L["o_ys"]
                S.dma("sp", lambda e: e.dma_start(out=dstd, in_=xt[j][:n, :]), "dxt%d" % j, reads=[R_xt[j]])
            S.barrier()


def build_program(n_pool, stage=99):
    nc = bass.Bass("TRN2", target_bir_lowering=False)
    din = lambda name, shape, dt=F32: nc.dram_tensor(name, shape, dt, kind="ExternalInput").ap()
    dout = lambda name, shape, dt=F32: nc.dram_tensor(name, shape, dt, kind="ExternalOutput").ap()

    xp = din("xp", [SEQ, D])
    xs = din("xs", [NSC, D])
    w_in = din("w_in", [D, DPROJ])
    nmix = din("nmix", [1, D])
    ropec = din("ropec", [NCOL, 8])
    ropes = din("ropes", [NCOL, 8])
    ident = din("ident", [128, 128])

    bcol_re = din("bcol_re", [128, 16, 128])
    bcol_im = din("bcol_im", [128, 16, 128])
    cpad_re = din("cpad_re", [128, 16, 128])
    cpad_im = din("cpad_im", [128, 16, 128])
    acol_re = din("acol_re", [128, 16])
    acol_im = din("acol_im", [128, 16])
    ldt_col = din("ldt_col", [128, 16])
    dcol = din("dcol", [128, 4])
    h0c_re = din("h0c_re", [128, 16, NSC])
    h0c_im = din("h0c_im", [128, 16, NSC])
    w_glu = din("w_glu", [512, 512])
    trow1 = din("trow1", [128, 512])
    o_hre = dout("o_hre", [16, 128])
    o_him = dout("o_him", [16, 128])
    o_hres = dout("o_hres", [NSC, 16, 128])
    o_hims = dout("o_hims", [NSC, 16, 128])

    lamv = din("lamv", [1, 4 * 64])
    gsub = din("gsub", [1, 128])
    trimask = din("trimask", [128, 128])
    cache_k = din("cache_k", [n_pool * 32, 2048])
    cache_v = din("cache_v", [n_pool * 32, 2048])
    ptrep = din("ptrep", [NSC, 128, 16], I32)
    poff = din("poff", [128, NSC * 16])
    selq = din("selq", [NSC, NSC, 128])
    mask0 = din("mask0", [128, 1])
    w_o = din("w_o", [D, D])
    nffn = din("nffn", [1, D])
    w_up = din("w_up", [D, 2 * DFF])
    convw = din("convw", [128, 3, NFC])
    convb = din("convb", [128, NFC])
    w_down = din("w_down", [DFF, D])
    fnorm = din("fnorm", [1, D])
    sconv = din("sconv", [NSC * 2, DFF])
    sconv1 = din("sconv1", [NSC, DFF])
    xscr = nc.dram_tensor("xscr", [NCOL, D], F32, kind="Internal").ap()
    qscr = nc.dram_tensor("qscr", [1, NSC * 512], F32, kind="Internal").ap()
    o_yp = dout("o_yp", [SEQ, D])
    o_ys = dout("o_ys", [NSC, D])
    o_cp = dout("o_cp", [2, DFF])
    o_cs = dout("o_cs", [NSC, 2, DFF])
    o_kp = dout("o_kp", [SEQ, 512])
    o_vp = dout("o_vp", [SEQ, 512])
    o_ks = dout("o_ks", [NSC, 512])
    o_vs = dout("o_vs", [NSC, 512])

    with ExitStack() as es:
        S = Sched(nc, es)

        def sb(name, shape, dt, stack=es):
            return stack.enter_context(nc.sbuf_tensor(name, shape, dt))

        def ps(name, shape, dt, stack=es):
            return stack.enter_context(nc.psum_tensor(name, shape, dt))

        idf = sb("idf", [128, 128], F32)
        idb = sb("idb", [128, 128], BF16)
        R_id = Res("id")
        S.dma("sp", lambda e: e.dma_start(out=idf[:], in_=ident), "c_id", writes=[R_id])
        S.op("dve", lambda e: e.tensor_copy(out=idb[:], in_=idf[:]), reads=[R_id], writes=[R_id])

        ones_f = sb("ones_f", [128, 128], F32)
        S.op("pool", lambda e: e.memset(ones_f[:], 1.0), writes=[R_id])
        ropc = sb("ropc", [128, 17, 8], F32)
        rops = sb("rops", [128, 17, 8], F32)
        R_rope = Res("rope")
        for tbl, src, key in ((ropc, ropec, "c_rc"), (rops, ropes, "c_rs")):
            S.dma("sp", lambda e: e.dma_start(out=tbl[:, 0:16, :],
                                              in_=src[0:SEQ, :].rearrange("(i p) e -> p i e", p=128)),
                  key, writes=[R_rope])
            S.dma("sp", lambda e: e.dma_start(out=tbl[0:NSC, 16, :], in_=src[SEQ:NCOL, :]), key, writes=[R_rope])


        mixT = sb("mixT", [128, 8, NCOL], BF16)
        R_mix = [Res("mix%d" % c) for c in range(8)]
        vS = sb("vS", [NSC, 512], BF16)
        R_vS = Res("vS")
        ksf = sb("ksf", [NSC, 512], F32)
        qsf = sb("qsf", [NSC, 512], F32)
        R_ksf, R_qsf = Res("ksf"), Res("qsf")
        pA = es.enter_context(ExitStack())
        qT = sb("qT", [128, 4, NCOL], BF16, pA)
        kT = sb("kT", [128, 4, NCOL], BF16, pA)
        R_qT = [Res("qT%d" % i) for i in range(17)]
        R_kT = [Res("kT%d" % i) for i in range(17)]
        vA = sb("vA", [128, 16, 4, 129], BF16, pA)
        R_vA = [Res("vA%d" % i) for i in range(16)]
        uT = sb("uT", [128, 4, NCOL], BF16, pA)
        R_uT = [Res("uT%d" % i) for i in range(5)]
        rr = sb("rr", [128, 17, 4], F32, pA)
        R_rr = [Res("rr%d" % i) for i in range(17)]

        pw = pA.enter_context(ExitStack())
        hnT = sb("hnT", [128, 8, NCOL], BF16, pw)
        R_hnT = [Res("hnT%d" % i) for i in range(17)]
        winb = sb("winb", [128, 8, DPROJ], BF16, pw)
        R_winb = [Res("winb")] * 8
        with ExitStack() as p12:
            xt = [sb("xt%d" % j, [128, D], F32, p12) for j in range(2)]
            R_xt = [Res("xt%d" % j) for j in range(2)]
            xb = [sb("xb%d" % j, [128, D], BF16, p12) for j in range(2)]
            R_xb = [Res("xb%d" % j) for j in range(2)]
            junk = sb("junk", [128, D], BF16, p12)
            R_junk = Res("junk")
            gmx = sb("gmx", [128, D], F32, p12)
            R_gmx = Res("gmx")
            S.dma("sp", lambda e: e.dma_start(out=gmx[:], in_=AP(nmix.tensor, 0, [[0, 128], [1, D]])), "c_nm", writes=[R_gmx])
            ptr = [ps("ptr%d" % j, [128, 8, 128], BF16, p12) for j in range(2)]
            R_ptr = [Res("ptr%d" % j) for j in range(2)]

            for kc in range(8):
                S.dma("pool", lambda e: e.dma_start(out=winb[:, kc, :], in_=w_in[kc * 128:(kc + 1) * 128, :]),
                      "winb", writes=[R_winb[kc]])

            for i, (c0, n) in enumerate(TILES):
                j = i % 2
                src = xp[c0:c0 + n, :] if i < 16 else xs
                S.dma("sp", lambda e: e.dma_start(out=xt[j][:n, :], in_=src), "xt%d" % j, writes=[R_xt[j]])
                S.op("act", lambda e: e.activation(out=junk[:n, :], in_=xt[j][:n, :], func=AF.Square,
                                                   accum_out=rr[:n, i, 0:1]),
                     reads=[R_xt[j]], writes=[R_junk, R_rr[i]])
                S.op("dve", lambda e: e.tensor_scalar(out=rr[:n, i, 1:2], in0=rr[:n, i, 0:1], scalar1=1.0 / D,
                                                      scalar2=EPS, op0=ALU.mult, op1=ALU.add),
                     reads=[R_rr[i]], writes=[R_rr[i]])
                S.op("act", lambda e: e.sqrt(out=rr[:n, i, 2:3], in_=rr[:n, i, 1:2]), reads=[R_rr[i]], writes=[R_rr[i]])
                S.op("dve", lambda e: e.reciprocal(out=rr[:n, i, 3:4], in_=rr[:n, i, 2:3]),
                     reads=[R_rr[i]], writes=[R_rr[i]])
                S.op("dve", lambda e: e.scalar_tensor_tensor(out=xb[j][:n, :], in0=xt[j][:n, :], scalar=rr[:n, i, 3:4], in1=gmx[:n, :],
                                                             op0=ALU.mult, op1=ALU.mult),
                     reads=[R_xt[j], R_rr[i], R_gmx], writes=[R_xb[j]])
                for kc in range(8):
                    S.op("pe", lambda e: e.transpose(out=ptr[j][:, kc, :n], in_=xb[j][:n, kc * 128:(kc + 1) * 128],
                                                     identity=idb[:n, :n]),
                         reads=[R_xb[j], R_id], writes=[R_ptr[j]])
                S.op("dve", lambda e: e.tensor_copy(out=hnT[:, :, c0:c0 + n], in_=ptr[j][:, :, :n]),
                     reads=[R_ptr[j]], writes=[R_hnT[i]])

        S.barrier()
        if stage <= 0:
            S.finish("sp")
            return nc
        with ExitStack() as p2:
            pq = [ps("pq%d" % j, [128, 512], F32, p2) for j in range(2)]
            pk = [ps("pk%d" % j, [128, 512], F32, p2) for j in range(2)]
            pv = [ps("pv%d" % j, [128, 512], F32, p2) for j in range(2)]
            R_pq = [Res("pq%d" % j) for j in range(2)]
            R_pk = [Res("pk%d" % j) for j in range(2)]
            R_pv = [Res("pv%d" % j) for j in range(2)]
            ptq = ps("ptq", [128, 2, 4, 128], BF16, p2)
            R_ptq = [Res("ptq"), Res("ptk")]
            qf = [sb("qf%d" % j, [128, 512], F32, p2) for j in range(2)]
            kf = [sb("kf%d" % j, [128, 512], F32, p2) for j in range(2)]
            vf = [sb("vf%d" % j, [128, 512], F32, p2) for j in range(2)]
            R_qf = [Res("qf%d" % j) for j in range(2)]
            R_kf = [Res("kf%d" % j) for j in range(2)]
            R_vf = [Res("vf%d" % j) for j in range(2)]
            qb = sb("qb", [128, 512], BF16, p2)
            kb = sb("kb", [128, 512], BF16, p2)
            R_qb, R_kb = Res("qb"), Res("kb")
            rtmp = sb("rtmp", [128, 4, 8, 8], F32, p2)
            R_rtmp = Res("rtmp")

            S.op("pool", lambda e: e.memset(vA[:, :, :, 128:129], 1.0), writes=R_vA)

            def rope(buf, n, i, R_buf):
                x1 = AP(buf, 0, [[512, n], [64, 8], [1, 8]])
                x2 = AP(buf, 8, [[512, n], [64, 8], [1, 8]])
                cs = AP(ropc, i * 8, [[17 * 8, n], [0, 8], [1, 8]])
                sn = AP(rops, i * 8, [[17 * 8, n], [0, 8], [1, 8]])
                for t, (a, b) in enumerate(((x1, cs), (x2, sn), (x2, cs), (x1, sn))):
                    S.op("dve", lambda e: e.tensor_tensor(out=rtmp[:n, t], in0=a, in1=b, op=ALU.mult),
                         reads=[R_buf, R_rope], writes=[R_rtmp])
                S.op("dve", lambda e: e.tensor_tensor(out=x1, in0=rtmp[:n, 0], in1=rtmp[:n, 1], op=ALU.subtract),
                     reads=[R_rtmp], writes=[R_buf])
                S.op("dve", lambda e: e.tensor_tensor(out=x2, in0=rtmp[:n, 2], in1=rtmp[:n, 3], op=ALU.add),
                     reads=[R_rtmp], writes=[R_buf])

            for i, (c0, n) in enumerate(TILES):
                j = i % 2
                for cb, (pX, R_pX) in enumerate(((pq, R_pq), (pk, R_pk), (pv, R_pv))):
                    for kc in range(8):
                        S.op("pe", lambda e: e.matmul(pX[j][:n, :], lhsT=hnT[:, kc, c0:c0 + n],
                                                      rhs=winb[:, kc, cb * 512:(cb + 1) * 512],
                                                      start=(kc == 0), stop=(kc == 7)),
                             reads=[R_hnT[i], R_winb[kc]], writes=[R_pX[j]])
                S.op("act", lambda e: e.activation(out=vf[j][:n, :], in_=pv[j][:n, :], func=AF.Copy),
                     reads=[R_pv[j]], writes=[R_vf[j]])
                if i < 16:
                    S.dma("sp", lambda e: e.dma_start(out=o_vp[c0:c0 + n, :], in_=vf[j][:n, :]), "vf%d" % j,
                          reads=[R_vf[j]])
                    S.op("pool", lambda e: e.tensor_copy(out=vA[:, i, :, 0:128],
                                                         in_=vf[j][:, :].rearrange("p (h e) -> p h e", h=4)),
                         reads=[R_vf[j]], writes=[R_vA[i]])
                else:
                    S.dma("sp", lambda e: e.dma_start(out=o_vs, in_=vf[j][:n, :]), "vf%d" % j, reads=[R_vf[j]])
                    S.op("pool", lambda e: e.tensor_copy(out=vS[:, :], in_=vf[j][:n, :]),
                         reads=[R_vf[j]], writes=[R_vS])
                if stage < 2:
                    continue
                S.op("dve", lambda e: e.tensor_copy(out=qf[j][:n, :], in_=pq[j][:n, :]),
                     reads=[R_pq[j]], writes=[R_qf[j]])
                rope(qf[j], n, i, R_qf[j])
                S.op("act", lambda e: e.activation(out=qb[:n, :], in_=qf[j][:n, :], func=AF.Copy, scale=0.125),
                     reads=[R_qf[j]], writes=[R_qb])
                if i == 16:
                    S.op("pool", lambda e: e.tensor_copy(out=qsf[:, :], in_=qf[j][:n, :]),
                         reads=[R_qf[j]], writes=[R_qsf])
                for h in range(4):
                    S.op("pe", lambda e: e.transpose(out=ptq[:, 0, h, :n], in_=qb[:n, h * 128:(h + 1) * 128],
                                                     identity=idb[:n, :n]),
                         reads=[R_qb, R_id], writes=[R_ptq[0]])
                S.op("act", lambda e: e.activation(out=qT[:, :, c0:c0 + n], in_=ptq[:, 0, :, :n], func=AF.Copy),
                     reads=[R_ptq[0]], writes=[R_qT[i]])
                if stage < 3:
                    continue
                S.op("dve", lambda e: e.tensor_copy(out=kf[j][:n, :], in_=pk[j][:n, :]),
                     reads=[R_pk[j]], writes=[R_kf[j]])
                rope(kf[j], n, i, R_kf[j])
                if i < 16:
                    S.dma("sp", lambda e: e.dma_start(out=o_kp[c0:c0 + n, :], in_=kf[j][:n, :]), "kf%d" % j,
                          reads=[R_kf[j]])
                else:
                    S.dma("sp", lambda e: e.dma_start(out=o_ks, in_=kf[j][:n, :]), "kf%d" % j, reads=[R_kf[j]])
                    S.op("pool", lambda e: e.tensor_copy(out=ksf[:, :], in_=kf[j][:n, :]),
                         reads=[R_kf[j]], writes=[R_ksf])
                S.op("act", lambda e: e.activation(out=kb[:n, :], in_=kf[j][:n, :], func=AF.Copy),
                     reads=[R_kf[j]], writes=[R_kb])
                for h in range(4):
                    S.op("pe", lambda e: e.transpose(out=ptq[:, 1, h, :n], in_=kb[:n, h * 128:(h + 1) * 128],
                                                     identity=idb[:n, :n]),
                         reads=[R_kb, R_id], writes=[R_ptq[1]])
                S.op("act", lambda e: e.activation(out=kT[:, :, c0:c0 + n], in_=ptq[:, 1, :, :n], func=AF.Copy),
                     reads=[R_ptq[1]], writes=[R_kT[i]])

            for cu in range(4 if stage >= 4 else 0):
                for bi, (c0, nb) in enumerate(BLOCKS):
                    j = (cu * 5 + bi) % 2
                    for kc in range(8):
                        S.op("pe", lambda e: e.matmul(pq[j][:, :nb], lhsT=winb[:, kc, 1536 + cu * 128:1536 + (cu + 1) * 128],
                                                      rhs=hnT[:, kc, c0:c0 + nb], start=(kc == 0), stop=(kc == 7)),
                             reads=R_hnT + [R_winb[kc]], writes=[R_pq[j]])
                    S.op("act", lambda e: e.activation(out=uT[:, cu, c0:c0 + nb], in_=pq[j][:, :nb], func=AF.Copy),
                         reads=[R_pq[j]], writes=[R_uT[bi]])

        pw.close()
        S.barrier()
        if stage >= 5:
            ssm_phase(nc, S, sb, ps, locals())
        if stage >= 6:
            attn_phase(nc, S, sb, ps, locals())
        pA.close()
        S.barrier()
        if stage >= 7:
            ffn_phase(nc, S, sb, ps, locals())

        S.finish("sp")
        print("insts", S.n_inst, "waits", S.n_wait)
    return nc


def _rope_tables():
    half = 8
    inv = (500000.0 ** (-np.arange(0, 16, 2, dtype=np.float32) / np.float32(16))).astype(np.float32)
    pos = np.concatenate([np.arange(SEQ), np.full(NSC, PAST)]).astype(np.float32)
    ang = (pos[:, None] * inv[None, :]).astype(np.float32)
    return np.cos(ang).astype(np.float32), np.sin(ang).astype(np.float32)


def _ssm_maps(inp, c):
    f = np.float32
    a_re, a_im = inp["ssm_a_re"][0], inp["ssm_a_im"][0]
    ldt = inp["ssm_log_dt"][0]
    b_re, b_im = inp["ssm_b_re"][0], inp["ssm_b_im"][0]
    c_re, c_im = inp["ssm_c_re"][0], inp["ssm_c_im"][0]
    bcol = [np.zeros((128, 16, 128), f) for _ in range(2)]
    cpad = [np.zeros((128, 16, 128), f) for _ in range(2)]
    for g in range(32):
        pr, g2 = g // 2, g % 2
        pl = pr % 4
        r0 = pl * 32 + g2 * 16
        for k, (bb, cc) in enumerate(((b_re, c_re), (b_im, c_im))):
            bcol[k][g2 * 64:(g2 + 1) * 64, pr, r0:r0 + 16] = bb[g]
            cpad[k][g2 * 64:(g2 + 1) * 64, pr, r0:r0 + 16] = cc[g].T
    col = lambda a: np.ascontiguousarray(a.reshape(16, 128).T)
    h0 = lambda a: np.ascontiguousarray(a[0, NSC * c:NSC * (c + 1)].reshape(NSC, 16, 128).transpose(2, 1, 0))
    return dict(
        bcol_re=bcol[0], bcol_im=bcol[1], cpad_re=cpad[0], cpad_im=cpad[1],
        acol_re=col(a_re), acol_im=col(a_im), ldt_col=col(np.repeat(ldt[:, None], 64, axis=1)),
        dcol=np.ascontiguousarray(inp["ssm_d"][0].reshape(4, 128).T),
        h0c_re=h0(inp["state_ssm_re"]), h0c_im=h0(inp["state_ssm_im"]),
        w_glu=np.ascontiguousarray(inp["w_glu"][0]),
        trow1=np.ascontiguousarray(np.broadcast_to(np.arange(1, 513, dtype=f)[None, :], (128, 512))),
    )


def _attn_maps(inp, c, dev_pool=None):
    f = np.float32
    lamv = np.concatenate([inp["lambda_q1"][0], inp["lambda_k1"][0], inp["lambda_q2"][0], inp["lambda_k2"][0]]).reshape(1, 256)
    kk, qq = np.meshgrid(np.arange(128), np.arange(128), indexing="ij")
    tri = (qq >= kk).astype(f)
    pt = inp["page_table"][NSC * c:NSC * (c + 1)]
    p = np.arange(128)
    ptrep = np.stack([pt[:, 4 * jj + p // 32] for jj in range(16)], axis=2).astype(np.int32)
    if dev_pool == "compact":
        used = pt.reshape(-1)
        ck = np.ascontiguousarray(inp["cache_k"][0][used]).reshape(-1, 2048)
        cv = np.ascontiguousarray(inp["cache_v"][0][used]).reshape(-1, 2048)
        pt = np.arange(used.size).reshape(pt.shape)
        ptrep = np.stack([pt[:, 4 * jj + p // 32] for jj in range(16)], axis=2).astype(np.int32)
    elif dev_pool is not None:
        ck = np.zeros((dev_pool * 32, 2048), f)
        cv = np.zeros((dev_pool * 32, 2048), f)
        ptrep = ptrep % dev_pool
    else:
        ck = inp["cache_k"][0].reshape(-1, 2048)
        cv = inp["cache_v"][0].reshape(-1, 2048)
    selq = np.zeros((NSC, NSC, 128), f)
    for s_ in range(NSC):
        selq[s_, s_, :] = 1.0
    m0 = np.zeros((128, 1), f)
    m0[0, 0] = 1.0
    return dict(lamv=lamv.astype(f), gsub=inp["subln_g"][0].reshape(1, 128), trimask=tri, cache_k=ck, cache_v=cv,
                ptrep=np.ascontiguousarray(ptrep), poff=np.ascontiguousarray(np.broadcast_to((p % 32).astype(np.float32)[:, None], (128, NSC * 16))), selq=selq, mask0=m0)


def _ffn_maps(inp, c):
    sc = inp["state_conv"][0, NSC * c:NSC * (c + 1)]
    return dict(
        w_o=np.ascontiguousarray(inp["w_o"][0]),
        nffn=np.ascontiguousarray(inp["norm_ffn"][0].reshape(1, D)),
        w_up=np.ascontiguousarray(inp["w_up"][0]),
        convw=np.ascontiguousarray(inp["conv_w"][0].reshape(3, NFC, 128).transpose(2, 0, 1)),
        convb=np.ascontiguousarray(inp["conv_b"][0].reshape(NFC, 128).T),
        w_down=np.ascontiguousarray(inp["w_down"][0]),
        fnorm=inp["final_norm"].reshape(1, D),
        sconv=np.ascontiguousarray(sc.reshape(NSC * 2, DFF)),
        sconv1=np.ascontiguousarray(sc[:, 1, :]),
    )


def make_in_maps(inp, cores, dev_pool=None):
    cosT, sinT = _rope_tables()
    ident = np.eye(128, dtype=np.float32)
    maps = []
    for c in cores:
        m = dict(
            xp=np.ascontiguousarray(inp["x_prompt"][c]),
            xs=np.ascontiguousarray(inp["x_sample"][NSC * c:NSC * (c + 1), 0, :]),
            w_in=np.ascontiguousarray(inp["w_in"][0]),
            nmix=np.ascontiguousarray(inp["norm_mix"][0].reshape(1, D)),
            ropec=cosT, ropes=sinT, ident=ident,
        )
        m.update(_ssm_maps(inp, c))
        m.update(_attn_maps(inp, c, dev_pool))
        m.update(_ffn_maps(inp, c))
        maps.append(m)
    return maps


def kernel(**inp):
    inp = {k: np.asarray(v) for k, v in inp.items()}
    n_pool = inp["cache_k"].shape[1]
    nc = build_program(n_pool)
    cores = list(range(8))
    res = run_bass_kernel_spmd(nc, make_in_maps(inp, cores), core_ids=cores)
    r = res.results
    f = np.float32
    cat = lambda key: np.stack([np.asarray(r[c][key], dtype=f) for c in cores])
    y_prompt = cat("o_yp").reshape(8, SEQ, D)
    y_sample = cat("o_ys").reshape(32, 1, D)
    k_prompt = cat("o_kp").reshape(1, 8, SEQ, 4, 128)
    v_prompt = cat("o_vp").reshape(1, 8, SEQ, 4, 128)
    hre_p = cat("o_hre").reshape(1, 8, 32, 64)
    him_p = cat("o_him").reshape(1, 8, 32, 64)
    conv_p = cat("o_cp").reshape(1, 8, 2, DFF)
    k_sample = cat("o_ks").reshape(1, 32, 1, 4, 128)
    v_sample = cat("o_vs").reshape(1, 32, 1, 4, 128)
    hre_s = cat("o_hres").reshape(1, 32, 32, 64)
    him_s = cat("o_hims").reshape(1, 32, 32, 64)
    conv_s = cat("o_cs").reshape(1, 32, 2, DFF)
    return (y_prompt, y_sample, k_prompt, v_prompt, hre_p, him_p, conv_p,
            k_sample, v_sample, hre_s, him_s, conv_s)
```

```python
import numpy as np
from contextlib import ExitStack
import concourse.bass as bass
import concourse.mybir as mybir
from concourse.bass_utils import run_bass_kernel_spmd

F32 = mybir.dt.float32
BF16 = mybir.dt.bfloat16
I32 = mybir.dt.int32
ALU = mybir.AluOpType
AF = mybir.ActivationFunctionType
AX = mybir.AxisListType

D = 1024
SEQ = 2048
NSC = 4
NCOL = SEQ + NSC
DPROJ = 2048
DFF = 2816
NFC = DFF // 128
PAGE = 128
NPAGES = 64
PAST = PAGE * NPAGES
EPS = 1e-6
SUBLN_EPS = 1e-5
LAM_INIT = 0.8 - 0.6 * 1.0
SELF_ORDERED = ("pe",)
TILES = [(128 * i, 128) for i in range(16)] + [(SEQ, NSC)]
BLOCKS = [(512 * i, 512) for i in range(4)] + [(SEQ, NSC)]


class Res:
    __slots__ = ("name", "w", "r")

    def __init__(self, name):
        self.name = name
        self.w = None
        self.r = []


class _Eng:
    def __init__(self, name, eng, sem):
        self.name = name
        self.eng = eng
        self.sem = sem
        self.count = 0
        self.seen = {}


class Sched:
    def __init__(self, nc, es):
        self.nc = nc
        self.es = es
        self.sems = {}
        self.engs = {}
        for name, eng in (("pe", nc.tensor), ("dve", nc.vector), ("act", nc.scalar),
                          ("pool", nc.gpsimd), ("sp", nc.sync)):
            sem = es.enter_context(nc.semaphore("sem_" + name))
            self.sems[name] = sem
            self.engs[name] = _Eng(name, eng, sem)
        self.dma_sems = {}
        self.n_inst = 0
        self.n_wait = 0

    def _dma_sem(self, key):
        if key not in self.dma_sems:
            sem = self.es.enter_context(self.nc.semaphore("dsem_" + key))
            self.sems["d:" + key] = sem
            self.dma_sems[key] = [sem, 0]
        return self.dma_sems[key]

    @staticmethod
    def _deps(reads, writes):
        deps = {}

        def add(d):
            if d is not None and deps.get(d[0], 0) < d[1]:
                deps[d[0]] = d[1]
        for r in reads:
            add(r.w)
        for w in writes:
            add(w.w)
            for rd in w.r:
                add(rd)
        return deps

    def _wait(self, E, deps):
        for k, v in deps.items():
            if E.seen.get(k, 0) >= v:
                continue
            E.eng.wait_ge(self.sems[k], v)
            E.seen[k] = v
            self.n_wait += 1

    @staticmethod
    def _mark(reads, writes, stamp):
        for r in reads:
            r.r.append(stamp)
        for w in writes:
            w.w = stamp
            w.r = []

    def op(self, ename, fn, reads=(), writes=()):
        E = self.engs[ename]
        deps = self._deps(reads, writes)
        if ename in SELF_ORDERED:
            deps.pop(ename, None)
        self._wait(E, deps)
        ins = fn(E.eng)
        E.count += 1
        ins.then_inc(E.sem, 1)
        self._mark(reads, writes, (ename, E.count))
        self.n_inst += 1
        return ins

    def dma(self, ename, fn, skey, reads=(), writes=()):
        E = self.engs[ename]
        self._wait(E, self._deps(reads, writes))
        ds = self._dma_sem(skey)
        ins = fn(E.eng)
        ds[1] += 16
        ins.then_inc(ds[0], 16)
        self._mark(reads, writes, ("d:" + skey, ds[1]))
        self.n_inst += 1
        return ins

    def barrier(self):
        deps = {n: e.count for n, e in self.engs.items() if e.count}
        for k, (sem, c) in self.dma_sems.items():
            if c:
                deps["d:" + k] = c
        for E in self.engs.values():
            d = dict(deps)
            self._wait(E, d)

    def finish(self, ename="sp"):
        E = self.engs[ename]
        deps = {n: e.count for n, e in self.engs.items() if e.count}
        for k, (sem, c) in self.dma_sems.items():
            if c:
                deps["d:" + k] = c
        self._wait(E, deps)


def AP(t, off, dims):
    return bass.AP(t, off, dims)


import math
PI = math.pi


def range_reduce(S, eng, out, a, ki, kf, m, n_part, R_out, R_tmp, reads):
    S.op(eng, lambda e: e.tensor_scalar(out=ki, in0=a, scalar1=1.0 / (2 * PI), scalar2=None, op0=ALU.mult),
         reads=reads, writes=[R_tmp])
    S.op(eng, lambda e: e.tensor_copy(out=kf, in_=ki), reads=[R_tmp], writes=[R_tmp])
    S.op("dve", lambda e: e.scalar_tensor_tensor(out=out, in0=kf, scalar=-2 * PI, in1=a, op0=ALU.mult, op1=ALU.add),
         reads=[R_tmp] + list(reads), writes=[R_out])
    for thr, corr in ((PI, -2 * PI),):
        S.op(eng, lambda e: e.tensor_scalar(out=m, in0=out, scalar1=thr, scalar2=corr, op0=ALU.is_gt, op1=ALU.mult),
             reads=[R_out], writes=[R_tmp])
        S.op(eng, lambda e: e.tensor_tensor(out=out, in0=out, in1=m, op=ALU.add), reads=[R_out, R_tmp], writes=[R_out])
    S.op(eng, lambda e: e.tensor_scalar(out=m, in0=out, scalar1=-PI, scalar2=2 * PI, op0=ALU.is_lt, op1=ALU.mult),
         reads=[R_out], writes=[R_tmp])
    S.op(eng, lambda e: e.tensor_tensor(out=out, in0=out, in1=m, op=ALU.add), reads=[R_out, R_tmp], writes=[R_out])


def sincos(S, eng, s_out, c_out, ang, scr, n_part, R_s, R_c, R_scr, reads, nbias):
    range_reduce(S, eng, scr["y"], ang, scr["ki"], scr["kf"], scr["m"], n_part, R_scr, R_scr, reads)
    S.op("act", lambda e: e.activation(out=s_out, in_=scr["y"], func=AF.Sin), reads=[R_scr], writes=[R_s])
    S.op(eng, lambda e: e.tensor_scalar(out=scr["a"], in0=ang, scalar1=PI / 2, scalar2=None, op0=ALU.add),
         reads=list(reads) + [R_scr], writes=[R_scr])
    range_reduce(S, eng, scr["y"], scr["a"], scr["ki"], scr["kf"], scr["m"], n_part, R_scr, R_scr, [R_scr])
    S.op("act", lambda e: e.activation(out=c_out, in_=scr["y"], func=AF.Sin), reads=[R_scr], writes=[R_c])


def ssm_phase(nc, S, sb, ps, L):
    uT, R_uT, mixT, R_mix, idf, R_id = L["uT"], L["R_uT"], L["mixT"], L["R_mix"], L["idf"], L["R_id"]
    stage = L["stage"]
    with ExitStack() as pz:
        col = sb("col", [128, 26, 16], F32, pz)
        R_col = Res("col")
        dco = sb("dco", [128, 4], F32, pz)
        h0r = sb("h0r", [128, 16, NSC], F32, pz)
        h0i = sb("h0i", [128, 16, NSC], F32, pz)
        R_h0 = Res("h0")
        tr1 = sb("tr1", [128, 128], F32, pz)
        R_tr1 = Res("tr1")
        wgb = sb("wgb", [128, 4, 512], BF16, pz)
        R_wgb = Res("wgb")
        ygT = sb("ygT", [128, 4, NCOL], BF16, pz)
        R_yg = [Res("yg%d" % b) for b in range(5)]
        hfin = sb("hfin", [128, 2, 16], F32, pz)
        R_hfin = Res("hfin")
        hsmp = sb("hsmp", [128, 2, NSC, 16], F32, pz)
        R_hsmp = Res("hsmp")
        S.dma("sp", lambda e: e.dma_start(out=dco[:], in_=L["dcol"]), "z_c", writes=[R_col])
        S.dma("sp", lambda e: e.dma_start(out=col[:, 0, :], in_=L["acol_re"]), "z_c", writes=[R_col])
        S.dma("sp", lambda e: e.dma_start(out=col[:, 1, :], in_=L["acol_im"]), "z_c", writes=[R_col])
        S.dma("sp", lambda e: e.dma_start(out=col[:, 2, :], in_=L["ldt_col"]), "z_c", writes=[R_col])
        S.dma("sp", lambda e: e.dma_start(out=h0r[:], in_=L["h0c_re"]), "z_h", writes=[R_h0])
        S.dma("sp", lambda e: e.dma_start(out=h0i[:], in_=L["h0c_im"]), "z_h", writes=[R_h0])
        S.dma("sp", lambda e: e.dma_start(out=tr1[:], in_=L["trow1"][:, 0:128]), "z_t", writes=[R_tr1])

        with ExitStack() as pc:
            cki = sb("c_ki", [128, 16], I32, pc)
            cs = {k: sb("c_" + k, [128, 16], F32, pc) for k in ("a", "kf", "m", "y")}
            S.op("act", lambda e: e.activation(out=col[:, 2, :], in_=col[:, 2, :], func=AF.Exp), reads=[R_col], writes=[R_col])
            S.op("dve", lambda e: e.tensor_tensor(out=col[:, 3, :], in0=col[:, 1, :], in1=col[:, 2, :], op=ALU.mult), reads=[R_col], writes=[R_col])
            S.op("dve", lambda e: e.tensor_tensor(out=col[:, 9, :], in0=col[:, 0, :], in1=col[:, 2, :], op=ALU.mult), reads=[R_col], writes=[R_col])
            S.op("act", lambda e: e.activation(out=col[:, 4, :], in_=col[:, 9, :], func=AF.Exp), reads=[R_col], writes=[R_col])
            scr = dict(a=cs["a"][:], ki=cki[:], kf=cs["kf"][:], m=cs["m"][:], y=cs["y"][:])
            sincos(S, "dve", col[:, 5, :], col[:, 6, :], col[:, 3, :], scr, 128, R_col, R_col, R_col, [R_col], None)
            S.op("dve", lambda e: e.tensor_tensor(out=col[:, 7, :], in0=col[:, 6, :], in1=col[:, 4, :], op=ALU.mult), reads=[R_col], writes=[R_col])
            S.op("dve", lambda e: e.tensor_tensor(out=col[:, 8, :], in0=col[:, 5, :], in1=col[:, 4, :], op=ALU.mult), reads=[R_col], writes=[R_col])

        cr = lambda k: col[:, k, :]
        tt_ = lambda o, x, y, op: S.op("dve", lambda e: e.tensor_tensor(out=cr(o), in0=cr(x), in1=cr(y), op=op), reads=[R_col], writes=[R_col])
        S.op("dve", lambda e: e.tensor_scalar(out=cr(9), in0=cr(7), scalar1=-1.0, scalar2=None, op0=ALU.add), reads=[R_col], writes=[R_col])
        tt_(11, 0, 0, ALU.mult); tt_(12, 1, 1, ALU.mult); tt_(10, 11, 12, ALU.add)
        S.op("dve", lambda e: e.reciprocal(out=cr(10), in_=cr(10)), reads=[R_col], writes=[R_col])
        tt_(11, 9, 0, ALU.mult); tt_(12, 8, 1, ALU.mult); tt_(11, 11, 12, ALU.add); tt_(13, 11, 10, ALU.mult)
        tt_(11, 8, 0, ALU.mult); tt_(12, 9, 1, ALU.mult); tt_(11, 11, 12, ALU.subtract); tt_(14, 11, 10, ALU.mult)

        def cmul(o_re, o_im, a_re, a_im, b_re, b_im):
            tt_(11, a_re, b_re, ALU.mult); tt_(12, a_im, b_im, ALU.mult); tt_(o_re, 11, 12, ALU.subtract)
            tt_(11, a_re, b_im, ALU.mult); tt_(12, a_im, b_re, ALU.mult); tt_(o_im, 11, 12, ALU.add)
        cmul(16, 17, 7, 8, 7, 8)
        cmul(18, 19, 16, 17, 7, 8)
        cmul(20, 21, 16, 17, 16, 17)
        S.op("dve", lambda e: e.tensor_scalar(out=cr(22), in0=cr(3), scalar1=4.0, scalar2=None, op0=ALU.mult), reads=[R_col], writes=[R_col])
        tt_(24, 4, 4, ALU.mult); tt_(23, 24, 24, ALU.mult)
        LBK = {1: (7, 8), 2: (16, 17), 3: (18, 19), 4: (20, 21)}
        for kc in range(4):
            S.dma("pool", lambda e: e.dma_start(out=wgb[:, kc, :], in_=L["w_glu"][kc * 128:(kc + 1) * 128, :]), "z_w", writes=[R_wgb])
        S.barrier()

        with ExitStack() as pm:
            wB_ = [[sb("wB%d%d" % (qq, c), [128, 4, 4, 128], BF16, pm) for c in range(2)] for qq in range(2)]
            wC_ = [[sb("wC%d%d" % (qq, c), [128, 4, 5, 128], BF16, pm) for c in range(2)] for qq in range(2)]
            KK_ = [sb("KK%d" % qq, [128, 4, 128], BF16, pm) for qq in range(2)]
            R_wB_ = [Res("wB%d" % qq) for qq in range(2)]
            R_wC_ = [Res("wC%d" % qq) for qq in range(2)]
            R_KK_ = [Res("KK%d" % qq) for qq in range(2)]
            braw = [sb("braw%d" % c, [128, 4, 128], F32, pm) for c in range(2)]
            craw = braw
            bbf = [sb("bbf%d" % c, [128, 4, 128], F32, pm) for c in range(2)]
            tq = [sb("tq%d" % c, [128, 4, 128], F32, pm) for c in range(2)]
            pcb = [[sb("pcb%d%d" % (k, c), [128, 4, 128], BF16, pm) for c in range(2)] for k in range(4)]
            R_raw, R_bbf, R_tq = Res("raw"), Res("bbf"), Res("tq")
            R_pcb = [Res("pcb%d" % k) for k in range(4)]
            ptb = ps("r_ptb", [128, 4, 128], BF16, pm)
            pkk = ps("r_pkk", [128, 128], F32, pm)
            Rp, Rpk = Res("ptb"), Res("pkk")
            etc_ = [sb("etc%d" % qq, [128, 4, 128], F32, pm) for qq in range(2)]
            ets_ = [sb("ets%d" % qq, [128, 4, 128], F32, pm) for qq in range(2)]
            R_et_ = [[Res("et%d%d" % (qq, k)) for k in range(4)] for qq in range(2)]
            eki = sb("e_ki", [128, 128], I32, pm)
            esc = [sb("esc%d" % k, [128, 128], F32, pm) for k in range(5)]
            R_esc = Res("esc")
            pbr = [ps("pbr%d" % j, [128, 128], F32, pm) for j in range(2)]
            pbi = [ps("pbi%d" % j, [128, 128], F32, pm) for j in range(2)]
            R_pb = [Res("pb%d" % j) for j in range(2)]
            py = [ps("py%d" % j, [128, 512], F32, pm) for j in range(2)]
            R_py = [Res("py%d" % j) for j in range(2)]
            tt = [sb("tt%d" % k, [128, 128], F32, pm) for k in range(4)]
            R_tt = Res("tt")
            w_r = [sb("w_r0", [128, 128], F32, pm)] * 2
            w_i = [sb("w_i0", [128, 128], F32, pm)] * 2
            R_w = [Res("w0")] * 2
            g_r = [sb("g_r%d" % j, [128, 128], F32, pm) for j in range(2)]
            g_i = [sb("g_i%d" % j, [128, 128], F32, pm) for j in range(2)]
            R_g = [Res("g%d" % j) for j in range(2)]
            uu = [sb("uu%d" % k, [128, 128], F32, pm) for k in range(2)]
            R_uu = Res("uu")
            hr = [sb("hr%d" % j, [128, 128], F32, pm) for j in range(2)]
            hi = [sb("hi%d" % j, [128, 128], F32, pm) for j in range(2)]
            R_h = [Res("h%d" % j) for j in range(2)]
            hrb = [sb("hrb%d" % j, [128, 128], BF16, pm) for j in range(2)]
            hib = [sb("hib%d" % j, [128, 128], BF16, pm) for j in range(2)]
            R_hb = [Res("hb%d" % j) for j in range(2)]
            yf = sb("yf", [128, 512], F32, pm)
            R_yf = Res("yf")
            carry = sb("carry", [128, 2, 16], F32, pm)
            R_carry = [Res("carry%d" % p) for p in range(16)]
            S.op("dve", lambda e: e.memset(carry[:], 0.0), writes=R_carry)

            def prep(q):
                qs = slice(4 * q, 4 * q + 4)
                wB, wC, KK, R_wB, R_wC, R_KK = wB_[q % 2], wC_[q % 2], KK_[q % 2], R_wB_[q % 2], R_wC_[q % 2], R_KK_[q % 2]
                etc, ets, R_et = etc_[q % 2], ets_[q % 2], R_et_[q % 2]
                cb = lambda row: col[:, row, qs].unsqueeze(2).to_broadcast([128, 4, 128])
                S.dma("sp", lambda e: e.dma_start(out=braw[0][:], in_=L["bcol_re"][:, qs, :]), "z_b", writes=[R_raw])
                S.dma("sp", lambda e: e.dma_start(out=braw[1][:], in_=L["bcol_im"][:, qs, :]), "z_b", writes=[R_raw])

                def cplx(o_re, o_im, R_o, x_re, x_im, R_x, rr_, ri_, neg_im=False):
                    S.op("dve", lambda e: e.tensor_tensor(out=tq[0][:], in0=x_re, in1=cb(rr_), op=ALU.mult), reads=[R_x, R_col], writes=[R_tq])
                    S.op("dve", lambda e: e.tensor_tensor(out=tq[1][:], in0=x_im, in1=cb(ri_), op=ALU.mult), reads=[R_x, R_col], writes=[R_tq])
                    S.op("dve", lambda e: e.tensor_tensor(out=o_re, in0=tq[0][:], in1=tq[1][:], op=ALU.subtract), reads=[R_tq], writes=[R_o])
                    yield
                    S.op("dve", lambda e: e.tensor_tensor(out=tq[0][:], in0=x_im, in1=cb(rr_), op=ALU.mult), reads=[R_x, R_col, R_o], writes=[R_tq])
                    S.op("dve", lambda e: e.tensor_tensor(out=tq[1][:], in0=x_re, in1=cb(ri_), op=ALU.mult), reads=[R_x, R_col, R_o], writes=[R_tq])
                    if neg_im:
                        S.op("dve", lambda e: e.tensor_tensor(out=tq[0][:], in0=tq[0][:], in1=tq[1][:], op=ALU.add), reads=[R_tq], writes=[R_tq])
                        S.op("dve", lambda e: e.tensor_scalar(out=o_im, in0=tq[0][:], scalar1=-1.0, scalar2=None, op0=ALU.mult), reads=[R_tq], writes=[R_o])
                    else:
                        S.op("dve", lambda e: e.tensor_tensor(out=o_im, in0=tq[0][:], in1=tq[1][:], op=ALU.add), reads=[R_tq], writes=[R_o])
                    yield
                yield from cplx(bbf[0][:], bbf[1][:], R_bbf, braw[0][:], braw[1][:], R_raw, 13, 14)
                S.dma("sp", lambda e: e.dma_start(out=craw[0][:], in_=L["cpad_re"][:, qs, :]), "z_b", writes=[R_raw])
                S.dma("sp", lambda e: e.dma_start(out=craw[1][:], in_=L["cpad_im"][:, qs, :]), "z_b", writes=[R_raw])
                S.op("act", lambda e: e.activation(out=pcb[0][0][:], in_=bbf[0][:], func=AF.Copy), reads=[R_bbf], writes=[R_pcb[0]])
                S.op("act", lambda e: e.activation(out=pcb[0][1][:], in_=bbf[1][:], func=AF.Copy), reads=[R_bbf], writes=[R_pcb[0]])
                for k in (1, 2, 3):
                    yield from cplx(pcb[k][0][:], pcb[k][1][:], R_pcb[k], bbf[0][:], bbf[1][:], R_bbf, *LBK[k])
                for c in range(2):
                    for k in range(4):
                        for pl in range(4):
                            S.op("pe", lambda e: e.transpose(out=ptb[:, pl, :], in_=pcb[k][c][:, pl, :], identity=L["idb"][:, :]),
                                 reads=[R_pcb[k], R_id], writes=[Rp])
                        S.op("act", lambda e: e.activation(out=wB[c][:, :, k, :], in_=ptb[:], func=AF.Copy), reads=[Rp], writes=[R_wB])
                        yield
                S.op("act", lambda e: e.activation(out=wC[0][:, :, 0, :], in_=craw[0][:], func=AF.Copy), reads=[R_raw], writes=[R_wC])
                S.op("act", lambda e: e.activation(out=wC[1][:, :, 0, :], in_=craw[1][:], func=AF.Copy, scale=-1.0), reads=[R_raw], writes=[R_wC])
                for m in (1, 2, 3, 4):
                    yield from cplx(wC[0][:, :, m, :], wC[1][:, :, m, :], R_wC, craw[0][:], craw[1][:], R_raw, *LBK[m], neg_im=True)
                for tau in range(4):
                    n_ = 0
                    for pl in range(4):
                        for c in range(2):
                            S.op("pe", lambda e: e.matmul(pkk[:, :], lhsT=pcb[tau][c][:, pl, :], rhs=wC[c][:, pl, 0, :], start=(n_ == 0), stop=(n_ == 7)),
                                 reads=[R_pcb[tau], R_wC], writes=[Rpk])
                            n_ += 1
                    S.op("act", lambda e: e.activation(out=KK[:, tau, :], in_=pkk[:, :], func=AF.Copy), reads=[Rpk], writes=[R_KK])
                    yield
                for pl in range(4):
                    pr_ = 4 * q + pl
                    S.op("dve", lambda e: e.tensor_scalar(out=esc[4][:], in0=tr1[:, 0:128], scalar1=col[:, 22, pr_:pr_ + 1],
                                                          scalar2=None, op0=ALU.mult), reads=[R_tr1, R_col], writes=[R_esc])
                    scr = dict(a=esc[0][:], ki=eki[:], kf=esc[1][:], m=esc[2][:], y=esc[3][:])
                    sincos(S, "dve", ets[:, pl, :], etc[:, pl, :], esc[4][:], scr, 128, R_et[pl], R_et[pl], R_esc, [R_esc], None)
                    yield

            def u_str(q, c0, off, n):
                return AP(uT, q * NCOL + c0 + off, [[4 * NCOL, 128], [4, n]])

            def stA(n, q, b, pl):
                c0, nb = BLOCKS[b]
                wB, wC, KK, R_wB, R_wC, R_KK = wB_[q % 2], wC_[q % 2], KK_[q % 2], R_wB_[q % 2], R_wC_[q % 2], R_KK_[q % 2]
                etc, ets, R_et = etc_[q % 2], ets_[q % 2], R_et_[q % 2]
                j = n % 2
                if b < 4:
                    for c, pX in enumerate((pbr, pbi)):
                        for r_ in range(4):
                            S.op("pe", lambda e: e.matmul(pX[j][:, :], lhsT=wB[c][:, pl, 3 - r_, :], rhs=u_str(q, c0, r_, 128), start=(r_ == 0), stop=(r_ == 3)),
                                 reads=[R_wB, R_uT[b]], writes=[R_pb[j]])
                else:
                    for c, pX in enumerate((pbr, pbi)):
                        S.op("pe", lambda e: e.matmul(pX[j][:, :nb], lhsT=wB[c][:, pl, 0, :], rhs=uT[:, q, c0:c0 + nb], start=True, stop=True),
                             reads=[R_wB, R_uT[b]], writes=[R_pb[j]])

            def stB(n, q, b, pl):
                c0, nb = BLOCKS[b]
                wB, wC, KK, R_wB, R_wC, R_KK = wB_[q % 2], wC_[q % 2], KK_[q % 2], R_wB_[q % 2], R_wC_[q % 2], R_KK_[q % 2]
                etc, ets, R_et = etc_[q % 2], ets_[q % 2], R_et_[q % 2]
                pr_, j = 4 * q + pl, n % 2
                if b < 4:
                    c_, s_ = etc[:, pl, :], ets[:, pl, :]
                    S.op("dve", lambda e: e.tensor_tensor(out=tt[0][:], in0=pbr[j][:], in1=c_, op=ALU.mult), reads=[R_pb[j], R_et[pl]], writes=[R_tt])
                    S.op("dve", lambda e: e.tensor_tensor(out=tt[1][:], in0=pbi[j][:], in1=s_, op=ALU.mult), reads=[R_pb[j], R_et[pl]], writes=[R_tt])
                    S.op("dve", lambda e: e.tensor_tensor(out=tt[2][:], in0=pbi[j][:], in1=c_, op=ALU.mult), reads=[R_pb[j], R_et[pl]], writes=[R_tt])
                    S.op("dve", lambda e: e.tensor_tensor(out=tt[3][:], in0=pbr[j][:], in1=s_, op=ALU.mult), reads=[R_pb[j], R_et[pl]], writes=[R_tt])
                    S.op("dve", lambda e: e.tensor_tensor(out=w_r[j][:], in0=tt[0][:], in1=tt[1][:], op=ALU.add), reads=[R_tt], writes=[R_w[j]])
                    S.op("dve", lambda e: e.tensor_tensor(out=w_i[j][:], in0=tt[2][:], in1=tt[3][:], op=ALU.subtract), reads=[R_tt], writes=[R_w[j]])
                    S.op("dve", lambda e: e.tensor_tensor_scan(out=g_r[j][:], data0=col[:, 23, pr_:pr_ + 1].to_broadcast([128, 128]), data1=w_r[j][:],
                                                               initial=carry[:, 0, pr_:pr_ + 1], op0=ALU.mult, op1=ALU.add),
                         reads=[R_w[j], R_col, R_carry[pr_]], writes=[R_g[j]])
                    S.op("dve", lambda e: e.tensor_tensor_scan(out=g_i[j][:], data0=col[:, 23, pr_:pr_ + 1].to_broadcast([128, 128]), data1=w_i[j][:],
                                                               initial=carry[:, 1, pr_:pr_ + 1], op0=ALU.mult, op1=ALU.add),
                         reads=[R_w[j], R_col, R_carry[pr_]], writes=[R_g[j]])
                else:
                    lr, li = col[:, 7, pr_:pr_ + 1], col[:, 8, pr_:pr_ + 1]
                    S.op("dve", lambda e: e.tensor_scalar(out=tt[0][:, :nb], in0=h0r[:, pr_, :], scalar1=lr, scalar2=None, op0=ALU.mult), reads=[R_h0, R_col], writes=[R_tt])
                    S.op("dve", lambda e: e.tensor_scalar(out=tt[1][:, :nb], in0=h0i[:, pr_, :], scalar1=li, scalar2=None, op0=ALU.mult), reads=[R_h0, R_col], writes=[R_tt])
                    S.op("dve", lambda e: e.tensor_scalar(out=tt[2][:, :nb], in0=h0i[:, pr_, :], scalar1=lr, scalar2=None, op0=ALU.mult), reads=[R_h0, R_col], writes=[R_tt])
                    S.op("dve", lambda e: e.tensor_scalar(out=tt[3][:, :nb], in0=h0r[:, pr_, :], scalar1=li, scalar2=None, op0=ALU.mult), reads=[R_h0, R_col], writes=[R_tt])
                    S.op("dve", lambda e: e.tensor_tensor(out=tt[0][:, :nb], in0=tt[0][:, :nb], in1=tt[1][:, :nb], op=ALU.subtract), reads=[R_tt], writes=[R_tt])
                    S.op("dve", lambda e: e.tensor_tensor(out=tt[2][:, :nb], in0=tt[2][:, :nb], in1=tt[3][:, :nb], op=ALU.add), reads=[R_tt], writes=[R_tt])
                    S.op("dve", lambda e: e.tensor_tensor(out=hr[j][:, :nb], in0=tt[0][:, :nb], in1=pbr[j][:, :nb], op=ALU.add), reads=[R_tt, R_pb[j]], writes=[R_h[j]])
                    S.op("dve", lambda e: e.tensor_tensor(out=hi[j][:, :nb], in0=tt[2][:, :nb], in1=pbi[j][:, :nb], op=ALU.add), reads=[R_tt, R_pb[j]], writes=[R_h[j]])

            def stC(n, q, b, pl):
                c0, nb = BLOCKS[b]
                wB, wC, KK, R_wB, R_wC, R_KK = wB_[q % 2], wC_[q % 2], KK_[q % 2], R_wB_[q % 2], R_wC_[q % 2], R_KK_[q % 2]
                etc, ets, R_et = etc_[q % 2], ets_[q % 2], R_et_[q % 2]
                pr_, j = 4 * q + pl, n % 2
                if b < 4:
                    c_, s_ = etc[:, pl, :], ets[:, pl, :]
                    S.op("pool", lambda e: e.tensor_tensor(out=uu[0][:], in0=g_r[j][:], in1=c_, op=ALU.mult), reads=[R_g[j], R_et[pl]], writes=[R_uu])
                    S.op("pool", lambda e: e.tensor_tensor(out=uu[1][:], in0=g_i[j][:], in1=s_, op=ALU.mult), reads=[R_g[j], R_et[pl]], writes=[R_uu])
                    S.op("pool", lambda e: e.tensor_tensor(out=hr[j][:], in0=uu[0][:], in1=uu[1][:], op=ALU.subtract), reads=[R_uu], writes=[R_h[j]])
                    S.op("pool", lambda e: e.tensor_tensor(out=uu[0][:], in0=g_i[j][:], in1=c_, op=ALU.mult), reads=[R_g[j], R_et[pl]], writes=[R_uu])
                    S.op("pool", lambda e: e.tensor_tensor(out=uu[1][:], in0=g_r[j][:], in1=s_, op=ALU.mult), reads=[R_g[j], R_et[pl]], writes=[R_uu])
                    S.op("pool", lambda e: e.tensor_tensor(out=hi[j][:], in0=uu[0][:], in1=uu[1][:], op=ALU.add), reads=[R_uu], writes=[R_h[j]])
                    S.op("act", lambda e: e.activation(out=hrb[j][:, 0:1], in_=carry[:, 0, pr_:pr_ + 1], func=AF.Copy), reads=[R_carry[pr_]], writes=[R_hb[j]])
                    S.op("act", lambda e: e.activation(out=hib[j][:, 0:1], in_=carry[:, 1, pr_:pr_ + 1], func=AF.Copy), reads=[R_carry[pr_]], writes=[R_hb[j]])
                    S.op("act", lambda e: e.activation(out=hrb[j][:, 1:128], in_=hr[j][:, 0:127], func=AF.Copy), reads=[R_h[j]], writes=[R_hb[j]])
                    S.op("act", lambda e: e.activation(out=hib[j][:, 1:128], in_=hi[j][:, 0:127], func=AF.Copy), reads=[R_h[j]], writes=[R_hb[j]])
                    S.op("act", lambda e: e.activation(out=carry[:, 0, pr_:pr_ + 1], in_=hr[j][:, 127:128], func=AF.Copy), reads=[R_h[j], R_hb[j]], writes=[R_carry[pr_]])
                    S.op("act", lambda e: e.activation(out=carry[:, 1, pr_:pr_ + 1], in_=hi[j][:, 127:128], func=AF.Copy), reads=[R_h[j], R_hb[j]], writes=[R_carry[pr_]])
                    if b == 3:
                        S.op("act", lambda e: e.activation(out=hfin[:, 0, pr_:pr_ + 1], in_=hr[j][:, 127:128], func=AF.Copy), reads=[R_h[j]], writes=[R_hfin])
                        S.op("act", lambda e: e.activation(out=hfin[:, 1, pr_:pr_ + 1], in_=hi[j][:, 127:128], func=AF.Copy), reads=[R_h[j]], writes=[R_hfin])
                else:
                    S.op("act", lambda e: e.activation(out=hsmp[:, 0, :, pr_], in_=hr[j][:, :nb], func=AF.Copy), reads=[R_h[j]], writes=[R_hsmp])
                    S.op("act", lambda e: e.activation(out=hsmp[:, 1, :, pr_], in_=hi[j][:, :nb], func=AF.Copy), reads=[R_h[j]], writes=[R_hsmp])
                    S.op("act", lambda e: e.activation(out=hrb[j][:, :nb], in_=hr[j][:, :nb], func=AF.Copy), reads=[R_h[j]], writes=[R_hb[j]])
                    S.op("act", lambda e: e.activation(out=hib[j][:, :nb], in_=hi[j][:, :nb], func=AF.Copy), reads=[R_h[j]], writes=[R_hb[j]])

            def stD(n, q, b, pl):
                c0, nb = BLOCKS[b]
                wB, wC, KK, R_wB, R_wC, R_KK = wB_[q % 2], wC_[q % 2], KK_[q % 2], R_wB_[q % 2], R_wC_[q % 2], R_KK_[q % 2]
                etc, ets, R_et = etc_[q % 2], ets_[q % 2], R_et_[q % 2]
                pr_, j = 4 * q + pl, n % 2
                jy = (q * 5 + b) % 2
                if b < 4:
                    for r_ in range(4):
                        o_ap = AP(py[jy], r_, [[512, 128], [4, 128]])
                        for c, hX in enumerate((hrb, hib)):
                            S.op("pe", lambda e: e.matmul(o_ap, lhsT=wC[c][:, pl, r_ + 1, :], rhs=hX[j][:, :], start=(pl == 0 and r_ == 0 and c == 0), stop=False,
                                                          skip_group_check=True),
                                 reads=[R_wC, R_hb[j]], writes=[R_py[jy]])
                    if pl == 3:
                        for r_ in range(4):
                            o_ap = AP(py[jy], r_, [[512, 128], [4, 128]])
                            for tau in range(r_ + 1):
                                S.op("pe", lambda e: e.matmul(o_ap, lhsT=KK[:, tau, :], rhs=u_str(q, c0, r_ - tau, 128), start=False, stop=(r_ == 3 and tau == 3),
                                                              skip_group_check=True),
                                     reads=[R_KK, R_uT[b]], writes=[R_py[jy]])
                else:
                    S.op("pe", lambda e: e.matmul(py[jy][:, :nb], lhsT=wC[0][:, pl, 0, :], rhs=hrb[j][:, :nb], start=(pl == 0), stop=False),
                         reads=[R_wC, R_hb[j]], writes=[R_py[jy]])
                    S.op("pe", lambda e: e.matmul(py[jy][:, :nb], lhsT=wC[1][:, pl, 0, :], rhs=hib[j][:, :nb], start=False, stop=(pl == 3)),
                         reads=[R_wC, R_hb[j]], writes=[R_py[jy]])
                if pl == 3:
                    S.op("dve", lambda e: e.scalar_tensor_tensor(out=yf[:, :nb], in0=uT[:, q, c0:c0 + nb], scalar=dco[:, q:q + 1],
                                                                 in1=py[jy][:, :nb], op0=ALU.mult, op1=ALU.add),
                         reads=[R_uT[b], R_col, R_py[jy]], writes=[R_yf])
                    S.op("act", lambda e: e.activation(out=ygT[:, q, c0:c0 + nb], in_=yf[:, :nb], func=AF.Gelu_apprx_tanh),
                         reads=[R_yf], writes=[R_yg[b]])

            its = [(q, b, pl) for q in range(4) for b in range(5) for pl in range(4)]
            for _ in prep(0):
                pass
            pg = prep(1)
            stA(0, *its[0])
            for n, itx in enumerate(its):
                nxt = its[n + 1] if n + 1 < len(its) else None
                if nxt is not None and nxt[0] == itx[0]:
                    stA(n + 1, *nxt)
                stB(n, *itx)
                stC(n, *itx)
                stD(n, *itx)
                if pg is not None:
                    for _ in range(3):
                        if next(pg, "end") == "end":
                            pg = None
                            break
                if nxt is not None and nxt[0] != itx[0]:
                    if pg is not None:
                        for _ in pg:
                            pass
                    pg = prep(nxt[0] + 1) if nxt[0] + 1 < 4 else None
                    stA(n + 1, *nxt)
            for co in range(4):
                for b, (c0, nb) in enumerate(BLOCKS):
                    jy = (co * 5 + b) % 2
                    for kc in range(4):
                        S.op("pe", lambda e: e.matmul(py[jy][:, :nb], lhsT=wgb[:, kc, co * 128:(co + 1) * 128], rhs=ygT[:, kc, c0:c0 + nb],
                                                      start=(kc == 0), stop=(kc == 3)),
                             reads=[R_wgb, R_yg[b]], writes=[R_py[jy]])
                    S.op("act", lambda e: e.activation(out=yf[:, :nb], in_=py[jy][:, :nb], func=AF.Sigmoid), reads=[R_py[jy]], writes=[R_yf])
                    S.op("dve", lambda e: e.tensor_tensor(out=mixT[:, 4 + co, c0:c0 + nb], in0=yf[:, :nb], in1=ygT[:, co, c0:c0 + nb], op=ALU.mult),
                         reads=[R_yf, R_yg[b]], writes=[R_mix[4 + co]])
            S.barrier()
            pst = py[0]
            _ot = (braw[0], braw[1], bbf[0])
            oslot = lambda k: _ot[k // 4][0:16, k % 4, :]
            R_ost = Res("ost")
            srcs = [hfin[:, 0, :], hfin[:, 1, :]] + [hsmp[:, ri, s_, :] for ri in range(2) for s_ in range(NSC)]
            for k, src in enumerate(srcs):
                S.op("pe", lambda e: e.transpose(out=pst[0:16, 0:128], in_=src, identity=idf[:, :]),
                     reads=[R_hfin, R_hsmp, R_id], writes=[R_py[0]])
                S.op("dve", lambda e: e.tensor_copy(out=oslot(k), in_=pst[0:16, 0:128]), reads=[R_py[0]], writes=[R_ost])
            S.dma("sp", lambda e: e.dma_start(out=L["o_hre"], in_=oslot(0)), "z_o", reads=[R_ost])
            S.dma("sp", lambda e: e.dma_start(out=L["o_him"], in_=oslot(1)), "z_o", reads=[R_ost])
            for s_ in range(NSC):
                S.dma("sp", lambda e: e.dma_start(out=L["o_hres"][s_], in_=oslot(2 + s_)), "z_o", reads=[R_ost])
                S.dma("sp", lambda e: e.dma_start(out=L["o_hims"][s_], in_=oslot(2 + NSC + s_)), "z_o", reads=[R_ost])
            S.barrier()

def attn_phase(nc, S, sb, ps, L):
    qT, kT, vA, mixT = L["qT"], L["kT"], L["vA"], L["mixT"]
    R_qT, R_kT, R_vA, R_mix = L["R_qT"], L["R_kT"], L["R_vA"], L["R_mix"]
    idb, idf, R_id = L["idb"], L["idf"], L["R_id"]
    with ExitStack() as pa:
        lv = sb("lv", [128, 4, 64], F32, pa)
        lam = sb("lam", [128, 8], F32, pa)
        gs = sb("gs", [128, 128], F32, pa)
        trf = sb("trf", [128, 128], F32, pa)
        tri = sb("tri", [128, 128], BF16, pa)
        R_par = Res("apar")
        S.dma("sp", lambda e: e.dma_start(out=lv[:].rearrange("p a b -> p (a b)"), in_=AP(L["lamv"].tensor, 0, [[0, 128], [1, 256]])), "a_p", writes=[R_par])
        S.dma("sp", lambda e: e.dma_start(out=gs[:], in_=AP(L["gsub"].tensor, 0, [[0, 128], [1, 128]])), "a_p", writes=[R_par])
        S.dma("sp", lambda e: e.dma_start(out=trf[:], in_=L["trimask"]), "a_p", writes=[R_par])
        S.op("dve", lambda e: e.tensor_copy(out=tri[:], in_=trf[:]), reads=[R_par], writes=[R_par])
        S.op("dve", lambda e: e.tensor_scalar(out=gs[:], in0=gs[:], scalar1=1.0 - LAM_INIT, scalar2=None, op0=ALU.mult), reads=[R_par], writes=[R_par])
        S.op("dve", lambda e: e.tensor_tensor(out=lv[:, 0, :], in0=lv[:, 0, :], in1=lv[:, 1, :], op=ALU.mult), reads=[R_par], writes=[R_par])
        S.op("dve", lambda e: e.tensor_tensor(out=lv[:, 2, :], in0=lv[:, 2, :], in1=lv[:, 3, :], op=ALU.mult), reads=[R_par], writes=[R_par])
        S.op("dve", lambda e: e.reduce_sum(out=lam[:, 1:2], in_=lv[:, 0, :], axis=AX.X), reads=[R_par], writes=[R_par])
        S.op("dve", lambda e: e.reduce_sum(out=lam[:, 2:3], in_=lv[:, 2, :], axis=AX.X), reads=[R_par], writes=[R_par])
        S.op("act", lambda e: e.activation(out=lam[:, 1:3], in_=lam[:, 1:3], func=AF.Exp), reads=[R_par], writes=[R_par])
        S.op("dve", lambda e: e.tensor_tensor(out=lam[:, 0:1], in0=lam[:, 1:2], in1=lam[:, 2:3], op=ALU.subtract), reads=[R_par], writes=[R_par])
        S.op("dve", lambda e: e.tensor_scalar(out=lam[:, 0:1], in0=lam[:, 0:1], scalar1=LAM_INIT, scalar2=None, op0=ALU.add), reads=[R_par], writes=[R_par])

        ptp = ps("ptp", [128, 128], BF16, pa)
        R_ptp = Res("ptp")

        def subln_store(o_ap, n, dst_fn, scr):
            junk, st, ob, R_s = scr
            S.op("act", lambda e: e.activation(out=junk[:n, :], in_=o_ap, func=AF.Square, accum_out=st[:n, 0:1]), reads=[R_s], writes=[R_s])
            S.op("dve", lambda e: e.tensor_scalar(out=st[:n, 1:2], in0=st[:n, 0:1], scalar1=1.0 / 128, scalar2=SUBLN_EPS, op0=ALU.mult, op1=ALU.add), reads=[R_s], writes=[R_s])
            S.op("act", lambda e: e.sqrt(out=st[:n, 1:2], in_=st[:n, 1:2]), reads=[R_s], writes=[R_s])
            S.op("dve", lambda e: e.reciprocal(out=st[:n, 1:2], in_=st[:n, 1:2]), reads=[R_s], writes=[R_s])
            S.op("dve", lambda e: e.scalar_tensor_tensor(out=ob[:n, :], in0=o_ap, scalar=st[:n, 1:2], in1=gs[:n, :], op0=ALU.mult, op1=ALU.mult),
                 reads=[R_s, R_par], writes=[R_s])
            S.op("pe", lambda e: e.transpose(out=ptp[:, :n], in_=ob[:n, :], identity=idb[:n, :n]), reads=[R_s, R_id], writes=[R_ptp])
            dst_fn()

        psS = [ps("psS%d" % j, [128, 512], F32, pa) for j in range(2)]
        R_S = [Res("psS%d" % j) for j in range(2)]
        psO = ps("psO", [128, 2, 512], F32, pa)
        _rb = [Res("psO_b0"), Res("psO_b1")]
        R_O = [_rb[0], _rb[0], _rb[0], _rb[1]]
        accO = lambda k: psO[:, k // 3, (k % 3) * 129:(k % 3) * 129 + 129]
        eE = [sb("eE%d" % j, [128, 512], BF16, pa) for j in range(2)]
        R_E = [Res("eE%d" % j) for j in range(2)]
        osb = [sb("osb%d" % g, [128, 2, 4, 129], F32, pa) for g in range(2)]
        R_osb = [[Res("osb%d_%d" % (g, qt)) for qt in range(4)] for g in range(2)]
        of = [sb("of%d" % qt, [128, 128], F32, pa) for qt in range(4)]
        junk = [sb("ajunk%d" % qt, [128, 128], BF16, pa) for qt in range(4)]
        st = [sb("ast%d" % qt, [128, 4], F32, pa) for qt in range(4)]
        ob = [sb("aob%d" % qt, [128, 128], BF16, pa) for qt in range(4)]
        R_scr = [Res("ascr%d" % qt) for qt in range(4)]

        def prompt_group(gi):
            h, QB = gi // 4, gi % 4
            g2 = gi % 2
            nkb = 4 * QB + 4
            for c in range(2):
                def qk(kb):
                    o = kb - 4 * QB
                    qlo = max(o, 0) * 128
                    S.op("pe", lambda e: e.matmul(psS[kb % 2][:, qlo:512], lhsT=kT[64 * c:64 * c + 64, h, kb * 128:(kb + 1) * 128],
                                                  rhs=qT[64 * c:64 * c + 64, h, QB * 512 + qlo:QB * 512 + 512], start=True, stop=True),
                         reads=[R_kT[kb]] + R_qT[4 * QB:4 * QB + 4], writes=[R_S[kb % 2]])
                qk(0)
                for kb in range(nkb):
                    o = kb - 4 * QB
                    qlo = max(o, 0) * 128
                    j = kb % 2
                    if kb + 1 < nkb:
                        qk(kb + 1)
                    if kb % 2 == c:
                        sample_step()
                    S.op("act", lambda e: e.activation(out=eE[j][:, qlo:512], in_=psS[j][:, qlo:512], func=AF.Exp),
                         reads=[R_S[j]], writes=[R_E[j]])
                    if o >= 0:
                        S.op("pool", lambda e: e.tensor_tensor(out=eE[j][:, qlo:qlo + 128], in0=eE[j][:, qlo:qlo + 128], in1=tri[:], op=ALU.mult),
                             reads=[R_E[j], R_par], writes=[R_E[j]])
                    for qt in range(max(o, 0), 4):
                        S.op("pe", lambda e: e.matmul(accO(qt), lhsT=eE[j][:, qt * 128:(qt + 1) * 128], rhs=vA[:, kb, h, :],
                                                      start=(kb == 0 and qt in (0, 3)), stop=(kb == 4 * QB + qt), skip_group_check=True),
                             reads=[R_E[j], R_vA[kb]], writes=[R_O[qt]])
                for qt in range(4):
                    S.op("act", lambda e: e.activation(out=osb[g2][:, c, qt, :], in_=accO(qt), func=AF.Copy), reads=[R_O[qt]], writes=[R_osb[g2][qt]])
            for qt in range(4):
                o1, o2 = osb[g2][:, 0, qt, :], osb[g2][:, 1, qt, :]
                Rq = R_scr[qt]
                S.op("dve", lambda e: e.reciprocal(out=st[qt][:, 2:3], in_=o1[:, 128:129]), reads=[R_osb[g2][qt]], writes=[Rq])
                S.op("dve", lambda e: e.reciprocal(out=st[qt][:, 3:4], in_=o2[:, 128:129]), reads=[R_osb[g2][qt]], writes=[Rq])
                S.op("dve", lambda e: e.tensor_tensor(out=st[qt][:, 3:4], in0=st[qt][:, 3:4], in1=lam[:, 0:1], op=ALU.mult), reads=[Rq, R_par], writes=[Rq])
                S.op("dve", lambda e: e.tensor_scalar(out=o2[:, 0:128], in0=o2[:, 0:128], scalar1=st[qt][:, 3:4], scalar2=None, op0=ALU.mult),
                     reads=[Rq, R_osb[g2][qt]], writes=[R_osb[g2][qt]])
                S.op("dve", lambda e: e.scalar_tensor_tensor(out=of[qt][:, :], in0=o1[:, 0:128], scalar=st[qt][:, 2:3], in1=o2[:, 0:128], op0=ALU.mult, op1=ALU.subtract),
                     reads=[Rq, R_osb[g2][qt]], writes=[Rq])
                c0 = QB * 512 + qt * 128
                def dst(c0=c0, h=h):
                    S.op("act", lambda e: e.activation(out=mixT[:, h, c0:c0 + 128], in_=ptp[:, :128], func=AF.Copy), reads=[R_ptp], writes=[R_mix[h]])
                subln_store(of[qt][:, :], 128, dst, (junk[qt], st[qt], ob[qt], Rq))

        do_samples = L["stage"] >= 8
        sample_gen = None
        state = {"gen": None}

        def sample_step():
            if state["gen"] is not None:
                try:
                    next(state["gen"])
                except StopIteration:
                    state["gen"] = None
        if do_samples:
            ptr_ = sb("s_ptr", [128, NSC, 16], I32, pa)
            pof = sb("s_pof", [128, NSC * 16], F32, pa)
            pff = sb("s_pff", [128, NSC * 16], F32, pa)
            idx = sb("s_idx", [128, NSC, 16], I32, pa)
            R_idx = Res("idx")
            selt = sb("s_sel", [NSC, NSC, 128], F32, pa)
            m0 = sb("s_m0", [128, 1], F32, pa)
            S.dma("sp", lambda e: e.dma_start(out=ptr_[:], in_=L["ptrep"].rearrange("s p j -> p s j")), "s_p", writes=[R_idx])
            S.dma("sp", lambda e: e.dma_start(out=pof[:], in_=L["poff"]), "s_p", writes=[R_idx])
            S.dma("sp", lambda e: e.dma_start(out=selt[:], in_=L["selq"]), "s_p", writes=[R_idx])
            S.dma("sp", lambda e: e.dma_start(out=m0[:], in_=L["mask0"]), "s_p", writes=[R_idx])
            f2 = lambda t: t[:].rearrange("p a b -> p (a b)")
            S.op("dve", lambda e: e.tensor_copy(out=pff[:, :], in_=f2(ptr_)), reads=[R_idx], writes=[R_idx])
            S.op("dve", lambda e: e.scalar_tensor_tensor(out=pff[:, :], in0=pff[:, :], scalar=32.0, in1=pof[:, :], op0=ALU.mult, op1=ALU.add), reads=[R_idx], writes=[R_idx])
            S.op("dve", lambda e: e.tensor_copy(out=f2(idx), in_=pff[:, :]), reads=[R_idx], writes=[R_idx])
            NKB = 3
            kt_ = [sb("s_kt%d" % j, [128, 4, 512], F32, pa) for j in range(NKB)]
            R_kt = [Res("s_kt%d" % j) for j in range(NKB)]
            NVB = 3
            vb_ = [sb("s_vb%d" % j, [128, 4, 512], BF16, pa) for j in range(NVB)]
            R_vb = [Res("s_vb%d" % j) for j in range(NVB)]
            prod = [sb("s_prod%d" % j, [128, 4, 512], F32, pa) for j in range(2)]
            R_prod = [Res("s_prod%d" % j) for j in range(2)]
            qbc = [sb("s_qbc%d" % j, [128, 4, 512], F32, pa) for j in range(2)]
            R_qbc = [Res("qbc%d" % j) for j in range(2)]
            R_qscr = Res("qscr")
            sc = sb("s_sc", [128, 64, 8], F32, pa)
            R_sc = Res("s_sc")
            pdz = sb("s_pdz", [128, 64, 4, NSC], BF16, pa)
            R_pdz = Res("pdz")
            zz = sb("s_zz", [128, 6, 8], F32, pa)
            R_zz = Res("zz")
            sself = sb("s_self", [NSC, 4, 8], F32, pa)
            prs = sb("s_prs", [NSC, 512], F32, pa)
            R_self = Res("sself")
            coef = sb("s_coef", [NSC, 2, 8], F32, pa)
            R_coef = Res("coef")
            osm = sb("s_osm", [NSC, 4, 128], F32, pa)
            sjunk = sb("s_junk", [NSC, 128], BF16, pa)
            sst = sb("s_st", [NSC, 4], F32, pa)
            sob = sb("s_ob", [NSC, 128], BF16, pa)
            R_sscr = Res("s_scr")
            pz = ps("s_pz", [128, 16], F32, pa)
            R_pz = Res("pz")
            pso = ps("s_pso", [NSC, 512], F32, pa)
            R_pso = Res("pso")
            S.op("pool", lambda e: e.memset(pdz[:], 0.0), writes=[R_pdz])
            S.op("dve", lambda e: e.memset(coef[:], 0.0), writes=[R_coef])
            S.dma("sp", lambda e: e.dma_start(out=L["qscr"], in_=L["qsf"][:, :]), "s_q", reads=[L["R_qsf"]], writes=[R_qscr])
            S.op("dve", lambda e: e.tensor_tensor(out=prs[:, :], in0=L["qsf"][:, :], in1=L["ksf"][:, :], op=ALU.mult), reads=[L["R_qsf"], L["R_ksf"]], writes=[R_self])
            S.op("dve", lambda e: e.reduce_sum(out=sself[:, 0, :], in_=prs[:, :].rearrange("p (a d) -> p a d", d=64), axis=AX.X), reads=[R_self], writes=[R_self])
            S.op("act", lambda e: e.activation(out=sself[:, 1, :], in_=sself[:, 0, :], func=AF.Exp, scale=0.125), reads=[R_self], writes=[R_self])

            def sample_k(s_):
                def gather(jj):
                    j = (s_ * 16 + jj) % NKB
                    S.dma("pool", lambda e: e.indirect_dma_start(out=kt_[j][:].rearrange("p a b -> p (a b)"), out_offset=None, in_=L["cache_k"],
                                                                 in_offset=bass.IndirectOffsetOnAxis(ap=idx[:, s_, jj:jj + 1], axis=0)),
                          "s_kt%d" % j, reads=[R_idx], writes=[R_kt[j]])
                qb_ = qbc[s_ % 2]
                S.dma("sp", lambda e: e.dma_start(out=qb_[:], in_=AP(L["qscr"].tensor, s_ * 512, [[0, 128], [0, 4], [1, 512]])),
                      "s_qb%d" % (s_ % 2), reads=[R_qscr], writes=[R_qbc[s_ % 2]])
                gather(0)
                gather(1)
                for jj in range(16):
                    j = (s_ * 16 + jj) % NKB
                    if jj + 2 < 16:
                        gather(jj + 2)
                    S.op("dve", lambda e: e.tensor_tensor(out=prod[jj % 2][:], in0=kt_[j][:], in1=qb_[:], op=ALU.mult),
                         reads=[R_kt[j], R_qbc[s_ % 2]], writes=[R_prod[jj % 2]])
                    S.op("dve", lambda e: e.reduce_sum(out=sc[:, jj * 4:(jj + 1) * 4, :], in_=prod[jj % 2][:].rearrange("p t (a d) -> p t a d", d=64), axis=AX.X),
                         reads=[R_prod[jj % 2]], writes=[R_sc])
                    yield
                S.op("act", lambda e: e.activation(out=sc[:], in_=sc[:], func=AF.Exp, scale=0.125), reads=[R_sc], writes=[R_sc])
                S.op("dve", lambda e: e.reduce_sum(out=zz[:, 0, :], in_=sc[:].rearrange("p t a -> p a t"), axis=AX.X), reads=[R_sc], writes=[R_zz])
                S.op("pe", lambda e: e.matmul(pz[:, 0:8], lhsT=selt[:, s_, :], rhs=sself[:, 1, :], start=True, stop=True), reads=[R_idx, R_self], writes=[R_pz])
                S.op("dve", lambda e: e.scalar_tensor_tensor(out=zz[:, 0, :], in0=pz[:, 0:8], scalar=m0[:, 0:1], in1=zz[:, 0, :], op0=ALU.mult, op1=ALU.add),
                     reads=[R_pz, R_idx, R_zz], writes=[R_zz])
                S.op("pe", lambda e: e.matmul(pz[:, 8:16], lhsT=L["ones_f"][:, :], rhs=zz[:, 0, :], start=True, stop=True), reads=[R_zz, R_id], writes=[R_pz])
                S.op("dve", lambda e: e.reciprocal(out=zz[:, 1, :], in_=pz[:, 8:16]), reads=[R_pz], writes=[R_zz])
                z4 = lambda r: zz[:, r, :].rearrange("p (h c) -> p h c", c=2)
                S.op("dve", lambda e: e.tensor_scalar(out=zz[:, 2, :], in0=zz[:, 1, :], scalar1=lam[:, 0:1], scalar2=None, op0=ALU.mult), reads=[R_zz, R_par], writes=[R_zz])
                for r_ in range(2):
                    S.op("dve", lambda e: e.scalar_tensor_tensor(out=coef[:, r_, :], in0=zz[0:NSC, 1 + r_, :], scalar=idf[0:NSC, s_:s_ + 1], in1=coef[:, r_, :],
                                                                 op0=ALU.mult, op1=ALU.add), reads=[R_zz, R_id, R_coef], writes=[R_coef])
                sc4 = sc[:].rearrange("p t (h c) -> p t h c", c=2)
                S.op("dve", lambda e: e.tensor_tensor(out=sc4[:, :, :, 0], in0=sc4[:, :, :, 0], in1=z4(1)[:, :, 0].unsqueeze(1).to_broadcast([128, 64, 4]), op=ALU.mult), reads=[R_sc, R_zz], writes=[R_sc])
                S.op("dve", lambda e: e.tensor_tensor(out=sc4[:, :, :, 1], in0=sc4[:, :, :, 1], in1=z4(2)[:, :, 1].unsqueeze(1).to_broadcast([128, 64, 4]), op=ALU.mult), reads=[R_sc, R_zz], writes=[R_sc])
                if s_ > 0:
                    S.op("pool", lambda e: e.memset(pdz[:, :, :, s_ - 1], 0.0), writes=[R_pdz])
                S.op("dve", lambda e: e.tensor_tensor(out=pdz[:, :, :, s_], in0=sc4[:, :, :, 0], in1=sc4[:, :, :, 1], op=ALU.subtract), reads=[R_sc], writes=[R_pdz])

            def sample_v(s_):
                def gather(jj):
                    jv = jj % NVB
                    S.dma("pool", lambda e: e.indirect_dma_start(out=vb_[jv][:].rearrange("p a b -> p (a b)"), out_offset=None, in_=L["cache_v"],
                                                                 in_offset=bass.IndirectOffsetOnAxis(ap=idx[:, s_, jj:jj + 1], axis=0)),
                          "s_vb%d" % jv, reads=[R_idx], writes=[R_vb[jv]])
                gather(0)
                gather(1)
                for jj in range(16):
                    jv = jj % NVB
                    if jj + 2 < 16:
                        gather(jj + 2)
                    for t4 in range(4):
                        for h in range(4):
                            first = (s_ == 0 and jj == 0 and t4 == 0 and h == 0)
                            last = (s_ == NSC - 1 and jj == 15 and t4 == 3)
                            S.op("pe", lambda e: e.matmul(pso[:, h * 128:(h + 1) * 128], lhsT=pdz[:, jj * 4 + t4, h, :], rhs=vb_[jv][:, t4, h * 128:(h + 1) * 128],
                                                          start=first, stop=last, skip_group_check=True),
                                 reads=[R_pdz, R_vb[jv]], writes=[R_pso])
                    yield

            def sample_all():
                for s_ in range(NSC):
                    yield from sample_k(s_)
                    yield from sample_v(s_)
            sample_gen = sample_all()


        if do_samples:
            state["gen"] = sample_gen
        for gi in range(16):
            prompt_group(gi)
        while state["gen"] is not None:
            sample_step()

        if do_samples:
            e4 = sself[:, 1, :].rearrange("p (h c) -> p h c", c=2)
            cA = coef[:, 0, :].rearrange("p (h c) -> p h c", c=2)
            cB = coef[:, 1, :].rearrange("p (h c) -> p h c", c=2)
            pd4 = sself[:, 2, :].rearrange("p (h c) -> p h c", c=2)
            S.op("dve", lambda e: e.tensor_tensor(out=pd4[:, :, 0], in0=e4[:, :, 0], in1=cA[:, :, 0], op=ALU.mult), reads=[R_self, R_coef], writes=[R_self])
            S.op("dve", lambda e: e.tensor_tensor(out=pd4[:, :, 1], in0=e4[:, :, 1], in1=cB[:, :, 1], op=ALU.mult), reads=[R_self, R_coef], writes=[R_self])
            S.op("dve", lambda e: e.tensor_tensor(out=pd4[:, :, 0], in0=pd4[:, :, 0], in1=pd4[:, :, 1], op=ALU.subtract), reads=[R_self], writes=[R_self])
            for h in range(4):
                S.op("dve", lambda e: e.scalar_tensor_tensor(out=osm[:, h, :], in0=L["vS"][:, h * 128:(h + 1) * 128], scalar=pd4[:, h, 0:1], in1=pso[:, h * 128:(h + 1) * 128],
                                                             op0=ALU.mult, op1=ALU.add), reads=[L["R_vS"], R_self, R_pso], writes=[R_sscr])
                def dst(h=h):
                    S.op("act", lambda e: e.activation(out=mixT[:, h, SEQ:NCOL], in_=ptp[:, :NSC], func=AF.Copy), reads=[R_ptp], writes=[R_mix[h]])
                subln_store(osm[:, h, :], NSC, dst, (sjunk, sst, sob, R_sscr))
        S.barrier()


def ffn_phase(nc, S, sb, ps, L):
    mixT, R_mix, idb, idf, R_id = L["mixT"], L["R_mix"], L["idb"], L["idf"], L["R_id"]
    xp, xs, xscr = L["xp"], L["xs"], L["xscr"]
    with ExitStack() as pf:
        mT = sb("mT", [128, NFC, NCOL], BF16, pf)
        R_mT = [Res("mT%d" % b) for b in range(5)]
        rs2 = sb("rs2", [128, 17, 4], F32, pf)
        R_rs2 = [Res("rs2_%d" % i) for i in range(17)]
        small = sb("fsmall", [128, 8 + 3 * NFC + NFC], F32, pf)
        R_small = Res("fsmall")
        S.dma("sp", lambda e: e.dma_start(out=small[:, 8:8 + 3 * NFC], in_=L["convw"].rearrange("p a b -> p (a b)")), "f_s", writes=[R_small])
        S.dma("sp", lambda e: e.dma_start(out=small[:, 8 + 3 * NFC:], in_=L["convb"]), "f_s", writes=[R_small])
        cw = lambda j, fc: small[:, 8 + j * NFC + fc:8 + j * NFC + fc + 1]
        cb = lambda fc: small[:, 8 + 3 * NFC + fc:8 + 3 * NFC + fc + 1]
        asp = sb("asp", [8, DFF], F32, pf)
        R_asp = Res("asp")
        with ExitStack() as pu:
            hn2T = sb("hn2T", [128, 8, NCOL], BF16, pu)
            R_hn2 = [Res("hn2_%d" % i) for i in range(17)]
            with ExitStack() as po:
                wob = sb("wob", [128, 8, D], BF16, po)
                R_wob = Res("wob")
                for kc in range(8):
                    S.dma("pool", lambda e: e.dma_start(out=wob[:, kc, :], in_=L["w_o"][kc * 128:(kc + 1) * 128, :]), "wob", writes=[R_wob])
                gfx = sb("gfx", [128, D], F32, po)
                R_gfx = Res("gfx")
                S.dma("sp", lambda e: e.dma_start(out=gfx[:], in_=AP(L["nffn"].tensor, 0, [[0, 128], [1, D]])), "f_g2", writes=[R_gfx])
                xt = [sb("oxt%d" % j, [128, D], F32, po) for j in range(2)]
                R_xt = [Res("oxt%d" % j) for j in range(2)]
                xb = [sb("oxb0", [128, D], BF16, po)] * 2
                R_xb = [Res("oxb0")] * 2
                junk = xb[0]
                R_junk = R_xb[0]
                px = [ps("opx%d" % j, [128, 2, 512], F32, po) for j in range(2)]
                R_px = [Res("opx%d" % j) for j in range(2)]
                ptr = [ps("optr%d" % j, [128, 8, 128], BF16, po) for j in range(2)]
                R_ptr = [Res("optr%d" % j) for j in range(2)]
                for i, (c0, n) in enumerate(TILES):
                    j = i % 2
                    src = xp[c0:c0 + n, :] if i < 16 else xs
                    S.dma("sp", lambda e: e.dma_start(out=xt[j][:n, :], in_=src), "oxt%d" % j, writes=[R_xt[j]])
                    for hf in range(2):
                        for kc in range(8):
                            S.op("pe", lambda e: e.matmul(px[j][:n, hf, :], lhsT=mixT[:, kc, c0:c0 + n], rhs=wob[:, kc, hf * 512:(hf + 1) * 512],
                                                          start=(kc == 0), stop=(kc == 7)), reads=R_mix + [R_wob], writes=[R_px[j]])
                    S.op("dve", lambda e: e.tensor_tensor(out=xt[j][:n, :], in0=xt[j][:n, :], in1=px[j][:n, :, :].rearrange("p a b -> p (a b)"), op=ALU.add),
                         reads=[R_xt[j], R_px[j]], writes=[R_xt[j]])
                    S.dma("sp", lambda e: e.dma_start(out=xscr[c0:c0 + n, :], in_=xt[j][:n, :]), "oxt%d" % j, reads=[R_xt[j]])
                    S.op("act", lambda e: e.activation(out=junk[:n, :], in_=xt[j][:n, :], func=AF.Square, accum_out=rs2[:n, i, 0:1]),
                         reads=[R_xt[j]], writes=[R_junk, R_rs2[i]])
                    S.op("dve", lambda e: e.tensor_scalar(out=rs2[:n, i, 1:2], in0=rs2[:n, i, 0:1], scalar1=1.0 / D, scalar2=EPS, op0=ALU.mult, op1=ALU.add),
                         reads=[R_rs2[i]], writes=[R_rs2[i]])
                    S.op("act", lambda e: e.sqrt(out=rs2[:n, i, 2:3], in_=rs2[:n, i, 1:2]), reads=[R_rs2[i]], writes=[R_rs2[i]])
                    S.op("dve", lambda e: e.reciprocal(out=rs2[:n, i, 3:4], in_=rs2[:n, i, 2:3]), reads=[R_rs2[i]], writes=[R_rs2[i]])
                    S.op("dve", lambda e: e.scalar_tensor_tensor(out=xb[j][:n, :], in0=xt[j][:n, :], scalar=rs2[:n, i, 3:4], in1=gfx[:n, :],
                                                                 op0=ALU.mult, op1=ALU.mult),
                         reads=[R_xt[j], R_rs2[i], R_gfx], writes=[R_xb[j]])
                    for kc in range(8):
                        S.op("pe", lambda e: e.transpose(out=ptr[j][:, kc, :n], in_=xb[j][:n, kc * 128:(kc + 1) * 128], identity=idb[:n, :n]),
                             reads=[R_xb[j], R_id], writes=[R_ptr[j]])
                    S.op("dve", lambda e: e.tensor_copy(out=hn2T[:, :, c0:c0 + n], in_=ptr[j][:, :, :n]), reads=[R_ptr[j]], writes=[R_hn2[i]])
                S.barrier()
            with ExitStack() as pv:
                wab = [sb("wab%d" % j, [128, 8, 128], BF16, pv) for j in range(2)]
                wgb_ = [sb("wgb_%d" % j, [128, 8, 128], BF16, pv) for j in range(2)]
                R_wab = [Res("wab%d" % j) for j in range(2)]
                R_wgb = [Res("wgb_%d" % j) for j in range(2)]
                pa_ = [ps("fpa%d" % j, [128, 512], F32, pv) for j in range(2)]
                pg_ = [ps("fpg%d" % j, [128, 512], F32, pv) for j in range(2)]
                R_pa = [Res("fpa%d" % j) for j in range(2)]
                R_pg = [Res("fpg%d" % j) for j in range(2)]
                psp = ps("fpsp", [8, 128], F32, pv)
                R_psp = Res("fpsp")
                pbt = ps("fpbt", [128, NFC, 8], F32, pv)
                R_pbt = Res("fpbt")
                abuf = sb("abuf", [128, 2 + 512], F32, pv)
                cbuf = sb("cbuf", [128, 512], F32, pv)
                R_ab, R_cbuf = Res("abuf"), Res("cbuf")
                scv = sb("scv", [8, DFF], F32, pv)
                bufT = sb("bufT", [128, NFC, 8], F32, pv)
                R_bufT = Res("bufT")
                S.dma("sp", lambda e: e.dma_start(out=scv[:, :], in_=L["sconv"]), "f_c", writes=[R_bufT])
                for fc in range(NFC):
                    S.op("pe", lambda e: e.transpose(out=pbt[:, fc, :], in_=scv[:, fc * 128:(fc + 1) * 128], identity=idf[:8, :8]),
                         reads=[R_bufT, R_id], writes=[R_pbt])
                S.op("dve", lambda e: e.tensor_copy(out=bufT[:], in_=pbt[:]), reads=[R_pbt], writes=[R_bufT])
                for fc in range(NFC):
                    jw = fc % 2
                    for half, (wb, R_wb) in enumerate(((wab, R_wab), (wgb_, R_wgb))):
                        col0 = half * DFF + fc * 128
                        S.dma("pool", lambda e: e.dma_start(out=wb[jw][:], in_=L["w_up"][:, col0:col0 + 128].rearrange("(k p) c -> p k c", p=128)),
                              "wup%d%d" % (half, jw), writes=[R_wb[jw]])
                    for kc in range(8):
                        S.op("pe", lambda e: e.matmul(psp[0:6, :], lhsT=hn2T[:, kc, SEQ - 2:NCOL], rhs=wab[jw][:, kc, :], start=(kc == 0), stop=(kc == 7)),
                             reads=R_hn2[15:17] + [R_wab[jw]], writes=[R_psp])
                    S.op("act", lambda e: e.activation(out=asp[0:6, fc * 128:(fc + 1) * 128], in_=psp[0:6, :], func=AF.Copy), reads=[R_psp], writes=[R_asp])
                    for b, (c0, nb) in enumerate(BLOCKS):
                        j = (fc * 5 + b) % 2
                        for kc in range(8):
                            S.op("pe", lambda e: e.matmul(pa_[j][:, :nb], lhsT=wab[jw][:, kc, :], rhs=hn2T[:, kc, c0:c0 + nb], start=(kc == 0), stop=(kc == 7)),
                                 reads=R_hn2 + [R_wab[jw]], writes=[R_pa[j]])
                        for kc in range(8):
                            S.op("pe", lambda e: e.matmul(pg_[j][:, :nb], lhsT=wgb_[jw][:, kc, :], rhs=hn2T[:, kc, c0:c0 + nb], start=(kc == 0), stop=(kc == 7)),
                                 reads=R_hn2 + [R_wgb[jw]], writes=[R_pg[j]])
                        if b < 4:
                            if b == 0:
                                S.op("dve", lambda e: e.memset(abuf[:, 0:2], 0.0), reads=[R_ab], writes=[R_ab])
                            else:
                                S.op("dve", lambda e: e.tensor_copy(out=abuf[:, 0:2], in_=abuf[:, 512:514]), reads=[R_ab], writes=[R_ab])
                            S.op("act", lambda e: e.activation(out=abuf[:, 2:514], in_=pa_[j][:, :], func=AF.Copy), reads=[R_pa[j], R_ab], writes=[R_ab])
                            S.op("act", lambda e: e.activation(out=cbuf[:, :], in_=pa_[j][:, :], func=AF.Identity, scale=cw(2, fc), bias=cb(fc)),
                                 reads=[R_pa[j], R_small], writes=[R_cbuf])
                            S.op("dve", lambda e: e.scalar_tensor_tensor(out=cbuf[:, :], in0=abuf[:, 1:513], scalar=cw(1, fc), in1=cbuf[:, :], op0=ALU.mult, op1=ALU.add),
                                 reads=[R_ab, R_small, R_cbuf], writes=[R_cbuf])
                            S.op("dve", lambda e: e.scalar_tensor_tensor(out=cbuf[:, :], in0=abuf[:, 0:512], scalar=cw(0, fc), in1=cbuf[:, :], op0=ALU.mult, op1=ALU.add),
                                 reads=[R_ab, R_small, R_cbuf], writes=[R_cbuf])
                        else:
                            b3 = bufT[:, fc, :].rearrange("p (s j) -> p s j", j=2)
                            S.op("act", lambda e: e.activation(out=cbuf[:, :nb], in_=pa_[j][:, :nb], func=AF.Identity, scale=cw(2, fc), bias=cb(fc)),
                                 reads=[R_pa[j], R_small], writes=[R_cbuf])
                            S.op("dve", lambda e: e.scalar_tensor_tensor(out=cbuf[:, :nb], in0=b3[:, :, 1], scalar=cw(1, fc), in1=cbuf[:, :nb], op0=ALU.mult, op1=ALU.add),
                                 reads=[R_bufT, R_small, R_cbuf], writes=[R_cbuf])
                            S.op("dve", lambda e: e.scalar_tensor_tensor(out=cbuf[:, :nb], in0=b3[:, :, 0], scalar=cw(0, fc), in1=cbuf[:, :nb], op0=ALU.mult, op1=ALU.add),
                                 reads=[R_bufT, R_small, R_cbuf], writes=[R_cbuf])
                        S.op("act", lambda e: e.activation(out=cbuf[:, :nb], in_=cbuf[:, :nb], func=AF.Silu), reads=[R_cbuf], writes=[R_cbuf])
                        S.op("dve", lambda e: e.tensor_tensor(out=mT[:, fc, c0:c0 + nb], in0=cbuf[:, :nb], in1=pg_[j][:, :nb], op=ALU.mult),
                             reads=[R_cbuf, R_pg[j]], writes=[R_mT[b]])
                S.dma("sp", lambda e: e.dma_start(out=L["o_cp"], in_=asp[0:2, :]), "f_o", reads=[R_asp])
                S.dma("sp", lambda e: e.dma_start(out=L["o_cs"][:, 1, :], in_=asp[2:6, :]), "f_o", reads=[R_asp])
                S.dma("sp", lambda e: e.dma_start(out=L["o_cs"][:, 0, :], in_=L["sconv1"]), "f_o2")
                S.barrier()
        with ExitStack() as pd_:
            wdb = sb("wdb", [128, NFC, D], BF16, pd_)
            R_wdb = Res("wdb")
            R_wdh = [Res("wdb0"), Res("wdb1")]
            for hf in range(2):
                for fc in range(NFC):
                    S.dma("pool", lambda e: e.dma_start(out=wdb[:, fc, hf * 512:(hf + 1) * 512], in_=L["w_down"][fc * 128:(fc + 1) * 128, hf * 512:(hf + 1) * 512]),
                          "wdb%d" % hf, writes=[R_wdh[hf]])
            gfn = sb("gfn", [128, D], F32, pd_)
            R_gfn = Res("gfn")
            S.dma("sp", lambda e: e.dma_start(out=gfn[:], in_=AP(L["fnorm"].tensor, 0, [[0, 128], [1, D]])), "f_g", writes=[R_gfn])
            xt = [sb("dxt%d" % j, [128, D], F32, pd_) for j in range(2)]
            R_xt = [Res("dxt%d" % j) for j in range(2)]
            junk = sb("djunk", [128, D], BF16, pd_)
            R_junk = Res("djunk")
            px = [ps("dpx%d" % j, [128, 2, 512], F32, pd_) for j in range(2)]
            R_px = [Res("dpx%d" % j) for j in range(2)]
            for i, (c0, n) in enumerate(TILES):
                j = i % 2
                S.dma("sp", lambda e: e.dma_start(out=xt[j][:n, :], in_=xscr[c0:c0 + n, :]), "dxt%d" % j, writes=[R_xt[j]])
                for hf in range(2):
                    for fc in range(NFC):
                        S.op("pe", lambda e: e.matmul(px[j][:n, hf, :], lhsT=mT[:, fc, c0:c0 + n], rhs=wdb[:, fc, hf * 512:(hf + 1) * 512],
                                                      start=(fc == 0), stop=(fc == NFC - 1)), reads=R_mT + [R_wdh[hf]], writes=[R_px[j]])
                S.op("dve", lambda e: e.tensor_tensor(out=xt[j][:n, :], in0=xt[j][:n, :], in1=px[j][:n, :, :].rearrange("p a b -> p (a b)"), op=ALU.add),
                     reads=[R_xt[j], R_px[j]], writes=[R_xt[j]])
                S.op("act", lambda e: e.activation(out=junk[:n, :], in_=xt[j][:n, :], func=AF.Square, accum_out=rs2[:n, i, 0:1]),
                     reads=[R_xt[j]], writes=[R_junk, R_rs2[i]])
                S.op("dve", lambda e: e.tensor_scalar(out=rs2[:n, i, 1:2], in0=rs2[:n, i, 0:1], scalar1=1.0 / D, scalar2=EPS, op0=ALU.mult, op1=ALU.add),
                     reads=[R_rs2[i]], writes=[R_rs2[i]])
                S.op("act", lambda e: e.sqrt(out=rs2[:n, i, 2:3], in_=rs2[:n, i, 1:2]), reads=[R_rs2[i]], writes=[R_rs2[i]])
                S.op("dve", lambda e: e.reciprocal(out=rs2[:n, i, 3:4], in_=rs2[:n, i, 2:3]), reads=[R_rs2[i]], writes=[R_rs2[i]])
                S.op("dve", lambda e: e.scalar_tensor_tensor(out=xt[j][:n, :], in0=xt[j][:n, :], scalar=rs2[:n, i, 3:4], in1=gfn[:n, :], op0=ALU.mult, op1=ALU.mult),
                     reads=[R_xt[j], R_rs2[i], R_gfn], writes=[R_xt[j]])
                dstd = L["o_yp"][c0:c0 + n, :] if i < 16 else L["o_ys"]
                S.dma("sp", lambda e: e.dma_start(out=dstd, in_=xt[j][:n, :]), "dxt%d" % j, reads=[R_xt[j]])
            S.barrier()


def build_program(n_pool, stage=99):
    nc = bass.Bass("TRN2", target_bir_lowering=False)
    din = lambda name, shape, dt=F32: nc.dram_tensor(name, shape, dt, kind="ExternalInput").ap()
    dout = lambda name, shape, dt=F32: nc.dram_tensor(name, shape, dt, kind="ExternalOutput").ap()

    xp = din("xp", [SEQ, D])
    xs = din("xs", [NSC, D])
    w_in = din("w_in", [D, DPROJ])
    nmix = din("nmix", [1, D])
    ropec = din("ropec", [NCOL, 8])
    ropes = din("ropes", [NCOL, 8])
    ident = din("ident", [128, 128])

    bcol_re = din("bcol_re", [128, 16, 128])
    bcol_im = din("bcol_im", [128, 16, 128])
    cpad_re = din("cpad_re", [128, 16, 128])
    cpad_im = din("cpad_im", [128, 16, 128])
    acol_re = din("acol_re", [128, 16])
    acol_im = din("acol_im", [128, 16])
    ldt_col = din("ldt_col", [128, 16])
    dcol = din("dcol", [128, 4])
    h0c_re = din("h0c_re", [128, 16, NSC])
    h0c_im = din("h0c_im", [128, 16, NSC])
    w_glu = din("w_glu", [512, 512])
    trow1 = din("trow1", [128, 512])
    o_hre = dout("o_hre", [16, 128])
    o_him = dout("o_him", [16, 128])
    o_hres = dout("o_hres", [NSC, 16, 128])
    o_hims = dout("o_hims", [NSC, 16, 128])

    lamv = din("lamv", [1, 4 * 64])
    gsub = din("gsub", [1, 128])
    trimask = din("trimask", [128, 128])
    cache_k = din("cache_k", [n_pool * 32, 2048])
    cache_v = din("cache_v", [n_pool * 32, 2048])
    ptrep = din("ptrep", [NSC, 128, 16], I32)
    poff = din("poff", [128, NSC * 16])
    selq = din("selq", [NSC, NSC, 128])
    mask0 = din("mask0", [128, 1])
    w_o = din("w_o", [D, D])
    nffn = din("nffn", [1, D])
    w_up = din("w_up", [D, 2 * DFF])
    convw = din("convw", [128, 3, NFC])
    convb = din("convb", [128, NFC])
    w_down = din("w_down", [DFF, D])
    fnorm = din("fnorm", [1, D])
    sconv = din("sconv", [NSC * 2, DFF])
    sconv1 = din("sconv1", [NSC, DFF])
    xscr = nc.dram_tensor("xscr", [NCOL, D], F32, kind="Internal").ap()
    qscr = nc.dram_tensor("qscr", [1, NSC * 512], F32, kind="Internal").ap()
    o_yp = dout("o_yp", [SEQ, D])
    o_ys = dout("o_ys", [NSC, D])
    o_cp = dout("o_cp", [2, DFF])
    o_cs = dout("o_cs", [NSC, 2, DFF])
    o_kp = dout("o_kp", [SEQ, 512])
    o_vp = dout("o_vp", [SEQ, 512])
    o_ks = dout("o_ks", [NSC, 512])
    o_vs = dout("o_vs", [NSC, 512])

    with ExitStack() as es:
        S = Sched(nc, es)

        def sb(name, shape, dt, stack=es):
            return stack.enter_context(nc.sbuf_tensor(name, shape, dt))

        def ps(name, shape, dt, stack=es):
            return stack.enter_context(nc.psum_tensor(name, shape, dt))

        idf = sb("idf", [128, 128], F32)
        idb = sb("idb", [128, 128], BF16)
        R_id = Res("id")
        S.dma("sp", lambda e: e.dma_start(out=idf[:], in_=ident), "c_id", writes=[R_id])
        S.op("dve", lambda e: e.tensor_copy(out=idb[:], in_=idf[:]), reads=[R_id], writes=[R_id])

        ones_f = sb("ones_f", [128, 128], F32)
        S.op("pool", lambda e: e.memset(ones_f[:], 1.0), writes=[R_id])
        ropc = sb("ropc", [128, 17, 8], F32)
        rops = sb("rops", [128, 17, 8], F32)
        R_rope = Res("rope")
        for tbl, src, key in ((ropc, ropec, "c_rc"), (rops, ropes, "c_rs")):
            S.dma("sp", lambda e: e.dma_start(out=tbl[:, 0:16, :],
                                              in_=src[0:SEQ, :].rearrange("(i p) e -> p i e", p=128)),
                  key, writes=[R_rope])
            S.dma("sp", lambda e: e.dma_start(out=tbl[0:NSC, 16, :], in_=src[SEQ:NCOL, :]), key, writes=[R_rope])


        mixT = sb("mixT", [128, 8, NCOL], BF16)
        R_mix = [Res("mix%d" % c) for c in range(8)]
        vS = sb("vS", [NSC, 512], BF16)
        R_vS = Res("vS")
        ksf = sb("ksf", [NSC, 512], F32)
        qsf = sb("qsf", [NSC, 512], F32)
        R_ksf, R_qsf = Res("ksf"), Res("qsf")
        pA = es.enter_context(ExitStack())
        qT = sb("qT", [128, 4, NCOL], BF16, pA)
        kT = sb("kT", [128, 4, NCOL], BF16, pA)
        R_qT = [Res("qT%d" % i) for i in range(17)]
        R_kT = [Res("kT%d" % i) for i in range(17)]
        vA = sb("vA", [128, 16, 4, 129], BF16, pA)
        R_vA = [Res("vA%d" % i) for i in range(16)]
        uT = sb("uT", [128, 4, NCOL], BF16, pA)
        R_uT = [Res("uT%d" % i) for i in range(5)]
        rr = sb("rr", [128, 17, 4], F32, pA)
        R_rr = [Res("rr%d" % i) for i in range(17)]

        pw = pA.enter_context(ExitStack())
        hnT = sb("hnT", [128, 8, NCOL], BF16, pw)
        R_hnT = [Res("hnT%d" % i) for i in range(17)]
        winb = sb("winb", [128, 8, DPROJ], BF16, pw)
        R_winb = [Res("winb")] * 8
        with ExitStack() as p12:
            xt = [sb("xt%d" % j, [128, D], F32, p12) for j in range(2)]
            R_xt = [Res("xt%d" % j) for j in range(2)]
            xb = [sb("xb%d" % j, [128, D], BF16, p12) for j in range(2)]
            R_xb = [Res("xb%d" % j) for j in range(2)]
            junk = sb("junk", [128, D], BF16, p12)
            R_junk = Res("junk")
            gmx = sb("gmx", [128, D], F32, p12)
            R_gmx = Res("gmx")
            S.dma("sp", lambda e: e.dma_start(out=gmx[:], in_=AP(nmix.tensor, 0, [[0, 128], [1, D]])), "c_nm", writes=[R_gmx])
            ptr = [ps("ptr%d" % j, [128, 8, 128], BF16, p12) for j in range(2)]
            R_ptr = [Res("ptr%d" % j) for j in range(2)]

            for kc in range(8):
                S.dma("pool", lambda e: e.dma_start(out=winb[:, kc, :], in_=w_in[kc * 128:(kc + 1) * 128, :]),
                      "winb", writes=[R_winb[kc]])

            for i, (c0, n) in enumerate(TILES):
                j = i % 2
                src = xp[c0:c0 + n, :] if i < 16 else xs
                S.dma("sp", lambda e: e.dma_start(out=xt[j][:n, :], in_=src), "xt%d" % j, writes=[R_xt[j]])
                S.op("act", lambda e: e.activation(out=junk[:n, :], in_=xt[j][:n, :], func=AF.Square,
                                                   accum_out=rr[:n, i, 0:1]),
                     reads=[R_xt[j]], writes=[R_junk, R_rr[i]])
                S.op("dve", lambda e: e.tensor_scalar(out=rr[:n, i, 1:2], in0=rr[:n, i, 0:1], scalar1=1.0 / D,
                                                      scalar2=EPS, op0=ALU.mult, op1=ALU.add),
                     reads=[R_rr[i]], writes=[R_rr[i]])
                S.op("act", lambda e: e.sqrt(out=rr[:n, i, 2:3], in_=rr[:n, i, 1:2]), reads=[R_rr[i]], writes=[R_rr[i]])
                S.op("dve", lambda e: e.reciprocal(out=rr[:n, i, 3:4], in_=rr[:n, i, 2:3]),
                     reads=[R_rr[i]], writes=[R_rr[i]])
                S.op("dve", lambda e: e.scalar_tensor_tensor(out=xb[j][:n, :], in0=xt[j][:n, :], scalar=rr[:n, i, 3:4], in1=gmx[:n, :],
                                                             op0=ALU.mult, op1=ALU.mult),
                     reads=[R_xt[j], R_rr[i], R_gmx], writes=[R_xb[j]])
                for kc in range(8):
                    S.op("pe", lambda e: e.transpose(out=ptr[j][:, kc, :n], in_=xb[j][:n, kc * 128:(kc + 1) * 128],
                                                     identity=idb[:n, :n]),
                         reads=[R_xb[j], R_id], writes=[R_ptr[j]])
                S.op("dve", lambda e: e.tensor_copy(out=hnT[:, :, c0:c0 + n], in_=ptr[j][:, :, :n]),
                     reads=[R_ptr[j]], writes=[R_hnT[i]])

        S.barrier()
        if stage <= 0:
            S.finish("sp")
            return nc
        with ExitStack() as p2:
            pq = [ps("pq%d" % j, [128, 512], F32, p2) for j in range(2)]
            pk = [ps("pk%d" % j, [128, 512], F32, p2) for j in range(2)]
            pv = [ps("pv%d" % j, [128, 512], F32, p2) for j in range(2)]
            R_pq = [Res("pq%d" % j) for j in range(2)]
            R_pk = [Res("pk%d" % j) for j in range(2)]
            R_pv = [Res("pv%d" % j) for j in range(2)]
            ptq = ps("ptq", [128, 2, 4, 128], BF16, p2)
            R_ptq = [Res("ptq"), Res("ptk")]
            qf = [sb("qf%d" % j, [128, 512], F32, p2) for j in range(2)]
            kf = [sb("kf%d" % j, [128, 512], F32, p2) for j in range(2)]
            vf = [sb("vf%d" % j, [128, 512], F32, p2) for j in range(2)]
            R_qf = [Res("qf%d" % j) for j in range(2)]
            R_kf = [Res("kf%d" % j) for j in range(2)]
            R_vf = [Res("vf%d" % j) for j in range(2)]
            qb = sb("qb", [128, 512], BF16, p2)
            kb = sb("kb", [128, 512], BF16, p2)
            R_qb, R_kb = Res("qb"), Res("kb")
            rtmp = sb("rtmp", [128, 4, 8, 8], F32, p2)
            R_rtmp = Res("rtmp")

            S.op("pool", lambda e: e.memset(vA[:, :, :, 128:129], 1.0), writes=R_vA)

            def rope(buf, n, i, R_buf):
                x1 = AP(buf, 0, [[512, n], [64, 8], [1, 8]])
                x2 = AP(buf, 8, [[512, n], [64, 8], [1, 8]])
                cs = AP(ropc, i * 8, [[17 * 8, n], [0, 8], [1, 8]])
                sn = AP(rops, i * 8, [[17 * 8, n], [0, 8], [1, 8]])
                for t, (a, b) in enumerate(((x1, cs), (x2, sn), (x2, cs), (x1, sn))):
                    S.op("dve", lambda e: e.tensor_tensor(out=rtmp[:n, t], in0=a, in1=b, op=ALU.mult),
                         reads=[R_buf, R_rope], writes=[R_rtmp])
                S.op("dve", lambda e: e.tensor_tensor(out=x1, in0=rtmp[:n, 0], in1=rtmp[:n, 1], op=ALU.subtract),
                     reads=[R_rtmp], writes=[R_buf])
                S.op("dve", lambda e: e.tensor_tensor(out=x2, in0=rtmp[:n, 2], in1=rtmp[:n, 3], op=ALU.add),
                     reads=[R_rtmp], writes=[R_buf])

            for i, (c0, n) in enumerate(TILES):
                j = i % 2
                for cb, (pX, R_pX) in enumerate(((pq, R_pq), (pk, R_pk), (pv, R_pv))):
                    for kc in range(8):
                        S.op("pe", lambda e: e.matmul(pX[j][:n, :], lhsT=hnT[:, kc, c0:c0 + n],
                                                      rhs=winb[:, kc, cb * 512:(cb + 1) * 512],
                                                      start=(kc == 0), stop=(kc == 7)),
                             reads=[R_hnT[i], R_winb[kc]], writes=[R_pX[j]])
                S.op("act", lambda e: e.activation(out=vf[j][:n, :], in_=pv[j][:n, :], func=AF.Copy),
                     reads=[R_pv[j]], writes=[R_vf[j]])
                if i < 16:
                    S.dma("sp", lambda e: e.dma_start(out=o_vp[c0:c0 + n, :], in_=vf[j][:n, :]), "vf%d" % j,
                          reads=[R_vf[j]])
                    S.op("pool", lambda e: e.tensor_copy(out=vA[:, i, :, 0:128],
                                                         in_=vf[j][:, :].rearrange("p (h e) -> p h e", h=4)),
                         reads=[R_vf[j]], writes=[R_vA[i]])
                else:
                    S.dma("sp", lambda e: e.dma_start(out=o_vs, in_=vf[j][:n, :]), "vf%d" % j, reads=[R_vf[j]])
                    S.op("pool", lambda e: e.tensor_copy(out=vS[:, :], in_=vf[j][:n, :]),
                         reads=[R_vf[j]], writes=[R_vS])
                if stage < 2:
                    continue
                S.op("dve", lambda e: e.tensor_copy(out=qf[j][:n, :], in_=pq[j][:n, :]),
                     reads=[R_pq[j]], writes=[R_qf[j]])
                rope(qf[j], n, i, R_qf[j])
                S.op("act", lambda e: e.activation(out=qb[:n, :], in_=qf[j][:n, :], func=AF.Copy, scale=0.125),
                     reads=[R_qf[j]], writes=[R_qb])
                if i == 16:
                    S.op("pool", lambda e: e.tensor_copy(out=qsf[:, :], in_=qf[j][:n, :]),
                         reads=[R_qf[j]], writes=[R_qsf])
                for h in range(4):
                    S.op("pe", lambda e: e.transpose(out=ptq[:, 0, h, :n], in_=qb[:n, h * 128:(h + 1) * 128],
                                                     identity=idb[:n, :n]),
                         reads=[R_qb, R_id], writes=[R_ptq[0]])
                S.op("act", lambda e: e.activation(out=qT[:, :, c0:c0 + n], in_=ptq[:, 0, :, :n], func=AF.Copy),
                     reads=[R_ptq[0]], writes=[R_qT[i]])
                if stage < 3:
                    continue
                S.op("dve", lambda e: e.tensor_copy(out=kf[j][:n, :], in_=pk[j][:n, :]),
                     reads=[R_pk[j]], writes=[R_kf[j]])
                rope(kf[j], n, i, R_kf[j])
                if i < 16:
                    S.dma("sp", lambda e: e.dma_start(out=o_kp[c0:c0 + n, :], in_=kf[j][:n, :]), "kf%d" % j,
                          reads=[R_kf[j]])
                else:
                    S.dma("sp", lambda e: e.dma_start(out=o_ks, in_=kf[j][:n, :]), "kf%d" % j, reads=[R_kf[j]])
                    S.op("pool", lambda e: e.tensor_copy(out=ksf[:, :], in_=kf[j][:n, :]),
                         reads=[R_kf[j]], writes=[R_ksf])
                S.op("act", lambda e: e.activation(out=kb[:n, :], in_=kf[j][:n, :], func=AF.Copy),
                     reads=[R_kf[j]], writes=[R_kb])
                for h in range(4):
                    S.op("pe", lambda e: e.transpose(out=ptq[:, 1, h, :n], in_=kb[:n, h * 128:(h + 1) * 128],
                                                     identity=idb[:n, :n]),
                         reads=[R_kb, R_id], writes=[R_ptq[1]])
                S.op("act", lambda e: e.activation(out=kT[:, :, c0:c0 + n], in_=ptq[:, 1, :, :n], func=AF.Copy),
                     reads=[R_ptq[1]], writes=[R_kT[i]])

            for cu in range(4 if stage >= 4 else 0):
                for bi, (c0, nb) in enumerate(BLOCKS):
                    j = (cu * 5 + bi) % 2
                    for kc in range(8):
                        S.op("pe", lambda e: e.matmul(pq[j][:, :nb], lhsT=winb[:, kc, 1536 + cu * 128:1536 + (cu + 1) * 128],
                                                      rhs=hnT[:, kc, c0:c0 + nb], start=(kc == 0), stop=(kc == 7)),
                             reads=R_hnT + [R_winb[kc]], writes=[R_pq[j]])
                    S.op("act", lambda e: e.activation(out=uT[:, cu, c0:c0 + nb], in_=pq[j][:, :nb], func=AF.Copy),
                         reads=[R_pq[j]], writes=[R_uT[bi]])

        pw.close()
        S.barrier()
        if stage >= 5:
            ssm_phase(nc, S, sb, ps, locals())
        if stage >= 6:
            attn_phase(nc, S, sb, ps, locals())
        pA.close()
        S.barrier()
        if stage >= 7:
            ffn_phase(nc, S, sb, ps, locals())

        S.finish("sp")
        print("insts", S.n_inst, "waits", S.n_wait)
    return nc


def _rope_tables():
    half = 8
    inv = (500000.0 ** (-np.arange(0, 16, 2, dtype=np.float32) / np.float32(16))).astype(np.float32)
    pos = np.concatenate([np.arange(SEQ), np.full(NSC, PAST)]).astype(np.float32)
    ang = (pos[:, None] * inv[None, :]).astype(np.float32)
    return np.cos(ang).astype(np.float32), np.sin(ang).astype(np.float32)


def _ssm_maps(inp, c):
    f = np.float32
    a_re, a_im = inp["ssm_a_re"][0], inp["ssm_a_im"][0]
    ldt = inp["ssm_log_dt"][0]
    b_re, b_im = inp["ssm_b_re"][0], inp["ssm_b_im"][0]
    c_re, c_im = inp["ssm_c_re"][0], inp["ssm_c_im"][0]
    bcol = [np.zeros((128, 16, 128), f) for _ in range(2)]
    cpad = [np.zeros((128, 16, 128), f) for _ in range(2)]
    for g in range(32):
        pr, g2 = g // 2, g % 2
        pl = pr % 4
        r0 = pl * 32 + g2 * 16
        for k, (bb, cc) in enumerate(((b_re, c_re), (b_im, c_im))):
            bcol[k][g2 * 64:(g2 + 1) * 64, pr, r0:r0 + 16] = bb[g]
            cpad[k][g2 * 64:(g2 + 1) * 64, pr, r0:r0 + 16] = cc[g].T
    col = lambda a: np.ascontiguousarray(a.reshape(16, 128).T)
    h0 = lambda a: np.ascontiguousarray(a[0, NSC * c:NSC * (c + 1)].reshape(NSC, 16, 128).transpose(2, 1, 0))
    return dict(
        bcol_re=bcol[0], bcol_im=bcol[1], cpad_re=cpad[0], cpad_im=cpad[1],
        acol_re=col(a_re), acol_im=col(a_im), ldt_col=col(np.repeat(ldt[:, None], 64, axis=1)),
        dcol=np.ascontiguousarray(inp["ssm_d"][0].reshape(4, 128).T),
        h0c_re=h0(inp["state_ssm_re"]), h0c_im=h0(inp["state_ssm_im"]),
        w_glu=np.ascontiguousarray(inp["w_glu"][0]),
        trow1=np.ascontiguousarray(np.broadcast_to(np.arange(1, 513, dtype=f)[None, :], (128, 512))),
    )


def _attn_maps(inp, c, dev_pool=None):
    f = np.float32
    lamv = np.concatenate([inp["lambda_q1"][0], inp["lambda_k1"][0], inp["lambda_q2"][0], inp["lambda_k2"][0]]).reshape(1, 256)
    kk, qq = np.meshgrid(np.arange(128), np.arange(128), indexing="ij")
    tri = (qq >= kk).astype(f)
    pt = inp["page_table"][NSC * c:NSC * (c + 1)]
    p = np.arange(128)
    ptrep = np.stack([pt[:, 4 * jj + p // 32] for jj in range(16)], axis=2).astype(np.int32)
    if dev_pool == "compact":
        used = pt.reshape(-1)
        ck = np.ascontiguousarray(inp["cache_k"][0][used]).reshape(-1, 2048)
        cv = np.ascontiguousarray(inp["cache_v"][0][used]).reshape(-1, 2048)
        pt = np.arange(used.size).reshape(pt.shape)
        ptrep = np.stack([pt[:, 4 * jj + p // 32] for jj in range(16)], axis=2).astype(np.int32)
    elif dev_pool is not None:
        ck = np.zeros((dev_pool * 32, 2048), f)
        cv = np.zeros((dev_pool * 32, 2048), f)
        ptrep = ptrep % dev_pool
    else:
        ck = inp["cache_k"][0].reshape(-1, 2048)
        cv = inp["cache_v"][0].reshape(-1, 2048)
    selq = np.zeros((NSC, NSC, 128), f)
    for s_ in range(NSC):
        selq[s_, s_, :] = 1.0
    m0 = np.zeros((128, 1), f)
    m0[0, 0] = 1.0
    return dict(lamv=lamv.astype(f), gsub=inp["subln_g"][0].reshape(1, 128), trimask=tri, cache_k=ck, cache_v=cv,
                ptrep=np.ascontiguousarray(ptrep), poff=np.ascontiguousarray(np.broadcast_to((p % 32).astype(np.float32)[:, None], (128, NSC * 16))), selq=selq, mask0=m0)


def _ffn_maps(inp, c):
    sc = inp["state_conv"][0, NSC * c:NSC * (c + 1)]
    return dict(
        w_o=np.ascontiguousarray(inp["w_o"][0]),
        nffn=np.ascontiguousarray(inp["norm_ffn"][0].reshape(1, D)),
        w_up=np.ascontiguousarray(inp["w_up"][0]),
        convw=np.ascontiguousarray(inp["conv_w"][0].reshape(3, NFC, 128).transpose(2, 0, 1)),
        convb=np.ascontiguousarray(inp["conv_b"][0].reshape(NFC, 128).T),
        w_down=np.ascontiguousarray(inp["w_down"][0]),
        fnorm=inp["final_norm"].reshape(1, D),
        sconv=np.ascontiguousarray(sc.reshape(NSC * 2, DFF)),
        sconv1=np.ascontiguousarray(sc[:, 1, :]),
    )


def make_in_maps(inp, cores, dev_pool=None):
    cosT, sinT = _rope_tables()
    ident = np.eye(128, dtype=np.float32)
    maps = []
    for c in cores:
        m = dict(
            xp=np.ascontiguousarray(inp["x_prompt"][c]),
            xs=np.ascontiguousarray(inp["x_sample"][NSC * c:NSC * (c + 1), 0, :]),
            w_in=np.ascontiguousarray(inp["w_in"][0]),
            nmix=np.ascontiguousarray(inp["norm_mix"][0].reshape(1, D)),
            ropec=cosT, ropes=sinT, ident=ident,
        )
        m.update(_ssm_maps(inp, c))
        m.update(_attn_maps(inp, c, dev_pool))
        m.update(_ffn_maps(inp, c))
        maps.append(m)
    return maps


def kernel(**inp):
    inp = {k: np.asarray(v) for k, v in inp.items()}
    n_pool = inp["cache_k"].shape[1]
    nc = build_program(n_pool)
    cores = list(range(8))
    res = run_bass_kernel_spmd(nc, make_in_maps(inp, cores), core_ids=cores)
    r = res.results
    f = np.float32
    cat = lambda key: np.stack([np.asarray(r[c][key], dtype=f) for c in cores])
    y_prompt = cat("o_yp").reshape(8, SEQ, D)
    y_sample = cat("o_ys").reshape(32, 1, D)
    k_prompt = cat("o_kp").reshape(1, 8, SEQ, 4, 128)
    v_prompt = cat("o_vp").reshape(1, 8, SEQ, 4, 128)
    hre_p = cat("o_hre").reshape(1, 8, 32, 64)
    him_p = cat("o_him").reshape(1, 8, 32, 64)
    conv_p = cat("o_cp").reshape(1, 8, 2, DFF)
    k_sample = cat("o_ks").reshape(1, 32, 1, 4, 128)
    v_sample = cat("o_vs").reshape(1, 32, 1, 4, 128)
    hre_s = cat("o_hres").reshape(1, 32, 32, 64)
    him_s = cat("o_hims").reshape(1, 32, 32, 64)
    conv_s = cat("o_cs").reshape(1, 32, 2, DFF)
    return (y_prompt, y_sample, k_prompt, v_prompt, hre_p, him_p, conv_p,
            k_sample, v_sample, hre_s, him_s, conv_s)
```

```python
import numpy as np
from contextlib import ExitStack
import concourse.bass as bass
import concourse.mybir as mybir
from concourse.bass_utils import run_bass_kernel_spmd

F32 = mybir.dt.float32
BF16 = mybir.dt.bfloat16
I32 = mybir.dt.int32
ALU = mybir.AluOpType
AF = mybir.ActivationFunctionType
AX = mybir.AxisListType

D = 1024
SEQ = 2048
NSC = 4
NCOL = SEQ + NSC
DPROJ = 2048
DFF = 2816
NFC = DFF // 128
PAGE = 128
NPAGES = 64
PAST = PAGE * NPAGES
EPS = 1e-6
SUBLN_EPS = 1e-5
LAM_INIT = 0.8 - 0.6 * 1.0
SELF_ORDERED = ("pe",)
TILES = [(128 * i, 128) for i in range(16)] + [(SEQ, NSC)]
BLOCKS = [(512 * i, 512) for i in range(4)] + [(SEQ, NSC)]


class Res:
    __slots__ = ("name", "w", "r")

    def __init__(self, name):
        self.name = name
        self.w = None
        self.r = []


class _Eng:
    def __init__(self, name, eng, sem):
        self.name = name
        self.eng = eng
        self.sem = sem
        self.count = 0
        self.seen = {}


class Sched:
    def __init__(self, nc, es):
        self.nc = nc
        self.es = es
        self.sems = {}
        self.engs = {}
        for name, eng in (("pe", nc.tensor), ("dve", nc.vector), ("act", nc.scalar),
                          ("pool", nc.gpsimd), ("sp", nc.sync)):
            sem = es.enter_context(nc.semaphore("sem_" + name))
            self.sems[name] = sem
            self.engs[name] = _Eng(name, eng, sem)
        self.dma_sems = {}
        self.n_inst = 0
        self.n_wait = 0

    def _dma_sem(self, key):
        if key not in self.dma_sems:
            sem = self.es.enter_context(self.nc.semaphore("dsem_" + key))
            self.sems["d:" + key] = sem
            self.dma_sems[key] = [sem, 0]
        return self.dma_sems[key]

    @staticmethod
    def _deps(reads, writes):
        deps = {}

        def add(d):
            if d is not None and deps.get(d[0], 0) < d[1]:
                deps[d[0]] = d[1]
        for r in reads:
            add(r.w)
        for w in writes:
            add(w.w)
            for rd in w.r:
                add(rd)
        return deps

    def _wait(self, E, deps):
        for k, v in deps.items():
            if E.seen.get(k, 0) >= v:
                continue
            E.eng.wait_ge(self.sems[k], v)
            E.seen[k] = v
            self.n_wait += 1

    @staticmethod
    def _mark(reads, writes, stamp):
        for r in reads:
            r.r.append(stamp)
        for w in writes:
            w.w = stamp
            w.r = []

    def op(self, ename, fn, reads=(), writes=()):
        E = self.engs[ename]
        deps = self._deps(reads, writes)
        if ename in SELF_ORDERED:
            deps.pop(ename, None)
        self._wait(E, deps)
        ins = fn(E.eng)
        E.count += 1
        ins.then_inc(E.sem, 1)
        self._mark(reads, writes, (ename, E.count))
        self.n_inst += 1
        return ins

    def dma(self, ename, fn, skey, reads=(), writes=()):
        E = self.engs[ename]
        self._wait(E, self._deps(reads, writes))
        ds = self._dma_sem(skey)
        ins = fn(E.eng)
        ds[1] += 16
        ins.then_inc(ds[0], 16)
        self._mark(reads, writes, ("d:" + skey, ds[1]))
        self.n_inst += 1
        return ins

    def barrier(self):
        deps = {n: e.count for n, e in self.engs.items() if e.count}
        for k, (sem, c) in self.dma_sems.items():
            if c:
                deps["d:" + k] = c
        for E in self.engs.values():
            d = dict(deps)
            self._wait(E, d)

    def finish(self, ename="sp"):
        E = self.engs[ename]
        deps = {n: e.count for n, e in self.engs.items() if e.count}
        for k, (sem, c) in self.dma_sems.items():
            if c:
                deps["d:" + k] = c
        self._wait(E, deps)


def AP(t, off, dims):
    return bass.AP(t, off, dims)


import math
PI = math.pi


def range_reduce(S, eng, out, a, ki, kf, m, n_part, R_out, R_tmp, reads):
    S.op(eng, lambda e: e.tensor_scalar(out=ki, in0=a, scalar1=1.0 / (2 * PI), scalar2=None, op0=ALU.mult),
         reads=reads, writes=[R_tmp])
    S.op(eng, lambda e: e.tensor_copy(out=kf, in_=ki), reads=[R_tmp], writes=[R_tmp])
    S.op("dve", lambda e: e.scalar_tensor_tensor(out=out, in0=kf, scalar=-2 * PI, in1=a, op0=ALU.mult, op1=ALU.add),
         reads=[R_tmp] + list(reads), writes=[R_out])
    for thr, corr in ((PI, -2 * PI),):
        S.op(eng, lambda e: e.tensor_scalar(out=m, in0=out, scalar1=thr, scalar2=corr, op0=ALU.is_gt, op1=ALU.mult),
             reads=[R_out], writes=[R_tmp])
        S.op(eng, lambda e: e.tensor_tensor(out=out, in0=out, in1=m, op=ALU.add), reads=[R_out, R_tmp], writes=[R_out])
    S.op(eng, lambda e: e.tensor_scalar(out=m, in0=out, scalar1=-PI, scalar2=2 * PI, op0=ALU.is_lt, op1=ALU.mult),
         reads=[R_out], writes=[R_tmp])
    S.op(eng, lambda e: e.tensor_tensor(out=out, in0=out, in1=m, op=ALU.add), reads=[R_out, R_tmp], writes=[R_out])


def sincos(S, eng, s_out, c_out, ang, scr, n_part, R_s, R_c, R_scr, reads, nbias):
    range_reduce(S, eng, scr["y"], ang, scr["ki"], scr["kf"], scr["m"], n_part, R_scr, R_scr, reads)
    S.op("act", lambda e: e.activation(out=s_out, in_=scr["y"], func=AF.Sin), reads=[R_scr], writes=[R_s])
    S.op(eng, lambda e: e.tensor_scalar(out=scr["a"], in0=ang, scalar1=PI / 2, scalar2=None, op0=ALU.add),
         reads=list(reads) + [R_scr], writes=[R_scr])
    range_reduce(S, eng, scr["y"], scr["a"], scr["ki"], scr["kf"], scr["m"], n_part, R_scr, R_scr, [R_scr])
    S.op("act", lambda e: e.activation(out=c_out, in_=scr["y"], func=AF.Sin), reads=[R_scr], writes=[R_c])


def ssm_phase(nc, S, sb, ps, L):
    uT, R_uT, mixT, R_mix, idf, R_id = L["uT"], L["R_uT"], L["mixT"], L["R_mix"], L["idf"], L["R_id"]
    stage = L["stage"]
    with ExitStack() as pz:
        col = sb("col", [128, 26, 16], F32, pz)
        R_col = Res("col")
        dco = sb("dco", [128, 4], F32, pz)
        h0r = sb("h0r", [128, 16, NSC], F32, pz)
        h0i = sb("h0i", [128, 16, NSC], F32, pz)
        R_h0 = Res("h0")
        tr1 = sb("tr1", [128, 512], F32, pz)
        R_tr1 = Res("tr1")
        wgb = sb("wgb", [128, 4, 512], BF16, pz)
        R_wgb = Res("wgb")
        ygT = sb("ygT", [128, 4, NCOL], BF16, pz)
        R_yg = [Res("yg%d" % b) for b in range(5)]
        hfin = sb("hfin", [128, 2, 16], F32, pz)
        R_hfin = Res("hfin")
        hsmp = sb("hsmp", [128, 2, NSC, 16], F32, pz)
        R_hsmp = Res("hsmp")
        S.dma("sp", lambda e: e.dma_start(out=dco[:], in_=L["dcol"]), "z_c", writes=[R_col])
        S.dma("sp", lambda e: e.dma_start(out=col[:, 0, :], in_=L["acol_re"]), "z_c", writes=[R_col])
        S.dma("sp", lambda e: e.dma_start(out=col[:, 1, :], in_=L["acol_im"]), "z_c", writes=[R_col])
        S.dma("sp", lambda e: e.dma_start(out=col[:, 2, :], in_=L["ldt_col"]), "z_c", writes=[R_col])
        S.dma("sp", lambda e: e.dma_start(out=h0r[:], in_=L["h0c_re"]), "z_h", writes=[R_h0])
        S.dma("sp", lambda e: e.dma_start(out=h0i[:], in_=L["h0c_im"]), "z_h", writes=[R_h0])
        S.dma("sp", lambda e: e.dma_start(out=tr1[:], in_=L["trow1"]), "z_t", writes=[R_tr1])

        with ExitStack() as pc:
            cki = sb("c_ki", [128, 16], I32, pc)
            cs = {k: sb("c_" + k, [128, 16], F32, pc) for k in ("a", "kf", "m", "y")}
            S.op("act", lambda e: e.activation(out=col[:, 2, :], in_=col[:, 2, :], func=AF.Exp), reads=[R_col], writes=[R_col])
            S.op("dve", lambda e: e.tensor_tensor(out=col[:, 3, :], in0=col[:, 1, :], in1=col[:, 2, :], op=ALU.mult), reads=[R_col], writes=[R_col])
            S.op("dve", lambda e: e.tensor_tensor(out=col[:, 9, :], in0=col[:, 0, :], in1=col[:, 2, :], op=ALU.mult), reads=[R_col], writes=[R_col])
            S.op("act", lambda e: e.activation(out=col[:, 4, :], in_=col[:, 9, :], func=AF.Exp), reads=[R_col], writes=[R_col])
            scr = dict(a=cs["a"][:], ki=cki[:], kf=cs["kf"][:], m=cs["m"][:], y=cs["y"][:])
            sincos(S, "dve", col[:, 5, :], col[:, 6, :], col[:, 3, :], scr, 128, R_col, R_col, R_col, [R_col], None)
            S.op("dve", lambda e: e.tensor_tensor(out=col[:, 7, :], in0=col[:, 6, :], in1=col[:, 4, :], op=ALU.mult), reads=[R_col], writes=[R_col])
            S.op("dve", lambda e: e.tensor_tensor(out=col[:, 8, :], in0=col[:, 5, :], in1=col[:, 4, :], op=ALU.mult), reads=[R_col], writes=[R_col])

        cr = lambda k: col[:, k, :]
        tt_ = lambda o, x, y, op: S.op("dve", lambda e: e.tensor_tensor(out=cr(o), in0=cr(x), in1=cr(y), op=op), reads=[R_col], writes=[R_col])
        S.op("dve", lambda e: e.tensor_scalar(out=cr(9), in0=cr(7), scalar1=-1.0, scalar2=None, op0=ALU.add), reads=[R_col], writes=[R_col])
        tt_(11, 0, 0, ALU.mult); tt_(12, 1, 1, ALU.mult); tt_(10, 11, 12, ALU.add)
        S.op("dve", lambda e: e.reciprocal(out=cr(10), in_=cr(10)), reads=[R_col], writes=[R_col])
        tt_(11, 9, 0, ALU.mult); tt_(12, 8, 1, ALU.mult); tt_(11, 11, 12, ALU.add); tt_(13, 11, 10, ALU.mult)
        tt_(11, 8, 0, ALU.mult); tt_(12, 9, 1, ALU.mult); tt_(11, 11, 12, ALU.subtract); tt_(14, 11, 10, ALU.mult)

        def cmul(o_re, o_im, a_re, a_im, b_re, b_im):
            tt_(11, a_re, b_re, ALU.mult); tt_(12, a_im, b_im, ALU.mult); tt_(o_re, 11, 12, ALU.subtract)
            tt_(11, a_re, b_im, ALU.mult); tt_(12, a_im, b_re, ALU.mult); tt_(o_im, 11, 12, ALU.add)
        cmul(16, 17, 7, 8, 7, 8)
        cmul(18, 19, 16, 17, 7, 8)
        cmul(20, 21, 16, 17, 16, 17)
        S.op("dve", lambda e: e.tensor_scalar(out=cr(22), in0=cr(3), scalar1=4.0, scalar2=None, op0=ALU.mult), reads=[R_col], writes=[R_col])
        tt_(24, 4, 4, ALU.mult); tt_(23, 24, 24, ALU.mult)
        LBK = {1: (7, 8), 2: (16, 17), 3: (18, 19), 4: (20, 21)}
        for kc in range(4):
            S.dma("pool", lambda e: e.dma_start(out=wgb[:, kc, :], in_=L["w_glu"][kc * 128:(kc + 1) * 128, :]), "z_w", writes=[R_wgb])
        S.barrier()

        with ExitStack() as pm:
            wB = [sb("wB%d" % c, [128, 4, 4, 128], BF16, pm) for c in range(2)]
            wC = [sb("wC%d" % c, [128, 4, 5, 128], BF16, pm) for c in range(2)]
            KK = sb("KK", [128, 4, 128], BF16, pm)
            R_wB, R_wC, R_KK = Res("wB"), Res("wC"), Res("KK")
            braw = [sb("braw%d" % c, [128, 4, 128], F32, pm) for c in range(2)]
            craw = [sb("craw%d" % c, [128, 4, 128], F32, pm) for c in range(2)]
            bbf = [sb("bbf%d" % c, [128, 4, 128], F32, pm) for c in range(2)]
            tq = [sb("tq%d" % c, [128, 4, 128], F32, pm) for c in range(2)]
            pcb = [[sb("pcb%d%d" % (k, c), [128, 4, 128], BF16, pm) for c in range(2)] for k in range(4)]
            R_raw, R_bbf, R_tq = Res("raw"), Res("bbf"), Res("tq")
            R_pcb = [Res("pcb%d" % k) for k in range(4)]
            ptb = ps("r_ptb", [128, 4, 128], BF16, pm)
            pkk = ps("r_pkk", [128, 128], F32, pm)
            Rp, Rpk = Res("ptb"), Res("pkk")
            ecs = sb("ecs", [128, 4, 2, 128], F32, pm)
            esc2 = sb("esc2", [128, 4, 2, 128], F32, pm)
            etc = ecs[:, :, 0, :]
            ets = ecs[:, :, 1, :]
            R_et = [Res("et%d" % k) for k in range(4)]
            eki = sb("e_ki", [128, 128], I32, pm)
            pbb = [ps("pbb%d" % j, [128, 2, 128], F32, pm) for j in range(2)]
            pbr = [pbb[j][:, 0, :] for j in range(2)]
            pbi = [pbb[j][:, 1, :] for j in range(2)]
            R_pb = [Res("pb%d" % j) for j in range(2)]
            py = [ps("py%d" % j, [128, 512], F32, pm) for j in range(2)]
            R_py = [Res("py%d" % j) for j in range(2)]
            ttA = sb("ttA", [128, 2, 128], F32, pm)
            ttB = sb("ttB", [128, 2, 128], F32, pm)
            tt = [ttA[:, 0, :], ttA[:, 1, :], ttB[:, 0, :], ttB[:, 1, :]]
            R_tt = Res("tt")
            w_r = [sb("w_r0", [128, 128], F32, pm)] * 2
            w_i = [sb("w_i0", [128, 128], F32, pm)] * 2
            R_w = [Res("w0")] * 2
            gg = [sb("gg%d" % j, [128, 2, 128], F32, pm) for j in range(2)]
            g_r = [gg[j][:, 0, :] for j in range(2)]
            g_i = [gg[j][:, 1, :] for j in range(2)]
            R_g = [Res("g%d" % j) for j in range(2)]
            uuA = sb("uuA", [128, 2, 128], F32, pm)
            uu = [uuA[:, 0, :], uuA[:, 1, :]]
            R_uu = Res("uu")
            hr = [sb("hr%d" % j, [128, 128], F32, pm) for j in range(2)]
            hi = [sb("hi%d" % j, [128, 128], F32, pm) for j in range(2)]
            R_h = [Res("h%d" % j) for j in range(2)]
            hrb = [sb("hrb%d" % j, [128, 128], BF16, pm) for j in range(2)]
            hib = [sb("hib%d" % j, [128, 128], BF16, pm) for j in range(2)]
            R_hb = [Res("hb%d" % j) for j in range(2)]
            yf = sb("yf", [128, 512], F32, pm)
            R_yf = Res("yf")
            carry = sb("carry", [128, 2, 16], F32, pm)
            R_carry = [Res("carry%d" % p) for p in range(16)]
            S.op("dve", lambda e: e.memset(carry[:], 0.0), writes=R_carry)

            def prep(q):
                qs = slice(4 * q, 4 * q + 4)
                cb = lambda row: col[:, row, qs].unsqueeze(2).to_broadcast([128, 4, 128])
                S.dma("sp", lambda e: e.dma_start(out=braw[0][:], in_=L["bcol_re"][:, qs, :]), "z_b", writes=[R_raw])
                S.dma("sp", lambda e: e.dma_start(out=braw[1][:], in_=L["bcol_im"][:, qs, :]), "z_b", writes=[R_raw])
                S.dma("sp", lambda e: e.dma_start(out=craw[0][:], in_=L["cpad_re"][:, qs, :]), "z_b", writes=[R_raw])
                S.dma("sp", lambda e: e.dma_start(out=craw[1][:], in_=L["cpad_im"][:, qs, :]), "z_b", writes=[R_raw])

                def cplx(o_re, o_im, R_o, x_re, x_im, R_x, rr_, ri_, neg_im=False):
                    S.op("dve", lambda e: e.tensor_tensor(out=tq[0][:], in0=x_re, in1=cb(rr_), op=ALU.mult), reads=[R_x, R_col], writes=[R_tq])
                    S.op("dve", lambda e: e.tensor_tensor(out=tq[1][:], in0=x_im, in1=cb(ri_), op=ALU.mult), reads=[R_x, R_col], writes=[R_tq])
                    S.op("dve", lambda e: e.tensor_tensor(out=o_re, in0=tq[0][:], in1=tq[1][:], op=ALU.subtract), reads=[R_tq], writes=[R_o])
                    S.op("dve", lambda e: e.tensor_tensor(out=tq[0][:], in0=x_im, in1=cb(rr_), op=ALU.mult), reads=[R_x, R_col, R_o], writes=[R_tq])
                    S.op("dve", lambda e: e.tensor_tensor(out=tq[1][:], in0=x_re, in1=cb(ri_), op=ALU.mult), reads=[R_x, R_col, R_o], writes=[R_tq])
                    if neg_im:
                        S.op("dve", lambda e: e.scalar_tensor_tensor(out=o_im, in0=tq[0][:], scalar=-1.0, in1=tq[1][:], op0=ALU.mult, op1=ALU.subtract),
                             reads=[R_tq], writes=[R_o])
                    else:
                        S.op("dve", lambda e: e.tensor_tensor(out=o_im, in0=tq[0][:], in1=tq[1][:], op=ALU.add), reads=[R_tq], writes=[R_o])
                cplx(bbf[0][:], bbf[1][:], R_bbf, braw[0][:], braw[1][:], R_raw, 13, 14)
                S.op("act", lambda e: e.activation(out=pcb[0][0][:], in_=bbf[0][:], func=AF.Copy), reads=[R_bbf], writes=[R_pcb[0]])
                S.op("act", lambda e: e.activation(out=pcb[0][1][:], in_=bbf[1][:], func=AF.Copy), reads=[R_bbf], writes=[R_pcb[0]])
                for k in (1, 2, 3):
                    cplx(pcb[k][0][:], pcb[k][1][:], R_pcb[k], bbf[0][:], bbf[1][:], R_bbf, *LBK[k])
                for c in range(2):
                    for k in range(4):
                        for pl in range(4):
                            S.op("pe", lambda e: e.transpose(out=ptb[:, pl, :], in_=pcb[k][c][:, pl, :], identity=L["idb"][:, :]),
                                 reads=[R_pcb[k], R_id], writes=[Rp])
                        S.op("act", lambda e: e.activation(out=wB[c][:, :, k, :], in_=ptb[:], func=AF.Copy), reads=[Rp], writes=[R_wB])
                S.op("act", lambda e: e.activation(out=wC[0][:, :, 0, :], in_=craw[0][:], func=AF.Copy), reads=[R_raw], writes=[R_wC])
                S.op("act", lambda e: e.activation(out=wC[1][:, :, 0, :], in_=craw[1][:], func=AF.Copy, scale=-1.0), reads=[R_raw], writes=[R_wC])
                for m in (1, 2, 3, 4):
                    cplx(wC[0][:, :, m, :], wC[1][:, :, m, :], R_wC, craw[0][:], craw[1][:], R_raw, *LBK[m], neg_im=True)
                for tau in range(4):
                    n_ = 0
                    for pl in range(4):
                        for c in range(2):
                            S.op("pe", lambda e: e.matmul(pkk[:, :], lhsT=pcb[tau][c][:, pl, :], rhs=wC[c][:, pl, 0, :], start=(n_ == 0), stop=(n_ == 7)),
                                 reads=[R_pcb[tau], R_wC], writes=[Rpk])
                            n_ += 1
                    S.op("act", lambda e: e.activation(out=KK[:, tau, :], in_=pkk[:, :], func=AF.Copy), reads=[Rpk], writes=[R_KK])
                for pl in range(4):
                    pr_ = 4 * q + pl
                    S.op("dve", lambda e: e.tensor_scalar(out=w_r[0][:], in0=tr1[:, 0:128], scalar1=col[:, 22, pr_:pr_ + 1],
                                                          scalar2=None, op0=ALU.mult), reads=[R_tr1, R_col], writes=[R_w[0]])
                    scr = dict(a=tt[0][:], ki=eki[:], kf=tt[1][:], m=tt[2][:], y=tt[3][:])
                    sincos(S, "dve", ets[:, pl, :], etc[:, pl, :], w_r[0][:], scr, 128, R_et[pl], R_et[pl], R_tt, [R_w[0]], None)
                    S.op("act", lambda e: e.activation(out=esc2[:, pl, 0, :], in_=ets[:, pl, :], func=AF.Copy), reads=[R_et[pl]], writes=[R_et[pl]])
                    S.op("act", lambda e: e.activation(out=esc2[:, pl, 1, :], in_=etc[:, pl, :], func=AF.Copy), reads=[R_et[pl]], writes=[R_et[pl]])

            def u_str(q, c0, off, n):
                return AP(uT, q * NCOL + c0 + off, [[4 * NCOL, 128], [4, n]])

            def stA(n, q, b, pl):
                c0, nb = BLOCKS[b]
                j = n % 2
                if b < 4:
                    for c, pX in enumerate((pbr, pbi)):
                        for r_ in range(4):
                            S.op("pe", lambda e: e.matmul(pX[j][:, :], lhsT=wB[c][:, pl, 3 - r_, :], rhs=u_str(q, c0, r_, 128), start=(r_ == 0), stop=(r_ == 3)),
                                 reads=[R_wB, R_uT[b]], writes=[R_pb[j]])
                else:
                    for c, pX in enumerate((pbr, pbi)):
                        S.op("pe", lambda e: e.matmul(pX[j][:, :nb], lhsT=wB[c][:, pl, 0, :], rhs=uT[:, q, c0:c0 + nb], start=True, stop=True),
                             reads=[R_wB, R_uT[b]], writes=[R_pb[j]])

            def stB(n, q, b, pl):
                c0, nb = BLOCKS[b]
                pr_, j = 4 * q + pl, n % 2
                if b < 4:
                    c_, s_ = etc[:, pl, :], ets[:, pl, :]
                    S.op("dve", lambda e: e.tensor_tensor(out=ttA[:], in0=pbb[j][:], in1=ecs[:, pl, :, :], op=ALU.mult), reads=[R_pb[j], R_et[pl]], writes=[R_tt])
                    S.op("dve", lambda e: e.tensor_tensor(out=ttB[:], in0=pbb[j][:], in1=esc2[:, pl, :, :], op=ALU.mult), reads=[R_pb[j], R_et[pl]], writes=[R_tt])
                    S.op("dve", lambda e: e.tensor_tensor(out=w_r[j][:], in0=tt[0][:], in1=tt[1][:], op=ALU.add), reads=[R_tt], writes=[R_w[j]])
                    S.op("dve", lambda e: e.tensor_tensor(out=w_i[j][:], in0=tt[3][:], in1=tt[2][:], op=ALU.subtract), reads=[R_tt], writes=[R_w[j]])
                    S.op("dve", lambda e: e.tensor_tensor_scan(out=g_r[j][:], data0=col[:, 23, pr_:pr_ + 1].to_broadcast([128, 128]), data1=w_r[j][:],
                                                               initial=carry[:, 0, pr_:pr_ + 1], op0=ALU.mult, op1=ALU.add),
                         reads=[R_w[j], R_col, R_carry[pr_]], writes=[R_g[j]])
                    S.op("dve", lambda e: e.tensor_tensor_scan(out=g_i[j][:], data0=col[:, 23, pr_:pr_ + 1].to_broadcast([128, 128]), data1=w_i[j][:],
                                                               initial=carry[:, 1, pr_:pr_ + 1], op0=ALU.mult, op1=ALU.add),
                         reads=[R_w[j], R_col, R_carry[pr_]], writes=[R_g[j]])
                else:
                    lr, li = col[:, 7, pr_:pr_ + 1], col[:, 8, pr_:pr_ + 1]
                    S.op("dve", lambda e: e.tensor_scalar(out=tt[0][:, :nb], in0=h0r[:, pr_, :], scalar1=lr, scalar2=None, op0=ALU.mult), reads=[R_h0, R_col], writes=[R_tt])
                    S.op("dve", lambda e: e.tensor_scalar(out=tt[1][:, :nb], in0=h0i[:, pr_, :], scalar1=li, scalar2=None, op0=ALU.mult), reads=[R_h0, R_col], writes=[R_tt])
                    S.op("dve", lambda e: e.tensor_scalar(out=tt[2][:, :nb], in0=h0i[:, pr_, :], scalar1=lr, scalar2=None, op0=ALU.mult), reads=[R_h0, R_col], writes=[R_tt])
                    S.op("dve", lambda e: e.tensor_scalar(out=tt[3][:, :nb], in0=h0r[:, pr_, :], scalar1=li, scalar2=None, op0=ALU.mult), reads=[R_h0, R_col], writes=[R_tt])
                    S.op("dve", lambda e: e.tensor_tensor(out=tt[0][:, :nb], in0=tt[0][:, :nb], in1=tt[1][:, :nb], op=ALU.subtract), reads=[R_tt], writes=[R_tt])
                    S.op("dve", lambda e: e.tensor_tensor(out=tt[2][:, :nb], in0=tt[2][:, :nb], in1=tt[3][:, :nb], op=ALU.add), reads=[R_tt], writes=[R_tt])
                    S.op("dve", lambda e: e.tensor_tensor(out=hr[j][:, :nb], in0=tt[0][:, :nb], in1=pbr[j][:, :nb], op=ALU.add), reads=[R_tt, R_pb[j]], writes=[R_h[j]])
                    S.op("dve", lambda e: e.tensor_tensor(out=hi[j][:, :nb], in0=tt[2][:, :nb], in1=pbi[j][:, :nb], op=ALU.add), reads=[R_tt, R_pb[j]], writes=[R_h[j]])

            def stC(n, q, b, pl):
                c0, nb = BLOCKS[b]
                pr_, j = 4 * q + pl, n % 2
                if b < 4:
                    c_, s_ = etc[:, pl, :], ets[:, pl, :]
                    S.op("pool", lambda e: e.tensor_tensor(out=uuA[:], in0=gg[j][:], in1=ecs[:, pl, :, :], op=ALU.mult), reads=[R_g[j], R_et[pl]], writes=[R_uu])
                    S.op("pool", lambda e: e.tensor_tensor(out=hr[j][:], in0=uu[0][:], in1=uu[1][:], op=ALU.subtract), reads=[R_uu], writes=[R_h[j]])
                    S.op("pool", lambda e: e.tensor_tensor(out=uuA[:], in0=gg[j][:], in1=esc2[:, pl, :, :], op=ALU.mult), reads=[R_g[j], R_et[pl]], writes=[R_uu])
                    S.op("pool", lambda e: e.tensor_tensor(out=hi[j][:], in0=uu[0][:], in1=uu[1][:], op=ALU.add), reads=[R_uu], writes=[R_h[j]])
                    S.op("act", lambda e: e.activation(out=hrb[j][:, 0:1], in_=carry[:, 0, pr_:pr_ + 1], func=AF.Copy), reads=[R_carry[pr_]], writes=[R_hb[j]])
                    S.op("act", lambda e: e.activation(out=hib[j][:, 0:1], in_=carry[:, 1, pr_:pr_ + 1], func=AF.Copy), reads=[R_carry[pr_]], writes=[R_hb[j]])
                    S.op("act", lambda e: e.activation(out=hrb[j][:, 1:128], in_=hr[j][:, 0:127], func=AF.Copy), reads=[R_h[j]], writes=[R_hb[j]])
                    S.op("act", lambda e: e.activation(out=hib[j][:, 1:128], in_=hi[j][:, 0:127], func=AF.Copy), reads=[R_h[j]], writes=[R_hb[j]])
                    S.op("act", lambda e: e.activation(out=carry[:, 0, pr_:pr_ + 1], in_=hr[j][:, 127:128], func=AF.Copy), reads=[R_h[j], R_hb[j]], writes=[R_carry[pr_]])
                    S.op("act", lambda e: e.activation(out=carry[:, 1, pr_:pr_ + 1], in_=hi[j][:, 127:128], func=AF.Copy), reads=[R_h[j], R_hb[j]], writes=[R_carry[pr_]])
                    if b == 3:
                        S.op("act", lambda e: e.activation(out=hfin[:, 0, pr_:pr_ + 1], in_=hr[j][:, 127:128], func=AF.Copy), reads=[R_h[j]], writes=[R_hfin])
                        S.op("act", lambda e: e.activation(out=hfin[:, 1, pr_:pr_ + 1], in_=hi[j][:, 127:128], func=AF.Copy), reads=[R_h[j]], writes=[R_hfin])
                else:
                    S.op("act", lambda e: e.activation(out=hsmp[:, 0, :, pr_], in_=hr[j][:, :nb], func=AF.Copy), reads=[R_h[j]], writes=[R_hsmp])
                    S.op("act", lambda e: e.activation(out=hsmp[:, 1, :, pr_], in_=hi[j][:, :nb], func=AF.Copy), reads=[R_h[j]], writes=[R_hsmp])
                    S.op("act", lambda e: e.activation(out=hrb[j][:, :nb], in_=hr[j][:, :nb], func=AF.Copy), reads=[R_h[j]], writes=[R_hb[j]])
                    S.op("act", lambda e: e.activation(out=hib[j][:, :nb], in_=hi[j][:, :nb], func=AF.Copy), reads=[R_h[j]], writes=[R_hb[j]])

            def stD(n, q, b, pl):
                c0, nb = BLOCKS[b]
                pr_, j = 4 * q + pl, n % 2
                jy = (q * 5 + b) % 2
                if b < 4:
                    for r_ in range(4):
                        o_ap = AP(py[jy], r_, [[512, 128], [4, 128]])
                        for c, hX in enumerate((hrb, hib)):
                            S.op("pe", lambda e: e.matmul(o_ap, lhsT=wC[c][:, pl, r_ + 1, :], rhs=hX[j][:, :], start=(pl == 0 and r_ == 0 and c == 0), stop=False,
                                                          skip_group_check=True),
                                 reads=[R_wC, R_hb[j]], writes=[R_py[jy]])
                    if pl == 3:
                        for r_ in range(4):
                            o_ap = AP(py[jy], r_, [[512, 128], [4, 128]])
                            for tau in range(r_ + 1):
                                S.op("pe", lambda e: e.matmul(o_ap, lhsT=KK[:, tau, :], rhs=u_str(q, c0, r_ - tau, 128), start=False, stop=(r_ == 3 and tau == 3),
                                                              skip_group_check=True),
                                     reads=[R_KK, R_uT[b]], writes=[R_py[jy]])
                else:
                    S.op("pe", lambda e: e.matmul(py[jy][:, :nb], lhsT=wC[0][:, pl, 0, :], rhs=hrb[j][:, :nb], start=(pl == 0), stop=False),
                         reads=[R_wC, R_hb[j]], writes=[R_py[jy]])
                    S.op("pe", lambda e: e.matmul(py[jy][:, :nb], lhsT=wC[1][:, pl, 0, :], rhs=hib[j][:, :nb], start=False, stop=(pl == 3)),
                         reads=[R_wC, R_hb[j]], writes=[R_py[jy]])
                if pl == 3:
                    S.op("dve", lambda e: e.scalar_tensor_tensor(out=yf[:, :nb], in0=uT[:, q, c0:c0 + nb], scalar=dco[:, q:q + 1],
                                                                 in1=py[jy][:, :nb], op0=ALU.mult, op1=ALU.add),
                         reads=[R_uT[b], R_col, R_py[jy]], writes=[R_yf])
                    S.op("act", lambda e: e.activation(out=ygT[:, q, c0:c0 + nb], in_=yf[:, :nb], func=AF.Gelu_apprx_tanh),
                         reads=[R_yf], writes=[R_yg[b]])

            its = [(q, b, pl) for q in range(4) for b in range(5) for pl in range(4)]
            prep(0)
            stA(0, *its[0])
            for n, itx in enumerate(its):
                nxt = its[n + 1] if n + 1 < len(its) else None
                if nxt is not None and nxt[0] == itx[0]:
                    stA(n + 1, *nxt)
                stB(n, *itx)
                stC(n, *itx)
                stD(n, *itx)
                if nxt is not None and nxt[0] != itx[0]:
                    prep(nxt[0])
                    stA(n + 1, *nxt)
            for co in range(4):
                for b, (c0, nb) in enumerate(BLOCKS):
                    jy = (co * 5 + b) % 2
                    for kc in range(4):
                        S.op("pe", lambda e: e.matmul(py[jy][:, :nb], lhsT=wgb[:, kc, co * 128:(co + 1) * 128], rhs=ygT[:, kc, c0:c0 + nb],
                                                      start=(kc == 0), stop=(kc == 3)),
                             reads=[R_wgb, R_yg[b]], writes=[R_py[jy]])
                    S.op("act", lambda e: e.activation(out=yf[:, :nb], in_=py[jy][:, :nb], func=AF.Sigmoid), reads=[R_py[jy]], writes=[R_yf])
                    S.op("dve", lambda e: e.tensor_tensor(out=mixT[:, 4 + co, c0:c0 + nb], in0=yf[:, :nb], in1=ygT[:, co, c0:c0 + nb], op=ALU.mult),
                         reads=[R_yf, R_yg[b]], writes=[R_mix[4 + co]])
            S.barrier()
            pst = py[0]
            _ot = (braw[0], braw[1], bbf[0])
            oslot = lambda k: _ot[k // 4][0:16, k % 4, :]
            R_ost = Res("ost")
            srcs = [hfin[:, 0, :], hfin[:, 1, :]] + [hsmp[:, ri, s_, :] for ri in range(2) for s_ in range(NSC)]
            for k, src in enumerate(srcs):
                S.op("pe", lambda e: e.transpose(out=pst[0:16, 0:128], in_=src, identity=idf[:, :]),
                     reads=[R_hfin, R_hsmp, R_id], writes=[R_py[0]])
                S.op("dve", lambda e: e.tensor_copy(out=oslot(k), in_=pst[0:16, 0:128]), reads=[R_py[0]], writes=[R_ost])
            S.dma("sp", lambda e: e.dma_start(out=L["o_hre"], in_=oslot(0)), "z_o", reads=[R_ost])
            S.dma("sp", lambda e: e.dma_start(out=L["o_him"], in_=oslot(1)), "z_o", reads=[R_ost])
            for s_ in range(NSC):
                S.dma("sp", lambda e: e.dma_start(out=L["o_hres"][s_], in_=oslot(2 + s_)), "z_o", reads=[R_ost])
                S.dma("sp", lambda e: e.dma_start(out=L["o_hims"][s_], in_=oslot(2 + NSC + s_)), "z_o", reads=[R_ost])
            S.barrier()

def attn_phase(nc, S, sb, ps, L):
    qT, kT, vA, mixT = L["qT"], L["kT"], L["vA"], L["mixT"]
    R_qT, R_kT, R_vA, R_mix = L["R_qT"], L["R_kT"], L["R_vA"], L["R_mix"]
    idb, idf, R_id = L["idb"], L["idf"], L["R_id"]
    with ExitStack() as pa:
        lv = sb("lv", [128, 4, 64], F32, pa)
        lam = sb("lam", [128, 8], F32, pa)
        gs = sb("gs", [128, 128], F32, pa)
        trf = sb("trf", [128, 128], F32, pa)
        tri = sb("tri", [128, 128], BF16, pa)
        R_par = Res("apar")
        S.dma("sp", lambda e: e.dma_start(out=lv[:].rearrange("p a b -> p (a b)"), in_=AP(L["lamv"].tensor, 0, [[0, 128], [1, 256]])), "a_p", writes=[R_par])
        S.dma("sp", lambda e: e.dma_start(out=gs[:], in_=AP(L["gsub"].tensor, 0, [[0, 128], [1, 128]])), "a_p", writes=[R_par])
        S.dma("sp", lambda e: e.dma_start(out=trf[:], in_=L["trimask"]), "a_p", writes=[R_par])
        S.op("dve", lambda e: e.tensor_copy(out=tri[:], in_=trf[:]), reads=[R_par], writes=[R_par])
        S.op("dve", lambda e: e.tensor_scalar(out=gs[:], in0=gs[:], scalar1=1.0 - LAM_INIT, scalar2=None, op0=ALU.mult), reads=[R_par], writes=[R_par])
        S.op("dve", lambda e: e.tensor_tensor(out=lv[:, 0, :], in0=lv[:, 0, :], in1=lv[:, 1, :], op=ALU.mult), reads=[R_par], writes=[R_par])
        S.op("dve", lambda e: e.tensor_tensor(out=lv[:, 2, :], in0=lv[:, 2, :], in1=lv[:, 3, :], op=ALU.mult), reads=[R_par], writes=[R_par])
        S.op("dve", lambda e: e.reduce_sum(out=lam[:, 1:2], in_=lv[:, 0, :], axis=AX.X), reads=[R_par], writes=[R_par])
        S.op("dve", lambda e: e.reduce_sum(out=lam[:, 2:3], in_=lv[:, 2, :], axis=AX.X), reads=[R_par], writes=[R_par])
        S.op("act", lambda e: e.activation(out=lam[:, 1:3], in_=lam[:, 1:3], func=AF.Exp), reads=[R_par], writes=[R_par])
        S.op("dve", lambda e: e.tensor_tensor(out=lam[:, 0:1], in0=lam[:, 1:2], in1=lam[:, 2:3], op=ALU.subtract), reads=[R_par], writes=[R_par])
        S.op("dve", lambda e: e.tensor_scalar(out=lam[:, 0:1], in0=lam[:, 0:1], scalar1=LAM_INIT, scalar2=None, op0=ALU.add), reads=[R_par], writes=[R_par])

        ptp = ps("ptp", [128, 128], BF16, pa)
        R_ptp = Res("ptp")

        def subln_store(o_ap, n, dst_fn, scr):
            junk, st, ob, R_s = scr
            S.op("act", lambda e: e.activation(out=junk[:n, :], in_=o_ap, func=AF.Square, accum_out=st[:n, 0:1]), reads=[R_s], writes=[R_s])
            S.op("dve", lambda e: e.tensor_scalar(out=st[:n, 1:2], in0=st[:n, 0:1], scalar1=1.0 / 128, scalar2=SUBLN_EPS, op0=ALU.mult, op1=ALU.add), reads=[R_s], writes=[R_s])
            S.op("act", lambda e: e.sqrt(out=st[:n, 1:2], in_=st[:n, 1:2]), reads=[R_s], writes=[R_s])
            S.op("dve", lambda e: e.reciprocal(out=st[:n, 1:2], in_=st[:n, 1:2]), reads=[R_s], writes=[R_s])
            S.op("dve", lambda e: e.scalar_tensor_tensor(out=ob[:n, :], in0=o_ap, scalar=st[:n, 1:2], in1=gs[:n, :], op0=ALU.mult, op1=ALU.mult),
                 reads=[R_s, R_par], writes=[R_s])
            S.op("pe", lambda e: e.transpose(out=ptp[:, :n], in_=ob[:n, :], identity=idb[:n, :n]), reads=[R_s, R_id], writes=[R_ptp])
            dst_fn()

        psS = [ps("psS%d" % j, [128, 512], F32, pa) for j in range(2)]
        R_S = [Res("psS%d" % j) for j in range(2)]
        psO = ps("psO", [128, 2, 512], F32, pa)
        _rb = [Res("psO_b0"), Res("psO_b1")]
        R_O = [_rb[0], _rb[0], _rb[0], _rb[1]]
        accO = lambda k: psO[:, k // 3, (k % 3) * 129:(k % 3) * 129 + 129]
        eE = [sb("eE%d" % j, [128, 512], BF16, pa) for j in range(2)]
        R_E = [Res("eE%d" % j) for j in range(2)]
        osb = [sb("osb%d" % g, [128, 2, 4, 129], F32, pa) for g in range(2)]
        R_osb = [[Res("osb%d_%d" % (g, qt)) for qt in range(4)] for g in range(2)]
        of = [sb("of%d" % qt, [128, 128], F32, pa) for qt in range(4)]
        junk = [sb("ajunk%d" % qt, [128, 128], BF16, pa) for qt in range(4)]
        st = [sb("ast%d" % qt, [128, 4], F32, pa) for qt in range(4)]
        ob = [sb("aob%d" % qt, [128, 128], BF16, pa) for qt in range(4)]
        R_scr = [Res("ascr%d" % qt) for qt in range(4)]

        def prompt_group(gi):
            h, QB = gi // 4, gi % 4
            g2 = gi % 2
            nkb = 4 * QB + 4
            for c in range(2):
                def qk(kb):
                    o = kb - 4 * QB
                    qlo = max(o, 0) * 128
                    S.op("pe", lambda e: e.matmul(psS[kb % 2][:, qlo:512], lhsT=kT[64 * c:64 * c + 64, h, kb * 128:(kb + 1) * 128],
                                                  rhs=qT[64 * c:64 * c + 64, h, QB * 512 + qlo:QB * 512 + 512], start=True, stop=True),
                         reads=[R_kT[kb]] + R_qT[4 * QB:4 * QB + 4], writes=[R_S[kb % 2]])
                qk(0)
                for kb in range(nkb):
                    o = kb - 4 * QB
                    qlo = max(o, 0) * 128
                    j = kb % 2
                    if kb + 1 < nkb:
                        qk(kb + 1)
                    if kb % 2 == c:
                        sample_step()
                    S.op("act", lambda e: e.activation(out=eE[j][:, qlo:512], in_=psS[j][:, qlo:512], func=AF.Exp),
                         reads=[R_S[j]], writes=[R_E[j]])
                    if o >= 0:
                        S.op("pool", lambda e: e.tensor_tensor(out=eE[j][:, qlo:qlo + 128], in0=eE[j][:, qlo:qlo + 128], in1=tri[:], op=ALU.mult),
                             reads=[R_E[j], R_par], writes=[R_E[j]])
                    for qt in range(max(o, 0), 4):
                        S.op("pe", lambda e: e.matmul(accO(qt), lhsT=eE[j][:, qt * 128:(qt + 1) * 128], rhs=vA[:, kb, h, :],
                                                      start=(kb == 0 and qt in (0, 3)), stop=(kb == 4 * QB + qt), skip_group_check=True),
                             reads=[R_E[j], R_vA[kb]], writes=[R_O[qt]])
                for qt in range(4):
                    S.op("act", lambda e: e.activation(out=osb[g2][:, c, qt, :], in_=accO(qt), func=AF.Copy), reads=[R_O[qt]], writes=[R_osb[g2][qt]])
            for qt in range(4):
                o1, o2 = osb[g2][:, 0, qt, :], osb[g2][:, 1, qt, :]
                Rq = R_scr[qt]
                S.op("dve", lambda e: e.reciprocal(out=st[qt][:, 2:3], in_=o1[:, 128:129]), reads=[R_osb[g2][qt]], writes=[Rq])
                S.op("dve", lambda e: e.reciprocal(out=st[qt][:, 3:4], in_=o2[:, 128:129]), reads=[R_osb[g2][qt]], writes=[Rq])
                S.op("dve", lambda e: e.tensor_tensor(out=st[qt][:, 3:4], in0=st[qt][:, 3:4], in1=lam[:, 0:1], op=ALU.mult), reads=[Rq, R_par], writes=[Rq])
                S.op("dve", lambda e: e.tensor_scalar(out=o2[:, 0:128], in0=o2[:, 0:128], scalar1=st[qt][:, 3:4], scalar2=None, op0=ALU.mult),
                     reads=[Rq, R_osb[g2][qt]], writes=[R_osb[g2][qt]])
                S.op("dve", lambda e: e.scalar_tensor_tensor(out=of[qt][:, :], in0=o1[:, 0:128], scalar=st[qt][:, 2:3], in1=o2[:, 0:128], op0=ALU.mult, op1=ALU.subtract),
                     reads=[Rq, R_osb[g2][qt]], writes=[Rq])
                c0 = QB * 512 + qt * 128
                def dst(c0=c0, h=h):
                    S.op("act", lambda e: e.activation(out=mixT[:, h, c0:c0 + 128], in_=ptp[:, :128], func=AF.Copy), reads=[R_ptp], writes=[R_mix[h]])
                subln_store(of[qt][:, :], 128, dst, (junk[qt], st[qt], ob[qt], Rq))

        do_samples = L["stage"] >= 8
        sample_gen = None
        state = {"gen": None}

        def sample_step():
            if state["gen"] is not None:
                try:
                    next(state["gen"])
                except StopIteration:
                    state["gen"] = None
        if do_samples:
            ptr_ = sb("s_ptr", [128, NSC, 16], I32, pa)
            pof = sb("s_pof", [128, NSC * 16], F32, pa)
            pff = sb("s_pff", [128, NSC * 16], F32, pa)
            idx = sb("s_idx", [128, NSC, 16], I32, pa)
            R_idx = Res("idx")
            selt = sb("s_sel", [NSC, NSC, 128], F32, pa)
            m0 = sb("s_m0", [128, 1], F32, pa)
            S.dma("sp", lambda e: e.dma_start(out=ptr_[:], in_=L["ptrep"].rearrange("s p j -> p s j")), "s_p", writes=[R_idx])
            S.dma("sp", lambda e: e.dma_start(out=pof[:], in_=L["poff"]), "s_p", writes=[R_idx])
            S.dma("sp", lambda e: e.dma_start(out=selt[:], in_=L["selq"]), "s_p", writes=[R_idx])
            S.dma("sp", lambda e: e.dma_start(out=m0[:], in_=L["mask0"]), "s_p", writes=[R_idx])
            f2 = lambda t: t[:].rearrange("p a b -> p (a b)")
            S.op("dve", lambda e: e.tensor_copy(out=pff[:, :], in_=f2(ptr_)), reads=[R_idx], writes=[R_idx])
            S.op("dve", lambda e: e.scalar_tensor_tensor(out=pff[:, :], in0=pff[:, :], scalar=32.0, in1=pof[:, :], op0=ALU.mult, op1=ALU.add), reads=[R_idx], writes=[R_idx])
            S.op("dve", lambda e: e.tensor_copy(out=f2(idx), in_=pff[:, :]), reads=[R_idx], writes=[R_idx])
            NKB = 3
            kt_ = [sb("s_kt%d" % j, [128, 4, 512], F32, pa) for j in range(NKB)]
            R_kt = [Res("s_kt%d" % j) for j in range(NKB)]
            NVB = 3
            vb_ = [sb("s_vb%d" % j, [128, 4, 512], BF16, pa) for j in range(NVB)]
            R_vb = [Res("s_vb%d" % j) for j in range(NVB)]
            prod = [sb("s_prod%d" % j, [128, 4, 512], F32, pa) for j in range(2)]
            R_prod = [Res("s_prod%d" % j) for j in range(2)]
            qbc = [sb("s_qbc%d" % j, [128, 4, 512], F32, pa) for j in range(2)]
            R_qbc = [Res("qbc%d" % j) for j in range(2)]
            R_qscr = Res("qscr")
            sc = sb("s_sc", [128, 64, 8], F32, pa)
            R_sc = Res("s_sc")
            pdz = sb("s_pdz", [128, 64, 4, NSC], BF16, pa)
            R_pdz = Res("pdz")
            zz = sb("s_zz", [128, 6, 8], F32, pa)
            R_zz = Res("zz")
            sself = sb("s_self", [NSC, 4, 8], F32, pa)
            prs = sb("s_prs", [NSC, 512], F32, pa)
            R_self = Res("sself")
            coef = sb("s_coef", [NSC, 2, 8], F32, pa)
            R_coef = Res("coef")
            osm = sb("s_osm", [NSC, 4, 128], F32, pa)
            sjunk = sb("s_junk", [NSC, 128], BF16, pa)
            sst = sb("s_st", [NSC, 4], F32, pa)
            sob = sb("s_ob", [NSC, 128], BF16, pa)
            R_sscr = Res("s_scr")
            pz = ps("s_pz", [128, 16], F32, pa)
            R_pz = Res("pz")
            pso = ps("s_pso", [NSC, 512], F32, pa)
            R_pso = Res("pso")
            S.op("pool", lambda e: e.memset(pdz[:], 0.0), writes=[R_pdz])
            S.op("dve", lambda e: e.memset(coef[:], 0.0), writes=[R_coef])
            S.dma("sp", lambda e: e.dma_start(out=L["qscr"], in_=L["qsf"][:, :]), "s_q", reads=[L["R_qsf"]], writes=[R_qscr])
            S.op("dve", lambda e: e.tensor_tensor(out=prs[:, :], in0=L["qsf"][:, :], in1=L["ksf"][:, :], op=ALU.mult), reads=[L["R_qsf"], L["R_ksf"]], writes=[R_self])
            S.op("dve", lambda e: e.reduce_sum(out=sself[:, 0, :], in_=prs[:, :].rearrange("p (a d) -> p a d", d=64), axis=AX.X), reads=[R_self], writes=[R_self])
            S.op("act", lambda e: e.activation(out=sself[:, 1, :], in_=sself[:, 0, :], func=AF.Exp, scale=0.125), reads=[R_self], writes=[R_self])

            def sample_k(s_):
                def gather(jj):
                    j = (s_ * 16 + jj) % NKB
                    S.dma("pool", lambda e: e.indirect_dma_start(out=kt_[j][:].rearrange("p a b -> p (a b)"), out_offset=None, in_=L["cache_k"],
                                                                 in_offset=bass.IndirectOffsetOnAxis(ap=idx[:, s_, jj:jj + 1], axis=0)),
                          "s_kt%d" % j, reads=[R_idx], writes=[R_kt[j]])
                qb_ = qbc[s_ % 2]
                S.dma("sp", lambda e: e.dma_start(out=qb_[:], in_=AP(L["qscr"].tensor, s_ * 512, [[0, 128], [0, 4], [1, 512]])),
                      "s_qb%d" % (s_ % 2), reads=[R_qscr], writes=[R_qbc[s_ % 2]])
                gather(0)
                gather(1)
                for jj in range(16):
                    j = (s_ * 16 + jj) % NKB
                    if jj + 2 < 16:
                        gather(jj + 2)
                    S.op("dve", lambda e: e.tensor_tensor(out=prod[jj % 2][:], in0=kt_[j][:], in1=qb_[:], op=ALU.mult),
                         reads=[R_kt[j], R_qbc[s_ % 2]], writes=[R_prod[jj % 2]])
                    S.op("dve", lambda e: e.reduce_sum(out=sc[:, jj * 4:(jj + 1) * 4, :], in_=prod[jj % 2][:].rearrange("p t (a d) -> p t a d", d=64), axis=AX.X),
                         reads=[R_prod[jj % 2]], writes=[R_sc])
                    yield
                S.op("act", lambda e: e.activation(out=sc[:], in_=sc[:], func=AF.Exp, scale=0.125), reads=[R_sc], writes=[R_sc])
                S.op("dve", lambda e: e.reduce_sum(out=zz[:, 0, :], in_=sc[:].rearrange("p t a -> p a t"), axis=AX.X), reads=[R_sc], writes=[R_zz])
                S.op("pe", lambda e: e.matmul(pz[:, 0:8], lhsT=selt[:, s_, :], rhs=sself[:, 1, :], start=True, stop=True), reads=[R_idx, R_self], writes=[R_pz])
                S.op("dve", lambda e: e.scalar_tensor_tensor(out=zz[:, 0, :], in0=pz[:, 0:8], scalar=m0[:, 0:1], in1=zz[:, 0, :], op0=ALU.mult, op1=ALU.add),
                     reads=[R_pz, R_idx, R_zz], writes=[R_zz])
                S.op("pe", lambda e: e.matmul(pz[:, 8:16], lhsT=L["ones_f"][:, :], rhs=zz[:, 0, :], start=True, stop=True), reads=[R_zz, R_id], writes=[R_pz])
                S.op("dve", lambda e: e.reciprocal(out=zz[:, 1, :], in_=pz[:, 8:16]), reads=[R_pz], writes=[R_zz])
                z4 = lambda r: zz[:, r, :].rearrange("p (h c) -> p h c", c=2)
                S.op("dve", lambda e: e.tensor_scalar(out=zz[:, 2, :], in0=zz[:, 1, :], scalar1=lam[:, 0:1], scalar2=None, op0=ALU.mult), reads=[R_zz, R_par], writes=[R_zz])
                for r_ in range(2):
                    S.op("dve", lambda e: e.scalar_tensor_tensor(out=coef[:, r_, :], in0=zz[0:NSC, 1 + r_, :], scalar=idf[0:NSC, s_:s_ + 1], in1=coef[:, r_, :],
                                                                 op0=ALU.mult, op1=ALU.add), reads=[R_zz, R_id, R_coef], writes=[R_coef])
                sc4 = sc[:].rearrange("p t (h c) -> p t h c", c=2)
                S.op("dve", lambda e: e.tensor_tensor(out=sc4[:, :, :, 0], in0=sc4[:, :, :, 0], in1=z4(1)[:, :, 0].unsqueeze(1).to_broadcast([128, 64, 4]), op=ALU.mult), reads=[R_sc, R_zz], writes=[R_sc])
                S.op("dve", lambda e: e.tensor_tensor(out=sc4[:, :, :, 1], in0=sc4[:, :, :, 1], in1=z4(2)[:, :, 1].unsqueeze(1).to_broadcast([128, 64, 4]), op=ALU.mult), reads=[R_sc, R_zz], writes=[R_sc])
                if s_ > 0:
                    S.op("pool", lambda e: e.memset(pdz[:, :, :, s_ - 1], 0.0), writes=[R_pdz])
                S.op("dve", lambda e: e.tensor_tensor(out=pdz[:, :, :, s_], in0=sc4[:, :, :, 0], in1=sc4[:, :, :, 1], op=ALU.subtract), reads=[R_sc], writes=[R_pdz])

            def sample_v(s_):
                def gather(jj):
                    jv = jj % NVB
                    S.dma("pool", lambda e: e.indirect_dma_start(out=vb_[jv][:].rearrange("p a b -> p (a b)"), out_offset=None, in_=L["cache_v"],
                                                                 in_offset=bass.IndirectOffsetOnAxis(ap=idx[:, s_, jj:jj + 1], axis=0)),
                          "s_vb%d" % jv, reads=[R_idx], writes=[R_vb[jv]])
                gather(0)
                gather(1)
                for jj in range(16):
                    jv = jj % NVB
                    if jj + 2 < 16:
                        gather(jj + 2)
                    for t4 in range(4):
                        for h in range(4):
                            first = (s_ == 0 and jj == 0 and t4 == 0 and h == 0)
                            last = (s_ == NSC - 1 and jj == 15 and t4 == 3)
                            S.op("pe", lambda e: e.matmul(pso[:, h * 128:(h + 1) * 128], lhsT=pdz[:, jj * 4 + t4, h, :], rhs=vb_[jv][:, t4, h * 128:(h + 1) * 128],
                                                          start=first, stop=last, skip_group_check=True),
                                 reads=[R_pdz, R_vb[jv]], writes=[R_pso])
                    yield

            def sample_all():
                for s_ in range(NSC):
                    yield from sample_k(s_)
                    yield from sample_v(s_)
            sample_gen = sample_all()


        if do_samples:
            state["gen"] = sample_gen
        for gi in range(16):
            prompt_group(gi)
        while state["gen"] is not None:
            sample_step()

        if do_samples:
            e4 = sself[:, 1, :].rearrange("p (h c) -> p h c", c=2)
            cA = coef[:, 0, :].rearrange("p (h c) -> p h c", c=2)
            cB = coef[:, 1, :].rearrange("p (h c) -> p h c", c=2)
            pd4 = sself[:, 2, :].rearrange("p (h c) -> p h c", c=2)
            S.op("dve", lambda e: e.tensor_tensor(out=pd4[:, :, 0], in0=e4[:, :, 0], in1=cA[:, :, 0], op=ALU.mult), reads=[R_self, R_coef], writes=[R_self])
            S.op("dve", lambda e: e.tensor_tensor(out=pd4[:, :, 1], in0=e4[:, :, 1], in1=cB[:, :, 1], op=ALU.mult), reads=[R_self, R_coef], writes=[R_self])
            S.op("dve", lambda e: e.tensor_tensor(out=pd4[:, :, 0], in0=pd4[:, :, 0], in1=pd4[:, :, 1], op=ALU.subtract), reads=[R_self], writes=[R_self])
            for h in range(4):
                S.op("dve", lambda e: e.scalar_tensor_tensor(out=osm[:, h, :], in0=L["vS"][:, h * 128:(h + 1) * 128], scalar=pd4[:, h, 0:1], in1=pso[:, h * 128:(h + 1) * 128],
                                                             op0=ALU.mult, op1=ALU.add), reads=[L["R_vS"], R_self, R_pso], writes=[R_sscr])
                def dst(h=h):
                    S.op("act", lambda e: e.activation(out=mixT[:, h, SEQ:NCOL], in_=ptp[:, :NSC], func=AF.Copy), reads=[R_ptp], writes=[R_mix[h]])
                subln_store(osm[:, h, :], NSC, dst, (sjunk, sst, sob, R_sscr))
        S.barrier()


def ffn_phase(nc, S, sb, ps, L):
    mixT, R_mix, idb, idf, R_id = L["mixT"], L["R_mix"], L["idb"], L["idf"], L["R_id"]
    xp, xs, xscr = L["xp"], L["xs"], L["xscr"]
    with ExitStack() as pf:
        mT = sb("mT", [128, NFC, NCOL], BF16, pf)
        R_mT = [Res("mT%d" % b) for b in range(5)]
        rs2 = sb("rs2", [128, 17, 4], F32, pf)
        R_rs2 = [Res("rs2_%d" % i) for i in range(17)]
        small = sb("fsmall", [128, 8 + 3 * NFC + NFC], F32, pf)
        R_small = Res("fsmall")
        S.dma("sp", lambda e: e.dma_start(out=small[:, 8:8 + 3 * NFC], in_=L["convw"].rearrange("p a b -> p (a b)")), "f_s", writes=[R_small])
        S.dma("sp", lambda e: e.dma_start(out=small[:, 8 + 3 * NFC:], in_=L["convb"]), "f_s", writes=[R_small])
        cw = lambda j, fc: small[:, 8 + j * NFC + fc:8 + j * NFC + fc + 1]
        cb = lambda fc: small[:, 8 + 3 * NFC + fc:8 + 3 * NFC + fc + 1]
        asp = sb("asp", [8, DFF], F32, pf)
        R_asp = Res("asp")
        with ExitStack() as pu:
            hn2T = sb("hn2T", [128, 8, NCOL], BF16, pu)
            R_hn2 = [Res("hn2_%d" % i) for i in range(17)]
            with ExitStack() as po:
                wob = sb("wob", [128, 8, D], BF16, po)
                R_wob = Res("wob")
                for kc in range(8):
                    S.dma("pool", lambda e: e.dma_start(out=wob[:, kc, :], in_=L["w_o"][kc * 128:(kc + 1) * 128, :]), "wob", writes=[R_wob])
                gfx = sb("gfx", [128, D], F32, po)
                R_gfx = Res("gfx")
                S.dma("sp", lambda e: e.dma_start(out=gfx[:], in_=AP(L["nffn"].tensor, 0, [[0, 128], [1, D]])), "f_g2", writes=[R_gfx])
                xt = [sb("oxt%d" % j, [128, D], F32, po) for j in range(2)]
                R_xt = [Res("oxt%d" % j) for j in range(2)]
                xb = [sb("oxb0", [128, D], BF16, po)] * 2
                R_xb = [Res("oxb0")] * 2
                junk = xb[0]
                R_junk = R_xb[0]
                px = [ps("opx%d" % j, [128, 2, 512], F32, po) for j in range(2)]
                R_px = [Res("opx%d" % j) for j in range(2)]
                ptr = [ps("optr%d" % j, [128, 8, 128], BF16, po) for j in range(2)]
                R_ptr = [Res("optr%d" % j) for j in range(2)]
                for i, (c0, n) in enumerate(TILES):
                    j = i % 2
                    src = xp[c0:c0 + n, :] if i < 16 else xs
                    S.dma("sp", lambda e: e.dma_start(out=xt[j][:n, :], in_=src), "oxt%d" % j, writes=[R_xt[j]])
                    for hf in range(2):
                        for kc in range(8):
                            S.op("pe", lambda e: e.matmul(px[j][:n, hf, :], lhsT=mixT[:, kc, c0:c0 + n], rhs=wob[:, kc, hf * 512:(hf + 1) * 512],
                                                          start=(kc == 0), stop=(kc == 7)), reads=R_mix + [R_wob], writes=[R_px[j]])
                    S.op("dve", lambda e: e.tensor_tensor(out=xt[j][:n, :], in0=xt[j][:n, :], in1=px[j][:n, :, :].rearrange("p a b -> p (a b)"), op=ALU.add),
                         reads=[R_xt[j], R_px[j]], writes=[R_xt[j]])
                    S.dma("sp", lambda e: e.dma_start(out=xscr[c0:c0 + n, :], in_=xt[j][:n, :]), "oxt%d" % j, reads=[R_xt[j]])
                    S.op("act", lambda e: e.activation(out=junk[:n, :], in_=xt[j][:n, :], func=AF.Square, accum_out=rs2[:n, i, 0:1]),
                         reads=[R_xt[j]], writes=[R_junk, R_rs2[i]])
                    S.op("dve", lambda e: e.tensor_scalar(out=rs2[:n, i, 1:2], in0=rs2[:n, i, 0:1], scalar1=1.0 / D, scalar2=EPS, op0=ALU.mult, op1=ALU.add),
                         reads=[R_rs2[i]], writes=[R_rs2[i]])
                    S.op("act", lambda e: e.sqrt(out=rs2[:n, i, 2:3], in_=rs2[:n, i, 1:2]), reads=[R_rs2[i]], writes=[R_rs2[i]])
                    S.op("dve", lambda e: e.reciprocal(out=rs2[:n, i, 3:4], in_=rs2[:n, i, 2:3]), reads=[R_rs2[i]], writes=[R_rs2[i]])
                    S.op("dve", lambda e: e.scalar_tensor_tensor(out=xb[j][:n, :], in0=xt[j][:n, :], scalar=rs2[:n, i, 3:4], in1=gfx[:n, :],
                                                                 op0=ALU.mult, op1=ALU.mult),
                         reads=[R_xt[j], R_rs2[i], R_gfx], writes=[R_xb[j]])
                    for kc in range(8):
                        S.op("pe", lambda e: e.transpose(out=ptr[j][:, kc, :n], in_=xb[j][:n, kc * 128:(kc + 1) * 128], identity=idb[:n, :n]),
                             reads=[R_xb[j], R_id], writes=[R_ptr[j]])
                    S.op("dve", lambda e: e.tensor_copy(out=hn2T[:, :, c0:c0 + n], in_=ptr[j][:, :, :n]), reads=[R_ptr[j]], writes=[R_hn2[i]])
                S.barrier()
            with ExitStack() as pv:
                wab = [sb("wab%d" % j, [128, 8, 128], BF16, pv) for j in range(2)]
                wgb_ = [sb("wgb_%d" % j, [128, 8, 128], BF16, pv) for j in range(2)]
                R_wab = [Res("wab%d" % j) for j in range(2)]
                R_wgb = [Res("wgb_%d" % j) for j in range(2)]
                pa_ = [ps("fpa%d" % j, [128, 512], F32, pv) for j in range(2)]
                pg_ = [ps("fpg%d" % j, [128, 512], F32, pv) for j in range(2)]
                R_pa = [Res("fpa%d" % j) for j in range(2)]
                R_pg = [Res("fpg%d" % j) for j in range(2)]
                psp = ps("fpsp", [8, 128], F32, pv)
                R_psp = Res("fpsp")
                pbt = ps("fpbt", [128, NFC, 8], F32, pv)
                R_pbt = Res("fpbt")
                abuf = sb("abuf", [128, 2 + 512], F32, pv)
                cbuf = sb("cbuf", [128, 512], F32, pv)
                R_ab, R_cbuf = Res("abuf"), Res("cbuf")
                scv = sb("scv", [8, DFF], F32, pv)
                bufT = sb("bufT", [128, NFC, 8], F32, pv)
                R_bufT = Res("bufT")
                S.dma("sp", lambda e: e.dma_start(out=scv[:, :], in_=L["sconv"]), "f_c", writes=[R_bufT])
                for fc in range(NFC):
                    S.op("pe", lambda e: e.transpose(out=pbt[:, fc, :], in_=scv[:, fc * 128:(fc + 1) * 128], identity=idf[:8, :8]),
                         reads=[R_bufT, R_id], writes=[R_pbt])
                S.op("dve", lambda e: e.tensor_copy(out=bufT[:], in_=pbt[:]), reads=[R_pbt], writes=[R_bufT])
                for fc in range(NFC):
                    jw = fc % 2
                    for half, (wb, R_wb) in enumerate(((wab, R_wab), (wgb_, R_wgb))):
                        col0 = half * DFF + fc * 128
                        S.dma("pool", lambda e: e.dma_start(out=wb[jw][:], in_=L["w_up"][:, col0:col0 + 128].rearrange("(k p) c -> p k c", p=128)),
                              "wup%d%d" % (half, jw), writes=[R_wb[jw]])
                    for kc in range(8):
                        S.op("pe", lambda e: e.matmul(psp[0:6, :], lhsT=hn2T[:, kc, SEQ - 2:NCOL], rhs=wab[jw][:, kc, :], start=(kc == 0), stop=(kc == 7)),
                             reads=R_hn2[15:17] + [R_wab[jw]], writes=[R_psp])
                    S.op("act", lambda e: e.activation(out=asp[0:6, fc * 128:(fc + 1) * 128], in_=psp[0:6, :], func=AF.Copy), reads=[R_psp], writes=[R_asp])
                    for b, (c0, nb) in enumerate(BLOCKS):
                        j = (fc * 5 + b) % 2
                        for kc in range(8):
                            S.op("pe", lambda e: e.matmul(pa_[j][:, :nb], lhsT=wab[jw][:, kc, :], rhs=hn2T[:, kc, c0:c0 + nb], start=(kc == 0), stop=(kc == 7)),
                                 reads=R_hn2 + [R_wab[jw]], writes=[R_pa[j]])
                        for kc in range(8):
                            S.op("pe", lambda e: e.matmul(pg_[j][:, :nb], lhsT=wgb_[jw][:, kc, :], rhs=hn2T[:, kc, c0:c0 + nb], start=(kc == 0), stop=(kc == 7)),
                                 reads=R_hn2 + [R_wgb[jw]], writes=[R_pg[j]])
                        if b < 4:
                            if b == 0:
                                S.op("dve", lambda e: e.memset(abuf[:, 0:2], 0.0), reads=[R_ab], writes=[R_ab])
                            else:
                                S.op("dve", lambda e: e.tensor_copy(out=abuf[:, 0:2], in_=abuf[:, 512:514]), reads=[R_ab], writes=[R_ab])
                            S.op("act", lambda e: e.activation(out=abuf[:, 2:514], in_=pa_[j][:, :], func=AF.Copy), reads=[R_pa[j], R_ab], writes=[R_ab])
                            S.op("act", lambda e: e.activation(out=cbuf[:, :], in_=pa_[j][:, :], func=AF.Identity, scale=cw(2, fc), bias=cb(fc)),
                                 reads=[R_pa[j], R_small], writes=[R_cbuf])
                            S.op("dve", lambda e: e.scalar_tensor_tensor(out=cbuf[:, :], in0=abuf[:, 1:513], scalar=cw(1, fc), in1=cbuf[:, :], op0=ALU.mult, op1=ALU.add),
                                 reads=[R_ab, R_small, R_cbuf], writes=[R_cbuf])
                            S.op("dve", lambda e: e.scalar_tensor_tensor(out=cbuf[:, :], in0=abuf[:, 0:512], scalar=cw(0, fc), in1=cbuf[:, :], op0=ALU.mult, op1=ALU.add),
                                 reads=[R_ab, R_small, R_cbuf], writes=[R_cbuf])
                        else:
                            b3 = bufT[:, fc, :].rearrange("p (s j) -> p s j", j=2)
                            S.op("act", lambda e: e.activation(out=cbuf[:, :nb], in_=pa_[j][:, :nb], func=AF.Identity, scale=cw(2, fc), bias=cb(fc)),
                                 reads=[R_pa[j], R_small], writes=[R_cbuf])
                            S.op("dve", lambda e: e.scalar_tensor_tensor(out=cbuf[:, :nb], in0=b3[:, :, 1], scalar=cw(1, fc), in1=cbuf[:, :nb], op0=ALU.mult, op1=ALU.add),
                                 reads=[R_bufT, R_small, R_cbuf], writes=[R_cbuf])
                            S.op("dve", lambda e: e.scalar_tensor_tensor(out=cbuf[:, :nb], in0=b3[:, :, 0], scalar=cw(0, fc), in1=cbuf[:, :nb], op0=ALU.mult, op1=ALU.add),
                                 reads=[R_bufT, R_small, R_cbuf], writes=[R_cbuf])
                        S.op("act", lambda e: e.activation(out=cbuf[:, :nb], in_=cbuf[:, :nb], func=AF.Silu), reads=[R_cbuf], writes=[R_cbuf])
                        S.op("dve", lambda e: e.tensor_tensor(out=mT[:, fc, c0:c0 + nb], in0=cbuf[:, :nb], in1=pg_[j][:, :nb], op=ALU.mult),
                             reads=[R_cbuf, R_pg[j]], writes=[R_mT[b]])
                S.dma("sp", lambda e: e.dma_start(out=L["o_cp"], in_=asp[0:2, :]), "f_o", reads=[R_asp])
                S.dma("sp", lambda e: e.dma_start(out=L["o_cs"][:, 1, :], in_=asp[2:6, :]), "f_o", reads=[R_asp])
                S.dma("sp", lambda e: e.dma_start(out=L["o_cs"][:, 0, :], in_=L["sconv1"]), "f_o2")
                S.barrier()
        with ExitStack() as pd_:
            wdb = sb("wdb", [128, NFC, D], BF16, pd_)
            R_wdb = Res("wdb")
            for fc in range(NFC):
                S.dma("pool", lambda e: e.dma_start(out=wdb[:, fc, :], in_=L["w_down"][fc * 128:(fc + 1) * 128, :]), "wdb", writes=[R_wdb])
            gfn = sb("gfn", [128, D], F32, pd_)
            R_gfn = Res("gfn")
            S.dma("sp", lambda e: e.dma_start(out=gfn[:], in_=AP(L["fnorm"].tensor, 0, [[0, 128], [1, D]])), "f_g", writes=[R_gfn])
            xt = [sb("dxt%d" % j, [128, D], F32, pd_) for j in range(2)]
            R_xt = [Res("dxt%d" % j) for j in range(2)]
            junk = sb("djunk", [128, D], BF16, pd_)
            R_junk = Res("djunk")
            px = [ps("dpx%d" % j, [128, 2, 512], F32, pd_) for j in range(2)]
            R_px = [Res("dpx%d" % j) for j in range(2)]
            for i, (c0, n) in enumerate(TILES):
                j = i % 2
                S.dma("sp", lambda e: e.dma_start(out=xt[j][:n, :], in_=xscr[c0:c0 + n, :]), "dxt%d" % j, writes=[R_xt[j]])
                for hf in range(2):
                    for fc in range(NFC):
                        S.op("pe", lambda e: e.matmul(px[j][:n, hf, :], lhsT=mT[:, fc, c0:c0 + n], rhs=wdb[:, fc, hf * 512:(hf + 1) * 512],
                                                      start=(fc == 0), stop=(fc == NFC - 1)), reads=R_mT + [R_wdb], writes=[R_px[j]])
                S.op("dve", lambda e: e.tensor_tensor(out=xt[j][:n, :], in0=xt[j][:n, :], in1=px[j][:n, :, :].rearrange("p a b -> p (a b)"), op=ALU.add),
                     reads=[R_xt[j], R_px[j]], writes=[R_xt[j]])
                S.op("act", lambda e: e.activation(out=junk[:n, :], in_=xt[j][:n, :], func=AF.Square, accum_out=rs2[:n, i, 0:1]),
                     reads=[R_xt[j]], writes=[R_junk, R_rs2[i]])
                S.op("dve", lambda e: e.tensor_scalar(out=rs2[:n, i, 1:2], in0=rs2[:n, i, 0:1], scalar1=1.0 / D, scalar2=EPS, op0=ALU.mult, op1=ALU.add),
                     reads=[R_rs2[i]], writes=[R_rs2[i]])
                S.op("act", lambda e: e.sqrt(out=rs2[:n, i, 2:3], in_=rs2[:n, i, 1:2]), reads=[R_rs2[i]], writes=[R_rs2[i]])
                S.op("dve", lambda e: e.reciprocal(out=rs2[:n, i, 3:4], in_=rs2[:n, i, 2:3]), reads=[R_rs2[i]], writes=[R_rs2[i]])
                S.op("dve", lambda e: e.scalar_tensor_tensor(out=xt[j][:n, :], in0=xt[j][:n, :], scalar=rs2[:n, i, 3:4], in1=gfn[:n, :], op0=ALU.mult, op1=ALU.mult),
                     reads=[R_xt[j], R_rs2[i], R_gfn], writes=[R_xt[j]])
                dstd = L["o_yp"][c0:c0 + n, :] if i < 16 else L["o_ys"]
                S.dma("sp", lambda e: e.dma_start(out=dstd, in_=xt[j][:n, :]), "dxt%d" % j, reads=[R_xt[j]])
            S.barrier()


def build_program(n_pool, stage=99):
    nc = bass.Bass("TRN2", target_bir_lowering=False)
    din = lambda name, shape, dt=F32: nc.dram_tensor(name, shape, dt, kind="ExternalInput").ap()
    dout = lambda name, shape, dt=F32: nc.dram_tensor(name, shape, dt, kind="ExternalOutput").ap()

    xp = din("xp", [SEQ, D])
    xs = din("xs", [NSC, D])
    w_in = din("w_in", [D, DPROJ])
    nmix = din("nmix", [1, D])
    ropec = din("ropec", [NCOL, 8])
    ropes = din("ropes", [NCOL, 8])
    ident = din("ident", [128, 128])

    bcol_re = din("bcol_re", [128, 16, 128])
    bcol_im = din("bcol_im", [128, 16, 128])
    cpad_re = din("cpad_re", [128, 16, 128])
    cpad_im = din("cpad_im", [128, 16, 128])
    acol_re = din("acol_re", [128, 16])
    acol_im = din("acol_im", [128, 16])
    ldt_col = din("ldt_col", [128, 16])
    dcol = din("dcol", [128, 4])
    h0c_re = din("h0c_re", [128, 16, NSC])
    h0c_im = din("h0c_im", [128, 16, NSC])
    w_glu = din("w_glu", [512, 512])
    trow1 = din("trow1", [128, 512])
    o_hre = dout("o_hre", [16, 128])
    o_him = dout("o_him", [16, 128])
    o_hres = dout("o_hres", [NSC, 16, 128])
    o_hims = dout("o_hims", [NSC, 16, 128])

    lamv = din("lamv", [1, 4 * 64])
    gsub = din("gsub", [1, 128])
    trimask = din("trimask", [128, 128])
    cache_k = din("cache_k", [n_pool * 32, 2048])
    cache_v = din("cache_v", [n_pool * 32, 2048])
    ptrep = din("ptrep", [NSC, 128, 16], I32)
    poff = din("poff", [128, NSC * 16])
    selq = din("selq", [NSC, NSC, 128])
    mask0 = din("mask0", [128, 1])
    w_o = din("w_o", [D, D])
    nffn = din("nffn", [1, D])
    w_up = din("w_up", [D, 2 * DFF])
    convw = din("convw", [128, 3, NFC])
    convb = din("convb", [128, NFC])
    w_down = din("w_down", [DFF, D])
    fnorm = din("fnorm", [1, D])
    sconv = din("sconv", [NSC * 2, DFF])
    sconv1 = din("sconv1", [NSC, DFF])
    xscr = nc.dram_tensor("xscr", [NCOL, D], F32, kind="Internal").ap()
    qscr = nc.dram_tensor("qscr", [1, NSC * 512], F32, kind="Internal").ap()
    o_yp = dout("o_yp", [SEQ, D])
    o_ys = dout("o_ys", [NSC, D])
    o_cp = dout("o_cp", [2, DFF])
    o_cs = dout("o_cs", [NSC, 2, DFF])
    o_kp = dout("o_kp", [SEQ, 512])
    o_vp = dout("o_vp", [SEQ, 512])
    o_ks = dout("o_ks", [NSC, 512])
    o_vs = dout("o_vs", [NSC, 512])

    with ExitStack() as es:
        S = Sched(nc, es)

        def sb(name, shape, dt, stack=es):
            return stack.enter_context(nc.sbuf_tensor(name, shape, dt))

        def ps(name, shape, dt, stack=es):
            return stack.enter_context(nc.psum_tensor(name, shape, dt))

        idf = sb("idf", [128, 128], F32)
        idb = sb("idb", [128, 128], BF16)
        R_id = Res("id")
        S.dma("sp", lambda e: e.dma_start(out=idf[:], in_=ident), "c_id", writes=[R_id])
        S.op("dve", lambda e: e.tensor_copy(out=idb[:], in_=idf[:]), reads=[R_id], writes=[R_id])

        ones_f = sb("ones_f", [128, 128], F32)
        S.op("pool", lambda e: e.memset(ones_f[:], 1.0), writes=[R_id])
        ropc = sb("ropc", [128, 17, 8], F32)
        rops = sb("rops", [128, 17, 8], F32)
        R_rope = Res("rope")
        for tbl, src, key in ((ropc, ropec, "c_rc"), (rops, ropes, "c_rs")):
            S.dma("sp", lambda e: e.dma_start(out=tbl[:, 0:16, :],
                                              in_=src[0:SEQ, :].rearrange("(i p) e -> p i e", p=128)),
                  key, writes=[R_rope])
            S.dma("sp", lambda e: e.dma_start(out=tbl[0:NSC, 16, :], in_=src[SEQ:NCOL, :]), key, writes=[R_rope])


        mixT = sb("mixT", [128, 8, NCOL], BF16)
        R_mix = [Res("mix%d" % c) for c in range(8)]
        vS = sb("vS", [NSC, 512], BF16)
        R_vS = Res("vS")
        ksf = sb("ksf", [NSC, 512], F32)
        qsf = sb("qsf", [NSC, 512], F32)
        R_ksf, R_qsf = Res("ksf"), Res("qsf")
        pA = es.enter_context(ExitStack())
        qT = sb("qT", [128, 4, NCOL], BF16, pA)
        kT = sb("kT", [128, 4, NCOL], BF16, pA)
        R_qT = [Res("qT%d" % i) for i in range(17)]
        R_kT = [Res("kT%d" % i) for i in range(17)]
        vA = sb("vA", [128, 16, 4, 129], BF16, pA)
        R_vA = [Res("vA%d" % i) for i in range(16)]
        uT = sb("uT", [128, 4, NCOL], BF16, pA)
        R_uT = [Res("uT%d" % i) for i in range(5)]
        rr = sb("rr", [128, 17, 4], F32, pA)
        R_rr = [Res("rr%d" % i) for i in range(17)]

        pw = pA.enter_context(ExitStack())
        hnT = sb("hnT", [128, 8, NCOL], BF16, pw)
        R_hnT = [Res("hnT%d" % i) for i in range(17)]
        winb = sb("winb", [128, 8, DPROJ], BF16, pw)
        R_winb = [Res("winb")] * 8
        with ExitStack() as p12:
            xt = [sb("xt%d" % j, [128, D], F32, p12) for j in range(2)]
            R_xt = [Res("xt%d" % j) for j in range(2)]
            xb = [sb("xb%d" % j, [128, D], BF16, p12) for j in range(2)]
            R_xb = [Res("xb%d" % j) for j in range(2)]
            junk = sb("junk", [128, D], BF16, p12)
            R_junk = Res("junk")
            gmx = sb("gmx", [128, D], F32, p12)
            R_gmx = Res("gmx")
            S.dma("sp", lambda e: e.dma_start(out=gmx[:], in_=AP(nmix.tensor, 0, [[0, 128], [1, D]])), "c_nm", writes=[R_gmx])
            ptr = [ps("ptr%d" % j, [128, 8, 128], BF16, p12) for j in range(2)]
            R_ptr = [Res("ptr%d" % j) for j in range(2)]

            for kc in range(8):
                S.dma("pool", lambda e: e.dma_start(out=winb[:, kc, :], in_=w_in[kc * 128:(kc + 1) * 128, :]),
                      "winb", writes=[R_winb[kc]])

            for i, (c0, n) in enumerate(TILES):
                j = i % 2
                src = xp[c0:c0 + n, :] if i < 16 else xs
                S.dma("sp", lambda e: e.dma_start(out=xt[j][:n, :], in_=src), "xt%d" % j, writes=[R_xt[j]])
                S.op("act", lambda e: e.activation(out=junk[:n, :], in_=xt[j][:n, :], func=AF.Square,
                                                   accum_out=rr[:n, i, 0:1]),
                     reads=[R_xt[j]], writes=[R_junk, R_rr[i]])
                S.op("dve", lambda e: e.tensor_scalar(out=rr[:n, i, 1:2], in0=rr[:n, i, 0:1], scalar1=1.0 / D,
                                                      scalar2=EPS, op0=ALU.mult, op1=ALU.add),
                     reads=[R_rr[i]], writes=[R_rr[i]])
                S.op("act", lambda e: e.sqrt(out=rr[:n, i, 2:3], in_=rr[:n, i, 1:2]), reads=[R_rr[i]], writes=[R_rr[i]])
                S.op("dve", lambda e: e.reciprocal(out=rr[:n, i, 3:4], in_=rr[:n, i, 2:3]),
                     reads=[R_rr[i]], writes=[R_rr[i]])
                S.op("dve", lambda e: e.scalar_tensor_tensor(out=xb[j][:n, :], in0=xt[j][:n, :], scalar=rr[:n, i, 3:4], in1=gmx[:n, :],
                                                             op0=ALU.mult, op1=ALU.mult),
                     reads=[R_xt[j], R_rr[i], R_gmx], writes=[R_xb[j]])
                for kc in range(8):
                    S.op("pe", lambda e: e.transpose(out=ptr[j][:, kc, :n], in_=xb[j][:n, kc * 128:(kc + 1) * 128],
                                                     identity=idb[:n, :n]),
                         reads=[R_xb[j], R_id], writes=[R_ptr[j]])
                S.op("dve", lambda e: e.tensor_copy(out=hnT[:, :, c0:c0 + n], in_=ptr[j][:, :, :n]),
                     reads=[R_ptr[j]], writes=[R_hnT[i]])

        S.barrier()
        if stage <= 0:
            S.finish("sp")
            return nc
        with ExitStack() as p2:
            pq = [ps("pq%d" % j, [128, 512], F32, p2) for j in range(2)]
            pk = [ps("pk%d" % j, [128, 512], F32, p2) for j in range(2)]
            pv = [ps("pv%d" % j, [128, 512], F32, p2) for j in range(2)]
            R_pq = [Res("pq%d" % j) for j in range(2)]
            R_pk = [Res("pk%d" % j) for j in range(2)]
            R_pv = [Res("pv%d" % j) for j in range(2)]
            ptq = ps("ptq", [128, 2, 4, 128], BF16, p2)
            R_ptq = [Res("ptq"), Res("ptk")]
            qf = [sb("qf%d" % j, [128, 512], F32, p2) for j in range(2)]
            kf = [sb("kf%d" % j, [128, 512], F32, p2) for j in range(2)]
            vf = [sb("vf%d" % j, [128, 512], F32, p2) for j in range(2)]
            R_qf = [Res("qf%d" % j) for j in range(2)]
            R_kf = [Res("kf%d" % j) for j in range(2)]
            R_vf = [Res("vf%d" % j) for j in range(2)]
            qb = sb("qb", [128, 512], BF16, p2)
            kb = sb("kb", [128, 512], BF16, p2)
            R_qb, R_kb = Res("qb"), Res("kb")
            rtmp = sb("rtmp", [128, 4, 8, 8], F32, p2)
            R_rtmp = Res("rtmp")

            S.op("pool", lambda e: e.memset(vA[:, :, :, 128:129], 1.0), writes=R_vA)

            def rope(buf, n, i, R_buf):
                x1 = AP(buf, 0, [[512, n], [64, 8], [1, 8]])
                x2 = AP(buf, 8, [[512, n], [64, 8], [1, 8]])
                cs = AP(ropc, i * 8, [[17 * 8, n], [0, 8], [1, 8]])
                sn = AP(rops, i * 8, [[17 * 8, n], [0, 8], [1, 8]])
                for t, (a, b) in enumerate(((x1, cs), (x2, sn), (x2, cs), (x1, sn))):
                    S.op("dve", lambda e: e.tensor_tensor(out=rtmp[:n, t], in0=a, in1=b, op=ALU.mult),
                         reads=[R_buf, R_rope], writes=[R_rtmp])
                S.op("dve", lambda e: e.tensor_tensor(out=x1, in0=rtmp[:n, 0], in1=rtmp[:n, 1], op=ALU.subtract),
                     reads=[R_rtmp], writes=[R_buf])
                S.op("dve", lambda e: e.tensor_tensor(out=x2, in0=rtmp[:n, 2], in1=rtmp[:n, 3], op=ALU.add),
                     reads=[R_rtmp], writes=[R_buf])

            for i, (c0, n) in enumerate(TILES):
                j = i % 2
                for cb, (pX, R_pX) in enumerate(((pq, R_pq), (pk, R_pk), (pv, R_pv))):
                    for kc in range(8):
                        S.op("pe", lambda e: e.matmul(pX[j][:n, :], lhsT=hnT[:, kc, c0:c0 + n],
                                                      rhs=winb[:, kc, cb * 512:(cb + 1) * 512],
                                                      start=(kc == 0), stop=(kc == 7)),
                             reads=[R_hnT[i], R_winb[kc]], writes=[R_pX[j]])
                S.op("act", lambda e: e.activation(out=vf[j][:n, :], in_=pv[j][:n, :], func=AF.Copy),
                     reads=[R_pv[j]], writes=[R_vf[j]])
                if i < 16:
                    S.dma("sp", lambda e: e.dma_start(out=o_vp[c0:c0 + n, :], in_=vf[j][:n, :]), "vf%d" % j,
                          reads=[R_vf[j]])
                    S.op("pool", lambda e: e.tensor_copy(out=vA[:, i, :, 0:128],
                                                         in_=vf[j][:, :].rearrange("p (h e) -> p h e", h=4)),
                         reads=[R_vf[j]], writes=[R_vA[i]])
                else:
                    S.dma("sp", lambda e: e.dma_start(out=o_vs, in_=vf[j][:n, :]), "vf%d" % j, reads=[R_vf[j]])
                    S.op("pool", lambda e: e.tensor_copy(out=vS[:, :], in_=vf[j][:n, :]),
                         reads=[R_vf[j]], writes=[R_vS])
                if stage < 2:
                    continue
                S.op("dve", lambda e: e.tensor_copy(out=qf[j][:n, :], in_=pq[j][:n, :]),
                     reads=[R_pq[j]], writes=[R_qf[j]])
                rope(qf[j], n, i, R_qf[j])
                S.op("act", lambda e: e.activation(out=qb[:n, :], in_=qf[j][:n, :], func=AF.Copy, scale=0.125),
                     reads=[R_qf[j]], writes=[R_qb])
                if i == 16:
                    S.op("pool", lambda e: e.tensor_copy(out=qsf[:, :], in_=qf[j][:n, :]),
                         reads=[R_qf[j]], writes=[R_qsf])
                for h in range(4):
                    S.op("pe", lambda e: e.transpose(out=ptq[:, 0, h, :n], in_=qb[:n, h * 128:(h + 1) * 128],
                                                     identity=idb[:n, :n]),
                         reads=[R_qb, R_id], writes=[R_ptq[0]])
                S.op("act", lambda e: e.activation(out=qT[:, :, c0:c0 + n], in_=ptq[:, 0, :, :n], func=AF.Copy),
                     reads=[R_ptq[0]], writes=[R_qT[i]])
                if stage < 3:
                    continue
                S.op("dve", lambda e: e.tensor_copy(out=kf[j][:n, :], in_=pk[j][:n, :]),
                     reads=[R_pk[j]], writes=[R_kf[j]])
                rope(kf[j], n, i, R_kf[j])
                if i < 16:
                    S.dma("sp", lambda e: e.dma_start(out=o_kp[c0:c0 + n, :], in_=kf[j][:n, :]), "kf%d" % j,
                          reads=[R_kf[j]])
                else:
                    S.dma("sp", lambda e: e.dma_start(out=o_ks, in_=kf[j][:n, :]), "kf%d" % j, reads=[R_kf[j]])
                    S.op("pool", lambda e: e.tensor_copy(out=ksf[:, :], in_=kf[j][:n, :]),
                         reads=[R_kf[j]], writes=[R_ksf])
                S.op("act", lambda e: e.activation(out=kb[:n, :], in_=kf[j][:n, :], func=AF.Copy),
                     reads=[R_kf[j]], writes=[R_kb])
                for h in range(4):
                    S.op("pe", lambda e: e.transpose(out=ptq[:, 1, h, :n], in_=kb[:n, h * 128:(h + 1) * 128],
                                                     identity=idb[:n, :n]),
                         reads=[R_kb, R_id], writes=[R_ptq[1]])
                S.op("act", lambda e: e.activation(out=kT[:, :, c0:c0 + n], in_=ptq[:, 1, :, :n], func=AF.Copy),
                     reads=[R_ptq[1]], writes=[R_kT[i]])

            for cu in range(4 if stage >= 4 else 0):
                for bi, (c0, nb) in enumerate(BLOCKS):
                    j = (cu * 5 + bi) % 2
                    for kc in range(8):
                        S.op("pe", lambda e: e.matmul(pq[j][:, :nb], lhsT=winb[:, kc, 1536 + cu * 128:1536 + (cu + 1) * 128],
                                                      rhs=hnT[:, kc, c0:c0 + nb], start=(kc == 0), stop=(kc == 7)),
                             reads=R_hnT + [R_winb[kc]], writes=[R_pq[j]])
                    S.op("act", lambda e: e.activation(out=uT[:, cu, c0:c0 + nb], in_=pq[j][:, :nb], func=AF.Copy),
                         reads=[R_pq[j]], writes=[R_uT[bi]])

        pw.close()
        S.barrier()
        if stage >= 5:
            ssm_phase(nc, S, sb, ps, locals())
        if stage >= 6:
            attn_phase(nc, S, sb, ps, locals())
        pA.close()
        S.barrier()
        if stage >= 7:
            ffn_phase(nc, S, sb, ps, locals())

        S.finish("sp")
        print("insts", S.n_inst, "waits", S.n_wait)
    return nc


def _rope_tables():
    half = 8
    inv = (500000.0 ** (-np.arange(0, 16, 2, dtype=np.float32) / np.float32(16))).astype(np.float32)
    pos = np.concatenate([np.arange(SEQ), np.full(NSC, PAST)]).astype(np.float32)
    ang = (pos[:, None] * inv[None, :]).astype(np.float32)
    return np.cos(ang).astype(np.float32), np.sin(ang).astype(np.float32)


def _ssm_maps(inp, c):
    f = np.float32
    a_re, a_im = inp["ssm_a_re"][0], inp["ssm_a_im"][0]
    ldt = inp["ssm_log_dt"][0]
    b_re, b_im = inp["ssm_b_re"][0], inp["ssm_b_im"][0]
    c_re, c_im = inp["ssm_c_re"][0], inp["ssm_c_im"][0]
    bcol = [np.zeros((128, 16, 128), f) for _ in range(2)]
    cpad = [np.zeros((128, 16, 128), f) for _ in range(2)]
    for g in range(32):
        pr, g2 = g // 2, g % 2
        pl = pr % 4
        r0 = pl * 32 + g2 * 16
        for k, (bb, cc) in enumerate(((b_re, c_re), (b_im, c_im))):
            bcol[k][g2 * 64:(g2 + 1) * 64, pr, r0:r0 + 16] = bb[g]
            cpad[k][g2 * 64:(g2 + 1) * 64, pr, r0:r0 + 16] = cc[g].T
    col = lambda a: np.ascontiguousarray(a.reshape(16, 128).T)
    h0 = lambda a: np.ascontiguousarray(a[0, NSC * c:NSC * (c + 1)].reshape(NSC, 16, 128).transpose(2, 1, 0))
    return dict(
        bcol_re=bcol[0], bcol_im=bcol[1], cpad_re=cpad[0], cpad_im=cpad[1],
        acol_re=col(a_re), acol_im=col(a_im), ldt_col=col(np.repeat(ldt[:, None], 64, axis=1)),
        dcol=np.ascontiguousarray(inp["ssm_d"][0].reshape(4, 128).T),
        h0c_re=h0(inp["state_ssm_re"]), h0c_im=h0(inp["state_ssm_im"]),
        w_glu=np.ascontiguousarray(inp["w_glu"][0]),
        trow1=np.ascontiguousarray(np.broadcast_to(np.arange(1, 513, dtype=f)[None, :], (128, 512))),
    )


def _attn_maps(inp, c, dev_pool=None):
    f = np.float32
    lamv = np.concatenate([inp["lambda_q1"][0], inp["lambda_k1"][0], inp["lambda_q2"][0], inp["lambda_k2"][0]]).reshape(1, 256)
    kk, qq = np.meshgrid(np.arange(128), np.arange(128), indexing="ij")
    tri = (qq >= kk).astype(f)
    pt = inp["page_table"][NSC * c:NSC * (c + 1)]
    p = np.arange(128)
    ptrep = np.stack([pt[:, 4 * jj + p // 32] for jj in range(16)], axis=2).astype(np.int32)
    if dev_pool == "compact":
        used = pt.reshape(-1)
        ck = np.ascontiguousarray(inp["cache_k"][0][used]).reshape(-1, 2048)
        cv = np.ascontiguousarray(inp["cache_v"][0][used]).reshape(-1, 2048)
        pt = np.arange(used.size).reshape(pt.shape)
        ptrep = np.stack([pt[:, 4 * jj + p // 32] for jj in range(16)], axis=2).astype(np.int32)
    elif dev_pool is not None:
        ck = np.zeros((dev_pool * 32, 2048), f)
        cv = np.zeros((dev_pool * 32, 2048), f)
        ptrep = ptrep % dev_pool
    else:
        ck = inp["cache_k"][0].reshape(-1, 2048)
        cv = inp["cache_v"][0].reshape(-1, 2048)
    selq = np.zeros((NSC, NSC, 128), f)
    for s_ in range(NSC):
        selq[s_, s_, :] = 1.0
    m0 = np.zeros((128, 1), f)
    m0[0, 0] = 1.0
    return dict(lamv=lamv.astype(f), gsub=inp["subln_g"][0].reshape(1, 128), trimask=tri, cache_k=ck, cache_v=cv,
                ptrep=np.ascontiguousarray(ptrep), poff=np.ascontiguousarray(np.broadcast_to((p % 32).astype(np.float32)[:, None], (128, NSC * 16))), selq=selq, mask0=m0)


def _ffn_maps(inp, c):
    sc = inp["state_conv"][0, NSC * c:NSC * (c + 1)]
    return dict(
        w_o=np.ascontiguousarray(inp["w_o"][0]),
        nffn=np.ascontiguousarray(inp["norm_ffn"][0].reshape(1, D)),
        w_up=np.ascontiguousarray(inp["w_up"][0]),
        convw=np.ascontiguousarray(inp["conv_w"][0].reshape(3, NFC, 128).transpose(2, 0, 1)),
        convb=np.ascontiguousarray(inp["conv_b"][0].reshape(NFC, 128).T),
        w_down=np.ascontiguousarray(inp["w_down"][0]),
        fnorm=inp["final_norm"].reshape(1, D),
        sconv=np.ascontiguousarray(sc.reshape(NSC * 2, DFF)),
        sconv1=np.ascontiguousarray(sc[:, 1, :]),
    )


def make_in_maps(inp, cores, dev_pool=None):
    cosT, sinT = _rope_tables()
    ident = np.eye(128, dtype=np.float32)
    maps = []
    for c in cores:
        m = dict(
            xp=np.ascontiguousarray(inp["x_prompt"][c]),
            xs=np.ascontiguousarray(inp["x_sample"][NSC * c:NSC * (c + 1), 0, :]),
            w_in=np.ascontiguousarray(inp["w_in"][0]),
            nmix=np.ascontiguousarray(inp["norm_mix"][0].reshape(1, D)),
            ropec=cosT, ropes=sinT, ident=ident,
        )
        m.update(_ssm_maps(inp, c))
        m.update(_attn_maps(inp, c, dev_pool))
        m.update(_ffn_maps(inp, c))
        maps.append(m)
    return maps


def kernel(**inp):
    inp = {k: np.asarray(v) for k, v in inp.items()}
    n_pool = inp["cache_k"].shape[1]
    nc = build_program(n_pool)
    cores = list(range(8))
    res = run_bass_kernel_spmd(nc, make_in_maps(inp, cores), core_ids=cores)
    r = res.results
    f = np.float32
    cat = lambda key: np.stack([np.asarray(r[c][key], dtype=f) for c in cores])
    y_prompt = cat("o_yp").reshape(8, SEQ, D)
    y_sample = cat("o_ys").reshape(32, 1, D)
    k_prompt = cat("o_kp").reshape(1, 8, SEQ, 4, 128)
    v_prompt = cat("o_vp").reshape(1, 8, SEQ, 4, 128)
    hre_p = cat("o_hre").reshape(1, 8, 32, 64)
    him_p = cat("o_him").reshape(1, 8, 32, 64)
    conv_p = cat("o_cp").reshape(1, 8, 2, DFF)
    k_sample = cat("o_ks").reshape(1, 32, 1, 4, 128)
    v_sample = cat("o_vs").reshape(1, 32, 1, 4, 128)
    hre_s = cat("o_hres").reshape(1, 32, 32, 64)
    him_s = cat("o_hims").reshape(1, 32, 32, 64)
    conv_s = cat("o_cs").reshape(1, 32, 2, DFF)
    return (y_prompt, y_sample, k_prompt, v_prompt, hre_p, him_p, conv_p,
            k_sample, v_sample, hre_s, him_s, conv_s)
```

```python
import numpy as np
from contextlib import ExitStack
import concourse.bass as bass
import concourse.mybir as mybir
from concourse.bass_utils import run_bass_kernel_spmd

F32 = mybir.dt.float32
BF16 = mybir.dt.bfloat16
I32 = mybir.dt.int32
ALU = mybir.AluOpType
AF = mybir.ActivationFunctionType
AX = mybir.AxisListType

D = 1024
SEQ = 2048
NSC = 4
NCOL = SEQ + NSC
DPROJ = 2048
DFF = 2816
NFC = DFF // 128
PAGE = 128
NPAGES = 64
PAST = PAGE * NPAGES
EPS = 1e-6
SUBLN_EPS = 1e-5
LAM_INIT = 0.8 - 0.6 * 1.0
SELF_ORDERED = ("pe",)
TILES = [(128 * i, 128) for i in range(16)] + [(SEQ, NSC)]
BLOCKS = [(512 * i, 512) for i in range(4)] + [(SEQ, NSC)]


class Res:
    __slots__ = ("name", "w", "r")

    def __init__(self, name):
        self.name = name
        self.w = None
        self.r = []


class _Eng:
    def __init__(self, name, eng, sem):
        self.name = name
        self.eng = eng
        self.sem = sem
        self.count = 0
        self.seen = {}


class Sched:
    def __init__(self, nc, es):
        self.nc = nc
        self.es = es
        self.sems = {}
        self.engs = {}
        for name, eng in (("pe", nc.tensor), ("dve", nc.vector), ("act", nc.scalar),
                          ("pool", nc.gpsimd), ("sp", nc.sync)):
            sem = es.enter_context(nc.semaphore("sem_" + name))
            self.sems[name] = sem
            self.engs[name] = _Eng(name, eng, sem)
        self.dma_sems = {}
        self.n_inst = 0
        self.n_wait = 0

    def _dma_sem(self, key):
        if key not in self.dma_sems:
            sem = self.es.enter_context(self.nc.semaphore("dsem_" + key))
            self.sems["d:" + key] = sem
            self.dma_sems[key] = [sem, 0]
        return self.dma_sems[key]

    @staticmethod
    def _deps(reads, writes):
        deps = {}

        def add(d):
            if d is not None and deps.get(d[0], 0) < d[1]:
                deps[d[0]] = d[1]
        for r in reads:
            add(r.w)
        for w in writes:
            add(w.w)
            for rd in w.r:
                add(rd)
        return deps

    def _wait(self, E, deps):
        for k, v in deps.items():
            if E.seen.get(k, 0) >= v:
                continue
            E.eng.wait_ge(self.sems[k], v)
            E.seen[k] = v
            self.n_wait += 1

    @staticmethod
    def _mark(reads, writes, stamp):
        for r in reads:
            r.r.append(stamp)
        for w in writes:
            w.w = stamp
            w.r = []

    def op(self, ename, fn, reads=(), writes=()):
        E = self.engs[ename]
        deps = self._deps(reads, writes)
        if ename in SELF_ORDERED:
            deps.pop(ename, None)
        self._wait(E, deps)
        ins = fn(E.eng)
        E.count += 1
        ins.then_inc(E.sem, 1)
        self._mark(reads, writes, (ename, E.count))
        self.n_inst += 1
        return ins

    def dma(self, ename, fn, skey, reads=(), writes=()):
        E = self.engs[ename]
        self._wait(E, self._deps(reads, writes))
        ds = self._dma_sem(skey)
        ins = fn(E.eng)
        ds[1] += 16
        ins.then_inc(ds[0], 16)
        self._mark(reads, writes, ("d:" + skey, ds[1]))
        self.n_inst += 1
        return ins

    def barrier(self):
        deps = {n: e.count for n, e in self.engs.items() if e.count}
        for k, (sem, c) in self.dma_sems.items():
            if c:
                deps["d:" + k] = c
        for E in self.engs.values():
            d = dict(deps)
            self._wait(E, d)

    def finish(self, ename="sp"):
        E = self.engs[ename]
        deps = {n: e.count for n, e in self.engs.items() if e.count}
        for k, (sem, c) in self.dma_sems.items():
            if c:
                deps["d:" + k] = c
        self._wait(E, deps)


def AP(t, off, dims):
    return bass.AP(t, off, dims)


import math
PI = math.pi


def range_reduce(S, eng, out, a, ki, kf, m, n_part, R_out, R_tmp, reads):
    S.op(eng, lambda e: e.tensor_scalar(out=ki, in0=a, scalar1=1.0 / (2 * PI), scalar2=None, op0=ALU.mult),
         reads=reads, writes=[R_tmp])
    S.op(eng, lambda e: e.tensor_copy(out=kf, in_=ki), reads=[R_tmp], writes=[R_tmp])
    S.op("dve", lambda e: e.scalar_tensor_tensor(out=out, in0=kf, scalar=-2 * PI, in1=a, op0=ALU.mult, op1=ALU.add),
         reads=[R_tmp] + list(reads), writes=[R_out])
    for thr, corr in ((PI, -2 * PI),):
        S.op(eng, lambda e: e.tensor_scalar(out=m, in0=out, scalar1=thr, scalar2=corr, op0=ALU.is_gt, op1=ALU.mult),
             reads=[R_out], writes=[R_tmp])
        S.op(eng, lambda e: e.tensor_tensor(out=out, in0=out, in1=m, op=ALU.add), reads=[R_out, R_tmp], writes=[R_out])
    S.op(eng, lambda e: e.tensor_scalar(out=m, in0=out, scalar1=-PI, scalar2=2 * PI, op0=ALU.is_lt, op1=ALU.mult),
         reads=[R_out], writes=[R_tmp])
    S.op(eng, lambda e: e.tensor_tensor(out=out, in0=out, in1=m, op=ALU.add), reads=[R_out, R_tmp], writes=[R_out])


def sincos(S, eng, s_out, c_out, ang, scr, n_part, R_s, R_c, R_scr, reads, nbias):
    range_reduce(S, eng, scr["y"], ang, scr["ki"], scr["kf"], scr["m"], n_part, R_scr, R_scr, reads)
    S.op("act", lambda e: e.activation(out=s_out, in_=scr["y"], func=AF.Sin), reads=[R_scr], writes=[R_s])
    S.op(eng, lambda e: e.tensor_scalar(out=scr["a"], in0=ang, scalar1=PI / 2, scalar2=None, op0=ALU.add),
         reads=list(reads) + [R_scr], writes=[R_scr])
    range_reduce(S, eng, scr["y"], scr["a"], scr["ki"], scr["kf"], scr["m"], n_part, R_scr, R_scr, [R_scr])
    S.op("act", lambda e: e.activation(out=c_out, in_=scr["y"], func=AF.Sin), reads=[R_scr], writes=[R_c])


def ssm_phase(nc, S, sb, ps, L):
    uT, R_uT, mixT, R_mix, idf, R_id = L["uT"], L["R_uT"], L["mixT"], L["R_mix"], L["idf"], L["R_id"]
    stage = L["stage"]
    with ExitStack() as pz:
        col = sb("col", [128, 26, 16], F32, pz)
        R_col = Res("col")
        dco = sb("dco", [128, 4], F32, pz)
        h0r = sb("h0r", [128, 16, NSC], F32, pz)
        h0i = sb("h0i", [128, 16, NSC], F32, pz)
        R_h0 = Res("h0")
        tr1 = sb("tr1", [128, 512], F32, pz)
        R_tr1 = Res("tr1")
        wgb = sb("wgb", [128, 4, 512], BF16, pz)
        R_wgb = Res("wgb")
        ygT = sb("ygT", [128, 4, NCOL], BF16, pz)
        R_yg = [Res("yg%d" % b) for b in range(5)]
        hfin = sb("hfin", [128, 2, 16], F32, pz)
        R_hfin = Res("hfin")
        hsmp = sb("hsmp", [128, 2, NSC, 16], F32, pz)
        R_hsmp = Res("hsmp")
        S.dma("sp", lambda e: e.dma_start(out=dco[:], in_=L["dcol"]), "z_c", writes=[R_col])
        S.dma("sp", lambda e: e.dma_start(out=col[:, 0, :], in_=L["acol_re"]), "z_c", writes=[R_col])
        S.dma("sp", lambda e: e.dma_start(out=col[:, 1, :], in_=L["acol_im"]), "z_c", writes=[R_col])
        S.dma("sp", lambda e: e.dma_start(out=col[:, 2, :], in_=L["ldt_col"]), "z_c", writes=[R_col])
        S.dma("sp", lambda e: e.dma_start(out=h0r[:], in_=L["h0c_re"]), "z_h", writes=[R_h0])
        S.dma("sp", lambda e: e.dma_start(out=h0i[:], in_=L["h0c_im"]), "z_h", writes=[R_h0])
        S.dma("sp", lambda e: e.dma_start(out=tr1[:], in_=L["trow1"]), "z_t", writes=[R_tr1])

        with ExitStack() as pc:
            cki = sb("c_ki", [128, 16], I32, pc)
            cs = {k: sb("c_" + k, [128, 16], F32, pc) for k in ("a", "kf", "m", "y")}
            S.op("act", lambda e: e.activation(out=col[:, 2, :], in_=col[:, 2, :], func=AF.Exp), reads=[R_col], writes=[R_col])
            S.op("dve", lambda e: e.tensor_tensor(out=col[:, 3, :], in0=col[:, 1, :], in1=col[:, 2, :], op=ALU.mult), reads=[R_col], writes=[R_col])
            S.op("dve", lambda e: e.tensor_tensor(out=col[:, 9, :], in0=col[:, 0, :], in1=col[:, 2, :], op=ALU.mult), reads=[R_col], writes=[R_col])
            S.op("act", lambda e: e.activation(out=col[:, 4, :], in_=col[:, 9, :], func=AF.Exp), reads=[R_col], writes=[R_col])
            scr = dict(a=cs["a"][:], ki=cki[:], kf=cs["kf"][:], m=cs["m"][:], y=cs["y"][:])
            sincos(S, "dve", col[:, 5, :], col[:, 6, :], col[:, 3, :], scr, 128, R_col, R_col, R_col, [R_col], None)
            S.op("dve", lambda e: e.tensor_tensor(out=col[:, 7, :], in0=col[:, 6, :], in1=col[:, 4, :], op=ALU.mult), reads=[R_col], writes=[R_col])
            S.op("dve", lambda e: e.tensor_tensor(out=col[:, 8, :], in0=col[:, 5, :], in1=col[:, 4, :], op=ALU.mult), reads=[R_col], writes=[R_col])

        cr = lambda k: col[:, k, :]
        tt_ = lambda o, x, y, op: S.op("dve", lambda e: e.tensor_tensor(out=cr(o), in0=cr(x), in1=cr(y), op=op), reads=[R_col], writes=[R_col])
        S.op("dve", lambda e: e.tensor_scalar(out=cr(9), in0=cr(7), scalar1=-1.0, scalar2=None, op0=ALU.add), reads=[R_col], writes=[R_col])
        tt_(11, 0, 0, ALU.mult); tt_(12, 1, 1, ALU.mult); tt_(10, 11, 12, ALU.add)
        S.op("dve", lambda e: e.reciprocal(out=cr(10), in_=cr(10)), reads=[R_col], writes=[R_col])
        tt_(11, 9, 0, ALU.mult); tt_(12, 8, 1, ALU.mult); tt_(11, 11, 12, ALU.add); tt_(13, 11, 10, ALU.mult)
        tt_(11, 8, 0, ALU.mult); tt_(12, 9, 1, ALU.mult); tt_(11, 11, 12, ALU.subtract); tt_(14, 11, 10, ALU.mult)

        def cmul(o_re, o_im, a_re, a_im, b_re, b_im):
            tt_(11, a_re, b_re, ALU.mult); tt_(12, a_im, b_im, ALU.mult); tt_(o_re, 11, 12, ALU.subtract)
            tt_(11, a_re, b_im, ALU.mult); tt_(12, a_im, b_re, ALU.mult); tt_(o_im, 11, 12, ALU.add)
        cmul(16, 17, 7, 8, 7, 8)
        cmul(18, 19, 16, 17, 7, 8)
        cmul(20, 21, 16, 17, 16, 17)
        S.op("dve", lambda e: e.tensor_scalar(out=cr(22), in0=cr(3), scalar1=4.0, scalar2=None, op0=ALU.mult), reads=[R_col], writes=[R_col])
        tt_(24, 4, 4, ALU.mult); tt_(23, 24, 24, ALU.mult)
        LBK = {1: (7, 8), 2: (16, 17), 3: (18, 19), 4: (20, 21)}
        for kc in range(4):
            S.dma("pool", lambda e: e.dma_start(out=wgb[:, kc, :], in_=L["w_glu"][kc * 128:(kc + 1) * 128, :]), "z_w", writes=[R_wgb])
        S.barrier()

        with ExitStack() as pm:
            wB = [sb("wB%d" % c, [128, 4, 4, 128], BF16, pm) for c in range(2)]
            wC = [sb("wC%d" % c, [128, 4, 5, 128], BF16, pm) for c in range(2)]
            KK = sb("KK", [128, 4, 128], BF16, pm)
            R_wB, R_wC, R_KK = Res("wB"), Res("wC"), Res("KK")
            braw2 = sb("braw2", [128, 2, 4, 128], F32, pm)
            craw2 = sb("craw2", [128, 2, 4, 128], F32, pm)
            bbf2 = sb("bbf2", [128, 2, 4, 128], F32, pm)
            tq2 = [sb("tq2_%d" % c, [128, 2, 4, 128], F32, pm) for c in range(2)]
            braw = [braw2[:, 0], braw2[:, 1]]
            craw = [craw2[:, 0], craw2[:, 1]]
            bbf = [bbf2[:, 0], bbf2[:, 1]]
            pcb = [[sb("pcb%d%d" % (k, c), [128, 4, 128], BF16, pm) for c in range(2)] for k in range(4)]
            R_raw, R_bbf, R_tq = Res("raw"), Res("bbf"), Res("tq")
            R_pcb = [Res("pcb%d" % k) for k in range(4)]
            ptb = ps("r_ptb", [128, 4, 128], BF16, pm)
            pkk = ps("r_pkk", [128, 128], F32, pm)
            Rp, Rpk = Res("ptb"), Res("pkk")
            ecs = sb("ecs", [128, 4, 2, 128], F32, pm)
            esc2 = sb("esc2", [128, 4, 2, 128], F32, pm)
            etc = ecs[:, :, 0, :]
            ets = ecs[:, :, 1, :]
            R_et = [Res("et%d" % k) for k in range(4)]
            eki = sb("e_ki", [128, 128], I32, pm)
            pbb = [ps("pbb%d" % j, [128, 2, 128], F32, pm) for j in range(2)]
            pbr = [pbb[j][:, 0, :] for j in range(2)]
            pbi = [pbb[j][:, 1, :] for j in range(2)]
            R_pb = [Res("pb%d" % j) for j in range(2)]
            py = [ps("py%d" % j, [128, 512], F32, pm) for j in range(2)]
            R_py = [Res("py%d" % j) for j in range(2)]
            ttA = sb("ttA", [128, 2, 128], F32, pm)
            ttB = sb("ttB", [128, 2, 128], F32, pm)
            tt = [ttA[:, 0, :], ttA[:, 1, :], ttB[:, 0, :], ttB[:, 1, :]]
            R_tt = Res("tt")
            w_r = [sb("w_r0", [128, 128], F32, pm)] * 2
            w_i = [sb("w_i0", [128, 128], F32, pm)] * 2
            R_w = [Res("w0")] * 2
            gg = [sb("gg%d" % j, [128, 2, 128], F32, pm) for j in range(2)]
            g_r = [gg[j][:, 0, :] for j in range(2)]
            g_i = [gg[j][:, 1, :] for j in range(2)]
            R_g = [Res("g%d" % j) for j in range(2)]
            uuA = sb("uuA", [128, 2, 128], F32, pm)
            uu = [uuA[:, 0, :], uuA[:, 1, :]]
            R_uu = Res("uu")
            hr = [sb("hr%d" % j, [128, 128], F32, pm) for j in range(2)]
            hi = [sb("hi%d" % j, [128, 128], F32, pm) for j in range(2)]
            R_h = [Res("h%d" % j) for j in range(2)]
            hrb = [sb("hrb%d" % j, [128, 128], BF16, pm) for j in range(2)]
            hib = [sb("hib%d" % j, [128, 128], BF16, pm) for j in range(2)]
            R_hb = [Res("hb%d" % j) for j in range(2)]
            yf = sb("yf", [128, 512], F32, pm)
            R_yf = Res("yf")
            carry = sb("carry", [128, 2, 16], F32, pm)
            R_carry = [Res("carry%d" % p) for p in range(16)]
            S.op("dve", lambda e: e.memset(carry[:], 0.0), writes=R_carry)

            def prep(q):
                qs = slice(4 * q, 4 * q + 4)
                cb = lambda row: col[:, row, qs].unsqueeze(2).to_broadcast([128, 4, 128])
                S.dma("sp", lambda e: e.dma_start(out=braw[0], in_=L["bcol_re"][:, qs, :]), "z_b", writes=[R_raw])
                S.dma("sp", lambda e: e.dma_start(out=braw[1], in_=L["bcol_im"][:, qs, :]), "z_b", writes=[R_raw])
                S.dma("sp", lambda e: e.dma_start(out=craw[0], in_=L["cpad_re"][:, qs, :]), "z_b", writes=[R_raw])
                S.dma("sp", lambda e: e.dma_start(out=craw[1], in_=L["cpad_im"][:, qs, :]), "z_b", writes=[R_raw])

                cb2 = lambda row: col[:, row, qs].unsqueeze(1).unsqueeze(3).to_broadcast([128, 2, 4, 128])

                def cplx(o_re, o_im, R_o, X2, R_x, rr_, ri_, neg_im=False):
                    S.op("dve", lambda e: e.tensor_tensor(out=tq2[0][:], in0=X2[:], in1=cb2(rr_), op=ALU.mult), reads=[R_x, R_col], writes=[R_tq])
                    S.op("dve", lambda e: e.tensor_tensor(out=tq2[1][:], in0=X2[:], in1=cb2(ri_), op=ALU.mult), reads=[R_x, R_col], writes=[R_tq])
                    S.op("dve", lambda e: e.tensor_tensor(out=o_re, in0=tq2[0][:, 0], in1=tq2[1][:, 1], op=ALU.subtract), reads=[R_tq], writes=[R_o])
                    if neg_im:
                        S.op("dve", lambda e: e.scalar_tensor_tensor(out=o_im, in0=tq2[0][:, 1], scalar=-1.0, in1=tq2[1][:, 0], op0=ALU.mult, op1=ALU.subtract),
                             reads=[R_tq], writes=[R_o])
                    else:
                        S.op("dve", lambda e: e.tensor_tensor(out=o_im, in0=tq2[0][:, 1], in1=tq2[1][:, 0], op=ALU.add), reads=[R_tq], writes=[R_o])
                cplx(bbf[0], bbf[1], R_bbf, braw2, R_raw, 13, 14)
                S.op("act", lambda e: e.activation(out=pcb[0][0][:], in_=bbf[0], func=AF.Copy), reads=[R_bbf], writes=[R_pcb[0]])
                S.op("act", lambda e: e.activation(out=pcb[0][1][:], in_=bbf[1], func=AF.Copy), reads=[R_bbf], writes=[R_pcb[0]])
                for k in (1, 2, 3):
                    cplx(pcb[k][0][:], pcb[k][1][:], R_pcb[k], bbf2, R_bbf, *LBK[k])
                for c in range(2):
                    for k in range(4):
                        for pl in range(4):
                            S.op("pe", lambda e: e.transpose(out=ptb[:, pl, :], in_=pcb[k][c][:, pl, :], identity=L["idb"][:, :]),
                                 reads=[R_pcb[k], R_id], writes=[Rp])
                        S.op("act", lambda e: e.activation(out=wB[c][:, :, k, :], in_=ptb[:], func=AF.Copy), reads=[Rp], writes=[R_wB])
                S.op("act", lambda e: e.activation(out=wC[0][:, :, 0, :], in_=craw[0], func=AF.Copy), reads=[R_raw], writes=[R_wC])
                S.op("act", lambda e: e.activation(out=wC[1][:, :, 0, :], in_=craw[1], func=AF.Copy, scale=-1.0), reads=[R_raw], writes=[R_wC])
                for m in (1, 2, 3, 4):
                    cplx(wC[0][:, :, m, :], wC[1][:, :, m, :], R_wC, craw2, R_raw, *LBK[m], neg_im=True)
                for tau in range(4):
                    n_ = 0
                    for pl in range(4):
                        for c in range(2):
                            S.op("pe", lambda e: e.matmul(pkk[:, :], lhsT=pcb[tau][c][:, pl, :], rhs=wC[c][:, pl, 0, :], start=(n_ == 0), stop=(n_ == 7)),
                                 reads=[R_pcb[tau], R_wC], writes=[Rpk])
                            n_ += 1
                    S.op("act", lambda e: e.activation(out=KK[:, tau, :], in_=pkk[:, :], func=AF.Copy), reads=[Rpk], writes=[R_KK])
                for pl in range(4):
                    pr_ = 4 * q + pl
                    S.op("dve", lambda e: e.tensor_scalar(out=w_r[0][:], in0=tr1[:, 0:128], scalar1=col[:, 22, pr_:pr_ + 1],
                                                          scalar2=None, op0=ALU.mult), reads=[R_tr1, R_col], writes=[R_w[0]])
                    scr = dict(a=tt[0][:], ki=eki[:], kf=tt[1][:], m=tt[2][:], y=tt[3][:])
                    sincos(S, "dve", ets[:, pl, :], etc[:, pl, :], w_r[0][:], scr, 128, R_et[pl], R_et[pl], R_tt, [R_w[0]], None)
                    S.op("act", lambda e: e.activation(out=esc2[:, pl, 0, :], in_=ets[:, pl, :], func=AF.Copy), reads=[R_et[pl]], writes=[R_et[pl]])
                    S.op("act", lambda e: e.activation(out=esc2[:, pl, 1, :], in_=etc[:, pl, :], func=AF.Copy), reads=[R_et[pl]], writes=[R_et[pl]])

            def u_str(q, c0, off, n):
                return AP(uT, q * NCOL + c0 + off, [[4 * NCOL, 128], [4, n]])

            def stA(n, q, b, pl):
                c0, nb = BLOCKS[b]
                j = n % 2
                if b < 4:
                    for c, pX in enumerate((pbr, pbi)):
                        for r_ in range(4):
                            S.op("pe", lambda e: e.matmul(pX[j][:, :], lhsT=wB[c][:, pl, 3 - r_, :], rhs=u_str(q, c0, r_, 128), start=(r_ == 0), stop=(r_ == 3)),
                                 reads=[R_wB, R_uT[b]], writes=[R_pb[j]])
                else:
                    for c, pX in enumerate((pbr, pbi)):
                        S.op("pe", lambda e: e.matmul(pX[j][:, :nb], lhsT=wB[c][:, pl, 0, :], rhs=uT[:, q, c0:c0 + nb], start=True, stop=True),
                             reads=[R_wB, R_uT[b]], writes=[R_pb[j]])

            def stB(n, q, b, pl):
                c0, nb = BLOCKS[b]
                pr_, j = 4 * q + pl, n % 2
                if b < 4:
                    c_, s_ = etc[:, pl, :], ets[:, pl, :]
                    S.op("dve", lambda e: e.tensor_tensor(out=ttA[:], in0=pbb[j][:], in1=ecs[:, pl, :, :], op=ALU.mult), reads=[R_pb[j], R_et[pl]], writes=[R_tt])
                    S.op("dve", lambda e: e.tensor_tensor(out=ttB[:], in0=pbb[j][:], in1=esc2[:, pl, :, :], op=ALU.mult), reads=[R_pb[j], R_et[pl]], writes=[R_tt])
                    S.op("dve", lambda e: e.tensor_tensor(out=w_r[j][:], in0=tt[0][:], in1=tt[1][:], op=ALU.add), reads=[R_tt], writes=[R_w[j]])
                    S.op("dve", lambda e: e.tensor_tensor(out=w_i[j][:], in0=tt[3][:], in1=tt[2][:], op=ALU.subtract), reads=[R_tt], writes=[R_w[j]])
                    S.op("dve", lambda e: e.tensor_tensor_scan(out=g_r[j][:], data0=col[:, 23, pr_:pr_ + 1].to_broadcast([128, 128]), data1=w_r[j][:],
                                                               initial=carry[:, 0, pr_:pr_ + 1], op0=ALU.mult, op1=ALU.add),
                         reads=[R_w[j], R_col, R_carry[pr_]], writes=[R_g[j]])
                    S.op("dve", lambda e: e.tensor_tensor_scan(out=g_i[j][:], data0=col[:, 23, pr_:pr_ + 1].to_broadcast([128, 128]), data1=w_i[j][:],
                                                               initial=carry[:, 1, pr_:pr_ + 1], op0=ALU.mult, op1=ALU.add),
                         reads=[R_w[j], R_col, R_carry[pr_]], writes=[R_g[j]])
                else:
                    lr, li = col[:, 7, pr_:pr_ + 1], col[:, 8, pr_:pr_ + 1]
                    S.op("dve", lambda e: e.tensor_scalar(out=tt[0][:, :nb], in0=h0r[:, pr_, :], scalar1=lr, scalar2=None, op0=ALU.mult), reads=[R_h0, R_col], writes=[R_tt])
                    S.op("dve", lambda e: e.tensor_scalar(out=tt[1][:, :nb], in0=h0i[:, pr_, :], scalar1=li, scalar2=None, op0=ALU.mult), reads=[R_h0, R_col], writes=[R_tt])
                    S.op("dve", lambda e: e.tensor_scalar(out=tt[2][:, :nb], in0=h0i[:, pr_, :], scalar1=lr, scalar2=None, op0=ALU.mult), reads=[R_h0, R_col], writes=[R_tt])
                    S.op("dve", lambda e: e.tensor_scalar(out=tt[3][:, :nb], in0=h0r[:, pr_, :], scalar1=li, scalar2=None, op0=ALU.mult), reads=[R_h0, R_col], writes=[R_tt])
                    S.op("dve", lambda e: e.tensor_tensor(out=tt[0][:, :nb], in0=tt[0][:, :nb], in1=tt[1][:, :nb], op=ALU.subtract), reads=[R_tt], writes=[R_tt])
                    S.op("dve", lambda e: e.tensor_tensor(out=tt[2][:, :nb], in0=tt[2][:, :nb], in1=tt[3][:, :nb], op=ALU.add), reads=[R_tt], writes=[R_tt])
                    S.op("dve", lambda e: e.tensor_tensor(out=hr[j][:, :nb], in0=tt[0][:, :nb], in1=pbr[j][:, :nb], op=ALU.add), reads=[R_tt, R_pb[j]], writes=[R_h[j]])
                    S.op("dve", lambda e: e.tensor_tensor(out=hi[j][:, :nb], in0=tt[2][:, :nb], in1=pbi[j][:, :nb], op=ALU.add), reads=[R_tt, R_pb[j]], writes=[R_h[j]])

            def stC(n, q, b, pl):
                c0, nb = BLOCKS[b]
                pr_, j = 4 * q + pl, n % 2
                if b < 4:
                    c_, s_ = etc[:, pl, :], ets[:, pl, :]
                    S.op("pool", lambda e: e.tensor_tensor(out=uuA[:], in0=gg[j][:], in1=ecs[:, pl, :, :], op=ALU.mult), reads=[R_g[j], R_et[pl]], writes=[R_uu])
                    S.op("pool", lambda e: e.tensor_tensor(out=hr[j][:], in0=uu[0][:], in1=uu[1][:], op=ALU.subtract), reads=[R_uu], writes=[R_h[j]])
                    S.op("pool", lambda e: e.tensor_tensor(out=uuA[:], in0=gg[j][:], in1=esc2[:, pl, :, :], op=ALU.mult), reads=[R_g[j], R_et[pl]], writes=[R_uu])
                    S.op("pool", lambda e: e.tensor_tensor(out=hi[j][:], in0=uu[0][:], in1=uu[1][:], op=ALU.add), reads=[R_uu], writes=[R_h[j]])
                    S.op("act", lambda e: e.activation(out=hrb[j][:, 0:1], in_=carry[:, 0, pr_:pr_ + 1], func=AF.Copy), reads=[R_carry[pr_]], writes=[R_hb[j]])
                    S.op("act", lambda e: e.activation(out=hib[j][:, 0:1], in_=carry[:, 1, pr_:pr_ + 1], func=AF.Copy), reads=[R_carry[pr_]], writes=[R_hb[j]])
                    S.op("act", lambda e: e.activation(out=hrb[j][:, 1:128], in_=hr[j][:, 0:127], func=AF.Copy), reads=[R_h[j]], writes=[R_hb[j]])
                    S.op("act", lambda e: e.activation(out=hib[j][:, 1:128], in_=hi[j][:, 0:127], func=AF.Copy), reads=[R_h[j]], writes=[R_hb[j]])
                    S.op("act", lambda e: e.activation(out=carry[:, 0, pr_:pr_ + 1], in_=hr[j][:, 127:128], func=AF.Copy), reads=[R_h[j], R_hb[j]], writes=[R_carry[pr_]])
                    S.op("act", lambda e: e.activation(out=carry[:, 1, pr_:pr_ + 1], in_=hi[j][:, 127:128], func=AF.Copy), reads=[R_h[j], R_hb[j]], writes=[R_carry[pr_]])
                    if b == 3:
                        S.op("act", lambda e: e.activation(out=hfin[:, 0, pr_:pr_ + 1], in_=hr[j][:, 127:128], func=AF.Copy), reads=[R_h[j]], writes=[R_hfin])
                        S.op("act", lambda e: e.activation(out=hfin[:, 1, pr_:pr_ + 1], in_=hi[j][:, 127:128], func=AF.Copy), reads=[R_h[j]], writes=[R_hfin])
                else:
                    S.op("act", lambda e: e.activation(out=hsmp[:, 0, :, pr_], in_=hr[j][:, :nb], func=AF.Copy), reads=[R_h[j]], writes=[R_hsmp])
                    S.op("act", lambda e: e.activation(out=hsmp[:, 1, :, pr_], in_=hi[j][:, :nb], func=AF.Copy), reads=[R_h[j]], writes=[R_hsmp])
                    S.op("act", lambda e: e.activation(out=hrb[j][:, :nb], in_=hr[j][:, :nb], func=AF.Copy), reads=[R_h[j]], writes=[R_hb[j]])
                    S.op("act", lambda e: e.activation(out=hib[j][:, :nb], in_=hi[j][:, :nb], func=AF.Copy), reads=[R_h[j]], writes=[R_hb[j]])

            def stD(n, q, b, pl):
                c0, nb = BLOCKS[b]
                pr_, j = 4 * q + pl, n % 2
                jy = (q * 5 + b) % 2
                if b < 4:
                    for r_ in range(4):
                        o_ap = AP(py[jy], r_, [[512, 128], [4, 128]])
                        for c, hX in enumerate((hrb, hib)):
                            S.op("pe", lambda e: e.matmul(o_ap, lhsT=wC[c][:, pl, r_ + 1, :], rhs=hX[j][:, :], start=(pl == 0 and r_ == 0 and c == 0), stop=False,
                                                          skip_group_check=True),
                                 reads=[R_wC, R_hb[j]], writes=[R_py[jy]])
                    if pl == 3:
                        for r_ in range(4):
                            o_ap = AP(py[jy], r_, [[512, 128], [4, 128]])
                            for tau in range(r_ + 1):
                                S.op("pe", lambda e: e.matmul(o_ap, lhsT=KK[:, tau, :], rhs=u_str(q, c0, r_ - tau, 128), start=False, stop=(r_ == 3 and tau == 3),
                                                              skip_group_check=True),
                                     reads=[R_KK, R_uT[b]], writes=[R_py[jy]])
                else:
                    S.op("pe", lambda e: e.matmul(py[jy][:, :nb], lhsT=wC[0][:, pl, 0, :], rhs=hrb[j][:, :nb], start=(pl == 0), stop=False),
                         reads=[R_wC, R_hb[j]], writes=[R_py[jy]])
                    S.op("pe", lambda e: e.matmul(py[jy][:, :nb], lhsT=wC[1][:, pl, 0, :], rhs=hib[j][:, :nb], start=False, stop=(pl == 3)),
                         reads=[R_wC, R_hb[j]], writes=[R_py[jy]])
                if pl == 3:
                    S.op("dve", lambda e: e.scalar_tensor_tensor(out=yf[:, :nb], in0=uT[:, q, c0:c0 + nb], scalar=dco[:, q:q + 1],
                                                                 in1=py[jy][:, :nb], op0=ALU.mult, op1=ALU.add),
                         reads=[R_uT[b], R_col, R_py[jy]], writes=[R_yf])
                    S.op("act", lambda e: e.activation(out=ygT[:, q, c0:c0 + nb], in_=yf[:, :nb], func=AF.Gelu_apprx_tanh),
                         reads=[R_yf], writes=[R_yg[b]])

            its = [(q, b, pl) for q in range(4) for b in range(5) for pl in range(4)]
            prep(0)
            stA(0, *its[0])
            for n, itx in enumerate(its):
                nxt = its[n + 1] if n + 1 < len(its) else None
                if nxt is not None and nxt[0] == itx[0]:
                    stA(n + 1, *nxt)
                stB(n, *itx)
                stC(n, *itx)
                stD(n, *itx)
                if nxt is not None and nxt[0] != itx[0]:
                    prep(nxt[0])
                    stA(n + 1, *nxt)
            for co in range(4):
                for b, (c0, nb) in enumerate(BLOCKS):
                    jy = (co * 5 + b) % 2
                    for kc in range(4):
                        S.op("pe", lambda e: e.matmul(py[jy][:, :nb], lhsT=wgb[:, kc, co * 128:(co + 1) * 128], rhs=ygT[:, kc, c0:c0 + nb],
                                                      start=(kc == 0), stop=(kc == 3)),
                             reads=[R_wgb, R_yg[b]], writes=[R_py[jy]])
                    S.op("act", lambda e: e.activation(out=yf[:, :nb], in_=py[jy][:, :nb], func=AF.Sigmoid), reads=[R_py[jy]], writes=[R_yf])
                    S.op("dve", lambda e: e.tensor_tensor(out=mixT[:, 4 + co, c0:c0 + nb], in0=yf[:, :nb], in1=ygT[:, co, c0:c0 + nb], op=ALU.mult),
                         reads=[R_yf, R_yg[b]], writes=[R_mix[4 + co]])
            S.barrier()
            pst = py[0]
            _ot = (braw[0], braw[1], bbf[0])
            oslot = lambda k: _ot[k // 4][0:16, k % 4, :]
            R_ost = Res("ost")
            srcs = [hfin[:, 0, :], hfin[:, 1, :]] + [hsmp[:, ri, s_, :] for ri in range(2) for s_ in range(NSC)]
            for k, src in enumerate(srcs):
                S.op("pe", lambda e: e.transpose(out=pst[0:16, 0:128], in_=src, identity=idf[:, :]),
                     reads=[R_hfin, R_hsmp, R_id], writes=[R_py[0]])
                S.op("dve", lambda e: e.tensor_copy(out=oslot(k), in_=pst[0:16, 0:128]), reads=[R_py[0]], writes=[R_ost])
            S.dma("sp", lambda e: e.dma_start(out=L["o_hre"], in_=oslot(0)), "z_o", reads=[R_ost])
            S.dma("sp", lambda e: e.dma_start(out=L["o_him"], in_=oslot(1)), "z_o", reads=[R_ost])
            for s_ in range(NSC):
                S.dma("sp", lambda e: e.dma_start(out=L["o_hres"][s_], in_=oslot(2 + s_)), "z_o", reads=[R_ost])
                S.dma("sp", lambda e: e.dma_start(out=L["o_hims"][s_], in_=oslot(2 + NSC + s_)), "z_o", reads=[R_ost])
            S.barrier()

def attn_phase(nc, S, sb, ps, L):
    qT, kT, vA, mixT = L["qT"], L["kT"], L["vA"], L["mixT"]
    R_qT, R_kT, R_vA, R_mix = L["R_qT"], L["R_kT"], L["R_vA"], L["R_mix"]
    idb, idf, R_id = L["idb"], L["idf"], L["R_id"]
    with ExitStack() as pa:
        lv = sb("lv", [128, 4, 64], F32, pa)
        lam = sb("lam", [128, 8], F32, pa)
        gs = sb("gs", [128, 128], F32, pa)
        trf = sb("trf", [128, 128], F32, pa)
        tri = sb("tri", [128, 128], BF16, pa)
        R_par = Res("apar")
        S.dma("sp", lambda e: e.dma_start(out=lv[:].rearrange("p a b -> p (a b)"), in_=AP(L["lamv"].tensor, 0, [[0, 128], [1, 256]])), "a_p", writes=[R_par])
        S.dma("sp", lambda e: e.dma_start(out=gs[:], in_=AP(L["gsub"].tensor, 0, [[0, 128], [1, 128]])), "a_p", writes=[R_par])
        S.dma("sp", lambda e: e.dma_start(out=trf[:], in_=L["trimask"]), "a_p", writes=[R_par])
        S.op("dve", lambda e: e.tensor_copy(out=tri[:], in_=trf[:]), reads=[R_par], writes=[R_par])
        S.op("dve", lambda e: e.tensor_scalar(out=gs[:], in0=gs[:], scalar1=1.0 - LAM_INIT, scalar2=None, op0=ALU.mult), reads=[R_par], writes=[R_par])
        S.op("dve", lambda e: e.tensor_tensor(out=lv[:, 0, :], in0=lv[:, 0, :], in1=lv[:, 1, :], op=ALU.mult), reads=[R_par], writes=[R_par])
        S.op("dve", lambda e: e.tensor_tensor(out=lv[:, 2, :], in0=lv[:, 2, :], in1=lv[:, 3, :], op=ALU.mult), reads=[R_par], writes=[R_par])
        S.op("dve", lambda e: e.reduce_sum(out=lam[:, 1:2], in_=lv[:, 0, :], axis=AX.X), reads=[R_par], writes=[R_par])
        S.op("dve", lambda e: e.reduce_sum(out=lam[:, 2:3], in_=lv[:, 2, :], axis=AX.X), reads=[R_par], writes=[R_par])
        S.op("act", lambda e: e.activation(out=lam[:, 1:3], in_=lam[:, 1:3], func=AF.Exp), reads=[R_par], writes=[R_par])
        S.op("dve", lambda e: e.tensor_tensor(out=lam[:, 0:1], in0=lam[:, 1:2], in1=lam[:, 2:3], op=ALU.subtract), reads=[R_par], writes=[R_par])
        S.op("dve", lambda e: e.tensor_scalar(out=lam[:, 0:1], in0=lam[:, 0:1], scalar1=LAM_INIT, scalar2=None, op0=ALU.add), reads=[R_par], writes=[R_par])

        ptp = ps("ptp", [128, 128], BF16, pa)
        R_ptp = Res("ptp")

        def subln_store(o_ap, n, dst_fn, scr):
            junk, st, ob, R_s = scr
            S.op("act", lambda e: e.activation(out=junk[:n, :], in_=o_ap, func=AF.Square, accum_out=st[:n, 0:1]), reads=[R_s], writes=[R_s])
            S.op("dve", lambda e: e.tensor_scalar(out=st[:n, 1:2], in0=st[:n, 0:1], scalar1=1.0 / 128, scalar2=SUBLN_EPS, op0=ALU.mult, op1=ALU.add), reads=[R_s], writes=[R_s])
            S.op("act", lambda e: e.sqrt(out=st[:n, 1:2], in_=st[:n, 1:2]), reads=[R_s], writes=[R_s])
            S.op("dve", lambda e: e.reciprocal(out=st[:n, 1:2], in_=st[:n, 1:2]), reads=[R_s], writes=[R_s])
            S.op("dve", lambda e: e.scalar_tensor_tensor(out=ob[:n, :], in0=o_ap, scalar=st[:n, 1:2], in1=gs[:n, :], op0=ALU.mult, op1=ALU.mult),
                 reads=[R_s, R_par], writes=[R_s])
            S.op("pe", lambda e: e.transpose(out=ptp[:, :n], in_=ob[:n, :], identity=idb[:n, :n]), reads=[R_s, R_id], writes=[R_ptp])
            dst_fn()

        psS = [ps("psS%d" % j, [128, 512], F32, pa) for j in range(2)]
        R_S = [Res("psS%d" % j) for j in range(2)]
        psO = ps("psO", [128, 2, 512], F32, pa)
        _rb = [Res("psO_b0"), Res("psO_b1")]
        R_O = [_rb[0], _rb[0], _rb[0], _rb[1]]
        accO = lambda k: psO[:, k // 3, (k % 3) * 129:(k % 3) * 129 + 129]
        eE = [sb("eE%d" % j, [128, 512], BF16, pa) for j in range(2)]
        R_E = [Res("eE%d" % j) for j in range(2)]
        osb = [sb("osb%d" % g, [128, 2, 4, 129], F32, pa) for g in range(2)]
        R_osb = [[Res("osb%d_%d" % (g, qt)) for qt in range(4)] for g in range(2)]
        of = [sb("of%d" % qt, [128, 128], F32, pa) for qt in range(4)]
        junk = [sb("ajunk%d" % qt, [128, 128], BF16, pa) for qt in range(4)]
        st = [sb("ast%d" % qt, [128, 4], F32, pa) for qt in range(4)]
        ob = [sb("aob%d" % qt, [128, 128], BF16, pa) for qt in range(4)]
        R_scr = [Res("ascr%d" % qt) for qt in range(4)]

        def prompt_group(gi):
            h, QB = gi // 4, gi % 4
            g2 = gi % 2
            nkb = 4 * QB + 4
            for c in range(2):
                def qk(kb):
                    o = kb - 4 * QB
                    qlo = max(o, 0) * 128
                    S.op("pe", lambda e: e.matmul(psS[kb % 2][:, qlo:512], lhsT=kT[64 * c:64 * c + 64, h, kb * 128:(kb + 1) * 128],
                                                  rhs=qT[64 * c:64 * c + 64, h, QB * 512 + qlo:QB * 512 + 512], start=True, stop=True),
                         reads=[R_kT[kb]] + R_qT[4 * QB:4 * QB + 4], writes=[R_S[kb % 2]])
                qk(0)
                for kb in range(nkb):
                    o = kb - 4 * QB
                    qlo = max(o, 0) * 128
                    j = kb % 2
                    if kb + 1 < nkb:
                        qk(kb + 1)
                    if kb % 2 == c:
                        sample_step()
                    S.op("act", lambda e: e.activation(out=eE[j][:, qlo:512], in_=psS[j][:, qlo:512], func=AF.Exp),
                         reads=[R_S[j]], writes=[R_E[j]])
                    if o >= 0:
                        S.op("pool", lambda e: e.tensor_tensor(out=eE[j][:, qlo:qlo + 128], in0=eE[j][:, qlo:qlo + 128], in1=tri[:], op=ALU.mult),
                             reads=[R_E[j], R_par], writes=[R_E[j]])
                    for qt in range(max(o, 0), 4):
                        S.op("pe", lambda e: e.matmul(accO(qt), lhsT=eE[j][:, qt * 128:(qt + 1) * 128], rhs=vA[:, kb, h, :],
                                                      start=(kb == 0 and qt in (0, 3)), stop=(kb == 4 * QB + qt), skip_group_check=True),
                             reads=[R_E[j], R_vA[kb]], writes=[R_O[qt]])
                for qt in range(4):
                    S.op("act", lambda e: e.activation(out=osb[g2][:, c, qt, :], in_=accO(qt), func=AF.Copy), reads=[R_O[qt]], writes=[R_osb[g2][qt]])
            for qt in range(4):
                o1, o2 = osb[g2][:, 0, qt, :], osb[g2][:, 1, qt, :]
                Rq = R_scr[qt]
                S.op("dve", lambda e: e.reciprocal(out=st[qt][:, 2:3], in_=o1[:, 128:129]), reads=[R_osb[g2][qt]], writes=[Rq])
                S.op("dve", lambda e: e.reciprocal(out=st[qt][:, 3:4], in_=o2[:, 128:129]), reads=[R_osb[g2][qt]], writes=[Rq])
                S.op("dve", lambda e: e.tensor_tensor(out=st[qt][:, 3:4], in0=st[qt][:, 3:4], in1=lam[:, 0:1], op=ALU.mult), reads=[Rq, R_par], writes=[Rq])
                S.op("dve", lambda e: e.tensor_scalar(out=o2[:, 0:128], in0=o2[:, 0:128], scalar1=st[qt][:, 3:4], scalar2=None, op0=ALU.mult),
                     reads=[Rq, R_osb[g2][qt]], writes=[R_osb[g2][qt]])
                S.op("dve", lambda e: e.scalar_tensor_tensor(out=of[qt][:, :], in0=o1[:, 0:128], scalar=st[qt][:, 2:3], in1=o2[:, 0:128], op0=ALU.mult, op1=ALU.subtract),
                     reads=[Rq, R_osb[g2][qt]], writes=[Rq])
                c0 = QB * 512 + qt * 128
                def dst(c0=c0, h=h):
                    S.op("act", lambda e: e.activation(out=mixT[:, h, c0:c0 + 128], in_=ptp[:, :128], func=AF.Copy), reads=[R_ptp], writes=[R_mix[h]])
                subln_store(of[qt][:, :], 128, dst, (junk[qt], st[qt], ob[qt], Rq))

        do_samples = L["stage"] >= 8
        sample_gen = None
        state = {"gen": None}

        def sample_step():
            if state["gen"] is not None:
                try:
                    next(state["gen"])
                except StopIteration:
                    state["gen"] = None
        if do_samples:
            ptr_ = sb("s_ptr", [128, NSC, 16], I32, pa)
            pof = sb("s_pof", [128, NSC * 16], F32, pa)
            pff = sb("s_pff", [128, NSC * 16], F32, pa)
            idx = sb("s_idx", [128, NSC, 16], I32, pa)
            R_idx = Res("idx")
            selt = sb("s_sel", [NSC, NSC, 128], F32, pa)
            m0 = sb("s_m0", [128, 1], F32, pa)
            S.dma("sp", lambda e: e.dma_start(out=ptr_[:], in_=L["ptrep"].rearrange("s p j -> p s j")), "s_p", writes=[R_idx])
            S.dma("sp", lambda e: e.dma_start(out=pof[:], in_=L["poff"]), "s_p", writes=[R_idx])
            S.dma("sp", lambda e: e.dma_start(out=selt[:], in_=L["selq"]), "s_p", writes=[R_idx])
            S.dma("sp", lambda e: e.dma_start(out=m0[:], in_=L["mask0"]), "s_p", writes=[R_idx])
            f2 = lambda t: t[:].rearrange("p a b -> p (a b)")
            S.op("dve", lambda e: e.tensor_copy(out=pff[:, :], in_=f2(ptr_)), reads=[R_idx], writes=[R_idx])
            S.op("dve", lambda e: e.scalar_tensor_tensor(out=pff[:, :], in0=pff[:, :], scalar=32.0, in1=pof[:, :], op0=ALU.mult, op1=ALU.add), reads=[R_idx], writes=[R_idx])
            S.op("dve", lambda e: e.tensor_copy(out=f2(idx), in_=pff[:, :]), reads=[R_idx], writes=[R_idx])
            NKB = 3
            kt_ = [sb("s_kt%d" % j, [128, 4, 512], F32, pa) for j in range(NKB)]
            R_kt = [Res("s_kt%d" % j) for j in range(NKB)]
            NVB = 3
            vb_ = [sb("s_vb%d" % j, [128, 4, 512], BF16, pa) for j in range(NVB)]
            R_vb = [Res("s_vb%d" % j) for j in range(NVB)]
            prod = [sb("s_prod%d" % j, [128, 4, 512], F32, pa) for j in range(2)]
            R_prod = [Res("s_prod%d" % j) for j in range(2)]
            qbc = [sb("s_qbc%d" % j, [128, 4, 512], F32, pa) for j in range(2)]
            R_qbc = [Res("qbc%d" % j) for j in range(2)]
            R_qscr = Res("qscr")
            sc = sb("s_sc", [128, 64, 8], F32, pa)
            R_sc = Res("s_sc")
            pdz = sb("s_pdz", [128, 64, 4, NSC], BF16, pa)
            R_pdz = Res("pdz")
            zz = sb("s_zz", [128, 6, 8], F32, pa)
            R_zz = Res("zz")
            sself = sb("s_self", [NSC, 4, 8], F32, pa)
            prs = sb("s_prs", [NSC, 512], F32, pa)
            R_self = Res("sself")
            coef = sb("s_coef", [NSC, 2, 8], F32, pa)
            R_coef = Res("coef")
            osm = sb("s_osm", [NSC, 4, 128], F32, pa)
            sjunk = sb("s_junk", [NSC, 128], BF16, pa)
            sst = sb("s_st", [NSC, 4], F32, pa)
            sob = sb("s_ob", [NSC, 128], BF16, pa)
            R_sscr = Res("s_scr")
            pz = ps("s_pz", [128, 16], F32, pa)
            R_pz = Res("pz")
            pso = ps("s_pso", [NSC, 512], F32, pa)
            R_pso = Res("pso")
            S.op("pool", lambda e: e.memset(pdz[:], 0.0), writes=[R_pdz])
            S.op("dve", lambda e: e.memset(coef[:], 0.0), writes=[R_coef])
            S.dma("sp", lambda e: e.dma_start(out=L["qscr"], in_=L["qsf"][:, :]), "s_q", reads=[L["R_qsf"]], writes=[R_qscr])
            S.op("dve", lambda e: e.tensor_tensor(out=prs[:, :], in0=L["qsf"][:, :], in1=L["ksf"][:, :], op=ALU.mult), reads=[L["R_qsf"], L["R_ksf"]], writes=[R_self])
            S.op("dve", lambda e: e.reduce_sum(out=sself[:, 0, :], in_=prs[:, :].rearrange("p (a d) -> p a d", d=64), axis=AX.X), reads=[R_self], writes=[R_self])
            S.op("act", lambda e: e.activation(out=sself[:, 1, :], in_=sself[:, 0, :], func=AF.Exp, scale=0.125), reads=[R_self], writes=[R_self])

            def sample_k(s_):
                def gather(jj):
                    j = (s_ * 16 + jj) % NKB
                    S.dma("pool", lambda e: e.indirect_dma_start(out=kt_[j][:].rearrange("p a b -> p (a b)"), out_offset=None, in_=L["cache_k"],
                                                                 in_offset=bass.IndirectOffsetOnAxis(ap=idx[:, s_, jj:jj + 1], axis=0)),
                          "s_kt%d" % j, reads=[R_idx], writes=[R_kt[j]])
                qb_ = qbc[s_ % 2]
                S.dma("sp", lambda e: e.dma_start(out=qb_[:], in_=AP(L["qscr"].tensor, s_ * 512, [[0, 128], [0, 4], [1, 512]])),
                      "s_qb%d" % (s_ % 2), reads=[R_qscr], writes=[R_qbc[s_ % 2]])
                gather(0)
                gather(1)
                for jj in range(16):
                    j = (s_ * 16 + jj) % NKB
                    if jj + 2 < 16:
                        gather(jj + 2)
                    S.op("dve", lambda e: e.tensor_tensor(out=prod[jj % 2][:], in0=kt_[j][:], in1=qb_[:], op=ALU.mult),
                         reads=[R_kt[j], R_qbc[s_ % 2]], writes=[R_prod[jj % 2]])
                    S.op("dve", lambda e: e.reduce_sum(out=sc[:, jj * 4:(jj + 1) * 4, :], in_=prod[jj % 2][:].rearrange("p t (a d) -> p t a d", d=64), axis=AX.X),
                         reads=[R_prod[jj % 2]], writes=[R_sc])
                    yield
                S.op("act", lambda e: e.activation(out=sc[:], in_=sc[:], func=AF.Exp, scale=0.125), reads=[R_sc], writes=[R_sc])
                S.op("dve", lambda e: e.reduce_sum(out=zz[:, 0, :], in_=sc[:].rearrange("p t a -> p a t"), axis=AX.X), reads=[R_sc], writes=[R_zz])
                S.op("pe", lambda e: e.matmul(pz[:, 0:8], lhsT=selt[:, s_, :], rhs=sself[:, 1, :], start=True, stop=True), reads=[R_idx, R_self], writes=[R_pz])
                S.op("dve", lambda e: e.scalar_tensor_tensor(out=zz[:, 0, :], in0=pz[:, 0:8], scalar=m0[:, 0:1], in1=zz[:, 0, :], op0=ALU.mult, op1=ALU.add),
                     reads=[R_pz, R_idx, R_zz], writes=[R_zz])
                S.op("pe", lambda e: e.matmul(pz[:, 8:16], lhsT=L["ones_f"][:, :], rhs=zz[:, 0, :], start=True, stop=True), reads=[R_zz, R_id], writes=[R_pz])
                S.op("dve", lambda e: e.reciprocal(out=zz[:, 1, :], in_=pz[:, 8:16]), reads=[R_pz], writes=[R_zz])
                z4 = lambda r: zz[:, r, :].rearrange("p (h c) -> p h c", c=2)
                S.op("dve", lambda e: e.tensor_scalar(out=zz[:, 2, :], in0=zz[:, 1, :], scalar1=lam[:, 0:1], scalar2=None, op0=ALU.mult), reads=[R_zz, R_par], writes=[R_zz])
                for r_ in range(2):
                    S.op("dve", lambda e: e.scalar_tensor_tensor(out=coef[:, r_, :], in0=zz[0:NSC, 1 + r_, :], scalar=idf[0:NSC, s_:s_ + 1], in1=coef[:, r_, :],
                                                                 op0=ALU.mult, op1=ALU.add), reads=[R_zz, R_id, R_coef], writes=[R_coef])
                sc4 = sc[:].rearrange("p t (h c) -> p t h c", c=2)
                S.op("dve", lambda e: e.tensor_tensor(out=sc4[:, :, :, 0], in0=sc4[:, :, :, 0], in1=z4(1)[:, :, 0].unsqueeze(1).to_broadcast([128, 64, 4]), op=ALU.mult), reads=[R_sc, R_zz], writes=[R_sc])
                S.op("dve", lambda e: e.tensor_tensor(out=sc4[:, :, :, 1], in0=sc4[:, :, :, 1], in1=z4(2)[:, :, 1].unsqueeze(1).to_broadcast([128, 64, 4]), op=ALU.mult), reads=[R_sc, R_zz], writes=[R_sc])
                if s_ > 0:
                    S.op("pool", lambda e: e.memset(pdz[:, :, :, s_ - 1], 0.0), writes=[R_pdz])
                S.op("dve", lambda e: e.tensor_tensor(out=pdz[:, :, :, s_], in0=sc4[:, :, :, 0], in1=sc4[:, :, :, 1], op=ALU.subtract), reads=[R_sc], writes=[R_pdz])

            def sample_v(s_):
                def gather(jj):
                    jv = jj % NVB
                    S.dma("pool", lambda e: e.indirect_dma_start(out=vb_[jv][:].rearrange("p a b -> p (a b)"), out_offset=None, in_=L["cache_v"],
                                                                 in_offset=bass.IndirectOffsetOnAxis(ap=idx[:, s_, jj:jj + 1], axis=0)),
                          "s_vb%d" % jv, reads=[R_idx], writes=[R_vb[jv]])
                gather(0)
                gather(1)
                for jj in range(16):
                    jv = jj % NVB
                    if jj + 2 < 16:
                        gather(jj + 2)
                    for t4 in range(4):
                        for h in range(4):
                            first = (s_ == 0 and jj == 0 and t4 == 0 and h == 0)
                            last = (s_ == NSC - 1 and jj == 15 and t4 == 3)
                            S.op("pe", lambda e: e.matmul(pso[:, h * 128:(h + 1) * 128], lhsT=pdz[:, jj * 4 + t4, h, :], rhs=vb_[jv][:, t4, h * 128:(h + 1) * 128],
                                                          start=first, stop=last, skip_group_check=True),
                                 reads=[R_pdz, R_vb[jv]], writes=[R_pso])
                    yield

            def sample_all():
                for s_ in range(NSC):
                    yield from sample_k(s_)
                    yield from sample_v(s_)
            sample_gen = sample_all()


        if do_samples:
            state["gen"] = sample_gen
        for gi in range(16):
            prompt_group(gi)
        while state["gen"] is not None:
            sample_step()

        if do_samples:
            e4 = sself[:, 1, :].rearrange("p (h c) -> p h c", c=2)
            cA = coef[:, 0, :].rearrange("p (h c) -> p h c", c=2)
            cB = coef[:, 1, :].rearrange("p (h c) -> p h c", c=2)
            pd4 = sself[:, 2, :].rearrange("p (h c) -> p h c", c=2)
            S.op("dve", lambda e: e.tensor_tensor(out=pd4[:, :, 0], in0=e4[:, :, 0], in1=cA[:, :, 0], op=ALU.mult), reads=[R_self, R_coef], writes=[R_self])
            S.op("dve", lambda e: e.tensor_tensor(out=pd4[:, :, 1], in0=e4[:, :, 1], in1=cB[:, :, 1], op=ALU.mult), reads=[R_self, R_coef], writes=[R_self])
            S.op("dve", lambda e: e.tensor_tensor(out=pd4[:, :, 0], in0=pd4[:, :, 0], in1=pd4[:, :, 1], op=ALU.subtract), reads=[R_self], writes=[R_self])
            for h in range(4):
                S.op("dve", lambda e: e.scalar_tensor_tensor(out=osm[:, h, :], in0=L["vS"][:, h * 128:(h + 1) * 128], scalar=pd4[:, h, 0:1], in1=pso[:, h * 128:(h + 1) * 128],
                                                             op0=ALU.mult, op1=ALU.add), reads=[L["R_vS"], R_self, R_pso], writes=[R_sscr])
                def dst(h=h):
                    S.op("act", lambda e: e.activation(out=mixT[:, h, SEQ:NCOL], in_=ptp[:, :NSC], func=AF.Copy), reads=[R_ptp], writes=[R_mix[h]])
                subln_store(osm[:, h, :], NSC, dst, (sjunk, sst, sob, R_sscr))
        S.barrier()


def ffn_phase(nc, S, sb, ps, L):
    mixT, R_mix, idb, idf, R_id = L["mixT"], L["R_mix"], L["idb"], L["idf"], L["R_id"]
    xp, xs, xscr = L["xp"], L["xs"], L["xscr"]
    with ExitStack() as pf:
        mT = sb("mT", [128, NFC, NCOL], BF16, pf)
        R_mT = [Res("mT%d" % b) for b in range(5)]
        rs2 = sb("rs2", [128, 17, 4], F32, pf)
        R_rs2 = [Res("rs2_%d" % i) for i in range(17)]
        small = sb("fsmall", [128, 8 + 3 * NFC + NFC], F32, pf)
        R_small = Res("fsmall")
        S.dma("sp", lambda e: e.dma_start(out=small[:, 8:8 + 3 * NFC], in_=L["convw"].rearrange("p a b -> p (a b)")), "f_s", writes=[R_small])
        S.dma("sp", lambda e: e.dma_start(out=small[:, 8 + 3 * NFC:], in_=L["convb"]), "f_s", writes=[R_small])
        cw = lambda j, fc: small[:, 8 + j * NFC + fc:8 + j * NFC + fc + 1]
        cb = lambda fc: small[:, 8 + 3 * NFC + fc:8 + 3 * NFC + fc + 1]
        asp = sb("asp", [8, DFF], F32, pf)
        R_asp = Res("asp")
        with ExitStack() as pu:
            hn2T = sb("hn2T", [128, 8, NCOL], BF16, pu)
            R_hn2 = [Res("hn2_%d" % i) for i in range(17)]
            with ExitStack() as po:
                wob = sb("wob", [128, 8, D], BF16, po)
                R_wob = Res("wob")
                for kc in range(8):
                    S.dma("pool", lambda e: e.dma_start(out=wob[:, kc, :], in_=L["w_o"][kc * 128:(kc + 1) * 128, :]), "wob", writes=[R_wob])
                gfx = sb("gfx", [128, D], F32, po)
                R_gfx = Res("gfx")
                S.dma("sp", lambda e: e.dma_start(out=gfx[:], in_=AP(L["nffn"].tensor, 0, [[0, 128], [1, D]])), "f_g2", writes=[R_gfx])
                xt = [sb("oxt%d" % j, [128, D], F32, po) for j in range(2)]
                R_xt = [Res("oxt%d" % j) for j in range(2)]
                xb = [sb("oxb0", [128, D], BF16, po)] * 2
                R_xb = [Res("oxb0")] * 2
                junk = xb[0]
                R_junk = R_xb[0]
                px = [ps("opx%d" % j, [128, 2, 512], F32, po) for j in range(2)]
                R_px = [Res("opx%d" % j) for j in range(2)]
                ptr = [ps("optr%d" % j, [128, 8, 128], BF16, po) for j in range(2)]
                R_ptr = [Res("optr%d" % j) for j in range(2)]
                for i, (c0, n) in enumerate(TILES):
                    j = i % 2
                    src = xp[c0:c0 + n, :] if i < 16 else xs
                    S.dma("sp", lambda e: e.dma_start(out=xt[j][:n, :], in_=src), "oxt%d" % j, writes=[R_xt[j]])
                    for hf in range(2):
                        for kc in range(8):
                            S.op("pe", lambda e: e.matmul(px[j][:n, hf, :], lhsT=mixT[:, kc, c0:c0 + n], rhs=wob[:, kc, hf * 512:(hf + 1) * 512],
                                                          start=(kc == 0), stop=(kc == 7)), reads=R_mix + [R_wob], writes=[R_px[j]])
                    S.op("dve", lambda e: e.tensor_tensor(out=xt[j][:n, :], in0=xt[j][:n, :], in1=px[j][:n, :, :].rearrange("p a b -> p (a b)"), op=ALU.add),
                         reads=[R_xt[j], R_px[j]], writes=[R_xt[j]])
                    S.dma("sp", lambda e: e.dma_start(out=xscr[c0:c0 + n, :], in_=xt[j][:n, :]), "oxt%d" % j, reads=[R_xt[j]])
                    S.op("act", lambda e: e.activation(out=junk[:n, :], in_=xt[j][:n, :], func=AF.Square, accum_out=rs2[:n, i, 0:1]),
                         reads=[R_xt[j]], writes=[R_junk, R_rs2[i]])
                    S.op("dve", lambda e: e.tensor_scalar(out=rs2[:n, i, 1:2], in0=rs2[:n, i, 0:1], scalar1=1.0 / D, scalar2=EPS, op0=ALU.mult, op1=ALU.add),
                         reads=[R_rs2[i]], writes=[R_rs2[i]])
                    S.op("act", lambda e: e.sqrt(out=rs2[:n, i, 2:3], in_=rs2[:n, i, 1:2]), reads=[R_rs2[i]], writes=[R_rs2[i]])
                    S.op("dve", lambda e: e.reciprocal(out=rs2[:n, i, 3:4], in_=rs2[:n, i, 2:3]), reads=[R_rs2[i]], writes=[R_rs2[i]])
                    S.op("dve", lambda e: e.scalar_tensor_tensor(out=xb[j][:n, :], in0=xt[j][:n, :], scalar=rs2[:n, i, 3:4], in1=gfx[:n, :],
                                                                 op0=ALU.mult, op1=ALU.mult),
                         reads=[R_xt[j], R_rs2[i], R_gfx], writes=[R_xb[j]])
                    for kc in range(8):
                        S.op("pe", lambda e: e.transpose(out=ptr[j][:, kc, :n], in_=xb[j][:n, kc * 128:(kc + 1) * 128], identity=idb[:n, :n]),
                             reads=[R_xb[j], R_id], writes=[R_ptr[j]])
                    S.op("dve", lambda e: e.tensor_copy(out=hn2T[:, :, c0:c0 + n], in_=ptr[j][:, :, :n]), reads=[R_ptr[j]], writes=[R_hn2[i]])
                S.barrier()
            with ExitStack() as pv:
                wab = [sb("wab%d" % j, [128, 8, 128], BF16, pv) for j in range(2)]
                wgb_ = [sb("wgb_%d" % j, [128, 8, 128], BF16, pv) for j in range(2)]
                R_wab = [Res("wab%d" % j) for j in range(2)]
                R_wgb = [Res("wgb_%d" % j) for j in range(2)]
                pa_ = [ps("fpa%d" % j, [128, 512], F32, pv) for j in range(2)]
                pg_ = [ps("fpg%d" % j, [128, 512], F32, pv) for j in range(2)]
                R_pa = [Res("fpa%d" % j) for j in range(2)]
                R_pg = [Res("fpg%d" % j) for j in range(2)]
                psp = ps("fpsp", [8, 128], F32, pv)
                R_psp = Res("fpsp")
                pbt = ps("fpbt", [128, NFC, 8], F32, pv)
                R_pbt = Res("fpbt")
                abuf = sb("abuf", [128, 2 + 512], F32, pv)
                cbuf = sb("cbuf", [128, 512], F32, pv)
                R_ab, R_cbuf = Res("abuf"), Res("cbuf")
                scv = sb("scv", [8, DFF], F32, pv)
                bufT = sb("bufT", [128, NFC, 8], F32, pv)
                R_bufT = Res("bufT")
                S.dma("sp", lambda e: e.dma_start(out=scv[:, :], in_=L["sconv"]), "f_c", writes=[R_bufT])
                for fc in range(NFC):
                    S.op("pe", lambda e: e.transpose(out=pbt[:, fc, :], in_=scv[:, fc * 128:(fc + 1) * 128], identity=idf[:8, :8]),
                         reads=[R_bufT, R_id], writes=[R_pbt])
                S.op("dve", lambda e: e.tensor_copy(out=bufT[:], in_=pbt[:]), reads=[R_pbt], writes=[R_bufT])
                for fc in range(NFC):
                    jw = fc % 2
                    for half, (wb, R_wb) in enumerate(((wab, R_wab), (wgb_, R_wgb))):
                        col0 = half * DFF + fc * 128
                        S.dma("pool", lambda e: e.dma_start(out=wb[jw][:], in_=L["w_up"][:, col0:col0 + 128].rearrange("(k p) c -> p k c", p=128)),
                              "wup%d%d" % (half, jw), writes=[R_wb[jw]])
                    for kc in range(8):
                        S.op("pe", lambda e: e.matmul(psp[0:6, :], lhsT=hn2T[:, kc, SEQ - 2:NCOL], rhs=wab[jw][:, kc, :], start=(kc == 0), stop=(kc == 7)),
                             reads=R_hn2[15:17] + [R_wab[jw]], writes=[R_psp])
                    S.op("act", lambda e: e.activation(out=asp[0:6, fc * 128:(fc + 1) * 128], in_=psp[0:6, :], func=AF.Copy), reads=[R_psp], writes=[R_asp])
                    for b, (c0, nb) in enumerate(BLOCKS):
                        j = (fc * 5 + b) % 2
                        for kc in range(8):
                            S.op("pe", lambda e: e.matmul(pa_[j][:, :nb], lhsT=wab[jw][:, kc, :], rhs=hn2T[:, kc, c0:c0 + nb], start=(kc == 0), stop=(kc == 7)),
                                 reads=R_hn2 + [R_wab[jw]], writes=[R_pa[j]])
                        for kc in range(8):
                            S.op("pe", lambda e: e.matmul(pg_[j][:, :nb], lhsT=wgb_[jw][:, kc, :], rhs=hn2T[:, kc, c0:c0 + nb], start=(kc == 0), stop=(kc == 7)),
                                 reads=R_hn2 + [R_wgb[jw]], writes=[R_pg[j]])
                        if b < 4:
                            if b == 0:
                                S.op("dve", lambda e: e.memset(abuf[:, 0:2], 0.0), reads=[R_ab], writes=[R_ab])
                            else:
                                S.op("dve", lambda e: e.tensor_copy(out=abuf[:, 0:2], in_=abuf[:, 512:514]), reads=[R_ab], writes=[R_ab])
                            S.op("act", lambda e: e.activation(out=abuf[:, 2:514], in_=pa_[j][:, :], func=AF.Copy), reads=[R_pa[j], R_ab], writes=[R_ab])
                            S.op("act", lambda e: e.activation(out=cbuf[:, :], in_=pa_[j][:, :], func=AF.Identity, scale=cw(2, fc), bias=cb(fc)),
                                 reads=[R_pa[j], R_small], writes=[R_cbuf])
                            S.op("dve", lambda e: e.scalar_tensor_tensor(out=cbuf[:, :], in0=abuf[:, 1:513], scalar=cw(1, fc), in1=cbuf[:, :], op0=ALU.mult, op1=ALU.add),
                                 reads=[R_ab, R_small, R_cbuf], writes=[R_cbuf])
                            S.op("dve", lambda e: e.scalar_tensor_tensor(out=cbuf[:, :], in0=abuf[:, 0:512], scalar=cw(0, fc), in1=cbuf[:, :], op0=ALU.mult, op1=ALU.add),
                                 reads=[R_ab, R_small, R_cbuf], writes=[R_cbuf])
                        else:
                            b3 = bufT[:, fc, :].rearrange("p (s j) -> p s j", j=2)
                            S.op("act", lambda e: e.activation(out=cbuf[:, :nb], in_=pa_[j][:, :nb], func=AF.Identity, scale=cw(2, fc), bias=cb(fc)),
                                 reads=[R_pa[j], R_small], writes=[R_cbuf])
                            S.op("dve", lambda e: e.scalar_tensor_tensor(out=cbuf[:, :nb], in0=b3[:, :, 1], scalar=cw(1, fc), in1=cbuf[:, :nb], op0=ALU.mult, op1=ALU.add),
                                 reads=[R_bufT, R_small, R_cbuf], writes=[R_cbuf])
                            S.op("dve", lambda e: e.scalar_tensor_tensor(out=cbuf[:, :nb], in0=b3[:, :, 0], scalar=cw(0, fc), in1=cbuf[:, :nb], op0=ALU.mult, op1=ALU.add),
                                 reads=[R_bufT, R_small, R_cbuf], writes=[R_cbuf])
                        S.op("act", lambda e: e.activation(out=cbuf[:, :nb], in_=cbuf[:, :nb], func=AF.Silu), reads=[R_cbuf], writes=[R_cbuf])
                        S.op("dve", lambda e: e.tensor_tensor(out=mT[:, fc, c0:c0 + nb], in0=cbuf[:, :nb], in1=pg_[j][:, :nb], op=ALU.mult),
                             reads=[R_cbuf, R_pg[j]], writes=[R_mT[b]])
                S.dma("sp", lambda e: e.dma_start(out=L["o_cp"], in_=asp[0:2, :]), "f_o", reads=[R_asp])
                S.dma("sp", lambda e: e.dma_start(out=L["o_cs"][:, 1, :], in_=asp[2:6, :]), "f_o", reads=[R_asp])
                S.dma("sp", lambda e: e.dma_start(out=L["o_cs"][:, 0, :], in_=L["sconv1"]), "f_o2")
                S.barrier()
        with ExitStack() as pd_:
            wdb = sb("wdb", [128, NFC, D], BF16, pd_)
            R_wdb = Res("wdb")
            for fc in range(NFC):
                S.dma("pool", lambda e: e.dma_start(out=wdb[:, fc, :], in_=L["w_down"][fc * 128:(fc + 1) * 128, :]), "wdb", writes=[R_wdb])
            gfn = sb("gfn", [128, D], F32, pd_)
            R_gfn = Res("gfn")
            S.dma("sp", lambda e: e.dma_start(out=gfn[:], in_=AP(L["fnorm"].tensor, 0, [[0, 128], [1, D]])), "f_g", writes=[R_gfn])
            xt = [sb("dxt%d" % j, [128, D], F32, pd_) for j in range(2)]
            R_xt = [Res("dxt%d" % j) for j in range(2)]
            junk = sb("djunk", [128, D], BF16, pd_)
            R_junk = Res("djunk")
            px = [ps("dpx%d" % j, [128, 2, 512], F32, pd_) for j in range(2)]
            R_px = [Res("dpx%d" % j) for j in range(2)]
            for i, (c0, n) in enumerate(TILES):
                j = i % 2
                S.dma("sp", lambda e: e.dma_start(out=xt[j][:n, :], in_=xscr[c0:c0 + n, :]), "dxt%d" % j, writes=[R_xt[j]])
                for hf in range(2):
                    for fc in range(NFC):
                        S.op("pe", lambda e: e.matmul(px[j][:n, hf, :], lhsT=mT[:, fc, c0:c0 + n], rhs=wdb[:, fc, hf * 512:(hf + 1) * 512],
                                                      start=(fc == 0), stop=(fc == NFC - 1)), reads=R_mT + [R_wdb], writes=[R_px[j]])
                S.op("dve", lambda e: e.tensor_tensor(out=xt[j][:n, :], in0=xt[j][:n, :], in1=px[j][:n, :, :].rearrange("p a b -> p (a b)"), op=ALU.add),
                     reads=[R_xt[j], R_px[j]], writes=[R_xt[j]])
                S.op("act", lambda e: e.activation(out=junk[:n, :], in_=xt[j][:n, :], func=AF.Square, accum_out=rs2[:n, i, 0:1]),
                     reads=[R_xt[j]], writes=[R_junk, R_rs2[i]])
                S.op("dve", lambda e: e.tensor_scalar(out=rs2[:n, i, 1:2], in0=rs2[:n, i, 0:1], scalar1=1.0 / D, scalar2=EPS, op0=ALU.mult, op1=ALU.add),
                     reads=[R_rs2[i]], writes=[R_rs2[i]])
                S.op("act", lambda e: e.sqrt(out=rs2[:n, i, 2:3], in_=rs2[:n, i, 1:2]), reads=[R_rs2[i]], writes=[R_rs2[i]])
                S.op("dve", lambda e: e.reciprocal(out=rs2[:n, i, 3:4], in_=rs2[:n, i, 2:3]), reads=[R_rs2[i]], writes=[R_rs2[i]])
                S.op("dve", lambda e: e.scalar_tensor_tensor(out=xt[j][:n, :], in0=xt[j][:n, :], scalar=rs2[:n, i, 3:4], in1=gfn[:n, :], op0=ALU.mult, op1=ALU.mult),
                     reads=[R_xt[j], R_rs2[i], R_gfn], writes=[R_xt[j]])
                dstd = L["o_yp"][c0:c0 + n, :] if i < 16 else L["o_ys"]
                S.dma("sp", lambda e: e.dma_start(out=dstd, in_=xt[j][:n, :]), "dxt%d" % j, reads=[R_xt[j]])
            S.barrier()


def build_program(n_pool, stage=99):
    nc = bass.Bass("TRN2", target_bir_lowering=False)
    din = lambda name, shape, dt=F32: nc.dram_tensor(name, shape, dt, kind="ExternalInput").ap()
    dout = lambda name, shape, dt=F32: nc.dram_tensor(name, shape, dt, kind="ExternalOutput").ap()

    xp = din("xp", [SEQ, D])
    xs = din("xs", [NSC, D])
    w_in = din("w_in", [D, DPROJ])
    nmix = din("nmix", [1, D])
    ropec = din("ropec", [NCOL, 8])
    ropes = din("ropes", [NCOL, 8])
    ident = din("ident", [128, 128])

    bcol_re = din("bcol_re", [128, 16, 128])
    bcol_im = din("bcol_im", [128, 16, 128])
    cpad_re = din("cpad_re", [128, 16, 128])
    cpad_im = din("cpad_im", [128, 16, 128])
    acol_re = din("acol_re", [128, 16])
    acol_im = din("acol_im", [128, 16])
    ldt_col = din("ldt_col", [128, 16])
    dcol = din("dcol", [128, 4])
    h0c_re = din("h0c_re", [128, 16, NSC])
    h0c_im = din("h0c_im", [128, 16, NSC])
    w_glu = din("w_glu", [512, 512])
    trow1 = din("trow1", [128, 512])
    o_hre = dout("o_hre", [16, 128])
    o_him = dout("o_him", [16, 128])
    o_hres = dout("o_hres", [NSC, 16, 128])
    o_hims = dout("o_hims", [NSC, 16, 128])

    lamv = din("lamv", [1, 4 * 64])
    gsub = din("gsub", [1, 128])
    trimask = din("trimask", [128, 128])
    cache_k = din("cache_k", [n_pool * 32, 2048])
    cache_v = din("cache_v", [n_pool * 32, 2048])
    ptrep = din("ptrep", [NSC, 128, 16], I32)
    poff = din("poff", [128, NSC * 16])
    selq = din("selq", [NSC, NSC, 128])
    mask0 = din("mask0", [128, 1])
    w_o = din("w_o", [D, D])
    nffn = din("nffn", [1, D])
    w_up = din("w_up", [D, 2 * DFF])
    convw = din("convw", [128, 3, NFC])
    convb = din("convb", [128, NFC])
    w_down = din("w_down", [DFF, D])
    fnorm = din("fnorm", [1, D])
    sconv = din("sconv", [NSC * 2, DFF])
    sconv1 = din("sconv1", [NSC, DFF])
    xscr = nc.dram_tensor("xscr", [NCOL, D], F32, kind="Internal").ap()
    qscr = nc.dram_tensor("qscr", [1, NSC * 512], F32, kind="Internal").ap()
    o_yp = dout("o_yp", [SEQ, D])
    o_ys = dout("o_ys", [NSC, D])
    o_cp = dout("o_cp", [2, DFF])
    o_cs = dout("o_cs", [NSC, 2, DFF])
    o_kp = dout("o_kp", [SEQ, 512])
    o_vp = dout("o_vp", [SEQ, 512])
    o_ks = dout("o_ks", [NSC, 512])
    o_vs = dout("o_vs", [NSC, 512])

    with ExitStack() as es:
        S = Sched(nc, es)

        def sb(name, shape, dt, stack=es):
            return stack.enter_context(nc.sbuf_tensor(name, shape, dt))

        def ps(name, shape, dt, stack=es):
            return stack.enter_context(nc.psum_tensor(name, shape, dt))

        idf = sb("idf", [128, 128], F32)
        idb = sb("idb", [128, 128], BF16)
        R_id = Res("id")
        S.dma("sp", lambda e: e.dma_start(out=idf[:], in_=ident), "c_id", writes=[R_id])
        S.op("dve", lambda e: e.tensor_copy(out=idb[:], in_=idf[:]), reads=[R_id], writes=[R_id])

        ones_f = sb("ones_f", [128, 128], F32)
        S.op("pool", lambda e: e.memset(ones_f[:], 1.0), writes=[R_id])
        ropc = sb("ropc", [128, 17, 8], F32)
        rops = sb("rops", [128, 17, 8], F32)
        R_rope = Res("rope")
        for tbl, src, key in ((ropc, ropec, "c_rc"), (rops, ropes, "c_rs")):
            S.dma("sp", lambda e: e.dma_start(out=tbl[:, 0:16, :],
                                              in_=src[0:SEQ, :].rearrange("(i p) e -> p i e", p=128)),
                  key, writes=[R_rope])
            S.dma("sp", lambda e: e.dma_start(out=tbl[0:NSC, 16, :], in_=src[SEQ:NCOL, :]), key, writes=[R_rope])


        mixT = sb("mixT", [128, 8, NCOL], BF16)
        R_mix = [Res("mix%d" % c) for c in range(8)]
        vS = sb("vS", [NSC, 512], BF16)
        R_vS = Res("vS")
        ksf = sb("ksf", [NSC, 512], F32)
        qsf = sb("qsf", [NSC, 512], F32)
        R_ksf, R_qsf = Res("ksf"), Res("qsf")
        pA = es.enter_context(ExitStack())
        qT = sb("qT", [128, 4, NCOL], BF16, pA)
        kT = sb("kT", [128, 4, NCOL], BF16, pA)
        R_qT = [Res("qT%d" % i) for i in range(17)]
        R_kT = [Res("kT%d" % i) for i in range(17)]
        vA = sb("vA", [128, 16, 4, 129], BF16, pA)
        R_vA = [Res("vA%d" % i) for i in range(16)]
        uT = sb("uT", [128, 4, NCOL], BF16, pA)
        R_uT = [Res("uT%d" % i) for i in range(5)]
        rr = sb("rr", [128, 17, 4], F32, pA)
        R_rr = [Res("rr%d" % i) for i in range(17)]

        pw = pA.enter_context(ExitStack())
        hnT = sb("hnT", [128, 8, NCOL], BF16, pw)
        R_hnT = [Res("hnT%d" % i) for i in range(17)]
        winb = sb("winb", [128, 8, DPROJ], BF16, pw)
        R_winb = [Res("winb")] * 8
        with ExitStack() as p12:
            xt = [sb("xt%d" % j, [128, D], F32, p12) for j in range(2)]
            R_xt = [Res("xt%d" % j) for j in range(2)]
            xb = [sb("xb%d" % j, [128, D], BF16, p12) for j in range(2)]
            R_xb = [Res("xb%d" % j) for j in range(2)]
            junk = sb("junk", [128, D], BF16, p12)
            R_junk = Res("junk")
            gmx = sb("gmx", [128, D], F32, p12)
            R_gmx = Res("gmx")
            S.dma("sp", lambda e: e.dma_start(out=gmx[:], in_=AP(nmix.tensor, 0, [[0, 128], [1, D]])), "c_nm", writes=[R_gmx])
            ptr = [ps("ptr%d" % j, [128, 8, 128], BF16, p12) for j in range(2)]
            R_ptr = [Res("ptr%d" % j) for j in range(2)]

            for kc in range(8):
                S.dma("pool", lambda e: e.dma_start(out=winb[:, kc, :], in_=w_in[kc * 128:(kc + 1) * 128, :]),
                      "winb", writes=[R_winb[kc]])

            for i, (c0, n) in enumerate(TILES):
                j = i % 2
                src = xp[c0:c0 + n, :] if i < 16 else xs
                S.dma("sp", lambda e: e.dma_start(out=xt[j][:n, :], in_=src), "xt%d" % j, writes=[R_xt[j]])
                S.op("act", lambda e: e.activation(out=junk[:n, :], in_=xt[j][:n, :], func=AF.Square,
                                                   accum_out=rr[:n, i, 0:1]),
                     reads=[R_xt[j]], writes=[R_junk, R_rr[i]])
                S.op("dve", lambda e: e.tensor_scalar(out=rr[:n, i, 1:2], in0=rr[:n, i, 0:1], scalar1=1.0 / D,
                                                      scalar2=EPS, op0=ALU.mult, op1=ALU.add),
                     reads=[R_rr[i]], writes=[R_rr[i]])
                S.op("act", lambda e: e.sqrt(out=rr[:n, i, 2:3], in_=rr[:n, i, 1:2]), reads=[R_rr[i]], writes=[R_rr[i]])
                S.op("dve", lambda e: e.reciprocal(out=rr[:n, i, 3:4], in_=rr[:n, i, 2:3]),
                     reads=[R_rr[i]], writes=[R_rr[i]])
                S.op("dve", lambda e: e.scalar_tensor_tensor(out=xb[j][:n, :], in0=xt[j][:n, :], scalar=rr[:n, i, 3:4], in1=gmx[:n, :],
                                                             op0=ALU.mult, op1=ALU.mult),
                     reads=[R_xt[j], R_rr[i], R_gmx], writes=[R_xb[j]])
                for kc in range(8):
                    S.op("pe", lambda e: e.transpose(out=ptr[j][:, kc, :n], in_=xb[j][:n, kc * 128:(kc + 1) * 128],
                                                     identity=idb[:n, :n]),
                         reads=[R_xb[j], R_id], writes=[R_ptr[j]])
                S.op("dve", lambda e: e.tensor_copy(out=hnT[:, :, c0:c0 + n], in_=ptr[j][:, :, :n]),
                     reads=[R_ptr[j]], writes=[R_hnT[i]])

        S.barrier()
        if stage <= 0:
            S.finish("sp")
            return nc
        with ExitStack() as p2:
            pq = [ps("pq%d" % j, [128, 512], F32, p2) for j in range(2)]
            pk = [ps("pk%d" % j, [128, 512], F32, p2) for j in range(2)]
            pv = [ps("pv%d" % j, [128, 512], F32, p2) for j in range(2)]
            R_pq = [Res("pq%d" % j) for j in range(2)]
            R_pk = [Res("pk%d" % j) for j in range(2)]
            R_pv = [Res("pv%d" % j) for j in range(2)]
            ptq = ps("ptq", [128, 2, 4, 128], BF16, p2)
            R_ptq = [Res("ptq"), Res("ptk")]
            qf = [sb("qf%d" % j, [128, 512], F32, p2) for j in range(2)]
            kf = [sb("kf%d" % j, [128, 512], F32, p2) for j in range(2)]
            vf = [sb("vf%d" % j, [128, 512], F32, p2) for j in range(2)]
            R_qf = [Res("qf%d" % j) for j in range(2)]
            R_kf = [Res("kf%d" % j) for j in range(2)]
            R_vf = [Res("vf%d" % j) for j in range(2)]
            qb = sb("qb", [128, 512], BF16, p2)
            kb = sb("kb", [128, 512], BF16, p2)
            R_qb, R_kb = Res("qb"), Res("kb")
            rtmp = sb("rtmp", [128, 4, 8, 8], F32, p2)
            R_rtmp = Res("rtmp")

            S.op("pool", lambda e: e.memset(vA[:, :, :, 128:129], 1.0), writes=R_vA)

            def rope(buf, n, i, R_buf):
                x1 = AP(buf, 0, [[512, n], [64, 8], [1, 8]])
                x2 = AP(buf, 8, [[512, n], [64, 8], [1, 8]])
                cs = AP(ropc, i * 8, [[17 * 8, n], [0, 8], [1, 8]])
                sn = AP(rops, i * 8, [[17 * 8, n], [0, 8], [1, 8]])
                for t, (a, b) in enumerate(((x1, cs), (x2, sn), (x2, cs), (x1, sn))):
                    S.op("dve", lambda e: e.tensor_tensor(out=rtmp[:n, t], in0=a, in1=b, op=ALU.mult),
                         reads=[R_buf, R_rope], writes=[R_rtmp])
                S.op("dve", lambda e: e.tensor_tensor(out=x1, in0=rtmp[:n, 0], in1=rtmp[:n, 1], op=ALU.subtract),
                     reads=[R_rtmp], writes=[R_buf])
                S.op("dve", lambda e: e.tensor_tensor(out=x2, in0=rtmp[:n, 2], in1=rtmp[:n, 3], op=ALU.add),
                     reads=[R_rtmp], writes=[R_buf])

            for i, (c0, n) in enumerate(TILES):
                j = i % 2
                for cb, (pX, R_pX) in enumerate(((pq, R_pq), (pk, R_pk), (pv, R_pv))):
                    for kc in range(8):
                        S.op("pe", lambda e: e.matmul(pX[j][:n, :], lhsT=hnT[:, kc, c0:c0 + n],
                                                      rhs=winb[:, kc, cb * 512:(cb + 1) * 512],
                                                      start=(kc == 0), stop=(kc == 7)),
                             reads=[R_hnT[i], R_winb[kc]], writes=[R_pX[j]])
                S.op("act", lambda e: e.activation(out=vf[j][:n, :], in_=pv[j][:n, :], func=AF.Copy),
                     reads=[R_pv[j]], writes=[R_vf[j]])
                if i < 16:
                    S.dma("sp", lambda e: e.dma_start(out=o_vp[c0:c0 + n, :], in_=vf[j][:n, :]), "vf%d" % j,
                          reads=[R_vf[j]])
                    S.op("pool", lambda e: e.tensor_copy(out=vA[:, i, :, 0:128],
                                                         in_=vf[j][:, :].rearrange("p (h e) -> p h e", h=4)),
                         reads=[R_vf[j]], writes=[R_vA[i]])
                else:
                    S.dma("sp", lambda e: e.dma_start(out=o_vs, in_=vf[j][:n, :]), "vf%d" % j, reads=[R_vf[j]])
                    S.op("pool", lambda e: e.tensor_copy(out=vS[:, :], in_=vf[j][:n, :]),
                         reads=[R_vf[j]], writes=[R_vS])
                if stage < 2:
                    continue
                S.op("dve", lambda e: e.tensor_copy(out=qf[j][:n, :], in_=pq[j][:n, :]),
                     reads=[R_pq[j]], writes=[R_qf[j]])
                rope(qf[j], n, i, R_qf[j])
                S.op("act", lambda e: e.activation(out=qb[:n, :], in_=qf[j][:n, :], func=AF.Copy, scale=0.125),
                     reads=[R_qf[j]], writes=[R_qb])
                if i == 16:
                    S.op("pool", lambda e: e.tensor_copy(out=qsf[:, :], in_=qf[j][:n, :]),
                         reads=[R_qf[j]], writes=[R_qsf])
                for h in range(4):
                    S.op("pe", lambda e: e.transpose(out=ptq[:, 0, h, :n], in_=qb[:n, h * 128:(h + 1) * 128],
                                                     identity=idb[:n, :n]),
                         reads=[R_qb, R_id], writes=[R_ptq[0]])
                S.op("act", lambda e: e.activation(out=qT[:, :, c0:c0 + n], in_=ptq[:, 0, :, :n], func=AF.Copy),
                     reads=[R_ptq[0]], writes=[R_qT[i]])
                if stage < 3:
                    continue
                S.op("dve", lambda e: e.tensor_copy(out=kf[j][:n, :], in_=pk[j][:n, :]),
                     reads=[R_pk[j]], writes=[R_kf[j]])
                rope(kf[j], n, i, R_kf[j])
                if i < 16:
                    S.dma("sp", lambda e: e.dma_start(out=o_kp[c0:c0 + n, :], in_=kf[j][:n, :]), "kf%d" % j,
                          reads=[R_kf[j]])
                else:
                    S.dma("sp", lambda e: e.dma_start(out=o_ks, in_=kf[j][:n, :]), "kf%d" % j, reads=[R_kf[j]])
                    S.op("pool", lambda e: e.tensor_copy(out=ksf[:, :], in_=kf[j][:n, :]),
                         reads=[R_kf[j]], writes=[R_ksf])
                S.op("act", lambda e: e.activation(out=kb[:n, :], in_=kf[j][:n, :], func=AF.Copy),
                     reads=[R_kf[j]], writes=[R_kb])
                for h in range(4):
                    S.op("pe", lambda e: e.transpose(out=ptq[:, 1, h, :n], in_=kb[:n, h * 128:(h + 1) * 128],
                                                     identity=idb[:n, :n]),
                         reads=[R_kb, R_id], writes=[R_ptq[1]])
                S.op("act", lambda e: e.activation(out=kT[:, :, c0:c0 + n], in_=ptq[:, 1, :, :n], func=AF.Copy),
                     reads=[R_ptq[1]], writes=[R_kT[i]])

            for cu in range(4 if stage >= 4 else 0):
                for bi, (c0, nb) in enumerate(BLOCKS):
                    j = (cu * 5 + bi) % 2
                    for kc in range(8):
                        S.op("pe", lambda e: e.matmul(pq[j][:, :nb], lhsT=winb[:, kc, 1536 + cu * 128:1536 + (cu + 1) * 128],
                                                      rhs=hnT[:, kc, c0:c0 + nb], start=(kc == 0), stop=(kc == 7)),
                             reads=R_hnT + [R_winb[kc]], writes=[R_pq[j]])
                    S.op("act", lambda e: e.activation(out=uT[:, cu, c0:c0 + nb], in_=pq[j][:, :nb], func=AF.Copy),
                         reads=[R_pq[j]], writes=[R_uT[bi]])

        pw.close()
        S.barrier()
        if stage >= 5:
            ssm_phase(nc, S, sb, ps, locals())
        if stage >= 6:
            attn_phase(nc, S, sb, ps, locals())
        pA.close()
        S.barrier()
        if stage >= 7:
            ffn_phase(nc, S, sb, ps, locals())

        S.finish("sp")
        print("insts", S.n_inst, "waits", S.n_wait)
    return nc


def _rope_tables():
    half = 8
    inv = (500000.0 ** (-np.arange(0, 16, 2, dtype=np.float32) / np.float32(16))).astype(np.float32)
    pos = np.concatenate([np.arange(SEQ), np.full(NSC, PAST)]).astype(np.float32)
    ang = (pos[:, None] * inv[None, :]).astype(np.float32)
    return np.cos(ang).astype(np.float32), np.sin(ang).astype(np.float32)


def _ssm_maps(inp, c):
    f = np.float32
    a_re, a_im = inp["ssm_a_re"][0], inp["ssm_a_im"][0]
    ldt = inp["ssm_log_dt"][0]
    b_re, b_im = inp["ssm_b_re"][0], inp["ssm_b_im"][0]
    c_re, c_im = inp["ssm_c_re"][0], inp["ssm_c_im"][0]
    bcol = [np.zeros((128, 16, 128), f) for _ in range(2)]
    cpad = [np.zeros((128, 16, 128), f) for _ in range(2)]
    for g in range(32):
        pr, g2 = g // 2, g % 2
        pl = pr % 4
        r0 = pl * 32 + g2 * 16
        for k, (bb, cc) in enumerate(((b_re, c_re), (b_im, c_im))):
            bcol[k][g2 * 64:(g2 + 1) * 64, pr, r0:r0 + 16] = bb[g]
            cpad[k][g2 * 64:(g2 + 1) * 64, pr, r0:r0 + 16] = cc[g].T
    col = lambda a: np.ascontiguousarray(a.reshape(16, 128).T)
    h0 = lambda a: np.ascontiguousarray(a[0, NSC * c:NSC * (c + 1)].reshape(NSC, 16, 128).transpose(2, 1, 0))
    return dict(
        bcol_re=bcol[0], bcol_im=bcol[1], cpad_re=cpad[0], cpad_im=cpad[1],
        acol_re=col(a_re), acol_im=col(a_im), ldt_col=col(np.repeat(ldt[:, None], 64, axis=1)),
        dcol=np.ascontiguousarray(inp["ssm_d"][0].reshape(4, 128).T),
        h0c_re=h0(inp["state_ssm_re"]), h0c_im=h0(inp["state_ssm_im"]),
        w_glu=np.ascontiguousarray(inp["w_glu"][0]),
        trow1=np.ascontiguousarray(np.broadcast_to(np.arange(1, 513, dtype=f)[None, :], (128, 512))),
    )


def _attn_maps(inp, c, dev_pool=None):
    f = np.float32
    lamv = np.concatenate([inp["lambda_q1"][0], inp["lambda_k1"][0], inp["lambda_q2"][0], inp["lambda_k2"][0]]).reshape(1, 256)
    kk, qq = np.meshgrid(np.arange(128), np.arange(128), indexing="ij")
    tri = (qq >= kk).astype(f)
    pt = inp["page_table"][NSC * c:NSC * (c + 1)]
    p = np.arange(128)
    ptrep = np.stack([pt[:, 4 * jj + p // 32] for jj in range(16)], axis=2).astype(np.int32)
    if dev_pool == "compact":
        used = pt.reshape(-1)
        ck = np.ascontiguousarray(inp["cache_k"][0][used]).reshape(-1, 2048)
        cv = np.ascontiguousarray(inp["cache_v"][0][used]).reshape(-1, 2048)
        pt = np.arange(used.size).reshape(pt.shape)
        ptrep = np.stack([pt[:, 4 * jj + p // 32] for jj in range(16)], axis=2).astype(np.int32)
    elif dev_pool is not None:
        ck = np.zeros((dev_pool * 32, 2048), f)
        cv = np.zeros((dev_pool * 32, 2048), f)
        ptrep = ptrep % dev_pool
    else:
        ck = inp["cache_k"][0].reshape(-1, 2048)
        cv = inp["cache_v"][0].reshape(-1, 2048)
    selq = np.zeros((NSC, NSC, 128), f)
    for s_ in range(NSC):
        selq[s_, s_, :] = 1.0
    m0 = np.zeros((128, 1), f)
    m0[0, 0] = 1.0
    return dict(lamv=lamv.astype(f), gsub=inp["subln_g"][0].reshape(1, 128), trimask=tri, cache_k=ck, cache_v=cv,
                ptrep=np.ascontiguousarray(ptrep), poff=np.ascontiguousarray(np.broadcast_to((p % 32).astype(np.float32)[:, None], (128, NSC * 16))), selq=selq, mask0=m0)


def _ffn_maps(inp, c):
    sc = inp["state_conv"][0, NSC * c:NSC * (c + 1)]
    return dict(
        w_o=np.ascontiguousarray(inp["w_o"][0]),
        nffn=np.ascontiguousarray(inp["norm_ffn"][0].reshape(1, D)),
        w_up=np.ascontiguousarray(inp["w_up"][0]),
        convw=np.ascontiguousarray(inp["conv_w"][0].reshape(3, NFC, 128).transpose(2, 0, 1)),
        convb=np.ascontiguousarray(inp["conv_b"][0].reshape(NFC, 128).T),
        w_down=np.ascontiguousarray(inp["w_down"][0]),
        fnorm=inp["final_norm"].reshape(1, D),
        sconv=np.ascontiguousarray(sc.reshape(NSC * 2, DFF)),
        sconv1=np.ascontiguousarray(sc[:, 1, :]),
    )


def make_in_maps(inp, cores, dev_pool=None):
    cosT, sinT = _rope_tables()
    ident = np.eye(128, dtype=np.float32)
    maps = []
    for c in cores:
        m = dict(
            xp=np.ascontiguousarray(inp["x_prompt"][c]),
            xs=np.ascontiguousarray(inp["x_sample"][NSC * c:NSC * (c + 1), 0, :]),
            w_in=np.ascontiguousarray(inp["w_in"][0]),
            nmix=np.ascontiguousarray(inp["norm_mix"][0].reshape(1, D)),
            ropec=cosT, ropes=sinT, ident=ident,
        )
        m.update(_ssm_maps(inp, c))
        m.update(_attn_maps(inp, c, dev_pool))
        m.update(_ffn_maps(inp, c))
        maps.append(m)
    return maps


def kernel(**inp):
    inp = {k: np.asarray(v) for k, v in inp.items()}
    n_pool = inp["cache_k"].shape[1]
    nc = build_program(n_pool)
    cores = list(range(8))
    res = run_bass_kernel_spmd(nc, make_in_maps(inp, cores), core_ids=cores)
    r = res.results
    f = np.float32
    cat = lambda key: np.stack([np.asarray(r[c][key], dtype=f) for c in cores])
    y_prompt = cat("o_yp").reshape(8, SEQ, D)
    y_sample = cat("o_ys").reshape(32, 1, D)
    k_prompt = cat("o_kp").reshape(1, 8, SEQ, 4, 128)
    v_prompt = cat("o_vp").reshape(1, 8, SEQ, 4, 128)
    hre_p = cat("o_hre").reshape(1, 8, 32, 64)
    him_p = cat("o_him").reshape(1, 8, 32, 64)
    conv_p = cat("o_cp").reshape(1, 8, 2, DFF)
    k_sample = cat("o_ks").reshape(1, 32, 1, 4, 128)
    v_sample = cat("o_vs").reshape(1, 32, 1, 4, 128)
    hre_s = cat("o_hres").reshape(1, 32, 32, 64)
    him_s = cat("o_hims").reshape(1, 32, 32, 64)
    conv_s = cat("o_cs").reshape(1, 32, 2, DFF)
    return (y_prompt, y_sample, k_prompt, v_prompt, hre_p, him_p, conv_p,
            k_sample, v_sample, hre_s, him_s, conv_s)
```
